# Optimizing a Trainium2 kernel written in Bass

```python
import math
import jax
import jax.numpy as jnp
from jax import lax
import numpy as np

D_MODEL = 1024
BATCH = 8
SEQ = 4096
DEPTH = 2

RWKV_HEADS = 8
RWKV_HEAD_DIM = 64
RWKV_DIM = RWKV_HEADS * RWKV_HEAD_DIM
RWKV_W_LORA = 64
RWKV_A_LORA = 64
RWKV_G_LORA = 128
RWKV_GN_EPS = 64e-5
MLA_HEADS = 8
MLA_Q_LORA = 256
MLA_KV_LORA = 128
MLA_NOPE_DIM = 64
MLA_ROPE_DIM = 32
MLA_V_DIM = 64
MLA_DIM = MLA_HEADS * MLA_V_DIM
ROPE_BASE = 10000.0
DIFF_HEADS = 4
DIFF_QK_DIM = 64
DIFF_V_DIM = 2 * DIFF_QK_DIM
DIFF_DIM = DIFF_HEADS * DIFF_V_DIM
REL_BUCKETS = 32
REL_MAX_DISTANCE = 128
D_FF = 2816
CONV_WIDTH = 3
N_BRANCHES = 3
Q_BLOCK = 128
NORM_EPS = 1e-6

RWKV_COLS = 3 * RWKV_DIM + RWKV_W_LORA + RWKV_A_LORA + RWKV_G_LORA
MLA_COLS = MLA_Q_LORA + MLA_KV_LORA + MLA_ROPE_DIM
DIFF_COLS = 2 * (DIFF_HEADS * 2 * DIFF_QK_DIM) + DIFF_DIM
GATE_COLS = N_BRANCHES * D_MODEL
IN_COLS = RWKV_COLS + MLA_COLS + DIFF_COLS + GATE_COLS
IN_SPLITS = (RWKV_COLS, RWKV_COLS + MLA_COLS, RWKV_COLS + MLA_COLS + DIFF_COLS)
RWKV_SPLITS = (RWKV_DIM, 2 * RWKV_DIM, 3 * RWKV_DIM, 3 * RWKV_DIM + RWKV_W_LORA,
               3 * RWKV_DIM + RWKV_W_LORA + RWKV_A_LORA)

kernel_name = 'hybrid_rwkv7_mla_diffattn_convffn'


def _rmsnorm(x, g, eps=NORM_EPS):
    xf = x.astype(jnp.float32)
    y = xf * lax.rsqrt(jnp.mean(xf * xf, axis=-1, keepdims=True) + eps)
    return (y * g.astype(jnp.float32)).astype(x.dtype)


def _token_shift(p):
    return jnp.pad(p, ((0, 0), (1, 0), (0, 0)))[:, :-1]


def _split_heads(t, n_heads):
    return t.reshape(t.shape[0], t.shape[1], n_heads, t.shape[-1] // n_heads)


def _rope_tables(positions, dtype):
    inv_freq = ROPE_BASE ** (-jnp.arange(0, MLA_ROPE_DIM, 2, dtype=jnp.float32) / MLA_ROPE_DIM)
    ang = positions.astype(jnp.float32)[..., None] * inv_freq
    return jnp.cos(ang).astype(dtype), jnp.sin(ang).astype(dtype)


def _apply_rope(x, cos, sin):
    x1, x2 = jnp.split(x, 2, axis=-1)
    return jnp.concatenate([x1 * cos - x2 * sin, x1 * sin + x2 * cos], axis=-1)


def _t5_bucket(dist):
    n = jnp.maximum(dist, 0)
    max_exact = REL_BUCKETS // 2
    nf = jnp.maximum(n, 1).astype(jnp.float32)
    large = max_exact + (jnp.log(nf / max_exact) / math.log(REL_MAX_DISTANCE / max_exact)
                         * (REL_BUCKETS - max_exact)).astype(jnp.int32)
    large = jnp.minimum(large, REL_BUCKETS - 1)
    return jnp.where(n < max_exact, n, large)


def _masked_softmax(logits, mask):
    logits = jnp.where(mask, logits.astype(jnp.float32), jnp.finfo(jnp.float32).min)
    return jax.nn.softmax(logits, axis=-1)


def _causal_blocks(block_fn, seq_len):
    outs = []
    for i in range(seq_len // Q_BLOCK):
        q0, q1 = i * Q_BLOCK, (i + 1) * Q_BLOCK
        mask = jnp.arange(q0, q1)[:, None] >= jnp.arange(q1)[None, :]
        outs.append(block_fn(q0, q1, mask))
    return jnp.concatenate(outs, axis=2)


def _rwkv7_mix(p, mu, w0, w2, a0, a2, g2, k_k, k_a, r_k, ln_w, ln_b):
    B, S, _ = p.shape
    f32 = jnp.float32
    p = p + (_token_shift(p) - p) * mu
    r, k, v, pw, pa, pg = jnp.split(p, RWKV_SPLITS, axis=-1)
    w_log = -jax.nn.softplus(-(w0 + jnp.tanh(pw) @ w2)) - 0.5
    decay = jnp.exp(-jnp.exp(w_log.astype(f32)))
    a = jax.nn.sigmoid(a0 + pa @ a2)
    g = jax.nn.sigmoid(pg) @ g2
    kk = _split_heads(k * k_k, RWKV_HEADS).astype(f32)
    kk = kk / jnp.maximum(jnp.linalg.norm(kk, axis=-1, keepdims=True), 1e-12)
    k = k * (1 + (a - 1) * k_a)
    r_h, k_h, v_h = _split_heads(r, RWKV_HEADS), _split_heads(k, RWKV_HEADS), _split_heads(v, RWKV_HEADS)
    b_h = kk * _split_heads(a, RWKV_HEADS).astype(f32)
    xs = tuple(jnp.moveaxis(t.astype(f32), 1, 0)
               for t in (r_h, _split_heads(decay, RWKV_HEADS), k_h, v_h, kk, b_h))

    def step(state, inp):
        r_t, w_t, k_t, v_t, kk_t, b_t = inp
        sa = jnp.einsum('bhij,bhj->bhi', state, -kk_t)
        state = (state * w_t[:, :, None, :] + sa[..., None] * b_t[:, :, None, :]
                 + v_t[..., None] * k_t[:, :, None, :])
        return state, jnp.einsum('bhij,bhj->bhi', state, r_t)

    state0 = jnp.zeros((B, RWKV_HEADS, RWKV_HEAD_DIM, RWKV_HEAD_DIM), f32)
    _, y = lax.scan(step, state0, xs)
    y = jnp.moveaxis(y, 0, 1)
    mean = jnp.mean(y, axis=-1, keepdims=True)
    var = jnp.mean(jnp.square(y - mean), axis=-1, keepdims=True)
    y = ((y - mean) * lax.rsqrt(var + RWKV_GN_EPS)).reshape(B, S, RWKV_DIM) * ln_w + ln_b
    bonus = (jnp.sum(r_h * k_h * r_k, axis=-1, keepdims=True) * v_h).reshape(B, S, RWKV_DIM)
    y = (y + bonus.astype(f32)) * g.astype(f32)
    return y.astype(p.dtype)


def _mla_mix(p, cos, sin, q_norm, w_uq, kv_norm, w_ukv):
    B, S, _ = p.shape
    c_q, c_kv, k_rope = jnp.split(p, (MLA_Q_LORA, MLA_Q_LORA + MLA_KV_LORA), axis=-1)
    q = (_rmsnorm(c_q, q_norm) @ w_uq).reshape(B, S, MLA_HEADS, MLA_NOPE_DIM + MLA_ROPE_DIM)
    q = q.transpose(0, 2, 1, 3)
    q_nope = q[..., :MLA_NOPE_DIM]
    q_rope = _apply_rope(q[..., MLA_NOPE_DIM:], cos[:, None], sin[:, None])
    kv = (_rmsnorm(c_kv, kv_norm) @ w_ukv).reshape(B, S, MLA_HEADS, MLA_NOPE_DIM + MLA_V_DIM)
    kv = kv.transpose(0, 2, 1, 3)
    k_nope, v = kv[..., :MLA_NOPE_DIM], kv[..., MLA_NOPE_DIM:]
    k_rope = _apply_rope(k_rope, cos, sin)
    scale = (MLA_NOPE_DIM + MLA_ROPE_DIM) ** -0.5

    def block(q0, q1, mask):
        logits = (jnp.einsum('bhqd,bhkd->bhqk', q_nope[:, :, q0:q1], k_nope[:, :, :q1])
                  + jnp.einsum('bhqd,bkd->bhqk', q_rope[:, :, q0:q1], k_rope[:, :q1]))
        probs = _masked_softmax(logits * scale, mask)
        return jnp.einsum('bhqk,bhkd->bhqd', probs.astype(v.dtype), v[:, :, :q1])

    o = _causal_blocks(block, S)
    return o.transpose(0, 2, 1, 3).reshape(B, S, MLA_DIM)


def _diff_mix(p, positions, rel_bias, lam, subln, layer_idx):
    B, S, _ = p.shape
    qk_w = DIFF_HEADS * 2 * DIFF_QK_DIM
    q, k, v = jnp.split(p, (qk_w, 2 * qk_w), axis=-1)
    q = q.reshape(B, S, DIFF_HEADS, 2, DIFF_QK_DIM).transpose(0, 2, 3, 1, 4)
    k = k.reshape(B, S, DIFF_HEADS, 2, DIFF_QK_DIM).transpose(0, 2, 3, 1, 4)
    v = v.reshape(B, S, DIFF_HEADS, DIFF_V_DIM).transpose(0, 2, 1, 3)
    lambda_init = 0.8 - 0.6 * math.exp(-0.3 * layer_idx)
    lam = lam.astype(jnp.float32)
    lam_full = jnp.exp(jnp.sum(lam[0] * lam[1])) - jnp.exp(jnp.sum(lam[2] * lam[3])) + lambda_init
    scale = DIFF_QK_DIM ** -0.5

    def block(q0, q1, mask):
        dist = positions[:, q0:q1, None] - positions[:, None, :q1]
        bias = rel_bias[_t5_bucket(dist)].transpose(0, 3, 1, 2)[:, :, None]
        logits = jnp.einsum('bhmqd,bhmkd->bhmqk', q[:, :, :, q0:q1], k[:, :, :, :q1]) * scale + bias
        probs = _masked_softmax(logits, mask)
        attn = probs[:, :, 0] - lam_full * probs[:, :, 1]
        return jnp.einsum('bhqk,bhkd->bhqd', attn.astype(v.dtype), v[:, :, :q1])

    o = _causal_blocks(block, S)
    o = _rmsnorm(o, subln, eps=1e-5) * (1 - lambda_init)
    return o.transpose(0, 2, 1, 3).reshape(B, S, DIFF_DIM)


def _conv_ffn(h, w_up, conv_w, conv_b, w_down):
    S = h.shape[1]
    u = h @ w_up
    up = jnp.pad(u, ((0, 0), (CONV_WIDTH - 1, 0), (0, 0)))
    u = sum(conv_w[j] * up[:, j:j + S] for j in range(CONV_WIDTH)) + conv_b
    gate, val = jnp.split(u, 2, axis=-1)
    return (jax.nn.silu(gate) * val) @ w_down


def setup_inputs(seed: int = 0) -> dict:
    key = jax.random.key(seed)
    ks = iter(jax.random.split(key, 40))
    f32 = jnp.float32
    L = DEPTH

    def nrm(shape, scale):
        return jax.random.normal(next(ks), shape, f32) * scale

    def gain(shape):
        return 1.0 + nrm(shape, 0.05)

    offset = jax.random.randint(next(ks), (BATCH, 1), 0, 1024, dtype=jnp.int32)
    positions = offset + jnp.arange(SEQ, dtype=jnp.int32)[None, :]
    return {
        'x': nrm((BATCH, SEQ, D_MODEL), 1.0),
        'positions': positions,
        'rel_bias': nrm((REL_BUCKETS, DIFF_HEADS), 0.5),
        'norm_mix': gain((L, D_MODEL)),
        'w_in': nrm((L, D_MODEL, IN_COLS), D_MODEL ** -0.5),
        'b_gate': nrm((L, GATE_COLS), 0.1),
        'rwkv_mu': jax.random.uniform(next(ks), (L, RWKV_COLS), f32),
        'rwkv_w0': jax.random.uniform(next(ks), (L, RWKV_DIM), f32, minval=-6.0, maxval=-1.0),
        'rwkv_w2': nrm((L, RWKV_W_LORA, RWKV_DIM), 0.1 * RWKV_W_LORA ** -0.5),
        'rwkv_a0': nrm((L, RWKV_DIM), 0.1),
        'rwkv_a2': nrm((L, RWKV_A_LORA, RWKV_DIM), RWKV_A_LORA ** -0.5),
        'rwkv_g2': nrm((L, RWKV_G_LORA, RWKV_DIM), RWKV_G_LORA ** -0.5),
        'rwkv_k_k': 0.85 + nrm((L, RWKV_DIM), 0.05),
        'rwkv_k_a': gain((L, RWKV_DIM)),
        'rwkv_r_k': nrm((L, RWKV_HEADS, RWKV_HEAD_DIM), 0.1),
        'rwkv_ln_w': gain((L, RWKV_DIM)),
        'rwkv_ln_b': nrm((L, RWKV_DIM), 0.02),
        'mla_q_norm': gain((L, MLA_Q_LORA)),
        'mla_w_uq': nrm((L, MLA_Q_LORA, MLA_HEADS * (MLA_NOPE_DIM + MLA_ROPE_DIM)), MLA_Q_LORA ** -0.5),
        'mla_kv_norm': gain((L, MLA_KV_LORA)),
        'mla_w_ukv': nrm((L, MLA_KV_LORA, MLA_HEADS * (MLA_NOPE_DIM + MLA_V_DIM)), MLA_KV_LORA ** -0.5),
        'diff_lambda': nrm((L, 4, DIFF_QK_DIM), 0.1),
        'diff_subln': gain((L, DIFF_V_DIM)),
        'w_branch_rwkv': nrm((L, RWKV_DIM, D_MODEL), RWKV_DIM ** -0.5),
        'w_branch_mla': nrm((L, MLA_DIM, D_MODEL), MLA_DIM ** -0.5),
        'w_branch_diff': nrm((L, DIFF_DIM, D_MODEL), DIFF_DIM ** -0.5),
        'w_o': nrm((L, D_MODEL, D_MODEL), D_MODEL ** -0.5),
        'norm_ffn': gain((L, D_MODEL)),
        'ffn_w_up': nrm((L, D_MODEL, 2 * D_FF), D_MODEL ** -0.5),
        'ffn_conv_w': nrm((L, CONV_WIDTH, 2 * D_FF), CONV_WIDTH ** -0.5),
        'ffn_conv_b': nrm((L, 2 * D_FF), 0.02),
        'ffn_w_down': nrm((L, D_FF, D_MODEL), D_FF ** -0.5),
        'norm_final': gain((D_MODEL,)),
    }


def reference(x, positions, rel_bias, norm_mix, w_in, b_gate, rwkv_mu, rwkv_w0, rwkv_w2, rwkv_a0,
              rwkv_a2, rwkv_g2, rwkv_k_k, rwkv_k_a, rwkv_r_k, rwkv_ln_w, rwkv_ln_b, mla_q_norm,
              mla_w_uq, mla_kv_norm, mla_w_ukv, diff_lambda, diff_subln, w_branch_rwkv, w_branch_mla,
              w_branch_diff, w_o, norm_ffn, ffn_w_up, ffn_conv_w, ffn_conv_b, ffn_w_down, norm_final):
    B, S, _ = x.shape
    cos, sin = _rope_tables(positions, x.dtype)
    for l in range(DEPTH):
        h = _rmsnorm(x, norm_mix[l])
        p = h @ w_in[l]
        p_rwkv, p_mla, p_diff, p_gate = jnp.split(p, IN_SPLITS, axis=-1)
        o_rwkv = _rwkv7_mix(p_rwkv, rwkv_mu[l], rwkv_w0[l], rwkv_w2[l], rwkv_a0[l], rwkv_a2[l],
                            rwkv_g2[l], rwkv_k_k[l], rwkv_k_a[l], rwkv_r_k[l], rwkv_ln_w[l], rwkv_ln_b[l])
        o_mla = _mla_mix(p_mla, cos, sin, mla_q_norm[l], mla_w_uq[l], mla_kv_norm[l], mla_w_ukv[l])
        o_diff = _diff_mix(p_diff, positions, rel_bias, diff_lambda[l], diff_subln[l], l)
        gates = jax.nn.sigmoid(p_gate + b_gate[l]).reshape(B, S, N_BRANCHES, D_MODEL)
        merged = (gates[:, :, 0] * (o_rwkv @ w_branch_rwkv[l])
                  + gates[:, :, 1] * (o_mla @ w_branch_mla[l])
                  + gates[:, :, 2] * (o_diff @ w_branch_diff[l]))
        x = x + merged @ w_o[l]
        x = x + _conv_ffn(_rmsnorm(x, norm_ffn[l]), ffn_w_up[l], ffn_conv_w[l], ffn_conv_b[l], ffn_w_down[l])
    return _rmsnorm(x, norm_final)
```

```python
import math
import os
import numpy as np
from contextlib import ExitStack
import concourse.bass as bass
import concourse.mybir as mybir
from concourse.bass_utils import run_bass_kernel_spmd

F32 = mybir.dt.float32
BF16 = mybir.dt.bfloat16
I32 = mybir.dt.int32
AF = mybir.ActivationFunctionType
ALU = mybir.AluOpType
AX = mybir.AxisListType

S = 4096
NT = 32
D = 1024
DEPTH = 2
DFF = 2816
INC = 6816
C_RW, C_ML, C_DF, C_GT = 0, 1792, 2208, 3744

EPOCH = 24000
NDSEM = 48


class Buf:
    __slots__ = ("w", "r", "name")

    def __init__(self, name=""):
        self.w = {}
        self.r = {}
        self.name = name


class EngState:
    def __init__(self, name):
        self.name = name
        self.ops = []
        self.sem = None
        self.cnt = 0
        self.wm = {}
        self.pending = False


class P:
    def __init__(self, nc, stack):
        self.nc = nc
        self.stack = stack
        self.sems = []
        self.eng = {k: EngState(k) for k in ("sync", "act", "dve", "pool", "pe")}
        self.dsem = []
        self.dval = []
        self.dnext = 0
        for i in range(NDSEM):
            self.dsem.append(self._newsem("d%d" % i))
            self.dval.append(0)
        for e in self.eng.values():
            e.sem = self._newsem(e.name + "0")
        self.nops = 0

    def _newsem(self, name):
        s = self.stack.enter_context(self.nc.semaphore(name))
        self.sems.append(s)
        return len(self.sems) - 1

    def buf(self, name=""):
        return Buf(name)

    def bufs(self, n, name=""):
        return [Buf(name + str(i)) for i in range(n)]

    def _need(self, es, waits, sem, val):
        if es.wm.get(sem, 0) >= val:
            return
        es.wm[sem] = val
        for i, (s, v) in enumerate(waits):
            if s == sem:
                waits[i] = (s, max(v, val))
                return
        waits.append((sem, val))

    def _deps(self, es, mysem, waits, reads, writes):
        for b in reads:
            for s, v in b.w.items():
                self._need(es, waits, s, v)
        skip_own = (es.name == "pe")
        for b in writes:
            for s, v in b.w.items():
                if s != mysem or not skip_own:
                    self._need(es, waits, s, v)
            for s, v in b.r.items():
                if s != mysem or not skip_own:
                    self._need(es, waits, s, v)

    def _mark(self, sem, val, reads, writes):
        for b in reads:
            if b.r.get(sem, 0) < val:
                b.r[sem] = val
        for b in writes:
            b.w = {sem: val}
            b.r = {}

    def op(self, eng, fn, reads=(), writes=(), sig=True):
        es = self.eng[eng]
        if es.cnt >= EPOCH and not es.pending:
            es.sem = self._newsem(es.name + str(len(self.sems)))
            es.cnt = 0
        waits = []
        self._deps(es, es.sem, waits, reads, writes)
        val = es.cnt + 1
        if sig:
            es.cnt = val
            es.pending = False
        else:
            es.pending = True
        es.ops.append((waits, fn, es.sem if sig else None, 1))
        self._mark(es.sem, val, reads, writes)
        self.nops += 1

    def dma(self, q, fn, reads=(), writes=()):
        es = self.eng[q]
        i = self.dnext
        self.dnext = (self.dnext + 1) % NDSEM
        sem = self.dsem[i]
        waits = []
        if self.dval[i] > 0:
            self._need(es, waits, sem, self.dval[i])
        self._deps(es, sem, waits, reads, writes)
        self.dval[i] += 16
        es.ops.append((waits, fn, sem, 16))
        self._mark(sem, self.dval[i], reads, writes)
        self.nops += 1

    def barrier(self):
        targets = []
        for e in self.eng.values():
            if e.cnt > 0:
                assert not e.pending
                targets.append((e.sem, e.cnt))
        for i in range(NDSEM):
            if self.dval[i] > 0:
                targets.append((self.dsem[i], self.dval[i]))
        for es in self.eng.values():
            waits = []
            for s, v in targets:
                if s != es.sem:
                    self._need(es, waits, s, v)
            if waits:
                es.ops.append((waits, None, None, 0))
        self.emit()

    def emit(self):
        nc = self.nc
        sems = self.sems
        engs = self.eng

        def run(e, es):
            for waits, fn, sem, inc in es.ops:
                for s, v in waits:
                    e.wait_ge(sems[s], v)
                if fn is None:
                    continue
                ins = fn(e)
                if sem is not None:
                    ins.then_inc(sems[sem], inc)

        if not any(es.ops for es in engs.values()):
            return
        with nc.Block() as block:
            @block.sync
            def _(e):
                run(e, engs["sync"])

            @block.scalar
            def _(e):
                run(e, engs["act"])

            @block.vector
            def _(e):
                run(e, engs["dve"])

            @block.gpsimd
            def _(e):
                run(e, engs["pool"])

            @block.tensor
            def _(e):
                run(e, engs["pe"])
        for es in engs.values():
            es.ops = []


VEC_ROWS = ["norm_mix", "rwkv_mu", "rwkv_w0", "rwkv_a0", "rwkv_k_k", "rwkv_k_a", "rwkv_r_k",
            "rwkv_ln_w", "rwkv_ln_b", "mla_q_norm", "mla_kv_norm", "norm_ffn", "b_gate"]
VEC_LEN = {"norm_mix": 1024, "rwkv_mu": 1792, "rwkv_w0": 512, "rwkv_a0": 512, "rwkv_k_k": 512, "rwkv_k_a": 512,
           "rwkv_r_k": 512, "rwkv_ln_w": 512, "rwkv_ln_b": 512, "mla_q_norm": 256, "mla_kv_norm": 128,
           "norm_ffn": 1024, "b_gate": 3072}
MATS = {"w_in": (1024, INC), "rwkv_w2": (64, 512), "rwkv_a2": (64, 512), "rwkv_g2": (128, 512),
        "mla_w_uq": (256, 768), "mla_w_ukv": (128, 1024), "w_branch_rwkv": (512, 1024), "w_branch_mla": (512, 1024),
        "w_branch_diff": (512, 1024), "w_o": (1024, 1024), "ffn_w_up": (1024, 2 * DFF), "ffn_w_down": (DFF, 1024)}


class K:
    pass


def build(dbg=None, feed=None, phases=None):
    dbg = dbg or set()
    feed = feed or set()
    nc = bass.Bass("TRN2", target_bir_lowering=False)
    g = K()
    g.nc = nc

    def din(name, shape, dt=F32):
        return nc.dram_tensor(name, list(shape), dt, kind="ExternalInput").ap()

    def dscr(name, shape, dt=F32):
        kind = "ExternalOutput" if name in dbg else ("ExternalInput" if name in feed else "Internal")
        return nc.dram_tensor(name, list(shape), dt, kind=kind).ap()

    x_in = din("x", [S, D])
    pos_tm = din("pos_tm", [128, NT], I32)
    pos_row = din("pos_row", [1, 256], I32)
    rel_bias = din("rel_bias", [1, 128])
    vec = {n: din(n, [DEPTH, VEC_LEN[n]]) for n in VEC_ROWS}
    mats = {n: din(n, [DEPTH, MATS[n][0], MATS[n][1]]) for n in MATS}
    b_gate_cm = din("b_gate_cm", [DEPTH, 128, 24])
    conv_w_cm = din("conv_w_cm", [DEPTH, 128, 3, 44])
    conv_b_cm = din("conv_b_cm", [DEPTH, 128, 44])
    subln_cm = din("subln_cm", [DEPTH, 128, 1])
    diff_lambda = din("diff_lambda", [DEPTH, 1, 256])
    norm_final = din("norm_final", [1, D])
    out = nc.dram_tensor("out", [S, D], F32, kind="ExternalOutput").ap()

    xres = dscr("xres", [S, D])
    prw = dscr("prw", [S, 1792])
    pml = dscr("pml", [S, 416])
    qkT = dscr("qkT", [1024, S], BF16)
    vdf = dscr("vdf", [S, 512], BF16)
    gT = dscr("gT", [3072, S], BF16)
    oT = {n: dscr("oT_" + n, [512, S], BF16) for n in ("rwkv", "mla", "diff")}

    with ExitStack() as top:
        p = P(nc, top)
        g.p = p

        uid = [0]

        def sb(st, name, shape, dt):
            uid[0] += 1
            return st.enter_context(nc.sbuf_tensor("%s_u%d" % (name, uid[0]), list(shape), dt))

        def ps(st, name, shape, dt=F32):
            uid[0] += 1
            return st.enter_context(nc.psum_tensor("%s_u%d" % (name, uid[0]), list(shape), dt))

        B_x = p.bufs(NT, "x")
        B_prw = p.bufs(NT, "prw")
        B_pml = p.bufs(NT, "pml")
        B_qkT = p.bufs(8, "qkT")
        B_vdf = p.bufs(NT, "vdf")
        B_gT = p.bufs(8, "gT")
        B_oT = {n: p.bufs(8, "oT" + n) for n in oT}
        B_out = p.buf("out")

        ident = sb(top, "ident", [128, 128], BF16)
        b_ident = p.buf()
        p.op("pool", lambda e: e.memset(ident[:], 1.0), writes=[b_ident])
        p.op("pool", lambda e: e.affine_select(out=ident[:], in_=ident[:], pattern=[[-1, 128]],
                                               compare_op=ALU.is_equal, fill=0.0, base=0, channel_multiplier=1),
             reads=[b_ident], writes=[b_ident])
        ones_bf = sb(top, "ones_bf", [128, 128], BF16)
        ones_f = sb(top, "ones_f", [128, 128], F32)
        b_ones = p.buf()
        p.op("pool", lambda e: e.memset(ones_bf[:], 1.0), writes=[b_ones])
        p.op("pool", lambda e: e.memset(ones_f[:], 1.0), writes=[b_ones])
        g.ident, g.b_ident, g.ones_bf, g.ones_f, g.b_ones = ident, b_ident, ones_bf, ones_f, b_ones

        def norm_transpose_phase(st, src_ap_fn, src_bufs, gvec_ap, hT, b_hT, eps=1e-6):
            gt = sb(st, "nt_g", [128, D], F32)
            b_g = p.buf()
            p.dma("sync", lambda e: e.dma_start(out=gt[:], in_=gvec_ap.partition_broadcast(128)), writes=[b_g])
            xt = [sb(st, "nt_x%d" % i, [128, D], F32) for i in range(2)]
            b_xt = p.bufs(2)
            junk = sb(st, "nt_junk", [128, D], BF16)
            b_junk = p.buf()
            ss = [sb(st, "nt_ss%d" % i, [128, 1], F32) for i in range(2)]
            b_ss = p.bufs(2)
            hb = [sb(st, "nt_hb%d" % i, [128, D], BF16) for i in range(2)]
            b_hb = p.bufs(2)
            pt = [ps(st, "nt_pt%d" % i, [128, 8, 128], BF16) for i in range(2)]
            b_pt = p.bufs(2)
            for t in range(NT):
                i = t % 2
                p.dma("sync", lambda e, t=t, i=i: e.dma_start(out=xt[i][:], in_=src_ap_fn(t)),
                      reads=[src_bufs[t]], writes=[b_xt[i]])
                p.op("pool", lambda e, i=i: e.memset(ss[i][:], 0.0), writes=[b_ss[i]])
                p.op("act", lambda e, i=i: e.activation(out=junk[:], in_=xt[i][:], func=AF.Square, accum_out=ss[i][:]),
                     reads=[b_xt[i], b_ss[i]], writes=[b_junk, b_ss[i]])
                p.op("act", lambda e, i=i: e.activation(out=ss[i][:], in_=ss[i][:], func=AF.Sqrt, scale=1.0 / D, bias=g.eps_tiles[eps][:]),
                     reads=[b_ss[i]], writes=[b_ss[i]])
                p.op("dve", lambda e, i=i: e.reciprocal(out=ss[i][:], in_=ss[i][:]), reads=[b_ss[i]], writes=[b_ss[i]])
                p.op("dve", lambda e, i=i: e.scalar_tensor_tensor(out=hb[i][:], in0=xt[i][:], scalar=ss[i][:, 0:1], in1=gt[:],
                                                                  op0=ALU.mult, op1=ALU.mult),
                     reads=[b_xt[i], b_ss[i], b_g], writes=[b_hb[i]])
                for k in range(8):
                    p.op("pe", lambda e, i=i, k=k: e.transpose(out=pt[i][:, k, :], in_=hb[i][:, k * 128:(k + 1) * 128], identity=ident[:]),
                         reads=[b_hb[i], b_ident], writes=[b_pt[i]], sig=(k == 7))
                p.op("act", lambda e, i=i, t=t: e.copy(out=hT[:, :, t * 128:(t + 1) * 128], in_=pt[i][:]),
                     reads=[b_pt[i]], writes=[b_hT[t]])

        g.eps_tiles = {}
        b_eps = p.buf()
        for ev in (1e-6, 1e-5, 64e-5, 0.0, 1.0):
            tl = sb(top, "eps%d" % len(g.eps_tiles), [128, 1], F32)
            p.op("pool", lambda e, tl=tl, ev=ev: e.memset(tl[:], ev), writes=[b_eps])
            g.eps_tiles[ev] = tl

        g.wstg = [sb(top, "wstg%d" % i, [128, 8, 512], F32) for i in range(2)]
        g.b_wstg = p.bufs(2)
        g.nstg = [0]

        class WLoader:
            def __init__(self, st, name, kch, maxcol, nbuf=2):
                self.wb = [sb(st, name + "_b%d" % i, [128, kch, maxcol], BF16) for i in range(nbuf)]
                self.b_wb = p.bufs(nbuf)
                self.n = 0
                self.nbuf = nbuf
                self.kch = kch

            def load(self, wap, c0, ncol, krows=None):
                i = self.n % self.nbuf
                self.n += 1
                kch = self.kch
                si = g.nstg[0] % 2
                g.nstg[0] += 1
                stg, wb = g.wstg[si], self.wb[i]
                p.dma("sync", lambda e: e.dma_start(out=stg[:, 0:kch, 0:ncol],
                                                    in_=wap[:, c0:c0 + ncol].rearrange("(k p) n -> p k n", p=128)),
                      writes=[g.b_wstg[si]])
                p.op("pool", lambda e: e.tensor_copy(out=wb[:, :, 0:ncol], in_=stg[:, 0:kch, 0:ncol]),
                     reads=[g.b_wstg[si]], writes=[self.b_wb[i]])
                return wb, self.b_wb[i]

        def phase_proj(l):
            with ExitStack() as st:
                hT = sb(st, "hT", [128, 8, S], BF16)
                b_hT = p.bufs(NT)
                with ExitStack() as st2:
                    if l == 0:
                        norm_transpose_phase(st2, lambda t: x_in[t * 128:(t + 1) * 128, :], B_x, vec["norm_mix"][l:l + 1, :], hT, b_hT)
                    else:
                        norm_transpose_phase(st2, lambda t: xres[t * 128:(t + 1) * 128, :], B_x, vec["norm_mix"][l:l + 1, :], hT, b_hT)
                    p.barrier()
                wl = WLoader(st, "wl", 8, 512)
                pp = [ps(st, "pp%d" % i, [128, 512], F32) for i in range(4)]
                b_pp = p.bufs(4)
                ostf = [sb(st, "ostf%d" % i, [128, 512], F32) for i in range(4)]
                ostb = [sb(st, "ostb%d" % i, [128, 512], BF16) for i in range(4)]
                b_ost = p.bufs(4)
                bg = sb(st, "bgcm", [128, 24], F32)
                b_bg = p.buf()
                p.dma("sync", lambda e: e.dma_start(out=bg[:], in_=b_gate_cm[l]), writes=[b_bg])
                cnt = [0]
                win = mats["w_in"][l]

                def tok_major(c0, ncol, dst_fn, dst_bufs, bf):
                    wb, b_wb = wl.load(win, c0, ncol)
                    for t in range(NT):
                        i = cnt[0] % 4
                        cnt[0] += 1
                        for k in range(8):
                            p.op("pe", lambda e, i=i, k=k, t=t: e.matmul(pp[i][:, 0:ncol], lhsT=hT[:, k, t * 128:(t + 1) * 128],
                                                                         rhs=wb[:, k, 0:ncol], start=(k == 0), stop=(k == 7)),
                                 reads=[b_hT[t], b_wb], writes=[b_pp[i]], sig=(k == 7))
                        o = ostb[i] if bf else ostf[i]
                        eng = "act" if (cnt[0] % 2) else "dve"
                        if eng == "act":
                            p.op("act", lambda e, i=i, o=o: e.copy(out=o[:, 0:ncol], in_=pp[i][:, 0:ncol]), reads=[b_pp[i]], writes=[b_ost[i]])
                        else:
                            p.op("dve", lambda e, i=i, o=o: e.tensor_copy(out=o[:, 0:ncol], in_=pp[i][:, 0:ncol]), reads=[b_pp[i]], writes=[b_ost[i]])
                        p.dma("sync", lambda e, o=o, t=t: e.dma_start(out=dst_fn(t), in_=o[:, 0:ncol]), reads=[b_ost[i]], writes=[dst_bufs[t]])

                def ch_major(c0, nchunk, dst, dst_row0, dst_bufs, gate_idx0=None):
                    for cc0 in range(0, nchunk, 4):
                        ncc = min(4, nchunk - cc0)
                        wb, b_wb = wl.load(win, c0 + cc0 * 128, ncc * 128)
                        for cc in range(ncc):
                            for tq in range(8):
                                i = cnt[0] % 4
                                cnt[0] += 1
                                for k in range(8):
                                    p.op("pe", lambda e, i=i, k=k, tq=tq, cc=cc, wb=wb: e.matmul(pp[i][:, :], lhsT=wb[:, k, cc * 128:(cc + 1) * 128],
                                                                                          rhs=hT[:, k, tq * 512:(tq + 1) * 512], start=(k == 0), stop=(k == 7)),
                                         reads=[b_hT[tq * 4 + j] for j in range(4)] + [b_wb], writes=[b_pp[i]], sig=(k == 7))
                                o = ostb[i]
                                if gate_idx0 is not None:
                                    gi = gate_idx0 + cc0 + cc
                                    p.op("act", lambda e, i=i, o=o, gi=gi: e.activation(out=o[:], in_=pp[i][:], func=AF.Sigmoid, bias=bg[:, gi:gi + 1]),
                                         reads=[b_pp[i], b_bg], writes=[b_ost[i]])
                                else:
                                    p.op("dve", lambda e, i=i, o=o: e.tensor_copy(out=o[:], in_=pp[i][:]), reads=[b_pp[i]], writes=[b_ost[i]])
                                r0 = dst_row0 + (cc0 + cc) * 128
                                p.dma("sync", lambda e, o=o, r0=r0, tq=tq: e.dma_start(out=dst[r0:r0 + 128, tq * 512:(tq + 1) * 512], in_=o[:]),
                                      reads=[b_ost[i]], writes=[dst_bufs[tq]])

                for c0, ncol in ((0, 512), (512, 512), (1024, 512), (1536, 256)):
                    tok_major(C_RW + c0, ncol, lambda t, c0=c0, ncol=ncol: prw[t * 128:(t + 1) * 128, c0:c0 + ncol], B_prw, False)
                tok_major(C_ML, 416, lambda t: pml[t * 128:(t + 1) * 128, :], B_pml, False)
                ch_major(C_DF, 8, qkT, 0, B_qkT)
                tok_major(C_DF + 1024, 512, lambda t: vdf[t * 128:(t + 1) * 128, :], B_vdf, True)
                ch_major(C_GT, 24, gT, 0, B_gT, gate_idx0=0)
                p.barrier()

        def load_bcast(st, name, ap_row, n):
            t = sb(st, name, [128, n], F32)
            b = p.buf()
            p.dma("sync", lambda e: e.dma_start(out=t[:], in_=ap_row.partition_broadcast(128)), writes=[b])
            return t, b

        def load_weight_bf(st, name, wap, krows, ncols, part0=0):
            kch = max(1, krows // 128)
            pr = min(128, krows)
            wbt = sb(st, name, [128, kch, ncols], BF16)
            b_w = p.buf()
            stg = g.wstg
            b_stg = g.b_wstg
            cs = 512 if kch <= 8 else 128
            for c0 in range(0, ncols, cs):
                nc_ = min(cs, ncols - c0)
                i = g.nstg[0] % 2
                g.nstg[0] += 1
                if krows >= 128 and kch <= 8:
                    src = wap[:, c0:c0 + nc_].rearrange("(k p) n -> p k n", p=128)
                    p.dma("sync", lambda e, i=i, src=src, nc_=nc_: e.dma_start(out=stg[i][:, 0:kch, 0:nc_], in_=src), writes=[b_stg[i]])
                    p.op("pool", lambda e, i=i, c0=c0, nc_=nc_: e.tensor_copy(out=wbt[:, :, c0:c0 + nc_], in_=stg[i][:, 0:kch, 0:nc_]),
                         reads=[b_stg[i]], writes=[b_w])
                elif krows >= 128:
                    sv = stg[i][:].rearrange("p a b -> p (a b)")[:, 0:kch * 128].rearrange("p (k n) -> p k n", n=128)
                    src = wap[:, c0:c0 + nc_].rearrange("(k p) n -> p k n", p=128)
                    p.dma("sync", lambda e, sv=sv, src=src, nc_=nc_: e.dma_start(out=sv[:, :, 0:nc_], in_=src), writes=[b_stg[i]])
                    p.op("pool", lambda e, sv=sv, c0=c0, nc_=nc_: e.tensor_copy(out=wbt[:, :, c0:c0 + nc_], in_=sv[:, :, 0:nc_]),
                         reads=[b_stg[i]], writes=[b_w])
                else:
                    src = wap[:, c0:c0 + nc_]
                    p.dma("sync", lambda e, i=i, src=src, nc_=nc_: e.dma_start(out=stg[i][part0:part0 + pr, 0, 0:nc_], in_=src), writes=[b_stg[i]])
                    p.op("pool", lambda e, i=i, c0=c0, nc_=nc_: e.tensor_copy(out=wbt[part0:part0 + pr, 0, c0:c0 + nc_], in_=stg[i][part0:part0 + pr, 0, 0:nc_]),
                         reads=[b_stg[i]], writes=[b_w])
            return wbt, b_w

        def xsrc(l, t):
            return (x_in if l == 0 else xres)[t * 128:(t + 1) * 128, :]

        def phase_merge(l):
            with ExitStack() as st:
                wbr = []
                for n in ("rwkv", "mla", "diff"):
                    wbr.append(load_weight_bf(st, "wbr_" + n, mats["w_branch_" + n][l], 512, 1024))
                wo, b_wo = load_weight_bf(st, "wo", mats["w_o"][l], 1024, 1024)
                oc = [[sb(st, "oc%d_%d" % (i, j), [128, 4, 512], BF16) for j in range(3)] for i in range(2)]
                b_oc = [p.bufs(3) for i in range(2)]
                gc = [sb(st, "gc%d" % i, [128, 24, 512], BF16) for i in range(2)]
                b_gc = p.bufs(2)
                mT = [sb(st, "mT%d" % i, [128, 8, 512], BF16) for i in range(2)]
                b_mT = p.bufs(2)
                pb = [ps(st, "mpb%d" % i, [128, 512], F32) for i in range(6)]
                b_pb = p.bufs(6)
                m0 = [sb(st, "m0_%d" % i, [128, 512], F32) for i in range(2)]
                m1 = [sb(st, "m1_%d" % i, [128, 512], F32) for i in range(2)]
                b_m0 = p.bufs(2)
                b_m1 = p.bufs(2)
                xt = [sb(st, "mxt%d" % i, [128, D], F32) for i in range(2)]
                b_xt = p.bufs(2)
                po = [ps(st, "mpo%d" % i, [128, 512], F32) for i in range(2)]
                b_po = p.bufs(2)
                names = ("rwkv", "mla", "diff")
                nd = 0
                nx = 0
                npo = 0
                for c in range(8):
                    ci = c % 2
                    for j, n in enumerate(names):
                        p.dma("sync", lambda e, ci=ci, j=j, n=n, c=c: e.dma_start(
                            out=oc[ci][j][:], in_=oT[n][:, c * 512:(c + 1) * 512].rearrange("(k p) t -> p k t", p=128)),
                            reads=[B_oT[n][c]], writes=[b_oc[ci][j]])
                    p.dma("sync", lambda e, ci=ci, c=c: e.dma_start(
                        out=gc[ci][:], in_=gT[:, c * 512:(c + 1) * 512].rearrange("(k p) t -> p k t", p=128)),
                        reads=[B_gT[c]], writes=[b_gc[ci]])
                    for dc in range(8):
                        di = nd % 2
                        nd += 1
                        for j in range(3):
                            pj = di * 3 + j
                            w_j, b_wj = wbr[j]
                            for k in range(4):
                                p.op("pe", lambda e, pj=pj, k=k, dc=dc, ci=ci, j=j, w_j=w_j: e.matmul(
                                    pb[pj][:], lhsT=w_j[:, k, dc * 128:(dc + 1) * 128], rhs=oc[ci][j][:, k, :], start=(k == 0), stop=(k == 3)),
                                    reads=[b_wj, b_oc[ci][j]], writes=[b_pb[pj]], sig=(k == 3))
                        p.op("dve", lambda e, di=di, ci=ci, dc=dc: e.tensor_tensor(out=m0[di][:], in0=pb[di * 3][:], in1=gc[ci][:, dc, :], op=ALU.mult),
                             reads=[b_pb[di * 3], b_gc[ci]], writes=[b_m0[di]])
                        p.op("dve", lambda e, di=di, ci=ci, dc=dc: e.tensor_tensor(out=m1[di][:], in0=pb[di * 3 + 1][:], in1=gc[ci][:, 8 + dc, :], op=ALU.mult),
                             reads=[b_pb[di * 3 + 1], b_gc[ci]], writes=[b_m1[di]])
                        p.op("dve", lambda e, di=di: e.tensor_tensor(out=m0[di][:], in0=m0[di][:], in1=m1[di][:], op=ALU.add),
                             reads=[b_m0[di], b_m1[di]], writes=[b_m0[di]])
                        p.op("dve", lambda e, di=di, ci=ci, dc=dc: e.tensor_tensor(out=m1[di][:], in0=pb[di * 3 + 2][:], in1=gc[ci][:, 16 + dc, :], op=ALU.mult),
                             reads=[b_pb[di * 3 + 2], b_gc[ci]], writes=[b_m1[di]])
                        p.op("dve", lambda e, di=di, ci=ci, dc=dc: e.tensor_tensor(out=mT[ci][:, dc, :], in0=m0[di][:], in1=m1[di][:], op=ALU.add),
                             reads=[b_m0[di], b_m1[di]], writes=[b_mT[ci]])
                    for tt in range(4):
                        t = c * 4 + tt
                        xi = nx % 2
                        nx += 1
                        p.dma("sync", lambda e, xi=xi, t=t: e.dma_start(out=xt[xi][:], in_=xsrc(l, t)), reads=[B_x[t]], writes=[b_xt[xi]])
                        for hf in range(2):
                            pi = npo % 2
                            npo += 1
                            for k in range(8):
                                p.op("pe", lambda e, pi=pi, k=k, ci=ci, tt=tt, hf=hf: e.matmul(
                                    po[pi][:], lhsT=mT[ci][:, k, tt * 128:(tt + 1) * 128], rhs=wo[:, k, hf * 512:(hf + 1) * 512], start=(k == 0), stop=(k == 7)),
                                    reads=[b_mT[ci], b_wo], writes=[b_po[pi]], sig=(k == 7))
                            p.op("dve", lambda e, pi=pi, xi=xi, hf=hf: e.tensor_tensor(out=xt[xi][:, hf * 512:(hf + 1) * 512], in0=po[pi][:],
                                                                                 in1=xt[xi][:, hf * 512:(hf + 1) * 512], op=ALU.add),
                                 reads=[b_po[pi], b_xt[xi]], writes=[b_xt[xi]])
                        p.dma("sync", lambda e, xi=xi, t=t: e.dma_start(out=xres[t * 128:(t + 1) * 128, :], in_=xt[xi][:]),
                              reads=[b_xt[xi]], writes=[B_x[t]])
                p.barrier()

        def phase_ffn(l):
            G = 512
            NG = S // G
            TG = G // 128
            with ExitStack() as st:
                wdn, b_wdn = load_weight_bf(st, "wdn", mats["ffn_w_down"][l], DFF, 1024)
                gt, b_g = load_bcast(st, "f_g", vec["norm_ffn"][l:l + 1, :], D)
                cw = sb(st, "f_cw", [128, 3, 44], F32)
                cb = sb(st, "f_cb", [128, 44], F32)
                b_cw = p.buf()
                p.dma("sync", lambda e: e.dma_start(out=cw[:], in_=conv_w_cm[l]), writes=[b_cw])
                p.dma("sync", lambda e: e.dma_start(out=cb[:], in_=conv_b_cm[l]), writes=[b_cw])
                halo = sb(st, "f_halo", [128, 44, 2], F32)
                b_halo = p.bufs(44)
                p.op("pool", lambda e: e.memset(halo[:], 0.0), writes=b_halo)
                xg = sb(st, "f_xg", [128, TG, D], F32)
                b_xg = p.bufs(TG)
                hTg = sb(st, "f_hT", [128, 8, G], BF16)
                b_hTg = p.bufs(TG)
                junk = sb(st, "f_junk", [128, D], BF16)
                b_junk = p.buf()
                ss = [sb(st, "f_ss%d" % i, [128, 1], F32) for i in range(2)]
                b_ss = p.bufs(2)
                hb = [sb(st, "f_hb%d" % i, [128, D], BF16) for i in range(2)]
                b_hb = p.bufs(2)
                pt = [ps(st, "f_pt%d" % i, [128, 8, 128], BF16) for i in range(2)]
                b_pt = p.bufs(2)
                wst = [sb(st, "f_wst%d" % i, [128, 8, 256], F32) for i in range(2)]
                wub = [sb(st, "f_wub%d" % i, [128, 8, 256], BF16) for i in range(2)]
                b_wst = p.bufs(2)
                b_wub = p.bufs(2)
                pu = [ps(st, "f_pu%d" % i, [128, 512], F32) for i in range(4)]
                b_pu = p.bufs(4)
                ug = [sb(st, "f_ug%d" % i, [128, G + 2], F32) for i in range(2)]
                uv = [sb(st, "f_uv%d" % i, [128, G + 2], F32) for i in range(2)]
                b_ug = p.bufs(2)
                b_uv = p.bufs(2)
                cg = [sb(st, "f_cg%d" % i, [128, G], F32) for i in range(2)]
                cv = [sb(st, "f_cv%d" % i, [128, G], F32) for i in range(2)]
                b_cg = p.bufs(2)
                b_cv = p.bufs(2)
                actT = sb(st, "f_actT", [128, 22, G], BF16)
                b_act = p.bufs(22)
                pd = [ps(st, "f_pd%d" % i, [128, 512], F32) for i in range(2)]
                b_pd = p.bufs(2)
                wup = mats["ffn_w_up"][l]
                nw = 0
                npu = 0
                npd = 0
                NF_ = int(os.environ.get('FFN_NF', 22))
                for gi in range(int(os.environ.get('FFN_NG', NG))):
                    for tt in range(TG):
                        t = gi * TG + tt
                        i = tt % 2
                        p.dma("sync", lambda e, tt=tt, t=t: e.dma_start(out=xg[:, tt, :], in_=xres[t * 128:(t + 1) * 128, :]),
                              reads=[B_x[t]], writes=[b_xg[tt]])
                        p.op("pool", lambda e, i=i: e.memset(ss[i][:], 0.0), writes=[b_ss[i]])
                        p.op("act", lambda e, i=i, tt=tt: e.activation(out=junk[:], in_=xg[:, tt, :], func=AF.Square, accum_out=ss[i][:]),
                             reads=[b_xg[tt], b_ss[i]], writes=[b_junk, b_ss[i]])
                        p.op("act", lambda e, i=i: e.activation(out=ss[i][:], in_=ss[i][:], func=AF.Sqrt, scale=1.0 / D, bias=g.eps_tiles[1e-6][:]),
                             reads=[b_ss[i], b_eps], writes=[b_ss[i]])
                        p.op("dve", lambda e, i=i: e.reciprocal(out=ss[i][:], in_=ss[i][:]), reads=[b_ss[i]], writes=[b_ss[i]])
                        p.op("dve", lambda e, i=i, tt=tt: e.scalar_tensor_tensor(out=hb[i][:], in0=xg[:, tt, :], scalar=ss[i][:, 0:1], in1=gt[:],
                                                                             op0=ALU.mult, op1=ALU.mult),
                             reads=[b_xg[tt], b_ss[i], b_g], writes=[b_hb[i]])
                        for k in range(8):
                            p.op("pe", lambda e, i=i, k=k: e.transpose(out=pt[i][:, k, :], in_=hb[i][:, k * 128:(k + 1) * 128], identity=ident[:]),
                                 reads=[b_hb[i], b_ident], writes=[b_pt[i]], sig=(k == 7))
                        p.op("act", lambda e, i=i, tt=tt: e.copy(out=hTg[:, :, tt * 128:(tt + 1) * 128], in_=pt[i][:]),
                             reads=[b_pt[i]], writes=[b_hTg[tt]])
                    for f in range(NF_):
                        wi = nw % 2
                        nw += 1
                        p.dma("sync", lambda e, wi=wi, f=f: e.dma_start(out=wst[wi][:, :, 0:128],
                                                                        in_=wup[:, f * 128:(f + 1) * 128].rearrange("(k p) n -> p k n", p=128)),
                              writes=[b_wst[wi]])
                        p.dma("sync", lambda e, wi=wi, f=f: e.dma_start(out=wst[wi][:, :, 128:256],
                                                                        in_=wup[:, DFF + f * 128:DFF + (f + 1) * 128].rearrange("(k p) n -> p k n", p=128)),
                              writes=[b_wst[wi]])
                        p.op("pool", lambda e, wi=wi: e.tensor_copy(out=wub[wi][:], in_=wst[wi][:]), reads=[b_wst[wi]], writes=[b_wub[wi]])
                        ui = f % 2
                        p.op("pool", lambda e, ui=ui, f=f: e.tensor_copy(out=ug[ui][:, 0:2], in_=halo[:, f, :]), reads=[b_halo[f]], writes=[b_ug[ui]])
                        p.op("pool", lambda e, ui=ui, f=f: e.tensor_copy(out=uv[ui][:, 0:2], in_=halo[:, 22 + f, :]), reads=[b_halo[22 + f]], writes=[b_uv[ui]])
                        for gv in range(2):
                            for hf in range(G // 512):
                                pi = npu % 4
                                npu += 1
                                for k in range(8):
                                    p.op("pe", lambda e, pi=pi, k=k, wi=wi, gv=gv, hf=hf: e.matmul(
                                        pu[pi][:], lhsT=wub[wi][:, k, gv * 128:(gv + 1) * 128], rhs=hTg[:, k, hf * 512:(hf + 1) * 512],
                                        start=(k == 0), stop=(k == 7)),
                                        reads=[b_wub[wi]] + [b_hTg[hf * 4 + j] for j in range(4)], writes=[b_pu[pi]], sig=(k == 7))
                                dst = ug[ui] if gv == 0 else uv[ui]
                                b_dst = b_ug[ui] if gv == 0 else b_uv[ui]
                                p.op("act", lambda e, pi=pi, dst=dst, hf=hf: e.copy(out=dst[:, 2 + hf * 512:2 + (hf + 1) * 512], in_=pu[pi][:]),
                                     reads=[b_pu[pi]], writes=[b_dst])
                        p.op("pool", lambda e, ui=ui, f=f: e.tensor_copy(out=halo[:, f, :], in_=ug[ui][:, G:G + 2]), reads=[b_ug[ui]], writes=[b_halo[f]])
                        p.op("pool", lambda e, ui=ui, f=f: e.tensor_copy(out=halo[:, 22 + f, :], in_=uv[ui][:, G:G + 2]), reads=[b_uv[ui]], writes=[b_halo[22 + f]])
                        for eng, u, b_u, cdst, b_c, ch in (("dve", ug[ui], b_ug[ui], cg[ui], b_cg[ui], f), ("dve", uv[ui], b_uv[ui], cv[ui], b_cv[ui], 22 + f)):
                            p.op(eng, lambda e, u=u, cdst=cdst, ch=ch: e.tensor_scalar(out=cdst[:], in0=u[:, 2:G + 2], scalar1=cw[:, 2, ch:ch + 1],
                                                                                  scalar2=cb[:, ch:ch + 1], op0=ALU.mult, op1=ALU.add),
                                 reads=[b_u, b_cw], writes=[b_c])
                            p.op(eng, lambda e, u=u, cdst=cdst, ch=ch: e.scalar_tensor_tensor(out=cdst[:], in0=u[:, 1:G + 1], scalar=cw[:, 1, ch:ch + 1],
                                                                                         in1=cdst[:], op0=ALU.mult, op1=ALU.add),
                                 reads=[b_u, b_cw, b_c], writes=[b_c])
                            p.op(eng, lambda e, u=u, cdst=cdst, ch=ch: e.scalar_tensor_tensor(out=cdst[:], in0=u[:, 0:G], scalar=cw[:, 0, ch:ch + 1],
                                                                                         in1=cdst[:], op0=ALU.mult, op1=ALU.add),
                                 reads=[b_u, b_cw, b_c], writes=[b_c])
                        p.op("act", lambda e, ui=ui: e.activation(out=ug[ui][:, 2:G + 2], in_=cg[ui][:], func=AF.Silu),
                             reads=[b_cg[ui]], writes=[b_ug[ui]])
                        p.op("dve", lambda e, ui=ui, f=f: e.tensor_tensor(out=actT[:, f, :], in0=ug[ui][:, 2:G + 2], in1=cv[ui][:], op=ALU.mult),
                             reads=[b_ug[ui], b_cv[ui]], writes=[b_act[f]])
                    for tt in range(TG):
                        t = gi * TG + tt
                        for hf in range(2):
                            pi = npd % 2
                            npd += 1
                            for f in range(NF_):
                                p.op("pe", lambda e, pi=pi, f=f, tt=tt, hf=hf: e.matmul(
                                    pd[pi][:], lhsT=actT[:, f, tt * 128:(tt + 1) * 128], rhs=wdn[:, f, hf * 512:(hf + 1) * 512],
                                    start=(f == 0), stop=(f == NF_ - 1)),
                                    reads=[b_act[f], b_wdn], writes=[b_pd[pi]], sig=(f == NF_ - 1))
                            p.op("dve", lambda e, pi=pi, tt=tt, hf=hf: e.tensor_tensor(out=xg[:, tt, hf * 512:(hf + 1) * 512], in0=pd[pi][:],
                                                                                 in1=xg[:, tt, hf * 512:(hf + 1) * 512], op=ALU.add),
                                 reads=[b_pd[pi], b_xg[tt]], writes=[b_xg[tt]])
                        p.dma("sync", lambda e, tt=tt, t=t: e.dma_start(out=xres[t * 128:(t + 1) * 128, :], in_=xg[:, tt, :]),
                              reads=[b_xg[tt]], writes=[B_x[t]])
                p.barrier()

        def phase_final():
            with ExitStack() as st:
                gt, b_g = load_bcast(st, "fin_g", norm_final, D)
                xt = [sb(st, "fin_x%d" % i, [128, D], F32) for i in range(3)]
                b_xt = p.bufs(3)
                junk = sb(st, "fin_junk", [128, D], BF16)
                b_junk = p.buf()
                ss = [sb(st, "fin_ss%d" % i, [128, 1], F32) for i in range(3)]
                b_ss = p.bufs(3)
                for t in range(NT):
                    i = t % 3
                    p.dma("sync", lambda e, i=i, t=t: e.dma_start(out=xt[i][:], in_=xres[t * 128:(t + 1) * 128, :]), reads=[B_x[t]], writes=[b_xt[i]])
                    p.op("pool", lambda e, i=i: e.memset(ss[i][:], 0.0), writes=[b_ss[i]])
                    p.op("act", lambda e, i=i: e.activation(out=junk[:], in_=xt[i][:], func=AF.Square, accum_out=ss[i][:]),
                         reads=[b_xt[i], b_ss[i]], writes=[b_junk, b_ss[i]])
                    p.op("act", lambda e, i=i: e.activation(out=ss[i][:], in_=ss[i][:], func=AF.Sqrt, scale=1.0 / D, bias=g.eps_tiles[1e-6][:]),
                         reads=[b_ss[i], b_eps], writes=[b_ss[i]])
                    p.op("dve", lambda e, i=i: e.reciprocal(out=ss[i][:], in_=ss[i][:]), reads=[b_ss[i]], writes=[b_ss[i]])
                    p.op("dve", lambda e, i=i: e.scalar_tensor_tensor(out=xt[i][:], in0=xt[i][:], scalar=ss[i][:, 0:1], in1=gt[:], op0=ALU.mult, op1=ALU.mult),
                         reads=[b_xt[i], b_ss[i], b_g], writes=[b_xt[i]])
                    p.dma("sync", lambda e, i=i, t=t: e.dma_start(out=out[t * 128:(t + 1) * 128, :], in_=xt[i][:]), reads=[b_xt[i]], writes=[B_out])
                p.barrier()

        cs_t = sb(top, "cs_t", [128, NT, 16], F32)
        sn_t = sb(top, "sn_t", [128, NT, 16], F32)
        b_rope = p.buf()
        maskD = sb(top, "maskD", [128, 128], F32)
        b_maskD = p.buf()
        Dn = [sb(top, "Dn%d" % h, [128, 256], F32) for h in range(4)]
        b_Dn = p.buf()
        relb = sb(top, "relb", [128, 128], F32)
        b_relb = p.buf()
        negpi = sb(top, "negpi", [128, 1], F32)

        def setup_attn_consts():
            with ExitStack() as st:
                p.op("pool", lambda e: e.memset(negpi[:], -math.pi), writes=[b_rope])
                posi = sb(st, "posi", [128, NT], I32)
                posf = sb(st, "posf", [128, NT], F32)
                b_pos = p.buf()
                p.dma("sync", lambda e: e.dma_start(out=posi[:], in_=pos_tm), writes=[b_pos])
                p.op("dve", lambda e: e.tensor_copy(out=posf[:], in_=posi[:]), reads=[b_pos], writes=[b_pos])
                invf = sb(st, "invf", [128, 16], F32)
                b_invf = p.buf()
                for i in range(16):
                    v = float(np.float32(10000.0) ** np.float32(-(2.0 * i) / 32.0))
                    p.op("pool", lambda e, i=i, v=v: e.memset(invf[:, i:i + 1], v), writes=[b_invf])
                ang = sb(st, "ang", [128, NT, 16], F32)
                ang2 = sb(st, "ang2", [128, NT, 16], F32)
                b_ang = p.buf()
                for t in range(NT):
                    p.op("dve", lambda e, t=t: e.tensor_scalar(out=ang[:, t, :], in0=invf[:], scalar1=posf[:, t:t + 1], scalar2=None, op0=ALU.mult),
                         reads=[b_invf, b_pos], writes=[b_ang])
                ni = sb(st, "ang_ni", [128, NT, 16], I32)
                nf = sb(st, "ang_nf", [128, NT, 16], F32)
                mk = sb(st, "ang_mk", [128, NT, 16], F32)
                b_red = p.buf()

                def reduce_sin(src, add, dst):
                    p.op("dve", lambda e: e.tensor_scalar(out=ang2[:], in0=src[:], scalar1=1.0 / (2 * math.pi), scalar2=add, op0=ALU.mult, op1=ALU.add),
                         reads=[b_ang], writes=[b_red])
                    p.op("dve", lambda e: e.tensor_copy(out=ni[:], in_=ang2[:]), reads=[b_red], writes=[b_red])
                    p.op("dve", lambda e: e.tensor_copy(out=nf[:], in_=ni[:]), reads=[b_red], writes=[b_red])
                    p.op("dve", lambda e: e.tensor_tensor(out=ang2[:], in0=ang2[:], in1=nf[:], op=ALU.subtract), reads=[b_red], writes=[b_red])
                    p.op("dve", lambda e: e.tensor_single_scalar(out=mk[:], in_=ang2[:], scalar=0.5, op=ALU.is_gt), reads=[b_red], writes=[b_red])
                    p.op("dve", lambda e: e.tensor_tensor(out=ang2[:], in0=ang2[:], in1=mk[:], op=ALU.subtract), reads=[b_red], writes=[b_red])
                    p.op("dve", lambda e: e.tensor_single_scalar(out=mk[:], in_=ang2[:], scalar=-0.5, op=ALU.is_lt), reads=[b_red], writes=[b_red])
                    p.op("dve", lambda e: e.tensor_tensor(out=ang2[:], in0=ang2[:], in1=mk[:], op=ALU.add), reads=[b_red], writes=[b_red])
                    p.op("act", lambda e: e.activation(out=dst[:], in_=ang2[:], func=AF.Sin, scale=6.283184), reads=[b_red], writes=[b_rope])

                CST = int(os.environ.get("CSTAGE", "99"))
                if CST < 1:
                    p.barrier()
                    return
                reduce_sin(ang, 0.0, sn_t)
                reduce_sin(ang, 0.25, cs_t)
                if CST < 2:
                    p.barrier()
                    return
                p.op("pool", lambda e: e.memset(maskD[:], 0.0), writes=[b_maskD])
                p.op("pool", lambda e: e.affine_select(out=maskD[:], in_=maskD[:], pattern=[[1, 128]], compare_op=ALU.is_ge, fill=-30000.0,
                                                       base=0, channel_multiplier=-1), reads=[b_maskD], writes=[b_maskD])
                if CST < 3:
                    p.barrier()
                    return
                p.dma("sync", lambda e: e.dma_start(out=relb[:], in_=rel_bias.partition_broadcast(128)), writes=[b_relb])
                dl = sb(st, "dl", [128, 128], F32)
                b_dl = p.buf()
                p.op("dve", lambda e: e.tensor_tensor(out=dl[:, 4:128], in0=relb[:, 4:128], in1=relb[:, 0:124], op=ALU.subtract),
                     reads=[b_relb], writes=[b_dl])
                pri = sb(st, "pri", [128, 256], I32)
                prf = sb(st, "prf", [128, 256], F32)
                b_pr = p.buf()
                p.dma("sync", lambda e: e.dma_start(out=pri[:], in_=pos_row.partition_broadcast(128)), writes=[b_pr])
                p.op("dve", lambda e: e.tensor_copy(out=prf[:], in_=pri[:]), reads=[b_pr], writes=[b_pr])
                p.op("dve", lambda e: e.tensor_scalar(out=prf[:], in0=prf[:], scalar1=posf[:, 0:1], scalar2=0.0, op0=ALU.subtract, op1=ALU.max),
                     reads=[b_pr, b_pos], writes=[b_pr])
                if CST < 4:
                    p.barrier()
                    return
                ge = [sb(st, "ge%d" % i, [128, 256], F32) for i in range(2)]
                b_ge = p.bufs(2)
                for h in range(4):
                    p.op("dve", lambda e, h=h: e.tensor_scalar(out=Dn[h][:], in0=prf[:], scalar1=0.0, scalar2=relb[:, h:h + 1], op0=ALU.mult, op1=ALU.add),
                         reads=[b_pr, b_relb], writes=[b_Dn])
                def bucket(n):
                    if n < 16:
                        return n
                    return min(31, 16 + int(np.float32(np.log(np.float32(n) / np.float32(16))) / np.float32(math.log(128 / 16)) * np.float32(16)))
                thr = {}
                for n in range(0, 300):
                    bk = bucket(n)
                    for bb in range(1, bk + 1):
                        if bb not in thr:
                            thr[bb] = n
                for bb in range(1, 32):
                    gi = bb % 2
                    tv = float(thr[bb]) - 0.5
                    p.op("dve", lambda e, gi=gi, tv=tv: e.tensor_single_scalar(out=ge[gi][:], in_=prf[:], scalar=tv, op=ALU.is_ge),
                         reads=[b_pr], writes=[b_ge[gi]])
                    for h in range(4):
                        p.op("dve", lambda e, gi=gi, h=h, bb=bb: e.scalar_tensor_tensor(out=Dn[h][:], in0=ge[gi][:], scalar=dl[:, bb * 4 + h:bb * 4 + h + 1],
                                                                                    in1=Dn[h][:], op0=ALU.mult, op1=ALU.add),
                             reads=[b_ge[gi], b_dl, b_Dn], writes=[b_Dn])
                for h in range(4):
                    p.op("dve", lambda e, h=h: e.tensor_scalar(out=Dn[h][:], in0=Dn[h][:], scalar1=8.0, scalar2=None, op0=ALU.mult), reads=[b_Dn], writes=[b_Dn])
                    p.op("pool", lambda e, h=h: e.affine_select(out=Dn[h][:, 0:128], in_=Dn[h][:, 0:128], pattern=[[1, 128]], compare_op=ALU.is_ge,
                                                                fill=-30000.0, base=0, channel_multiplier=-1), reads=[b_Dn], writes=[b_Dn])
                p.barrier()

        class AttnRes:
            pass

        def make_attn_res(st):
            r = AttnRes()
            r.sc = [ps(st, "a_sc%d" % i, [128, 512], F32) for i in range(2)]
            r.b_sc = p.bufs(2)
            r.eT = [sb(st, "a_eT%d" % i, [128, 512], BF16) for i in range(3)]
            r.b_eT = p.bufs(3)
            r.nsc = 0
            r.neT = 0
            return r

        def attn_chunk(r, c, QTf, b_Q, KTf, b_K, Vf, b_V, dv, scale, d0, b_d0, d1, b_d1, farb, po, b_po, psm, b_psm):
            nk = 4 * c + 4
            for kt in range(nk):
                j0 = max(4 * c, kt)
                off = (j0 - 4 * c) * 128
                ncol = 512 - off
                si = r.nsc % 2
                r.nsc += 1
                sc, b_s = r.sc[si], r.b_sc[si]
                p.op("pe", lambda e, sc=sc, kt=kt, off=off, ncol=ncol: e.matmul(sc[:, off:512], lhsT=KTf(kt), rhs=QTf(c * 512 + off, ncol), start=True, stop=True),
                     reads=[b_K, b_Q], writes=[b_s])
                nnear = 0
                if kt >= 4 * c:
                    p.op("dve", lambda e, sc=sc, off=off: e.tensor_tensor(out=sc[:, off:off + 128], in0=sc[:, off:off + 128], in1=d0, op=ALU.add),
                         reads=[b_s, b_d0], writes=[b_s])
                    nnear = 1
                    if d1 is not None and kt + 1 <= 4 * c + 3:
                        p.op("dve", lambda e, sc=sc, off=off: e.tensor_tensor(out=sc[:, off + 128:off + 256], in0=sc[:, off + 128:off + 256], in1=d1, op=ALU.add),
                             reads=[b_s, b_d1], writes=[b_s])
                        nnear = 2
                elif d1 is not None and kt == 4 * c - 1:
                    p.op("dve", lambda e, sc=sc: e.tensor_tensor(out=sc[:, 0:128], in0=sc[:, 0:128], in1=d1, op=ALU.add),
                         reads=[b_s, b_d1], writes=[b_s])
                    nnear = 1
                ei = r.neT % 3
                r.neT += 1
                eT, b_e = r.eT[ei], r.b_eT[ei]
                if farb is None:
                    p.op("act", lambda e, sc=sc, eT=eT, off=off: e.activation(out=eT[:, off:512], in_=sc[:, off:512], func=AF.Exp, scale=scale),
                         reads=[b_s], writes=[b_e])
                else:
                    nn = nnear * 128
                    if nn > 0:
                        p.op("act", lambda e, sc=sc, eT=eT, off=off, nn=nn: e.activation(out=eT[:, off:off + nn], in_=sc[:, off:off + nn], func=AF.Exp, scale=scale),
                             reads=[b_s], writes=[b_e])
                    if off + nn < 512:
                        p.op("act", lambda e, sc=sc, eT=eT, off=off, nn=nn: e.activation(out=eT[:, off + nn:512], in_=sc[:, off + nn:512], func=AF.Exp, scale=scale, bias=farb),
                             reads=[b_s, b_relb], writes=[b_e])
                p.op("pe", lambda e, eT=eT, kt=kt, off=off: e.matmul(po[0:dv, off:512], lhsT=Vf(kt), rhs=eT[:, off:512], start=(kt == 0), stop=(kt == nk - 1)),
                     reads=[b_V, b_e], writes=[b_po], sig=False)
                p.op("pe", lambda e, eT=eT, kt=kt, off=off: e.matmul(psm[0:1, off:512], lhsT=ones_bf[:, 0:1], rhs=eT[:, off:512], start=(kt == 0), stop=(kt == nk - 1)),
                     reads=[b_e, b_ones], writes=[b_psm])

        def phase_mla(l):
            with ExitStack() as st:
                QT = sb(st, "m_QT", [128, 4, S], BF16)
                KT = sb(st, "m_KT", [128, 4, S], BF16)
                V = sb(st, "m_V", [128, NT, 4, 64], BF16)
                b_QT, b_KT, b_V = p.buf(), p.buf(), p.buf()
                for hg in range(2):
                    with ExitStack() as s2:
                        wuq, b_wuq = load_weight_bf(s2, "m_wuq", mats["mla_w_uq"][l], 256, 768)
                        wukv, b_wukv = load_weight_bf(s2, "m_wukv", mats["mla_w_ukv"][l], 128, 1024)
                        qn, b_qn = load_bcast(s2, "m_qn", vec["mla_q_norm"][l:l + 1, :], 256)
                        kvn, b_kvn = load_bcast(s2, "m_kvn", vec["mla_kv_norm"][l:l + 1, :], 128)
                        pt_ = [sb(s2, "m_p%d" % i, [128, 416], F32) for i in range(2)]
                        b_pt = p.bufs(2)
                        junk = sb(s2, "m_junk", [128, 256], BF16)
                        b_junk = p.buf()
                        ss = [sb(s2, "m_ss%d" % i, [128, 2], F32) for i in range(2)]
                        b_ss = p.bufs(2)
                        cb_ = [sb(s2, "m_cb%d" % i, [128, 384], BF16) for i in range(2)]
                        b_cb = p.bufs(2)
                        pT = ps(s2, "m_pT", [128, 3, 128], BF16)
                        b_pT = p.buf()
                        cT = [sb(s2, "m_cT%d" % i, [128, 3, 128], BF16) for i in range(2)]
                        b_cT = p.bufs(2)
                        pq = [ps(s2, "m_pq%d" % i, [128, 512], F32) for i in range(2)]
                        b_pq = p.bufs(2)
                        qb = [sb(s2, "m_qb%d" % i, [128, 8, 96], BF16) for i in range(2)]
                        b_qb = p.bufs(2)
                        tA = sb(s2, "m_tA", [128, 8, 16], F32)
                        tB = sb(s2, "m_tB", [128, 8, 16], F32)
                        tC = sb(s2, "m_tC", [128, 8, 16], F32)
                        tD = sb(s2, "m_tD", [128, 8, 16], F32)
                        qf = sb(s2, "m_qf", [128, 768], F32)
                        b_tA, b_tB, b_tC, b_tD, b_qf = p.buf(), p.buf(), p.buf(), p.buf(), p.buf()
                        pqT = ps(s2, "m_pqT", [128, 8, 128], BF16)
                        b_pqT = p.buf()
                        pkn = [ps(s2, "m_pkn%d" % i, [64, 4, 128], F32) for i in range(2)]
                        b_pkn = p.bufs(2)
                        pv = ps(s2, "m_pv", [128, 512], F32)
                        b_pv = p.buf()
                        kr = [sb(s2, "m_kr%d" % i, [128, 96], BF16) for i in range(2)]
                        b_kr = p.bufs(2)
                        for i in range(2):
                            p.op("pool", lambda e, i=i: e.memset(kr[i][:], 0.0), writes=[b_kr[i]])
                        pkr = ps(s2, "m_pkr", [128, 128], BF16)
                        b_pkr = p.buf()
                        krT = sb(s2, "m_krT", [128, 128], BF16)
                        b_krT = p.buf()
                        wukv_v = wukv[:, 0, :].rearrange("p (h c) -> p h c", c=128)
                        MSUB = int(os.environ.get('MLA_SUB', '99'))
                        for t in range(int(os.environ.get('MLA_NT', NT))):
                            i = t % 2
                            ts_ = slice(t * 128, (t + 1) * 128)
                            if MSUB < 0:
                                continue
                            p.dma("sync", lambda e, i=i, ts_=ts_: e.dma_start(out=pt_[i][:], in_=pml[ts_, :]), reads=[B_pml[t]], writes=[b_pt[i]])
                            p.op("pool", lambda e, i=i: e.memset(ss[i][:], 0.0), writes=[b_ss[i]])
                            p.op("act", lambda e, i=i: e.activation(out=junk[:, 0:256], in_=pt_[i][:, 0:256], func=AF.Square, accum_out=ss[i][:, 0:1]),
                                 reads=[b_pt[i], b_ss[i]], writes=[b_junk, b_ss[i]])
                            p.op("act", lambda e, i=i: e.activation(out=junk[:, 0:128], in_=pt_[i][:, 256:384], func=AF.Square, accum_out=ss[i][:, 1:2]),
                                 reads=[b_pt[i], b_ss[i]], writes=[b_junk, b_ss[i]])
                            p.op("act", lambda e, i=i: e.activation(out=ss[i][:, 0:1], in_=ss[i][:, 0:1], func=AF.Sqrt, scale=1.0 / 256, bias=g.eps_tiles[1e-6][:]),
                                 reads=[b_ss[i], b_eps], writes=[b_ss[i]])
                            p.op("act", lambda e, i=i: e.activation(out=ss[i][:, 1:2], in_=ss[i][:, 1:2], func=AF.Sqrt, scale=1.0 / 128, bias=g.eps_tiles[1e-6][:]),
                                 reads=[b_ss[i], b_eps], writes=[b_ss[i]])
                            p.op("dve", lambda e, i=i: e.reciprocal(out=ss[i][:], in_=ss[i][:]), reads=[b_ss[i]], writes=[b_ss[i]])
                            p.op("dve", lambda e, i=i: e.scalar_tensor_tensor(out=cb_[i][:, 0:256], in0=pt_[i][:, 0:256], scalar=ss[i][:, 0:1], in1=qn[:],
                                                                              op0=ALU.mult, op1=ALU.mult), reads=[b_pt[i], b_ss[i], b_qn], writes=[b_cb[i]])
                            p.op("dve", lambda e, i=i: e.scalar_tensor_tensor(out=cb_[i][:, 256:384], in0=pt_[i][:, 256:384], scalar=ss[i][:, 1:2], in1=kvn[:],
                                                                              op0=ALU.mult, op1=ALU.mult), reads=[b_pt[i], b_ss[i], b_kvn], writes=[b_cb[i]])
                            if MSUB < 2:
                                continue
                            for k in range(3):
                                p.op("pe", lambda e, i=i, k=k: e.transpose(out=pT[:, k, :], in_=cb_[i][:, k * 128:(k + 1) * 128], identity=ident[:]),
                                     reads=[b_cb[i], b_ident], writes=[b_pT], sig=(k == 2))
                            p.op("act", lambda e, i=i: e.copy(out=cT[i][:], in_=pT[:]), reads=[b_pT], writes=[b_cT[i]])
                            if MSUB < 3:
                                continue
                            for (c0, ncol, pi) in ((0, 512, 0), (512, 256, 1)):
                                for kk in range(2):
                                    p.op("pe", lambda e, i=i, kk=kk, c0=c0, ncol=ncol, pi=pi: e.matmul(pq[pi][:, 0:ncol], lhsT=cT[i][:, kk, :], rhs=wuq[:, kk, c0:c0 + ncol],
                                                                                                 start=(kk == 0), stop=(kk == 1)),
                                         reads=[b_cT[i], b_wuq], writes=[b_pq[pi]], sig=(kk == 1))
                            if MSUB < 4:
                                continue
                            p.op("act", lambda e: e.copy(out=qf[:, 0:512], in_=pq[0][:, 0:512]), reads=[b_pq[0]], writes=[b_qf])
                            p.op("act", lambda e: e.copy(out=qf[:, 512:768], in_=pq[1][:, 0:256]), reads=[b_pq[1]], writes=[b_qf])
                            for h in range(hg * 4, hg * 4 + 4):
                                c0 = h * 96
                                p.op("act", lambda e, i=i, h=h, c0=c0: e.copy(out=qb[i][:, h, 0:64], in_=qf[:, c0:c0 + 64]), reads=[b_qf], writes=[b_qb[i]])
                                x1 = qf[:, c0 + 64:c0 + 80]
                                x2 = qf[:, c0 + 80:c0 + 96]
                                cst = cs_t[:, t, :]
                                snt = sn_t[:, t, :]
                                p.op("dve", lambda e, h=h, x1=x1, cst=cst: e.tensor_tensor(out=tA[:, h, :], in0=x1, in1=cst, op=ALU.mult), reads=[b_qf, b_rope], writes=[b_tA])
                                p.op("dve", lambda e, h=h, x2=x2, snt=snt: e.tensor_tensor(out=tB[:, h, :], in0=x2, in1=snt, op=ALU.mult), reads=[b_qf, b_rope], writes=[b_tB])
                                p.op("dve", lambda e, i=i, h=h: e.tensor_tensor(out=qb[i][:, h, 64:80], in0=tA[:, h, :], in1=tB[:, h, :], op=ALU.subtract),
                                     reads=[b_tA, b_tB], writes=[b_qb[i]])
                                p.op("dve", lambda e, h=h, x1=x1, snt=snt: e.tensor_tensor(out=tC[:, h, :], in0=x1, in1=snt, op=ALU.mult), reads=[b_qf, b_rope], writes=[b_tC])
                                p.op("dve", lambda e, h=h, x2=x2, cst=cst: e.tensor_tensor(out=tD[:, h, :], in0=x2, in1=cst, op=ALU.mult), reads=[b_qf, b_rope], writes=[b_tD])
                                p.op("dve", lambda e, i=i, h=h: e.tensor_tensor(out=qb[i][:, h, 80:96], in0=tC[:, h, :], in1=tD[:, h, :], op=ALU.add),
                                     reads=[b_tC, b_tD], writes=[b_qb[i]])
                            if MSUB < 5:
                                continue
                            for hh in range(4):
                                h = hg * 4 + hh
                                p.op("pe", lambda e, i=i, h=h, hh=hh: e.transpose(out=pqT[0:96, hh, :], in_=qb[i][:, h, :], identity=ident[:]),
                                     reads=[b_qb[i], b_ident], writes=[b_pqT], sig=(hh == 3))
                            p.op("act", lambda e, ts_=ts_: e.copy(out=QT[0:96, :, ts_], in_=pqT[0:96, 0:4, :]), reads=[b_pqT], writes=[b_QT])
                            if MSUB < 6:
                                continue
                            for hh in range(4):
                                h = hg * 4 + hh
                                p.op("pe", lambda e, i=i, h=h, hh=hh: e.matmul(pkn[0][:, hh, :], lhsT=wukv[:, 0, h * 128:h * 128 + 64], rhs=cT[i][:, 2, :], start=True, stop=True),
                                     reads=[b_cT[i], b_wukv], writes=[b_pkn[0]], sig=(hh == 3))
                            p.op("dve", lambda e, ts_=ts_: e.tensor_copy(out=KT[0:64, :, ts_], in_=pkn[0][:]), reads=[b_pkn[0]], writes=[b_KT])
                            if MSUB < 7:
                                continue
                            for hh in range(4):
                                h = hg * 4 + hh
                                p.op("pe", lambda e, i=i, h=h, hh=hh: e.matmul(pv[:, hh * 64:(hh + 1) * 64], lhsT=cT[i][:, 2, :], rhs=wukv[:, 0, h * 128 + 64:h * 128 + 128], start=True, stop=True),
                                     reads=[b_cT[i], b_wukv], writes=[b_pv], sig=(hh == 3))
                            p.op("act", lambda e, t=t: e.copy(out=V[:, t, :, :], in_=pv[:, 0:256].rearrange("p (h c) -> p h c", c=64)), reads=[b_pv], writes=[b_V])
                            if MSUB < 8:
                                continue
                            cst = cs_t[:, t, :]
                            snt = sn_t[:, t, :]
                            x1 = pt_[i][:, 384:400]
                            x2 = pt_[i][:, 400:416]
                            p.op("dve", lambda e, x1=x1, cst=cst: e.tensor_tensor(out=tA[:, 0, :], in0=x1, in1=cst, op=ALU.mult), reads=[b_pt[i], b_rope], writes=[b_tA])
                            p.op("dve", lambda e, x2=x2, snt=snt: e.tensor_tensor(out=tB[:, 0, :], in0=x2, in1=snt, op=ALU.mult), reads=[b_pt[i], b_rope], writes=[b_tB])
                            p.op("dve", lambda e, i=i: e.tensor_tensor(out=kr[i][:, 64:80], in0=tA[:, 0, :], in1=tB[:, 0, :], op=ALU.subtract), reads=[b_tA, b_tB], writes=[b_kr[i]])
                            p.op("dve", lambda e, x1=x1, snt=snt: e.tensor_tensor(out=tA[:, 0, :], in0=x1, in1=snt, op=ALU.mult), reads=[b_pt[i], b_rope], writes=[b_tA])
                            p.op("dve", lambda e, x2=x2, cst=cst: e.tensor_tensor(out=tB[:, 0, :], in0=x2, in1=cst, op=ALU.mult), reads=[b_pt[i], b_rope], writes=[b_tB])
                            p.op("dve", lambda e, i=i: e.tensor_tensor(out=kr[i][:, 80:96], in0=tA[:, 0, :], in1=tB[:, 0, :], op=ALU.add), reads=[b_tA, b_tB], writes=[b_kr[i]])
                            p.op("pe", lambda e, i=i: e.transpose(out=pkr[0:96, :], in_=kr[i][:, :], identity=ident[:]), reads=[b_kr[i], b_ident], writes=[b_pkr])
                            p.op("act", lambda e: e.copy(out=krT[64:96, :], in_=pkr[64:96, :]), reads=[b_pkr], writes=[b_krT])
                            for h in range(4):
                                p.op("pool", lambda e, h=h, ts_=ts_: e.tensor_copy(out=KT[64:96, h, ts_], in_=krT[64:96, :]), reads=[b_krT], writes=[b_KT])
                        p.barrier()
                    if os.environ.get("MLA_STAGE") == "prep":
                        continue
                    with ExitStack() as s3:
                        r = make_attn_res(s3)
                        po = [ps(s3, "m_po%d" % i, [128, 512], F32) for i in range(2)]
                        b_po = p.bufs(2)
                        psm = [ps(s3, "m_psm%d" % i, [1, 512], F32) for i in range(2)]
                        b_psm = p.bufs(2)
                        pbc = ps(s3, "m_pbc", [64, 512], F32)
                        b_pbc = p.buf()
                        rc = [sb(s3, "m_rc%d" % i, [1, 512], F32) for i in range(2)]
                        b_rc = p.bufs(2)
                        bcs = [sb(s3, "m_bcs%d" % i, [64, 512], F32) for i in range(2)]
                        b_bcs = p.bufs(2)
                        ob = [sb(s3, "m_ob%d" % i, [64, 512], BF16) for i in range(2)]
                        b_ob = p.bufs(2)
                        n = 0
                        scale = 96 ** -0.5
                        for hh in range(4):
                            h = hg * 4 + hh
                            for c in range(int(os.environ.get('MLA_NC', 8))):
                                i = n % 2
                                n += 1
                                attn_chunk(r, c,
                                           lambda q0, nq, hh=hh: QT[0:96, hh, q0:q0 + nq], b_QT,
                                           lambda kt, hh=hh: KT[0:96, hh, kt * 128:(kt + 1) * 128], b_KT,
                                           lambda kt, hh=hh: V[:, kt, hh, :], b_V, 64, scale,
                                           maskD[:], b_maskD, None, None, None, po[i], b_po[i], psm[i], b_psm[i])
                                p.op("dve", lambda e, i=i: e.reciprocal(out=rc[i][:], in_=psm[i][:]), reads=[b_psm[i]], writes=[b_rc[i]])
                                p.op("pe", lambda e, i=i: e.matmul(pbc[:], lhsT=ones_f[0:1, 0:64], rhs=rc[i][:], start=True, stop=True),
                                     reads=[b_rc[i], b_ones], writes=[b_pbc])
                                p.op("act", lambda e, i=i: e.copy(out=bcs[i][:], in_=pbc[:]), reads=[b_pbc], writes=[b_bcs[i]])
                                p.op("dve", lambda e, i=i: e.tensor_tensor(out=ob[i][:], in0=po[i][0:64, :], in1=bcs[i][:], op=ALU.mult),
                                     reads=[b_po[i], b_bcs[i]], writes=[b_ob[i]])
                                p.dma("sync", lambda e, i=i, h=h, c=c: e.dma_start(out=oT["mla"][h * 64:(h + 1) * 64, c * 512:(c + 1) * 512], in_=ob[i][:]),
                                      reads=[b_ob[i]], writes=[B_oT["mla"][c]])
                        p.barrier()

        def phase_diff(l):
            lambda_init = 0.8 - 0.6 * math.exp(-0.3 * l)
            with ExitStack() as st:
                r = make_attn_res(st)
                lam = sb(st, "d_lam", [1, 256], F32)
                lamp = sb(st, "d_lamp", [1, 128], F32)
                lams = sb(st, "d_lams", [1, 4], F32)
                b_lam = p.buf()
                p.dma("sync", lambda e: e.dma_start(out=lam[:], in_=diff_lambda[l]), writes=[b_lam])
                p.op("pool", lambda e: e.memset(lams[:], 0.0), writes=[b_lam])
                p.op("dve", lambda e: e.tensor_tensor(out=lamp[:, 0:64], in0=lam[:, 0:64], in1=lam[:, 64:128], op=ALU.mult), reads=[b_lam], writes=[b_lam])
                p.op("dve", lambda e: e.tensor_tensor(out=lamp[:, 64:128], in0=lam[:, 128:192], in1=lam[:, 192:256], op=ALU.mult), reads=[b_lam], writes=[b_lam])
                p.op("dve", lambda e: e.tensor_reduce(out=lams[:, 0:2], in_=lamp[:].rearrange("p (a b) -> p a b", b=64), axis=AX.X, op=ALU.add), reads=[b_lam], writes=[b_lam])
                p.op("act", lambda e: e.activation(out=lams[:, 0:2], in_=lams[:, 0:2], func=AF.Exp), reads=[b_lam], writes=[b_lam])
                p.op("dve", lambda e: e.scalar_tensor_tensor(out=lams[:, 2:3], in0=lams[:, 1:2], scalar=-lambda_init, in1=lams[:, 0:1], op0=ALU.add, op1=ALU.subtract),
                     reads=[b_lam], writes=[b_lam])
                sub = sb(st, "d_sub", [128, 1], F32)
                b_sub = p.buf()
                p.dma("sync", lambda e: e.dma_start(out=sub[:], in_=subln_cm[l]), writes=[b_sub])
                p.op("dve", lambda e: e.tensor_scalar(out=sub[:], in0=sub[:], scalar1=1.0 - lambda_init, scalar2=None, op0=ALU.mult), reads=[b_sub], writes=[b_sub])
                QT = [sb(st, "d_QT%d" % i, [64, S], BF16) for i in range(2)]
                KT = [sb(st, "d_KT%d" % i, [64, S], BF16) for i in range(2)]
                b_QK = p.bufs(2)
                V = sb(st, "d_V", [128, NT, 128], BF16)
                b_V = p.buf()
                po = [ps(st, "d_po%d" % i, [128, 512], F32) for i in range(2)]
                b_po = p.bufs(2)
                psm = [ps(st, "d_psm%d" % i, [1, 512], F32) for i in range(2)]
                b_psm = p.bufs(2)
                pbc = ps(st, "d_pbc", [128, 512], F32)
                b_pbc = p.buf()
                rc = [sb(st, "d_rc%d" % i, [1, 512], F32) for i in range(2)]
                b_rc = p.bufs(2)
                bcs = [sb(st, "d_bcs%d" % i, [128, 512], F32) for i in range(2)]
                b_bcs = p.bufs(2)
                o0 = sb(st, "d_o0", [128, 512], F32)
                o1 = sb(st, "d_o1", [128, 512], F32)
                sq = sb(st, "d_sq", [128, 512], F32)
                b_o0, b_o1, b_sq = p.buf(), p.buf(), p.buf()
                ob = [sb(st, "d_ob%d" % i, [128, 512], BF16) for i in range(2)]
                b_ob = p.bufs(2)
                n = 0
                scale = 0.125
                for h in range(4):
                    for m in range(2):
                        rq = (h * 2 + m) * 64
                        p.dma("sync", lambda e, m=m, rq=rq: e.dma_start(out=QT[m][:], in_=qkT[rq:rq + 64, :]), reads=B_qkT, writes=[b_QK[m]])
                        p.dma("sync", lambda e, m=m, rq=rq: e.dma_start(out=KT[m][:], in_=qkT[512 + rq:512 + rq + 64, :]), reads=B_qkT, writes=[b_QK[m]])
                    p.dma("sync", lambda e, h=h: e.dma_start(out=V[:], in_=vdf[:, h * 128:(h + 1) * 128].rearrange("(n p) c -> p n c", p=128)),
                          reads=B_vdf, writes=[b_V])
                    farb = relb[:, 31 * 4 + h:31 * 4 + h + 1]
                    for c in range(8):
                        for m in range(2):
                            attn_chunk(r, c,
                                       lambda q0, nq, m=m: QT[m][:, q0:q0 + nq], b_QK[m],
                                       lambda kt, m=m: KT[m][:, kt * 128:(kt + 1) * 128], b_QK[m],
                                       lambda kt: V[:, kt, :], b_V, 128, scale,
                                       Dn[h][:, 0:128], b_Dn, Dn[h][:, 128:256], b_Dn, farb, po[m], b_po[m], psm[m], b_psm[m])
                        for m in range(2):
                            p.op("dve", lambda e, m=m: e.reciprocal(out=rc[m][:], in_=psm[m][:]), reads=[b_psm[m]], writes=[b_rc[m]])
                        p.op("dve", lambda e: e.tensor_scalar(out=rc[1][:], in0=rc[1][:], scalar1=lams[0:1, 2:3], scalar2=None, op0=ALU.mult),
                             reads=[b_rc[1], b_lam], writes=[b_rc[1]])
                        for m in range(2):
                            p.op("pe", lambda e, m=m: e.matmul(pbc[:], lhsT=ones_f[0:1, :], rhs=rc[m][:], start=True, stop=True),
                                 reads=[b_rc[m], b_ones], writes=[b_pbc])
                            p.op("act", lambda e, m=m: e.copy(out=bcs[m][:], in_=pbc[:]), reads=[b_pbc], writes=[b_bcs[m]])
                        p.op("dve", lambda e: e.tensor_tensor(out=o0[:], in0=po[0][:], in1=bcs[0][:], op=ALU.mult), reads=[b_po[0], b_bcs[0]], writes=[b_o0])
                        p.op("dve", lambda e: e.tensor_tensor(out=o1[:], in0=po[1][:], in1=bcs[1][:], op=ALU.mult), reads=[b_po[1], b_bcs[1]], writes=[b_o1])
                        p.op("dve", lambda e: e.tensor_tensor(out=o0[:], in0=o0[:], in1=o1[:], op=ALU.add), reads=[b_o0, b_o1], writes=[b_o0])
                        p.op("act", lambda e: e.activation(out=sq[:], in_=o0[:], func=AF.Square), reads=[b_o0], writes=[b_sq])
                        p.op("pe", lambda e: e.matmul(pbc[:], lhsT=ones_f[:, :], rhs=sq[:], start=True, stop=True), reads=[b_sq, b_ones], writes=[b_pbc])
                        p.op("act", lambda e: e.activation(out=sq[:], in_=pbc[:], func=AF.Sqrt, scale=1.0 / 128, bias=g.eps_tiles[1e-5][:]),
                             reads=[b_pbc, b_eps], writes=[b_sq])
                        p.op("dve", lambda e: e.reciprocal(out=sq[:], in_=sq[:]), reads=[b_sq], writes=[b_sq])
                        i = n % 2
                        n += 1
                        p.op("dve", lambda e, i=i: e.scalar_tensor_tensor(out=ob[i][:], in0=o0[:], scalar=sub[:, 0:1], in1=sq[:], op0=ALU.mult, op1=ALU.mult),
                             reads=[b_o0, b_sub, b_sq], writes=[b_ob[i]])
                        p.dma("sync", lambda e, i=i, h=h, c=c: e.dma_start(out=oT["diff"][h * 128:(h + 1) * 128, c * 512:(c + 1) * 512], in_=ob[i][:]),
                              reads=[b_ob[i]], writes=[B_oT["diff"][c]])
                p.barrier()

        def phase_rwkv(l):
            with ExitStack() as st:
                def T_(name, shape, dt=F32):
                    return sb(st, "r_" + name, shape, dt)
                mu, b_mu = load_bcast(st, "r_mu", vec["rwkv_mu"][l:l + 1, :], 1792)
                w0, b_w0 = load_bcast(st, "r_w0", vec["rwkv_w0"][l:l + 1, :], 512)
                a0, b_a0 = load_bcast(st, "r_a0", vec["rwkv_a0"][l:l + 1, :], 512)
                k_k, b_kk_ = load_bcast(st, "r_k_k", vec["rwkv_k_k"][l:l + 1, :], 512)
                k_a, b_ka_ = load_bcast(st, "r_k_a", vec["rwkv_k_a"][l:l + 1, :], 512)
                r_k, b_rk_ = load_bcast(st, "r_r_k", vec["rwkv_r_k"][l:l + 1, :], 512)
                ln_w, b_lnw = load_bcast(st, "r_ln_w", vec["rwkv_ln_w"][l:l + 1, :], 512)
                ln_b, b_lnb = load_bcast(st, "r_ln_b", vec["rwkv_ln_b"][l:l + 1, :], 512)
                w2, b_w2 = load_weight_bf(st, "r_w2", mats["rwkv_w2"][l], 64, 512, part0=0)
                a2, b_a2 = load_weight_bf(st, "r_a2", mats["rwkv_a2"][l], 64, 512, part0=64)
                g2, b_g2 = load_weight_bf(st, "r_g2", mats["rwkv_g2"][l], 128, 512)
                triU = T_("triU", [128, 128])
                mSI = T_("mSI", [128, 256])
                mSL = T_("mSL", [128, 128])
                b_msk = p.buf()
                p.op("pool", lambda e: e.memset(triU[:], 1.0), writes=[b_msk])
                p.op("pool", lambda e: e.affine_select(out=triU[:], in_=triU[:], pattern=[[1, 128]], compare_op=ALU.is_ge, fill=0.0, base=0, channel_multiplier=-1),
                     reads=[b_msk], writes=[b_msk])
                p.op("pool", lambda e: e.memset(mSI[:], 1.0), writes=[b_msk])
                p.op("pool", lambda e: e.affine_select(out=mSI[:, 0:128], in_=mSI[:, 0:128], pattern=[[1, 128]], compare_op=ALU.is_gt, fill=0.0, base=0, channel_multiplier=-1),
                     reads=[b_msk], writes=[b_msk])
                p.op("pool", lambda e: e.affine_select(out=mSI[:, 128:256], in_=mSI[:, 128:256], pattern=[[1, 128]], compare_op=ALU.is_ge, fill=0.0, base=0, channel_multiplier=-1),
                     reads=[b_msk], writes=[b_msk])
                p.op("pool", lambda e: e.memset(mSL[:], 1.0), writes=[b_msk])
                p.op("pool", lambda e: e.affine_select(out=mSL[:], in_=mSL[:], pattern=[[-1, 128]], compare_op=ALU.is_gt, fill=0.0, base=0, channel_multiplier=1),
                     reads=[b_msk], writes=[b_msk])
                identf = T_("identf", [128, 128], BF16)
                Sst = T_("S", [64, 8, 64])
                Sb = T_("Sb", [64, 8, 64], BF16)
                b_S, b_Sb = p.buf(), p.buf()
                p.op("pool", lambda e: e.memset(Sst[:], 0.0), writes=[b_S])
                p.op("pool", lambda e: e.memset(Sb[:], 0.0), writes=[b_Sb])
                gb = [ps(st, "r_gb%d" % i, [128, 512], F32) for i in range(6)]
                b_gb = p.bufs(6)
                gbn = [0]
                tb = [ps(st, "r_tb%d" % i, [128, 8, 128], BF16) for i in range(2)]
                b_tb = p.bufs(2)
                tbn = [0]

                def bank():
                    i = gbn[0] % 6
                    gbn[0] += 1
                    return gb[i], b_gb[i]

                def tbank():
                    i = tbn[0] % 2
                    tbn[0] += 1
                    return tb[i], b_tb[i]

                P0 = [T_("P0_0", [128, 1792])] * 2
                P1 = [T_("P1_0", [128, 1792])] * 2
                b_P0, b_P1 = [p.buf()] * 2, [p.buf()] * 2
                PM = [T_("PM%d" % i, [128, 1792]) for i in range(2)]
                b_PM = p.bufs(2)
                th = T_("th", [128, 256], BF16); b_th = p.buf()
                thT = T_("thT", [128, 2, 128], BF16); b_thT = p.buf()
                lw = T_("lw", [128, 512]); b_lw = p.buf()
                alr = T_("alr", [128, 512]); b_alr = p.buf()
                gg = [T_("gg%d" % i, [128, 512]) for i in range(2)]; b_gg = p.bufs(2)
                kk = T_("kk", [128, 512]); b_kk = p.buf()
                tmp = T_("tmp", [128, 512]); b_tmp = p.buf()
                tmp2 = T_("tmp2", [128, 512]); b_tmp2 = p.buf()
                k2 = [T_("k2_%d" % i, [128, 512]) for i in range(2)]; b_k2 = p.bufs(2)
                bt = T_("bt", [128, 512]); b_bt = p.buf()
                st8 = T_("st8", [128, 8]); b_st8 = p.buf()
                cumS = T_("cumS", [128, 512]); b_cumS = p.buf()
                eC = T_("eC", [128, 512]); b_eC = p.buf()
                eCi = T_("eCi", [128, 512]); b_eCi = p.buf()
                eCx = T_("eCx", [128, 512]); b_eCx = p.buf()
                eD = T_("eD", [128, 512]); b_eD = p.buf()
                gC = [T_("gC%d" % i, [64, 8]) for i in range(2)]; b_gC = p.bufs(2)
                X4 = T_("X4", [128, 4, 512], BF16); b_X4 = p.bufs(4)
                BH = [T_("BH%d" % i, [128, 512], BF16) for i in range(2)]; b_BH = p.bufs(2)
                KH = [T_("KH%d" % i, [128, 512], BF16) for i in range(2)]; b_KH = p.bufs(2)
                VB = [T_("VB%d" % i, [128, 512], BF16) for i in range(2)]; b_VB = p.bufs(2)
                CM = [T_("CM%d" % i, [64, 8, 4, 128], BF16) for i in range(2)]; b_CM = p.bufs(2)
                MM = [T_("MM%d" % i, [128, 8, 2, 256], BF16) for i in range(2)]; b_MM = p.bufs(2)
                Pp = [T_("Pp%d" % i, [128, 8, 128], BF16) for i in range(2)]; b_Pp = p.bufs(2)
                PTp = [T_("PTp%d" % i, [128, 8, 128], BF16) for i in range(2)]; b_PTp = p.bufs(2)
                Tp = [T_("Tp%d" % i, [128, 8, 128], BF16) for i in range(2)]; b_Tp = p.bufs(2)
                Tf = [T_("Tf%d" % i, [128, 8, 128], BF16) for i in range(2)]; b_Tf = p.bufs(2)
                Ws = T_("Ws", [128, 512], BF16); b_Ws = p.buf()
                Us = T_("Us", [128, 512], BF16); b_Us = p.buf()
                yt = T_("yt", [128, 512]); b_yt = p.buf()
                yc = T_("yc", [128, 512]); b_yc = p.buf()
                yo = T_("yo", [128, 512], BF16); b_yo = p.buf()
                oTt = [T_("oTt%d" % i, [128, 4, 128], BF16) for i in range(2)]; b_oTt = p.bufs(2)
                NEGE = -math.exp(-0.5)
                RSUB = int(os.environ.get('RW_SUB', '99'))

                def H(a, h):
                    return a[:, h * 64:(h + 1) * 64]

                def pre(t):
                    i = t % 2
                    t0 = t * 128
                    p.dma("sync", lambda e: e.dma_start(out=P0[i][:], in_=prw[t0:t0 + 128, :]), reads=[B_prw[t]], writes=[b_P0[i]])
                    if t == 0:
                        p.op("pool", lambda e: e.memset(P1[i][0:1, :], 0.0), writes=[b_P1[i]])
                        p.dma("sync", lambda e: e.dma_start(out=P1[i][1:128, :], in_=prw[0:127, :]), reads=[B_prw[0]], writes=[b_P1[i]])
                    else:
                        p.dma("sync", lambda e: e.dma_start(out=P1[i][:], in_=prw[t0 - 1:t0 + 127, :]), reads=[B_prw[t], B_prw[t - 1]], writes=[b_P1[i]])
                    pm = PM[i]
                    p.op("dve", lambda e: e.tensor_tensor(out=P1[i][:], in0=P1[i][:], in1=P0[i][:], op=ALU.subtract), reads=[b_P1[i], b_P0[i]], writes=[b_P1[i]])
                    p.op("dve", lambda e: e.tensor_tensor(out=P1[i][:], in0=P1[i][:], in1=mu[:], op=ALU.mult), reads=[b_P1[i], b_mu], writes=[b_P1[i]])
                    p.op("dve", lambda e: e.tensor_tensor(out=pm[:], in0=P1[i][:], in1=P0[i][:], op=ALU.add), reads=[b_P1[i], b_P0[i]], writes=[b_PM[i]])
                    r_ = pm[:, 0:512]
                    k_ = pm[:, 512:1024]
                    v_ = pm[:, 1024:1536]
                    if RSUB < 1:
                        return
                    p.op("act", lambda e: e.activation(out=th[:, 0:64], in_=pm[:, 1536:1600], func=AF.Tanh), reads=[b_PM[i]], writes=[b_th])
                    p.op("act", lambda e: e.copy(out=th[:, 64:128], in_=pm[:, 1600:1664]), reads=[b_PM[i]], writes=[b_th])
                    p.op("act", lambda e: e.activation(out=th[:, 128:256], in_=pm[:, 1664:1792], func=AF.Sigmoid), reads=[b_PM[i]], writes=[b_th])
                    tbk0, b_tbk0 = tbank()
                    for k in range(2):
                        p.op("pe", lambda e, k=k: e.transpose(out=tbk0[:, k, :], in_=th[:, k * 128:(k + 1) * 128], identity=ident[:]),
                             reads=[b_th, b_ident], writes=[b_tbk0], sig=(k == 1))
                    p.op("act", lambda e: e.copy(out=thT[:], in_=tbk0[:, 0:2, :]), reads=[b_tbk0], writes=[b_thT])
                    pw_, b_pw = bank()
                    p.op("pe", lambda e: e.matmul(pw_[:], lhsT=thT[0:64, 0, :], rhs=w2[0:64, 0, :], start=True, stop=True), reads=[b_thT, b_w2], writes=[b_pw])
                    pa_, b_pa = bank()
                    p.op("pe", lambda e: e.matmul(pa_[:], lhsT=thT[64:128, 0, :], rhs=a2[64:128, 0, :], start=True, stop=True), reads=[b_thT, b_a2], writes=[b_pa])
                    pg_, b_pg = bank()
                    p.op("pe", lambda e: e.matmul(pg_[:], lhsT=thT[:, 1, :], rhs=g2[:, 0, :], start=True, stop=True), reads=[b_thT, b_g2], writes=[b_pg])
                    p.op("dve", lambda e: e.tensor_tensor(out=lw[:], in0=pw_[:], in1=w0[:], op=ALU.add), reads=[b_pw, b_w0], writes=[b_lw])
                    p.op("act", lambda e: e.activation(out=lw[:], in_=lw[:], func=AF.Sigmoid), reads=[b_lw], writes=[b_lw])
                    p.op("dve", lambda e: e.tensor_scalar(out=lw[:], in0=lw[:], scalar1=NEGE, scalar2=None, op0=ALU.mult), reads=[b_lw], writes=[b_lw])
                    p.op("dve", lambda e: e.tensor_tensor(out=alr[:], in0=pa_[:], in1=a0[:], op=ALU.add), reads=[b_pa, b_a0], writes=[b_alr])
                    p.op("act", lambda e: e.activation(out=alr[:], in_=alr[:], func=AF.Sigmoid), reads=[b_alr], writes=[b_alr])
                    p.op("act", lambda e: e.copy(out=gg[i][:], in_=pg_[:]), reads=[b_pg], writes=[b_gg[i]])
                    if RSUB < 2:
                        return
                    p.op("dve", lambda e: e.tensor_tensor(out=kk[:], in0=k_, in1=k_k[:], op=ALU.mult), reads=[b_PM[i], b_kk_], writes=[b_kk])
                    p.op("dve", lambda e: e.tensor_tensor(out=tmp[:], in0=kk[:], in1=kk[:], op=ALU.mult), reads=[b_kk], writes=[b_tmp])
                    p.op("dve", lambda e: e.tensor_reduce(out=st8[:], in_=tmp[:].rearrange("p (h c) -> p h c", c=64), axis=AX.X, op=ALU.add), reads=[b_tmp], writes=[b_st8])
                    p.op("act", lambda e: e.activation(out=st8[:], in_=st8[:], func=AF.Sqrt), reads=[b_st8], writes=[b_st8])
                    p.op("dve", lambda e: e.tensor_scalar(out=st8[:], in0=st8[:], scalar1=1e-12, scalar2=None, op0=ALU.max), reads=[b_st8], writes=[b_st8])
                    p.op("dve", lambda e: e.reciprocal(out=st8[:], in_=st8[:]), reads=[b_st8], writes=[b_st8])
                    for h in range(8):
                        p.op("dve", lambda e, h=h: e.tensor_scalar(out=H(kk, h), in0=H(kk, h), scalar1=st8[:, h:h + 1], scalar2=None, op0=ALU.mult),
                             reads=[b_kk, b_st8], writes=[b_kk])
                    p.op("dve", lambda e: e.scalar_tensor_tensor(out=tmp[:], in0=alr[:], scalar=-1.0, in1=k_a[:], op0=ALU.add, op1=ALU.mult),
                         reads=[b_alr, b_ka_], writes=[b_tmp])
                    p.op("dve", lambda e: e.scalar_tensor_tensor(out=k2[i][:], in0=tmp[:], scalar=1.0, in1=k_, op0=ALU.add, op1=ALU.mult),
                         reads=[b_tmp, b_PM[i]], writes=[b_k2[i]])
                    p.op("dve", lambda e: e.tensor_tensor(out=bt[:], in0=kk[:], in1=alr[:], op=ALU.mult), reads=[b_kk, b_alr], writes=[b_bt])
                    if RSUB < 3:
                        return
                    pc_, b_pc = bank()
                    p.op("pe", lambda e: e.matmul(pc_[:], lhsT=triU[:], rhs=lw[:], start=True, stop=True), reads=[b_msk, b_lw], writes=[b_pc])
                    ptot, b_ptot = bank()
                    p.op("pe", lambda e: e.matmul(ptot[:], lhsT=ones_f[:], rhs=lw[:], start=True, stop=True), reads=[b_ones, b_lw], writes=[b_ptot])
                    pgc, b_pgc = bank()
                    for hd in range(8):
                        p.op("pe", lambda e, hd=hd: e.matmul(pgc[0:64, hd:hd + 1], lhsT=lw[:, hd * 64:(hd + 1) * 64], rhs=ones_f[:, 0:1], start=True, stop=True),
                             reads=[b_lw, b_ones], writes=[b_pgc], sig=(hd == 7))
                    p.op("act", lambda e: e.activation(out=gC[i][:], in_=pgc[0:64, 0:8], func=AF.Exp), reads=[b_pgc], writes=[b_gC[i]])
                    p.op("act", lambda e: e.copy(out=cumS[:], in_=pc_[:]), reads=[b_pc], writes=[b_cumS])
                    p.op("act", lambda e: e.activation(out=eC[:], in_=cumS[:], func=AF.Exp), reads=[b_cumS], writes=[b_eC])
                    p.op("act", lambda e: e.activation(out=eCi[:], in_=cumS[:], func=AF.Exp, scale=-1.0), reads=[b_cumS], writes=[b_eCi])
                    p.op("dve", lambda e: e.tensor_tensor(out=tmp[:], in0=cumS[:], in1=lw[:], op=ALU.subtract), reads=[b_cumS, b_lw], writes=[b_tmp])
                    p.op("act", lambda e: e.activation(out=eCx[:], in_=tmp[:], func=AF.Exp), reads=[b_tmp], writes=[b_eCx])
                    p.op("dve", lambda e: e.tensor_tensor(out=tmp2[:], in0=ptot[:], in1=cumS[:], op=ALU.subtract), reads=[b_ptot, b_cumS], writes=[b_tmp2])
                    p.op("act", lambda e: e.activation(out=eD[:], in_=tmp2[:], func=AF.Exp), reads=[b_tmp2], writes=[b_eD])
                    if RSUB < 4:
                        return
                    p.op("dve", lambda e: e.scalar_tensor_tensor(out=X4[:, 0, :], in0=kk[:], scalar=-1.0, in1=eCx[:], op0=ALU.mult, op1=ALU.mult),
                         reads=[b_kk, b_eCx], writes=[b_X4[0]])
                    p.op("dve", lambda e: e.tensor_tensor(out=X4[:, 1, :], in0=r_, in1=eC[:], op=ALU.mult), reads=[b_PM[i], b_eC], writes=[b_X4[1]])
                    p.op("dve", lambda e: e.tensor_tensor(out=X4[:, 2, :], in0=bt[:], in1=eCi[:], op=ALU.mult), reads=[b_bt, b_eCi], writes=[b_X4[2]])
                    p.op("dve", lambda e: e.tensor_tensor(out=X4[:, 3, :], in0=k2[i][:], in1=eCi[:], op=ALU.mult), reads=[b_k2[i], b_eCi], writes=[b_X4[3]])
                    p.op("dve", lambda e: e.tensor_tensor(out=BH[i][:], in0=bt[:], in1=eD[:], op=ALU.mult), reads=[b_bt, b_eD], writes=[b_BH[i]])
                    p.op("dve", lambda e: e.tensor_tensor(out=KH[i][:], in0=k2[i][:], in1=eD[:], op=ALU.mult), reads=[b_k2[i], b_eD], writes=[b_KH[i]])
                    p.op("act", lambda e: e.copy(out=VB[i][:], in_=v_), reads=[b_PM[i]], writes=[b_VB[i]])
                    if RSUB < 5:
                        return
                    for hd in range(8):
                        tbk, b_tbk = tbank()
                        for x in range(4):
                            p.op("pe", lambda e, hd=hd, x=x, tbk=tbk: e.transpose(out=tbk[0:64, x, :], in_=X4[:, x, hd * 64:(hd + 1) * 64], identity=ident[:]),
                                 reads=[b_X4[x], b_ident], writes=[b_tbk], sig=(x == 3))
                        if hd % 2 == 0:
                            p.op("act", lambda e, hd=hd, tbk=tbk: e.copy(out=CM[i][:, hd, :, :], in_=tbk[0:64, 0:4, :]), reads=[b_tbk], writes=[b_CM[i]])
                        else:
                            p.op("dve", lambda e, hd=hd, tbk=tbk: e.tensor_copy(out=CM[i][:, hd, :, :], in_=tbk[0:64, 0:4, :]), reads=[b_tbk], writes=[b_CM[i]])
                    if RSUB < 6:
                        return
                    for hd in range(8):
                        pm_, b_pm = bank()
                        ar = CM[i][:, hd, :, :].rearrange("p x t -> p (x t)")[:, 0:256]
                        p.op("pe", lambda e, pm_=pm_, hd=hd, ar=ar: e.matmul(pm_[:, 0:256], lhsT=CM[i][:, hd, 2, :], rhs=ar, start=True, stop=True),
                             reads=[b_CM[i]], writes=[b_pm], sig=False)
                        p.op("pe", lambda e, pm_=pm_, hd=hd, ar=ar: e.matmul(pm_[:, 256:512], lhsT=CM[i][:, hd, 3, :], rhs=ar, start=True, stop=True),
                             reads=[b_CM[i]], writes=[b_pm])
                        if os.environ.get('RW_M') == '0':
                            continue
                        p.op("dve", lambda e, pm_=pm_, hd=hd: e.tensor_tensor(out=MM[i][:, hd, 0, :], in0=pm_[:, 0:256], in1=mSI[:], op=ALU.mult),
                             reads=[b_pm, b_msk], writes=[b_MM[i]])
                        p.op("dve", lambda e, pm_=pm_, hd=hd: e.tensor_tensor(out=MM[i][:, hd, 1, :], in0=pm_[:, 256:512], in1=mSI[:], op=ALU.mult),
                             reads=[b_pm, b_msk], writes=[b_MM[i]])
                    if os.environ.get('RW_M') == '1':
                        return
                    for hg in range(2):
                        pp_, b_pp_ = bank()
                        for hq in range(4):
                            hd = hg * 4 + hq
                            p.op("pe", lambda e, pp_=pp_, hd=hd, hq=hq: e.matmul(pp_[:, hq * 128:(hq + 1) * 128], lhsT=CM[i][:, hd, 0, :], rhs=CM[i][:, hd, 2, :],
                                                                                   start=True, stop=True),
                                 reads=[b_CM[i]], writes=[b_pp_], sig=(hq == 3))
                        for hq in range(4):
                            hd = hg * 4 + hq
                            p.op("dve", lambda e, pp_=pp_, hq=hq, hd=hd: e.tensor_tensor(out=PTp[0][:, hd, :], in0=pp_[:, hq * 128:(hq + 1) * 128], in1=mSL[:], op=ALU.mult),
                                 reads=[b_pp_, b_msk], writes=[b_PTp[0]])
                    if RSUB < 7:
                        return
                    for hd in range(8):
                        p.op("pool", lambda e, hd=hd: e.tensor_copy(out=Pp[0][:, hd, :], in_=MM[i][:, hd, 0, 0:128]), reads=[b_MM[i]], writes=[b_Pp[0]])
                        p.op("dve", lambda e, hd=hd: e.tensor_tensor(out=Tp[0][:, hd, :], in0=MM[i][:, hd, 0, 0:128], in1=ident[:], op=ALU.add),
                             reads=[b_MM[i], b_ident], writes=[b_Tp[0]])
                    cur = 0
                    for lvl in range(6):
                        nxt = 1 - cur
                        last = (lvl == 5)
                        for hg in range(2):
                            if not last:
                                p2, b_p2 = bank()
                            p2t, b_p2t = bank()
                            for hq in range(4):
                                hd = hg * 4 + hq
                                sl = slice(hq * 128, (hq + 1) * 128)
                                if not last:
                                    p.op("pe", lambda e, p2=p2, hd=hd, sl=sl, cur=cur: e.matmul(p2[:, sl], lhsT=PTp[cur][:, hd, :], rhs=Pp[cur][:, hd, :], start=True, stop=True),
                                         reads=[b_PTp[cur], b_Pp[cur]], writes=[b_p2], sig=(hq == 3))
                                p.op("pe", lambda e, p2t=p2t, hd=hd, sl=sl, cur=cur: e.matmul(p2t[:, sl], lhsT=Pp[cur][:, hd, :], rhs=PTp[cur][:, hd, :], start=True, stop=True),
                                     reads=[b_PTp[cur], b_Pp[cur]], writes=[b_p2t], sig=(hq == 3))
                            hs = slice(hg * 4, hg * 4 + 4)
                            if not last:
                                p.op("act", lambda e, p2=p2, hs=hs, nxt=nxt: e.copy(out=Pp[nxt][:, hs, :], in_=p2[:].rearrange("p (h c) -> p h c", c=128)),
                                     reads=[b_p2], writes=[b_Pp[nxt]])
                            p.op("dve", lambda e, p2t=p2t, hs=hs, nxt=nxt: e.tensor_copy(out=PTp[nxt][:, hs, :], in_=p2t[:].rearrange("p (h c) -> p h c", c=128)),
                                 reads=[b_p2t], writes=[b_PTp[nxt]])
                            ptu, b_ptu = bank()
                            for hq in range(4):
                                hd = hg * 4 + hq
                                sl = slice(hq * 128, (hq + 1) * 128)
                                p.op("pe", lambda e, ptu=ptu, hd=hd, sl=sl, cur=cur, nxt=nxt: e.matmul(ptu[:, sl], lhsT=PTp[nxt][:, hd, :], rhs=Tp[cur][:, hd, :], start=True, stop=True),
                                     reads=[b_PTp[nxt], b_Tp[cur]], writes=[b_ptu], sig=(hq == 3))
                            dstT = Tf[i] if last else Tp[nxt]
                            b_dstT = b_Tf[i] if last else b_Tp[nxt]
                            p.op("dve", lambda e, ptu=ptu, hs=hs, cur=cur, dstT=dstT: e.tensor_tensor(out=dstT[:, hs, :], in0=ptu[:].rearrange("p (h c) -> p h c", c=128),
                                                                                                 in1=Tp[cur][:, hs, :], op=ALU.add),
                                 reads=[b_ptu, b_Tp[cur]], writes=[b_dstT])
                        cur = nxt

                def chain(t):
                    i = t % 2
                    if RSUB < 8:
                        return
                    pw_, b_pw = bank()
                    for hd in range(8):
                        hp, h2 = hd // 2, hd % 2
                        pr = slice(h2 * 64, h2 * 64 + 64)
                        cs_ = slice(hd * 64, hd * 64 + 64)
                        p.op("pe", lambda e, hd=hd, cs_=cs_: e.matmul(pw_[:, cs_], lhsT=MM[i][:, hd, 1, 0:128], rhs=VB[i][:, cs_], start=True, stop=False),
                             reads=[b_MM[i], b_VB[i]], writes=[b_pw], sig=False)
                        p.op("pe", lambda e, hd=hd, cs_=cs_: e.matmul(pw_[:, cs_], lhsT=CM[i][:, hd, 0, :], rhs=Sb[:, hd, :], start=False, stop=True),
                             reads=[b_CM[i], b_Sb], writes=[b_pw], sig=(hd == 7))
                    p.op("act", lambda e: e.copy(out=Ws[:], in_=pw_[:]), reads=[b_pw], writes=[b_Ws])
                    pu_, b_pu = bank()
                    for hd in range(8):
                        cs_ = slice(hd * 64, hd * 64 + 64)
                        p.op("pe", lambda e, hd=hd, cs_=cs_: e.matmul(pu_[:, cs_], lhsT=Tf[i][:, hd, :], rhs=Ws[:, cs_], start=True, stop=True),
                             reads=[b_Tf[i], b_Ws], writes=[b_pu], sig=(hd == 7))
                    p.op("dve", lambda e: e.tensor_copy(out=Us[:], in_=pu_[:]), reads=[b_pu], writes=[b_Us])
                    py_, b_py = bank()
                    for hd in range(8):
                        hp, h2 = hd // 2, hd % 2
                        pr = slice(h2 * 64, h2 * 64 + 64)
                        cs_ = slice(hd * 64, hd * 64 + 64)
                        p.op("pe", lambda e, hd=hd, cs_=cs_: e.matmul(py_[:, cs_], lhsT=MM[i][:, hd, 1, 128:256], rhs=VB[i][:, cs_], start=True, stop=False),
                             reads=[b_MM[i], b_VB[i]], writes=[b_py], sig=False)
                        p.op("pe", lambda e, hd=hd, cs_=cs_: e.matmul(py_[:, cs_], lhsT=CM[i][:, hd, 1, :], rhs=Sb[:, hd, :], start=False, stop=False),
                             reads=[b_CM[i], b_Sb], writes=[b_py], sig=False)
                        p.op("pe", lambda e, hd=hd, cs_=cs_: e.matmul(py_[:, cs_], lhsT=MM[i][:, hd, 0, 128:256], rhs=Us[:, cs_], start=False, stop=True),
                             reads=[b_MM[i], b_Us], writes=[b_py], sig=(hd == 7))
                    pS_, b_pS = bank()
                    for hd in range(8):
                        sl = slice(hd * 64, (hd + 1) * 64)
                        p.op("pe", lambda e, sl=sl: e.matmul(pS_[0:64, sl], lhsT=BH[i][:, sl], rhs=Us[:, sl], start=True, stop=False),
                             reads=[b_BH[i], b_Us], writes=[b_pS], sig=False)
                        p.op("pe", lambda e, sl=sl: e.matmul(pS_[0:64, sl], lhsT=KH[i][:, sl], rhs=VB[i][:, sl], start=False, stop=True),
                             reads=[b_KH[i], b_VB[i]], writes=[b_pS], sig=(hd == 7))
                    p.op("act", lambda e: e.copy(out=yt[:], in_=py_[:]), reads=[b_py], writes=[b_yt])
                    for hd in range(8):
                        sl = slice(hd * 64, (hd + 1) * 64)
                        p.op("dve", lambda e, hd=hd, sl=sl: e.scalar_tensor_tensor(out=Sst[:, hd, :], in0=Sst[:, hd, :], scalar=gC[i][:, hd:hd + 1],
                                                                             in1=pS_[0:64, sl], op0=ALU.mult, op1=ALU.add),
                             reads=[b_S, b_gC[i], b_pS], writes=[b_S])
                    p.op("act", lambda e: e.copy(out=Sb[:], in_=Sst[:]), reads=[b_S], writes=[b_Sb])

                def post(t):
                    i = t % 2
                    if RSUB < 9:
                        return
                    pm = PM[i]
                    r_ = pm[:, 0:512]
                    v_ = pm[:, 1024:1536]
                    p.op("dve", lambda e: e.tensor_reduce(out=st8[:], in_=yt[:].rearrange("p (h c) -> p h c", c=64), axis=AX.X, op=ALU.add), reads=[b_yt], writes=[b_st8])
                    p.op("dve", lambda e: e.tensor_scalar(out=st8[:], in0=st8[:], scalar1=1.0 / 64, scalar2=None, op0=ALU.mult), reads=[b_st8], writes=[b_st8])
                    for h in range(8):
                        p.op("dve", lambda e, h=h: e.tensor_scalar(out=H(yc, h), in0=H(yt, h), scalar1=st8[:, h:h + 1], scalar2=None, op0=ALU.subtract),
                             reads=[b_yt, b_st8], writes=[b_yc])
                    p.op("dve", lambda e: e.tensor_tensor(out=tmp[:], in0=yc[:], in1=yc[:], op=ALU.mult), reads=[b_yc], writes=[b_tmp])
                    p.op("dve", lambda e: e.tensor_reduce(out=st8[:], in_=tmp[:].rearrange("p (h c) -> p h c", c=64), axis=AX.X, op=ALU.add), reads=[b_tmp], writes=[b_st8])
                    p.op("act", lambda e: e.activation(out=st8[:], in_=st8[:], func=AF.Sqrt, scale=1.0 / 64, bias=g.eps_tiles[64e-5][:]), reads=[b_st8, b_eps], writes=[b_st8])
                    p.op("dve", lambda e: e.reciprocal(out=st8[:], in_=st8[:]), reads=[b_st8], writes=[b_st8])
                    for h in range(8):
                        p.op("dve", lambda e, h=h: e.tensor_scalar(out=H(yc, h), in0=H(yc, h), scalar1=st8[:, h:h + 1], scalar2=None, op0=ALU.mult),
                             reads=[b_yc, b_st8], writes=[b_yc])
                    p.op("dve", lambda e: e.tensor_tensor(out=yc[:], in0=yc[:], in1=ln_w[:], op=ALU.mult), reads=[b_yc, b_lnw], writes=[b_yc])
                    p.op("dve", lambda e: e.tensor_tensor(out=yc[:], in0=yc[:], in1=ln_b[:], op=ALU.add), reads=[b_yc, b_lnb], writes=[b_yc])
                    p.op("dve", lambda e: e.tensor_tensor(out=tmp2[:], in0=r_, in1=k2[i][:], op=ALU.mult), reads=[b_PM[i], b_k2[i]], writes=[b_tmp2])
                    p.op("dve", lambda e: e.tensor_tensor(out=tmp2[:], in0=tmp2[:], in1=r_k[:], op=ALU.mult), reads=[b_tmp2, b_rk_], writes=[b_tmp2])
                    p.op("dve", lambda e: e.tensor_reduce(out=st8[:], in_=tmp2[:].rearrange("p (h c) -> p h c", c=64), axis=AX.X, op=ALU.add), reads=[b_tmp2], writes=[b_st8])
                    for h in range(8):
                        p.op("dve", lambda e, h=h: e.scalar_tensor_tensor(out=H(yc, h), in0=v_[:, h * 64:(h + 1) * 64], scalar=st8[:, h:h + 1], in1=H(yc, h),
                                                                         op0=ALU.mult, op1=ALU.add),
                             reads=[b_PM[i], b_st8, b_yc], writes=[b_yc])
                    p.op("dve", lambda e: e.tensor_tensor(out=yo[:], in0=yc[:], in1=gg[i][:], op=ALU.mult), reads=[b_yc, b_gg[i]], writes=[b_yo])
                    tbk, b_tbk = tbank()
                    for k in range(4):
                        p.op("pe", lambda e, k=k: e.transpose(out=tbk[:, k, :], in_=yo[:, k * 128:(k + 1) * 128], identity=ident[:]),
                             reads=[b_yo, b_ident], writes=[b_tbk], sig=(k == 3))
                    p.op("act", lambda e: e.copy(out=oTt[i][:], in_=tbk[:, 0:4, :]), reads=[b_tbk], writes=[b_oTt[i]])
                    p.dma("sync", lambda e: e.dma_start(out=oT["rwkv"][:, t * 128:(t + 1) * 128].rearrange("(k p) t -> p k t", p=128), in_=oTt[i][:]),
                          reads=[b_oTt[i]], writes=[B_oT["rwkv"][t // 4]])

                NTR = int(os.environ.get('RW_NT', NT))
                pre(0)
                for t in range(NTR):
                    if t + 1 < NTR:
                        pre(t + 1)
                    chain(t)
                    post(t)
                p.barrier()

        p.barrier()
        PH = {"proj": phase_proj, "rwkv": phase_rwkv, "mla": phase_mla, "diff": phase_diff, "merge": phase_merge, "ffn": phase_ffn}
        if phases is None:
            phases = [("consts", 0)]
            for l in range(DEPTH):
                phases += [(n, l) for n in ("proj", "rwkv", "mla", "diff", "merge", "ffn")]
            phases += [("final", 0)]
        for (n, l) in phases:
            if n == "consts":
                setup_attn_consts()
            elif n == "final":
                phase_final()
            else:
                PH[n](l)
        p.barrier()
        p.emit()
    return nc


def prep_inputs(inputs, b):
    f = np.float32
    m = {}
    m["x"] = np.ascontiguousarray(inputs["x"][b])
    pos = np.asarray(inputs["positions"][b]).astype(np.int32)
    m["pos_tm"] = np.ascontiguousarray(pos.reshape(NT, 128).T)
    m["pos_row"] = np.ascontiguousarray(pos[:256].reshape(1, 256))
    m["rel_bias"] = np.ascontiguousarray(np.asarray(inputs["rel_bias"], f).reshape(1, 128))
    for n in VEC_ROWS:
        m[n] = np.ascontiguousarray(np.asarray(inputs[n], f).reshape(DEPTH, VEC_LEN[n]))
    for n in MATS:
        m[n] = np.ascontiguousarray(np.asarray(inputs[n], f))
    m["b_gate_cm"] = np.ascontiguousarray(np.asarray(inputs["b_gate"], f).reshape(DEPTH, 24, 128).transpose(0, 2, 1))
    m["conv_w_cm"] = np.ascontiguousarray(np.asarray(inputs["ffn_conv_w"], f).reshape(DEPTH, 3, 44, 128).transpose(0, 3, 1, 2))
    m["conv_b_cm"] = np.ascontiguousarray(np.asarray(inputs["ffn_conv_b"], f).reshape(DEPTH, 44, 128).transpose(0, 2, 1))
    m["subln_cm"] = np.ascontiguousarray(np.asarray(inputs["diff_subln"], f).reshape(DEPTH, 128, 1))
    m["diff_lambda"] = np.ascontiguousarray(np.asarray(inputs["diff_lambda"], f).reshape(DEPTH, 1, 256))
    m["norm_final"] = np.ascontiguousarray(np.asarray(inputs["norm_final"], f).reshape(1, D))
    return m


_NC = {}


def kernel(**inputs):
    if "nc" not in _NC:
        _NC["nc"] = build()
    nc = _NC["nc"]
    in_maps = [prep_inputs(inputs, b) for b in range(8)]
    res = run_bass_kernel_spmd(nc, in_maps, core_ids=list(range(8)))
    return np.stack([np.asarray(r["out"], np.float32) for r in res.results], axis=0)
```

```python
import math
import os
import numpy as np
from contextlib import ExitStack
import concourse.bass as bass
import concourse.mybir as mybir
from concourse.bass_utils import run_bass_kernel_spmd

F32 = mybir.dt.float32
BF16 = mybir.dt.bfloat16
I32 = mybir.dt.int32
AF = mybir.ActivationFunctionType
ALU = mybir.AluOpType
AX = mybir.AxisListType

S = 4096
NT = 32
D = 1024
DEPTH = 2
DFF = 2816
INC = 6816
C_RW, C_ML, C_DF, C_GT = 0, 1792, 2208, 3744

EPOCH = 24000
NDSEM = 48


class Buf:
    __slots__ = ("w", "r", "name")

    def __init__(self, name=""):
        self.w = {}
        self.r = {}
        self.name = name


class EngState:
    def __init__(self, name):
        self.name = name
        self.ops = []
        self.sem = None
        self.cnt = 0
        self.wm = {}
        self.pending = False


class P:
    def __init__(self, nc, stack):
        self.nc = nc
        self.stack = stack
        self.sems = []
        self.eng = {k: EngState(k) for k in ("sync", "act", "dve", "pool", "pe")}
        self.dsem = []
        self.dval = []
        self.dnext = 0
        for i in range(NDSEM):
            self.dsem.append(self._newsem("d%d" % i))
            self.dval.append(0)
        for e in self.eng.values():
            e.sem = self._newsem(e.name + "0")
        self.nops = 0

    def _newsem(self, name):
        s = self.stack.enter_context(self.nc.semaphore(name))
        self.sems.append(s)
        return len(self.sems) - 1

    def buf(self, name=""):
        return Buf(name)

    def bufs(self, n, name=""):
        return [Buf(name + str(i)) for i in range(n)]

    def _need(self, es, waits, sem, val):
        if es.wm.get(sem, 0) >= val:
            return
        es.wm[sem] = val
        for i, (s, v) in enumerate(waits):
            if s == sem:
                waits[i] = (s, max(v, val))
                return
        waits.append((sem, val))

    def _deps(self, es, mysem, waits, reads, writes):
        for b in reads:
            for s, v in b.w.items():
                self._need(es, waits, s, v)
        skip_own = (es.name == "pe")
        for b in writes:
            for s, v in b.w.items():
                if s != mysem or not skip_own:
                    self._need(es, waits, s, v)
            for s, v in b.r.items():
                if s != mysem or not skip_own:
                    self._need(es, waits, s, v)

    def _mark(self, sem, val, reads, writes):
        for b in reads:
            if b.r.get(sem, 0) < val:
                b.r[sem] = val
        for b in writes:
            b.w = {sem: val}
            b.r = {}

    def op(self, eng, fn, reads=(), writes=(), sig=True):
        es = self.eng[eng]
        if es.cnt >= EPOCH and not es.pending:
            es.sem = self._newsem(es.name + str(len(self.sems)))
            es.cnt = 0
        waits = []
        self._deps(es, es.sem, waits, reads, writes)
        val = es.cnt + 1
        if sig:
            es.cnt = val
            es.pending = False
        else:
            es.pending = True
        es.ops.append((waits, fn, es.sem if sig else None, 1))
        self._mark(es.sem, val, reads, writes)
        self.nops += 1

    def dma(self, q, fn, reads=(), writes=()):
        es = self.eng[q]
        i = self.dnext
        self.dnext = (self.dnext + 1) % NDSEM
        sem = self.dsem[i]
        waits = []
        if self.dval[i] > 0:
            self._need(es, waits, sem, self.dval[i])
        self._deps(es, sem, waits, reads, writes)
        self.dval[i] += 16
        es.ops.append((waits, fn, sem, 16))
        self._mark(sem, self.dval[i], reads, writes)
        self.nops += 1

    def barrier(self):
        targets = []
        for e in self.eng.values():
            if e.cnt > 0:
                assert not e.pending
                targets.append((e.sem, e.cnt))
        for i in range(NDSEM):
            if self.dval[i] > 0:
                targets.append((self.dsem[i], self.dval[i]))
        for es in self.eng.values():
            waits = []
            for s, v in targets:
                if s != es.sem:
                    self._need(es, waits, s, v)
            if waits:
                es.ops.append((waits, None, None, 0))
        self.emit()

    def emit(self):
        nc = self.nc
        sems = self.sems
        engs = self.eng

        def run(e, es):
            for waits, fn, sem, inc in es.ops:
                for s, v in waits:
                    e.wait_ge(sems[s], v)
                if fn is None:
                    continue
                ins = fn(e)
                if sem is not None:
                    ins.then_inc(sems[sem], inc)

        if not any(es.ops for es in engs.values()):
            return
        with nc.Block() as block:
            @block.sync
            def _(e):
                run(e, engs["sync"])

            @block.scalar
            def _(e):
                run(e, engs["act"])

            @block.vector
            def _(e):
                run(e, engs["dve"])

            @block.gpsimd
            def _(e):
                run(e, engs["pool"])

            @block.tensor
            def _(e):
                run(e, engs["pe"])
        for es in engs.values():
            es.ops = []


VEC_ROWS = ["norm_mix", "rwkv_mu", "rwkv_w0", "rwkv_a0", "rwkv_k_k", "rwkv_k_a", "rwkv_r_k",
            "rwkv_ln_w", "rwkv_ln_b", "mla_q_norm", "mla_kv_norm", "norm_ffn", "b_gate"]
VEC_LEN = {"norm_mix": 1024, "rwkv_mu": 1792, "rwkv_w0": 512, "rwkv_a0": 512, "rwkv_k_k": 512, "rwkv_k_a": 512,
           "rwkv_r_k": 512, "rwkv_ln_w": 512, "rwkv_ln_b": 512, "mla_q_norm": 256, "mla_kv_norm": 128,
           "norm_ffn": 1024, "b_gate": 3072}
MATS = {"w_in": (1024, INC), "rwkv_w2": (64, 512), "rwkv_a2": (64, 512), "rwkv_g2": (128, 512),
        "mla_w_uq": (256, 768), "mla_w_ukv": (128, 1024), "w_branch_rwkv": (512, 1024), "w_branch_mla": (512, 1024),
        "w_branch_diff": (512, 1024), "w_o": (1024, 1024), "ffn_w_up": (1024, 2 * DFF), "ffn_w_down": (DFF, 1024)}


class K:
    pass


def build(dbg=None, feed=None, phases=None):
    dbg = dbg or set()
    feed = feed or set()
    nc = bass.Bass("TRN2", target_bir_lowering=False)
    g = K()
    g.nc = nc

    def din(name, shape, dt=F32):
        return nc.dram_tensor(name, list(shape), dt, kind="ExternalInput").ap()

    def dscr(name, shape, dt=F32):
        kind = "ExternalOutput" if name in dbg else ("ExternalInput" if name in feed else "Internal")
        return nc.dram_tensor(name, list(shape), dt, kind=kind).ap()

    x_in = din("x", [S, D])
    pos_tm = din("pos_tm", [128, NT], I32)
    pos_row = din("pos_row", [1, 256], I32)
    rel_bias = din("rel_bias", [1, 128])
    vec = {n: din(n, [DEPTH, VEC_LEN[n]]) for n in VEC_ROWS}
    mats = {n: din(n, [DEPTH, MATS[n][0], MATS[n][1]]) for n in MATS}
    b_gate_cm = din("b_gate_cm", [DEPTH, 128, 24])
    conv_w_cm = din("conv_w_cm", [DEPTH, 128, 3, 44])
    conv_b_cm = din("conv_b_cm", [DEPTH, 128, 44])
    subln_cm = din("subln_cm", [DEPTH, 128, 1])
    diff_lambda = din("diff_lambda", [DEPTH, 1, 256])
    norm_final = din("norm_final", [1, D])
    out = nc.dram_tensor("out", [S, D], F32, kind="ExternalOutput").ap()

    xres = dscr("xres", [S, D])
    prw = dscr("prw", [S, 1792])
    pml = dscr("pml", [S, 416])
    qkT = dscr("qkT", [1024, S], BF16)
    vdf = dscr("vdf", [S, 512], BF16)
    gT = dscr("gT", [3072, S], BF16)
    oT = {n: dscr("oT_" + n, [512, S], BF16) for n in ("rwkv", "mla", "diff")}

    with ExitStack() as top:
        p = P(nc, top)
        g.p = p

        uid = [0]

        def sb(st, name, shape, dt):
            uid[0] += 1
            return st.enter_context(nc.sbuf_tensor("%s_u%d" % (name, uid[0]), list(shape), dt))

        def ps(st, name, shape, dt=F32):
            uid[0] += 1
            return st.enter_context(nc.psum_tensor("%s_u%d" % (name, uid[0]), list(shape), dt))

        B_x = p.bufs(NT, "x")
        B_prw = p.bufs(NT, "prw")
        B_pml = p.bufs(NT, "pml")
        B_qkT = p.bufs(8, "qkT")
        B_vdf = p.bufs(NT, "vdf")
        B_gT = p.bufs(8, "gT")
        B_oT = {n: p.bufs(8, "oT" + n) for n in oT}
        B_out = p.buf("out")

        ident = sb(top, "ident", [128, 128], BF16)
        b_ident = p.buf()
        p.op("pool", lambda e: e.memset(ident[:], 1.0), writes=[b_ident])
        p.op("pool", lambda e: e.affine_select(out=ident[:], in_=ident[:], pattern=[[-1, 128]],
                                               compare_op=ALU.is_equal, fill=0.0, base=0, channel_multiplier=1),
             reads=[b_ident], writes=[b_ident])
        ones_bf = sb(top, "ones_bf", [128, 128], BF16)
        ones_f = sb(top, "ones_f", [128, 128], F32)
        b_ones = p.buf()
        p.op("pool", lambda e: e.memset(ones_bf[:], 1.0), writes=[b_ones])
        p.op("pool", lambda e: e.memset(ones_f[:], 1.0), writes=[b_ones])
        g.ident, g.b_ident, g.ones_bf, g.ones_f, g.b_ones = ident, b_ident, ones_bf, ones_f, b_ones

        def norm_transpose_phase(st, src_ap_fn, src_bufs, gvec_ap, hT, b_hT, eps=1e-6):
            gt = sb(st, "nt_g", [128, D], F32)
            b_g = p.buf()
            p.dma("sync", lambda e: e.dma_start(out=gt[:], in_=gvec_ap.partition_broadcast(128)), writes=[b_g])
            xt = [sb(st, "nt_x%d" % i, [128, D], F32) for i in range(2)]
            b_xt = p.bufs(2)
            junk = sb(st, "nt_junk", [128, D], BF16)
            b_junk = p.buf()
            ss = [sb(st, "nt_ss%d" % i, [128, 1], F32) for i in range(2)]
            b_ss = p.bufs(2)
            hb = [sb(st, "nt_hb%d" % i, [128, D], BF16) for i in range(2)]
            b_hb = p.bufs(2)
            pt = [ps(st, "nt_pt%d" % i, [128, 8, 128], BF16) for i in range(2)]
            b_pt = p.bufs(2)
            for t in range(NT):
                i = t % 2
                p.dma("sync", lambda e, t=t, i=i: e.dma_start(out=xt[i][:], in_=src_ap_fn(t)),
                      reads=[src_bufs[t]], writes=[b_xt[i]])
                p.op("pool", lambda e, i=i: e.memset(ss[i][:], 0.0), writes=[b_ss[i]])
                p.op("act", lambda e, i=i: e.activation(out=junk[:], in_=xt[i][:], func=AF.Square, accum_out=ss[i][:]),
                     reads=[b_xt[i], b_ss[i]], writes=[b_junk, b_ss[i]])
                p.op("act", lambda e, i=i: e.activation(out=ss[i][:], in_=ss[i][:], func=AF.Sqrt, scale=1.0 / D, bias=g.eps_tiles[eps][:]),
                     reads=[b_ss[i]], writes=[b_ss[i]])
                p.op("dve", lambda e, i=i: e.reciprocal(out=ss[i][:], in_=ss[i][:]), reads=[b_ss[i]], writes=[b_ss[i]])
                p.op("dve", lambda e, i=i: e.scalar_tensor_tensor(out=hb[i][:], in0=xt[i][:], scalar=ss[i][:, 0:1], in1=gt[:],
                                                                  op0=ALU.mult, op1=ALU.mult),
                     reads=[b_xt[i], b_ss[i], b_g], writes=[b_hb[i]])
                for k in range(8):
                    p.op("pe", lambda e, i=i, k=k: e.transpose(out=pt[i][:, k, :], in_=hb[i][:, k * 128:(k + 1) * 128], identity=ident[:]),
                         reads=[b_hb[i], b_ident], writes=[b_pt[i]], sig=(k == 7))
                p.op("act", lambda e, i=i, t=t: e.copy(out=hT[:, :, t * 128:(t + 1) * 128], in_=pt[i][:]),
                     reads=[b_pt[i]], writes=[b_hT[t]])

        g.eps_tiles = {}
        b_eps = p.buf()
        for ev in (1e-6, 1e-5, 64e-5, 0.0, 1.0):
            tl = sb(top, "eps%d" % len(g.eps_tiles), [128, 1], F32)
            p.op("pool", lambda e, tl=tl, ev=ev: e.memset(tl[:], ev), writes=[b_eps])
            g.eps_tiles[ev] = tl

        g.wstg = [sb(top, "wstg%d" % i, [128, 8, 512], F32) for i in range(2)]
        g.b_wstg = p.bufs(2)
        g.nstg = [0]

        class WLoader:
            def __init__(self, st, name, kch, maxcol, nbuf=2):
                self.wb = [sb(st, name + "_b%d" % i, [128, kch, maxcol], BF16) for i in range(nbuf)]
                self.b_wb = p.bufs(nbuf)
                self.n = 0
                self.nbuf = nbuf
                self.kch = kch

            def load(self, wap, c0, ncol, krows=None):
                i = self.n % self.nbuf
                self.n += 1
                kch = self.kch
                si = g.nstg[0] % 2
                g.nstg[0] += 1
                stg, wb = g.wstg[si], self.wb[i]
                p.dma("sync", lambda e: e.dma_start(out=stg[:, 0:kch, 0:ncol],
                                                    in_=wap[:, c0:c0 + ncol].rearrange("(k p) n -> p k n", p=128)),
                      writes=[g.b_wstg[si]])
                p.op("pool", lambda e: e.tensor_copy(out=wb[:, :, 0:ncol], in_=stg[:, 0:kch, 0:ncol]),
                     reads=[g.b_wstg[si]], writes=[self.b_wb[i]])
                return wb, self.b_wb[i]

        def phase_proj(l):
            with ExitStack() as st:
                hT = sb(st, "hT", [128, 8, S], BF16)
                b_hT = p.bufs(NT)
                with ExitStack() as st2:
                    if l == 0:
                        norm_transpose_phase(st2, lambda t: x_in[t * 128:(t + 1) * 128, :], B_x, vec["norm_mix"][l:l + 1, :], hT, b_hT)
                    else:
                        norm_transpose_phase(st2, lambda t: xres[t * 128:(t + 1) * 128, :], B_x, vec["norm_mix"][l:l + 1, :], hT, b_hT)
                    p.barrier()
                wl = WLoader(st, "wl", 8, 512)
                pp = [ps(st, "pp%d" % i, [128, 512], F32) for i in range(4)]
                b_pp = p.bufs(4)
                ostf = [sb(st, "ostf%d" % i, [128, 512], F32) for i in range(4)]
                ostb = [sb(st, "ostb%d" % i, [128, 512], BF16) for i in range(4)]
                b_ost = p.bufs(4)
                bg = sb(st, "bgcm", [128, 24], F32)
                b_bg = p.buf()
                p.dma("sync", lambda e: e.dma_start(out=bg[:], in_=b_gate_cm[l]), writes=[b_bg])
                cnt = [0]
                win = mats["w_in"][l]

                def tok_major(c0, ncol, dst_fn, dst_bufs, bf):
                    wb, b_wb = wl.load(win, c0, ncol)
                    for t in range(NT):
                        i = cnt[0] % 4
                        cnt[0] += 1
                        for k in range(8):
                            p.op("pe", lambda e, i=i, k=k, t=t: e.matmul(pp[i][:, 0:ncol], lhsT=hT[:, k, t * 128:(t + 1) * 128],
                                                                         rhs=wb[:, k, 0:ncol], start=(k == 0), stop=(k == 7)),
                                 reads=[b_hT[t], b_wb], writes=[b_pp[i]], sig=(k == 7))
                        o = ostb[i] if bf else ostf[i]
                        eng = "act" if (cnt[0] % 2) else "dve"
                        if eng == "act":
                            p.op("act", lambda e, i=i, o=o: e.copy(out=o[:, 0:ncol], in_=pp[i][:, 0:ncol]), reads=[b_pp[i]], writes=[b_ost[i]])
                        else:
                            p.op("dve", lambda e, i=i, o=o: e.tensor_copy(out=o[:, 0:ncol], in_=pp[i][:, 0:ncol]), reads=[b_pp[i]], writes=[b_ost[i]])
                        p.dma("sync", lambda e, o=o, t=t: e.dma_start(out=dst_fn(t), in_=o[:, 0:ncol]), reads=[b_ost[i]], writes=[dst_bufs[t]])

                def ch_major(c0, nchunk, dst, dst_row0, dst_bufs, gate_idx0=None):
                    for cc0 in range(0, nchunk, 4):
                        ncc = min(4, nchunk - cc0)
                        wb, b_wb = wl.load(win, c0 + cc0 * 128, ncc * 128)
                        for cc in range(ncc):
                            for tq in range(8):
                                i = cnt[0] % 4
                                cnt[0] += 1
                                for k in range(8):
                                    p.op("pe", lambda e, i=i, k=k, tq=tq, cc=cc, wb=wb: e.matmul(pp[i][:, :], lhsT=wb[:, k, cc * 128:(cc + 1) * 128],
                                                                                          rhs=hT[:, k, tq * 512:(tq + 1) * 512], start=(k == 0), stop=(k == 7)),
                                         reads=[b_hT[tq * 4 + j] for j in range(4)] + [b_wb], writes=[b_pp[i]], sig=(k == 7))
                                o = ostb[i]
                                if gate_idx0 is not None:
                                    gi = gate_idx0 + cc0 + cc
                                    p.op("act", lambda e, i=i, o=o, gi=gi: e.activation(out=o[:], in_=pp[i][:], func=AF.Sigmoid, bias=bg[:, gi:gi + 1]),
                                         reads=[b_pp[i], b_bg], writes=[b_ost[i]])
                                else:
                                    p.op("dve", lambda e, i=i, o=o: e.tensor_copy(out=o[:], in_=pp[i][:]), reads=[b_pp[i]], writes=[b_ost[i]])
                                r0 = dst_row0 + (cc0 + cc) * 128
                                p.dma("sync", lambda e, o=o, r0=r0, tq=tq: e.dma_start(out=dst[r0:r0 + 128, tq * 512:(tq + 1) * 512], in_=o[:]),
                                      reads=[b_ost[i]], writes=[dst_bufs[tq]])

                for c0, ncol in ((0, 512), (512, 512), (1024, 512), (1536, 256)):
                    tok_major(C_RW + c0, ncol, lambda t, c0=c0, ncol=ncol: prw[t * 128:(t + 1) * 128, c0:c0 + ncol], B_prw, False)
                tok_major(C_ML, 416, lambda t: pml[t * 128:(t + 1) * 128, :], B_pml, False)
                ch_major(C_DF, 8, qkT, 0, B_qkT)
                tok_major(C_DF + 1024, 512, lambda t: vdf[t * 128:(t + 1) * 128, :], B_vdf, True)
                ch_major(C_GT, 24, gT, 0, B_gT, gate_idx0=0)
                p.barrier()

        def load_bcast(st, name, ap_row, n):
            t = sb(st, name, [128, n], F32)
            b = p.buf()
            p.dma("sync", lambda e: e.dma_start(out=t[:], in_=ap_row.partition_broadcast(128)), writes=[b])
            return t, b

        def load_weight_bf(st, name, wap, krows, ncols, part0=0):
            kch = max(1, krows // 128)
            pr = min(128, krows)
            wbt = sb(st, name, [128, kch, ncols], BF16)
            b_w = p.buf()
            stg = g.wstg
            b_stg = g.b_wstg
            cs = 512 if kch <= 8 else 128
            for c0 in range(0, ncols, cs):
                nc_ = min(cs, ncols - c0)
                i = g.nstg[0] % 2
                g.nstg[0] += 1
                if krows >= 128 and kch <= 8:
                    src = wap[:, c0:c0 + nc_].rearrange("(k p) n -> p k n", p=128)
                    p.dma("sync", lambda e, i=i, src=src, nc_=nc_: e.dma_start(out=stg[i][:, 0:kch, 0:nc_], in_=src), writes=[b_stg[i]])
                    p.op("pool", lambda e, i=i, c0=c0, nc_=nc_: e.tensor_copy(out=wbt[:, :, c0:c0 + nc_], in_=stg[i][:, 0:kch, 0:nc_]),
                         reads=[b_stg[i]], writes=[b_w])
                elif krows >= 128:
                    sv = stg[i][:].rearrange("p a b -> p (a b)")[:, 0:kch * 128].rearrange("p (k n) -> p k n", n=128)
                    src = wap[:, c0:c0 + nc_].rearrange("(k p) n -> p k n", p=128)
                    p.dma("sync", lambda e, sv=sv, src=src, nc_=nc_: e.dma_start(out=sv[:, :, 0:nc_], in_=src), writes=[b_stg[i]])
                    p.op("pool", lambda e, sv=sv, c0=c0, nc_=nc_: e.tensor_copy(out=wbt[:, :, c0:c0 + nc_], in_=sv[:, :, 0:nc_]),
                         reads=[b_stg[i]], writes=[b_w])
                else:
                    src = wap[:, c0:c0 + nc_]
                    p.dma("sync", lambda e, i=i, src=src, nc_=nc_: e.dma_start(out=stg[i][part0:part0 + pr, 0, 0:nc_], in_=src), writes=[b_stg[i]])
                    p.op("pool", lambda e, i=i, c0=c0, nc_=nc_: e.tensor_copy(out=wbt[part0:part0 + pr, 0, c0:c0 + nc_], in_=stg[i][part0:part0 + pr, 0, 0:nc_]),
                         reads=[b_stg[i]], writes=[b_w])
            return wbt, b_w

        def xsrc(l, t):
            return (x_in if l == 0 else xres)[t * 128:(t + 1) * 128, :]

        def phase_merge(l):
            with ExitStack() as st:
                wbr = []
                for n in ("rwkv", "mla", "diff"):
                    wbr.append(load_weight_bf(st, "wbr_" + n, mats["w_branch_" + n][l], 512, 1024))
                wo, b_wo = load_weight_bf(st, "wo", mats["w_o"][l], 1024, 1024)
                oc = [[sb(st, "oc%d_%d" % (i, j), [128, 4, 512], BF16) for j in range(3)] for i in range(2)]
                b_oc = [p.bufs(3) for i in range(2)]
                gc = [sb(st, "gc%d" % i, [128, 24, 512], BF16) for i in range(2)]
                b_gc = p.bufs(2)
                mT = [sb(st, "mT%d" % i, [128, 8, 512], BF16) for i in range(2)]
                b_mT = p.bufs(2)
                pb = [ps(st, "mpb%d" % i, [128, 512], F32) for i in range(6)]
                b_pb = p.bufs(6)
                m0 = [sb(st, "m0_%d" % i, [128, 512], F32) for i in range(2)]
                m1 = [sb(st, "m1_%d" % i, [128, 512], F32) for i in range(2)]
                b_m0 = p.bufs(2)
                b_m1 = p.bufs(2)
                xt = [sb(st, "mxt%d" % i, [128, D], F32) for i in range(2)]
                b_xt = p.bufs(2)
                po = [ps(st, "mpo%d" % i, [128, 512], F32) for i in range(2)]
                b_po = p.bufs(2)
                names = ("rwkv", "mla", "diff")
                nd = 0
                nx = 0
                npo = 0
                for c in range(8):
                    ci = c % 2
                    for j, n in enumerate(names):
                        p.dma("sync", lambda e, ci=ci, j=j, n=n, c=c: e.dma_start(
                            out=oc[ci][j][:], in_=oT[n][:, c * 512:(c + 1) * 512].rearrange("(k p) t -> p k t", p=128)),
                            reads=[B_oT[n][c]], writes=[b_oc[ci][j]])
                    p.dma("sync", lambda e, ci=ci, c=c: e.dma_start(
                        out=gc[ci][:], in_=gT[:, c * 512:(c + 1) * 512].rearrange("(k p) t -> p k t", p=128)),
                        reads=[B_gT[c]], writes=[b_gc[ci]])
                    for dc in range(8):
                        di = nd % 2
                        nd += 1
                        for j in range(3):
                            pj = di * 3 + j
                            w_j, b_wj = wbr[j]
                            for k in range(4):
                                p.op("pe", lambda e, pj=pj, k=k, dc=dc, ci=ci, j=j, w_j=w_j: e.matmul(
                                    pb[pj][:], lhsT=w_j[:, k, dc * 128:(dc + 1) * 128], rhs=oc[ci][j][:, k, :], start=(k == 0), stop=(k == 3)),
                                    reads=[b_wj, b_oc[ci][j]], writes=[b_pb[pj]], sig=(k == 3))
                        p.op("dve", lambda e, di=di, ci=ci, dc=dc: e.tensor_tensor(out=m0[di][:], in0=pb[di * 3][:], in1=gc[ci][:, dc, :], op=ALU.mult),
                             reads=[b_pb[di * 3], b_gc[ci]], writes=[b_m0[di]])
                        p.op("dve", lambda e, di=di, ci=ci, dc=dc: e.tensor_tensor(out=m1[di][:], in0=pb[di * 3 + 1][:], in1=gc[ci][:, 8 + dc, :], op=ALU.mult),
                             reads=[b_pb[di * 3 + 1], b_gc[ci]], writes=[b_m1[di]])
                        p.op("dve", lambda e, di=di: e.tensor_tensor(out=m0[di][:], in0=m0[di][:], in1=m1[di][:], op=ALU.add),
                             reads=[b_m0[di], b_m1[di]], writes=[b_m0[di]])
                        p.op("dve", lambda e, di=di, ci=ci, dc=dc: e.tensor_tensor(out=m1[di][:], in0=pb[di * 3 + 2][:], in1=gc[ci][:, 16 + dc, :], op=ALU.mult),
                             reads=[b_pb[di * 3 + 2], b_gc[ci]], writes=[b_m1[di]])
                        p.op("dve", lambda e, di=di, ci=ci, dc=dc: e.tensor_tensor(out=mT[ci][:, dc, :], in0=m0[di][:], in1=m1[di][:], op=ALU.add),
                             reads=[b_m0[di], b_m1[di]], writes=[b_mT[ci]])
                    for tt in range(4):
                        t = c * 4 + tt
                        xi = nx % 2
                        nx += 1
                        p.dma("sync", lambda e, xi=xi, t=t: e.dma_start(out=xt[xi][:], in_=xsrc(l, t)), reads=[B_x[t]], writes=[b_xt[xi]])
                        for hf in range(2):
                            pi = npo % 2
                            npo += 1
                            for k in range(8):
                                p.op("pe", lambda e, pi=pi, k=k, ci=ci, tt=tt, hf=hf: e.matmul(
                                    po[pi][:], lhsT=mT[ci][:, k, tt * 128:(tt + 1) * 128], rhs=wo[:, k, hf * 512:(hf + 1) * 512], start=(k == 0), stop=(k == 7)),
                                    reads=[b_mT[ci], b_wo], writes=[b_po[pi]], sig=(k == 7))
                            p.op("dve", lambda e, pi=pi, xi=xi, hf=hf: e.tensor_tensor(out=xt[xi][:, hf * 512:(hf + 1) * 512], in0=po[pi][:],
                                                                                 in1=xt[xi][:, hf * 512:(hf + 1) * 512], op=ALU.add),
                                 reads=[b_po[pi], b_xt[xi]], writes=[b_xt[xi]])
                        p.dma("sync", lambda e, xi=xi, t=t: e.dma_start(out=xres[t * 128:(t + 1) * 128, :], in_=xt[xi][:]),
                              reads=[b_xt[xi]], writes=[B_x[t]])
                p.barrier()

        def phase_ffn(l):
            G = 512
            NG = S // G
            TG = G // 128
            with ExitStack() as st:
                wdn, b_wdn = load_weight_bf(st, "wdn", mats["ffn_w_down"][l], DFF, 1024)
                gt, b_g = load_bcast(st, "f_g", vec["norm_ffn"][l:l + 1, :], D)
                cw = sb(st, "f_cw", [128, 3, 44], F32)
                cb = sb(st, "f_cb", [128, 44], F32)
                b_cw = p.buf()
                p.dma("sync", lambda e: e.dma_start(out=cw[:], in_=conv_w_cm[l]), writes=[b_cw])
                p.dma("sync", lambda e: e.dma_start(out=cb[:], in_=conv_b_cm[l]), writes=[b_cw])
                halo = sb(st, "f_halo", [128, 44, 2], F32)
                b_halo = p.bufs(44)
                p.op("pool", lambda e: e.memset(halo[:], 0.0), writes=b_halo)
                xg = sb(st, "f_xg", [128, TG, D], F32)
                b_xg = p.bufs(TG)
                hTg = sb(st, "f_hT", [128, 8, G], BF16)
                b_hTg = p.bufs(TG)
                junk = sb(st, "f_junk", [128, D], BF16)
                b_junk = p.buf()
                ss = [sb(st, "f_ss%d" % i, [128, 1], F32) for i in range(2)]
                b_ss = p.bufs(2)
                hb = [sb(st, "f_hb%d" % i, [128, D], BF16) for i in range(2)]
                b_hb = p.bufs(2)
                pt = [ps(st, "f_pt%d" % i, [128, 8, 128], BF16) for i in range(2)]
                b_pt = p.bufs(2)
                wubg = [sb(st, "f_wubg%d" % i, [128, 8, 512], BF16) for i in range(2)]
                wubv = [sb(st, "f_wubv%d" % i, [128, 8, 512], BF16) for i in range(2)]
                b_wubg = p.bufs(2)
                b_wubv = p.bufs(2)
                pu = [ps(st, "f_pu%d" % i, [128, 512], F32) for i in range(4)]
                b_pu = p.bufs(4)
                ug = [sb(st, "f_ug%d" % i, [128, G + 2], F32) for i in range(2)]
                uv = [sb(st, "f_uv%d" % i, [128, G + 2], F32) for i in range(2)]
                b_ug = p.bufs(2)
                b_uv = p.bufs(2)
                cg = [sb(st, "f_cg%d" % i, [128, G], F32) for i in range(2)]
                cv = [sb(st, "f_cv%d" % i, [128, G], F32) for i in range(2)]
                b_cg = p.bufs(2)
                b_cv = p.bufs(2)
                actT = sb(st, "f_actT", [128, 22, G], BF16)
                b_act = p.bufs(22)
                pd = [ps(st, "f_pd%d" % i, [128, 512], F32) for i in range(2)]
                b_pd = p.bufs(2)
                wup = mats["ffn_w_up"][l]
                nw = 0
                npu = 0
                npd = 0
                NF_ = int(os.environ.get('FFN_NF', 22))
                for gi in range(int(os.environ.get('FFN_NG', NG))):
                    for tt in range(TG):
                        t = gi * TG + tt
                        i = tt % 2
                        p.dma("sync", lambda e, tt=tt, t=t: e.dma_start(out=xg[:, tt, :], in_=xres[t * 128:(t + 1) * 128, :]),
                              reads=[B_x[t]], writes=[b_xg[tt]])
                        p.op("pool", lambda e, i=i: e.memset(ss[i][:], 0.0), writes=[b_ss[i]])
                        p.op("act", lambda e, i=i, tt=tt: e.activation(out=junk[:], in_=xg[:, tt, :], func=AF.Square, accum_out=ss[i][:]),
                             reads=[b_xg[tt], b_ss[i]], writes=[b_junk, b_ss[i]])
                        p.op("act", lambda e, i=i: e.activation(out=ss[i][:], in_=ss[i][:], func=AF.Sqrt, scale=1.0 / D, bias=g.eps_tiles[1e-6][:]),
                             reads=[b_ss[i], b_eps], writes=[b_ss[i]])
                        p.op("dve", lambda e, i=i: e.reciprocal(out=ss[i][:], in_=ss[i][:]), reads=[b_ss[i]], writes=[b_ss[i]])
                        p.op("dve", lambda e, i=i, tt=tt: e.scalar_tensor_tensor(out=hb[i][:], in0=xg[:, tt, :], scalar=ss[i][:, 0:1], in1=gt[:],
                                                                             op0=ALU.mult, op1=ALU.mult),
                             reads=[b_xg[tt], b_ss[i], b_g], writes=[b_hb[i]])
                        for k in range(8):
                            p.op("pe", lambda e, i=i, k=k: e.transpose(out=pt[i][:, k, :], in_=hb[i][:, k * 128:(k + 1) * 128], identity=ident[:]),
                                 reads=[b_hb[i], b_ident], writes=[b_pt[i]], sig=(k == 7))
                        p.op("act", lambda e, i=i, tt=tt: e.copy(out=hTg[:, :, tt * 128:(tt + 1) * 128], in_=pt[i][:]),
                             reads=[b_pt[i]], writes=[b_hTg[tt]])
                    for f in range(NF_):
                        fb = (f // 4) * 4
                        if f == fb:
                            nfb = min(4, 22 - fb)
                            wi = nw % 2
                            nw += 1
                            for (dstw, b_dstw, cbase) in ((wubg[wi], b_wubg[wi], fb * 128), (wubv[wi], b_wubv[wi], DFF + fb * 128)):
                                si = g.nstg[0] % 2
                                g.nstg[0] += 1
                                p.dma("sync", lambda e, si=si, cbase=cbase, nfb=nfb: e.dma_start(out=g.wstg[si][:, :, 0:nfb * 128],
                                                                                          in_=wup[:, cbase:cbase + nfb * 128].rearrange("(k p) n -> p k n", p=128)),
                                      writes=[g.b_wstg[si]])
                                p.op("pool", lambda e, si=si, dstw=dstw, nfb=nfb: e.tensor_copy(out=dstw[:, :, 0:nfb * 128], in_=g.wstg[si][:, :, 0:nfb * 128]),
                                     reads=[g.b_wstg[si]], writes=[b_dstw])
                        fo = (f - fb) * 128
                        ui = f % 2
                        p.op("pool", lambda e, ui=ui, f=f: e.tensor_copy(out=ug[ui][:, 0:2], in_=halo[:, f, :]), reads=[b_halo[f]], writes=[b_ug[ui]])
                        p.op("pool", lambda e, ui=ui, f=f: e.tensor_copy(out=uv[ui][:, 0:2], in_=halo[:, 22 + f, :]), reads=[b_halo[22 + f]], writes=[b_uv[ui]])
                        for gv in range(2):
                            for hf in range(G // 512):
                                pi = npu % 4
                                npu += 1
                                for k in range(8):
                                    wsrc = wubg[wi] if gv == 0 else wubv[wi]
                                    b_wsrc = b_wubg[wi] if gv == 0 else b_wubv[wi]
                                    p.op("pe", lambda e, pi=pi, k=k, wsrc=wsrc, fo=fo, hf=hf: e.matmul(
                                        pu[pi][:], lhsT=wsrc[:, k, fo:fo + 128], rhs=hTg[:, k, hf * 512:(hf + 1) * 512],
                                        start=(k == 0), stop=(k == 7)),
                                        reads=[b_wsrc] + [b_hTg[hf * 4 + j] for j in range(4)], writes=[b_pu[pi]], sig=(k == 7))
                                dst = ug[ui] if gv == 0 else uv[ui]
                                b_dst = b_ug[ui] if gv == 0 else b_uv[ui]
                                p.op("act", lambda e, pi=pi, dst=dst, hf=hf: e.copy(out=dst[:, 2 + hf * 512:2 + (hf + 1) * 512], in_=pu[pi][:]),
                                     reads=[b_pu[pi]], writes=[b_dst])
                        p.op("pool", lambda e, ui=ui, f=f: e.tensor_copy(out=halo[:, f, :], in_=ug[ui][:, G:G + 2]), reads=[b_ug[ui]], writes=[b_halo[f]])
                        p.op("pool", lambda e, ui=ui, f=f: e.tensor_copy(out=halo[:, 22 + f, :], in_=uv[ui][:, G:G + 2]), reads=[b_uv[ui]], writes=[b_halo[22 + f]])
                        for eng, u, b_u, cdst, b_c, ch in (("dve", ug[ui], b_ug[ui], cg[ui], b_cg[ui], f), ("dve", uv[ui], b_uv[ui], cv[ui], b_cv[ui], 22 + f)):
                            p.op(eng, lambda e, u=u, cdst=cdst, ch=ch: e.tensor_scalar(out=cdst[:], in0=u[:, 2:G + 2], scalar1=cw[:, 2, ch:ch + 1],
                                                                                  scalar2=cb[:, ch:ch + 1], op0=ALU.mult, op1=ALU.add),
                                 reads=[b_u, b_cw], writes=[b_c])
                            p.op(eng, lambda e, u=u, cdst=cdst, ch=ch: e.scalar_tensor_tensor(out=cdst[:], in0=u[:, 1:G + 1], scalar=cw[:, 1, ch:ch + 1],
                                                                                         in1=cdst[:], op0=ALU.mult, op1=ALU.add),
                                 reads=[b_u, b_cw, b_c], writes=[b_c])
                            p.op(eng, lambda e, u=u, cdst=cdst, ch=ch: e.scalar_tensor_tensor(out=cdst[:], in0=u[:, 0:G], scalar=cw[:, 0, ch:ch + 1],
                                                                                         in1=cdst[:], op0=ALU.mult, op1=ALU.add),
                                 reads=[b_u, b_cw, b_c], writes=[b_c])
                        p.op("act", lambda e, ui=ui: e.activation(out=ug[ui][:, 2:G + 2], in_=cg[ui][:], func=AF.Silu),
                             reads=[b_cg[ui]], writes=[b_ug[ui]])
                        p.op("dve", lambda e, ui=ui, f=f: e.tensor_tensor(out=actT[:, f, :], in0=ug[ui][:, 2:G + 2], in1=cv[ui][:], op=ALU.mult),
                             reads=[b_ug[ui], b_cv[ui]], writes=[b_act[f]])
                    for tt in range(TG):
                        t = gi * TG + tt
                        for hf in range(2):
                            pi = npd % 2
                            npd += 1
                            for f in range(NF_):
                                p.op("pe", lambda e, pi=pi, f=f, tt=tt, hf=hf: e.matmul(
                                    pd[pi][:], lhsT=actT[:, f, tt * 128:(tt + 1) * 128], rhs=wdn[:, f, hf * 512:(hf + 1) * 512],
                                    start=(f == 0), stop=(f == NF_ - 1)),
                                    reads=[b_act[f], b_wdn], writes=[b_pd[pi]], sig=(f == NF_ - 1))
                            p.op("dve", lambda e, pi=pi, tt=tt, hf=hf: e.tensor_tensor(out=xg[:, tt, hf * 512:(hf + 1) * 512], in0=pd[pi][:],
                                                                                 in1=xg[:, tt, hf * 512:(hf + 1) * 512], op=ALU.add),
                                 reads=[b_pd[pi], b_xg[tt]], writes=[b_xg[tt]])
                        p.dma("sync", lambda e, tt=tt, t=t: e.dma_start(out=xres[t * 128:(t + 1) * 128, :], in_=xg[:, tt, :]),
                              reads=[b_xg[tt]], writes=[B_x[t]])
                p.barrier()

        def phase_final():
            with ExitStack() as st:
                gt, b_g = load_bcast(st, "fin_g", norm_final, D)
                xt = [sb(st, "fin_x%d" % i, [128, D], F32) for i in range(3)]
                b_xt = p.bufs(3)
                junk = sb(st, "fin_junk", [128, D], BF16)
                b_junk = p.buf()
                ss = [sb(st, "fin_ss%d" % i, [128, 1], F32) for i in range(3)]
                b_ss = p.bufs(3)
                for t in range(NT):
                    i = t % 3
                    p.dma("sync", lambda e, i=i, t=t: e.dma_start(out=xt[i][:], in_=xres[t * 128:(t + 1) * 128, :]), reads=[B_x[t]], writes=[b_xt[i]])
                    p.op("pool", lambda e, i=i: e.memset(ss[i][:], 0.0), writes=[b_ss[i]])
                    p.op("act", lambda e, i=i: e.activation(out=junk[:], in_=xt[i][:], func=AF.Square, accum_out=ss[i][:]),
                         reads=[b_xt[i], b_ss[i]], writes=[b_junk, b_ss[i]])
                    p.op("act", lambda e, i=i: e.activation(out=ss[i][:], in_=ss[i][:], func=AF.Sqrt, scale=1.0 / D, bias=g.eps_tiles[1e-6][:]),
                         reads=[b_ss[i], b_eps], writes=[b_ss[i]])
                    p.op("dve", lambda e, i=i: e.reciprocal(out=ss[i][:], in_=ss[i][:]), reads=[b_ss[i]], writes=[b_ss[i]])
                    p.op("dve", lambda e, i=i: e.scalar_tensor_tensor(out=xt[i][:], in0=xt[i][:], scalar=ss[i][:, 0:1], in1=gt[:], op0=ALU.mult, op1=ALU.mult),
                         reads=[b_xt[i], b_ss[i], b_g], writes=[b_xt[i]])
                    p.dma("sync", lambda e, i=i, t=t: e.dma_start(out=out[t * 128:(t + 1) * 128, :], in_=xt[i][:]), reads=[b_xt[i]], writes=[B_out])
                p.barrier()

        cs_t = sb(top, "cs_t", [128, NT, 16], F32)
        sn_t = sb(top, "sn_t", [128, NT, 16], F32)
        b_rope = p.buf()
        maskD = sb(top, "maskD", [128, 128], F32)
        b_maskD = p.buf()
        Dn = [sb(top, "Dn%d" % h, [128, 256], F32) for h in range(4)]
        b_Dn = p.buf()
        relb = sb(top, "relb", [128, 128], F32)
        b_relb = p.buf()
        negpi = sb(top, "negpi", [128, 1], F32)

        def setup_attn_consts():
            with ExitStack() as st:
                p.op("pool", lambda e: e.memset(negpi[:], -math.pi), writes=[b_rope])
                posi = sb(st, "posi", [128, NT], I32)
                posf = sb(st, "posf", [128, NT], F32)
                b_pos = p.buf()
                p.dma("sync", lambda e: e.dma_start(out=posi[:], in_=pos_tm), writes=[b_pos])
                p.op("dve", lambda e: e.tensor_copy(out=posf[:], in_=posi[:]), reads=[b_pos], writes=[b_pos])
                invf = sb(st, "invf", [128, 16], F32)
                b_invf = p.buf()
                for i in range(16):
                    v = float(np.float32(10000.0) ** np.float32(-(2.0 * i) / 32.0))
                    p.op("pool", lambda e, i=i, v=v: e.memset(invf[:, i:i + 1], v), writes=[b_invf])
                ang = sb(st, "ang", [128, NT, 16], F32)
                ang2 = sb(st, "ang2", [128, NT, 16], F32)
                b_ang = p.buf()
                for t in range(NT):
                    p.op("dve", lambda e, t=t: e.tensor_scalar(out=ang[:, t, :], in0=invf[:], scalar1=posf[:, t:t + 1], scalar2=None, op0=ALU.mult),
                         reads=[b_invf, b_pos], writes=[b_ang])
                ni = sb(st, "ang_ni", [128, NT, 16], I32)
                nf = sb(st, "ang_nf", [128, NT, 16], F32)
                mk = sb(st, "ang_mk", [128, NT, 16], F32)
                b_red = p.buf()

                def reduce_sin(src, add, dst):
                    p.op("dve", lambda e: e.tensor_scalar(out=ang2[:], in0=src[:], scalar1=1.0 / (2 * math.pi), scalar2=add, op0=ALU.mult, op1=ALU.add),
                         reads=[b_ang], writes=[b_red])
                    p.op("dve", lambda e: e.tensor_copy(out=ni[:], in_=ang2[:]), reads=[b_red], writes=[b_red])
                    p.op("dve", lambda e: e.tensor_copy(out=nf[:], in_=ni[:]), reads=[b_red], writes=[b_red])
                    p.op("dve", lambda e: e.tensor_tensor(out=ang2[:], in0=ang2[:], in1=nf[:], op=ALU.subtract), reads=[b_red], writes=[b_red])
                    p.op("dve", lambda e: e.tensor_single_scalar(out=mk[:], in_=ang2[:], scalar=0.5, op=ALU.is_gt), reads=[b_red], writes=[b_red])
                    p.op("dve", lambda e: e.tensor_tensor(out=ang2[:], in0=ang2[:], in1=mk[:], op=ALU.subtract), reads=[b_red], writes=[b_red])
                    p.op("dve", lambda e: e.tensor_single_scalar(out=mk[:], in_=ang2[:], scalar=-0.5, op=ALU.is_lt), reads=[b_red], writes=[b_red])
                    p.op("dve", lambda e: e.tensor_tensor(out=ang2[:], in0=ang2[:], in1=mk[:], op=ALU.add), reads=[b_red], writes=[b_red])
                    p.op("act", lambda e: e.activation(out=dst[:], in_=ang2[:], func=AF.Sin, scale=6.283184), reads=[b_red], writes=[b_rope])

                CST = int(os.environ.get("CSTAGE", "99"))
                if CST < 1:
                    p.barrier()
                    return
                reduce_sin(ang, 0.0, sn_t)
                reduce_sin(ang, 0.25, cs_t)
                if CST < 2:
                    p.barrier()
                    return
                p.op("pool", lambda e: e.memset(maskD[:], 0.0), writes=[b_maskD])
                p.op("pool", lambda e: e.affine_select(out=maskD[:], in_=maskD[:], pattern=[[1, 128]], compare_op=ALU.is_ge, fill=-30000.0,
                                                       base=0, channel_multiplier=-1), reads=[b_maskD], writes=[b_maskD])
                if CST < 3:
                    p.barrier()
                    return
                p.dma("sync", lambda e: e.dma_start(out=relb[:], in_=rel_bias.partition_broadcast(128)), writes=[b_relb])
                dl = sb(st, "dl", [128, 128], F32)
                b_dl = p.buf()
                p.op("dve", lambda e: e.tensor_tensor(out=dl[:, 4:128], in0=relb[:, 4:128], in1=relb[:, 0:124], op=ALU.subtract),
                     reads=[b_relb], writes=[b_dl])
                pri = sb(st, "pri", [128, 256], I32)
                prf = sb(st, "prf", [128, 256], F32)
                b_pr = p.buf()
                p.dma("sync", lambda e: e.dma_start(out=pri[:], in_=pos_row.partition_broadcast(128)), writes=[b_pr])
                p.op("dve", lambda e: e.tensor_copy(out=prf[:], in_=pri[:]), reads=[b_pr], writes=[b_pr])
                p.op("dve", lambda e: e.tensor_scalar(out=prf[:], in0=prf[:], scalar1=posf[:, 0:1], scalar2=0.0, op0=ALU.subtract, op1=ALU.max),
                     reads=[b_pr, b_pos], writes=[b_pr])
                if CST < 4:
                    p.barrier()
                    return
                ge = [sb(st, "ge%d" % i, [128, 256], F32) for i in range(2)]
                b_ge = p.bufs(2)
                for h in range(4):
                    p.op("dve", lambda e, h=h: e.tensor_scalar(out=Dn[h][:], in0=prf[:], scalar1=0.0, scalar2=relb[:, h:h + 1], op0=ALU.mult, op1=ALU.add),
                         reads=[b_pr, b_relb], writes=[b_Dn])
                def bucket(n):
                    if n < 16:
                        return n
                    return min(31, 16 + int(np.float32(np.log(np.float32(n) / np.float32(16))) / np.float32(math.log(128 / 16)) * np.float32(16)))
                thr = {}
                for n in range(0, 300):
                    bk = bucket(n)
                    for bb in range(1, bk + 1):
                        if bb not in thr:
                            thr[bb] = n
                for bb in range(1, 32):
                    gi = bb % 2
                    tv = float(thr[bb]) - 0.5
                    p.op("dve", lambda e, gi=gi, tv=tv: e.tensor_single_scalar(out=ge[gi][:], in_=prf[:], scalar=tv, op=ALU.is_ge),
                         reads=[b_pr], writes=[b_ge[gi]])
                    for h in range(4):
                        p.op("dve", lambda e, gi=gi, h=h, bb=bb: e.scalar_tensor_tensor(out=Dn[h][:], in0=ge[gi][:], scalar=dl[:, bb * 4 + h:bb * 4 + h + 1],
                                                                                    in1=Dn[h][:], op0=ALU.mult, op1=ALU.add),
                             reads=[b_ge[gi], b_dl, b_Dn], writes=[b_Dn])
                for h in range(4):
                    p.op("dve", lambda e, h=h: e.tensor_scalar(out=Dn[h][:], in0=Dn[h][:], scalar1=8.0, scalar2=None, op0=ALU.mult), reads=[b_Dn], writes=[b_Dn])
                    p.op("pool", lambda e, h=h: e.affine_select(out=Dn[h][:, 0:128], in_=Dn[h][:, 0:128], pattern=[[1, 128]], compare_op=ALU.is_ge,
                                                                fill=-30000.0, base=0, channel_multiplier=-1), reads=[b_Dn], writes=[b_Dn])
                p.barrier()

        class AttnRes:
            pass

        def make_attn_res(st):
            r = AttnRes()
            r.sc = [ps(st, "a_sc%d" % i, [128, 512], F32) for i in range(3)]
            r.b_sc = p.bufs(3)
            r.eT = [sb(st, "a_eT%d" % i, [128, 512], BF16) for i in range(3)]
            r.b_eT = p.bufs(3)
            r.nsc = 0
            r.neT = 0
            return r

        def attn_chunk(r, c, QTf, b_Q, KTf, b_K, Vf, b_V, dv, scale, d0, b_d0, d1, b_d1, farb, po, b_po, psm, b_psm):
            nk = 4 * c + 4

            def qk(kt):
                j0 = max(4 * c, kt)
                off = (j0 - 4 * c) * 128
                ncol = 512 - off
                si = r.nsc % 3
                r.nsc += 1
                sc, b_s = r.sc[si], r.b_sc[si]
                p.op("pe", lambda e, sc=sc, kt=kt, off=off, ncol=ncol: e.matmul(sc[:, off:512], lhsT=KTf(kt), rhs=QTf(c * 512 + off, ncol), start=True, stop=True),
                     reads=[b_K, b_Q], writes=[b_s])
                nnear = 0
                if kt >= 4 * c:
                    p.op("dve", lambda e, sc=sc, off=off: e.tensor_tensor(out=sc[:, off:off + 128], in0=sc[:, off:off + 128], in1=d0, op=ALU.add),
                         reads=[b_s, b_d0], writes=[b_s])
                    nnear = 1
                    if d1 is not None and kt + 1 <= 4 * c + 3:
                        p.op("dve", lambda e, sc=sc, off=off: e.tensor_tensor(out=sc[:, off + 128:off + 256], in0=sc[:, off + 128:off + 256], in1=d1, op=ALU.add),
                             reads=[b_s, b_d1], writes=[b_s])
                        nnear = 2
                elif d1 is not None and kt == 4 * c - 1:
                    p.op("dve", lambda e, sc=sc: e.tensor_tensor(out=sc[:, 0:128], in0=sc[:, 0:128], in1=d1, op=ALU.add),
                         reads=[b_s, b_d1], writes=[b_s])
                    nnear = 1
                return sc, b_s, off, nnear

            cur = qk(0)
            for kt in range(nk):
                nxt = qk(kt + 1) if kt + 1 < nk else None
                sc, b_s, off, nnear = cur
                ei = r.neT % 3
                r.neT += 1
                eT, b_e = r.eT[ei], r.b_eT[ei]
                if farb is None:
                    p.op("act", lambda e, sc=sc, eT=eT, off=off: e.activation(out=eT[:, off:512], in_=sc[:, off:512], func=AF.Exp, scale=scale),
                         reads=[b_s], writes=[b_e])
                else:
                    nn = nnear * 128
                    if nn > 0:
                        p.op("act", lambda e, sc=sc, eT=eT, off=off, nn=nn: e.activation(out=eT[:, off:off + nn], in_=sc[:, off:off + nn], func=AF.Exp, scale=scale),
                             reads=[b_s], writes=[b_e])
                    if off + nn < 512:
                        p.op("act", lambda e, sc=sc, eT=eT, off=off, nn=nn: e.activation(out=eT[:, off + nn:512], in_=sc[:, off + nn:512], func=AF.Exp, scale=scale, bias=farb),
                             reads=[b_s, b_relb], writes=[b_e])
                if psm is None:
                    p.op("pe", lambda e, eT=eT, kt=kt, off=off: e.matmul(po[0:dv, off:512], lhsT=Vf(kt), rhs=eT[:, off:512], start=(kt == 0), stop=(kt == nk - 1)),
                         reads=[b_V, b_e], writes=[b_po])
                else:
                    p.op("pe", lambda e, eT=eT, kt=kt, off=off: e.matmul(po[0:dv, off:512], lhsT=Vf(kt), rhs=eT[:, off:512], start=(kt == 0), stop=(kt == nk - 1)),
                         reads=[b_V, b_e], writes=[b_po], sig=False)
                    p.op("pe", lambda e, eT=eT, kt=kt, off=off: e.matmul(psm[0:1, off:512], lhsT=ones_bf[:, 0:1], rhs=eT[:, off:512], start=(kt == 0), stop=(kt == nk - 1)),
                         reads=[b_e, b_ones], writes=[b_psm])
                cur = nxt

        def phase_mla(l):
            with ExitStack() as st:
                QT = sb(st, "m_QT", [128, 4, S], BF16)
                KT = sb(st, "m_KT", [128, 4, S], BF16)
                V = sb(st, "m_V", [128, NT, 4, 65], BF16)
                b_QT, b_KT, b_V = p.buf(), p.buf(), p.buf()
                for hg in range(2):
                    with ExitStack() as s2:
                        p.op("pool", lambda e: e.memset(V[:], 1.0), writes=[b_V])
                        wuq, b_wuq = load_weight_bf(s2, "m_wuq", mats["mla_w_uq"][l], 256, 768)
                        wukv, b_wukv = load_weight_bf(s2, "m_wukv", mats["mla_w_ukv"][l], 128, 1024)
                        qn, b_qn = load_bcast(s2, "m_qn", vec["mla_q_norm"][l:l + 1, :], 256)
                        kvn, b_kvn = load_bcast(s2, "m_kvn", vec["mla_kv_norm"][l:l + 1, :], 128)
                        pt_ = [sb(s2, "m_p%d" % i, [128, 416], F32) for i in range(2)]
                        b_pt = p.bufs(2)
                        junk = sb(s2, "m_junk", [128, 256], BF16)
                        b_junk = p.buf()
                        ss = [sb(s2, "m_ss%d" % i, [128, 2], F32) for i in range(2)]
                        b_ss = p.bufs(2)
                        cb_ = [sb(s2, "m_cb%d" % i, [128, 384], BF16) for i in range(2)]
                        b_cb = p.bufs(2)
                        pT = ps(s2, "m_pT", [128, 3, 128], BF16)
                        b_pT = p.buf()
                        cT = [sb(s2, "m_cT%d" % i, [128, 3, 128], BF16) for i in range(2)]
                        b_cT = p.bufs(2)
                        pq = [ps(s2, "m_pq%d" % i, [128, 512], F32) for i in range(2)]
                        b_pq = p.bufs(2)
                        qb = [sb(s2, "m_qb%d" % i, [128, 8, 96], BF16) for i in range(2)]
                        b_qb = p.bufs(2)
                        tA = sb(s2, "m_tA", [128, 8, 16], F32)
                        tB = sb(s2, "m_tB", [128, 8, 16], F32)
                        tC = sb(s2, "m_tC", [128, 8, 16], F32)
                        tD = sb(s2, "m_tD", [128, 8, 16], F32)
                        qf = sb(s2, "m_qf", [128, 768], F32)
                        b_tA, b_tB, b_tC, b_tD, b_qf = p.buf(), p.buf(), p.buf(), p.buf(), p.buf()
                        pqT = ps(s2, "m_pqT", [128, 8, 128], BF16)
                        b_pqT = p.buf()
                        pkn = [ps(s2, "m_pkn%d" % i, [64, 4, 128], F32) for i in range(2)]
                        b_pkn = p.bufs(2)
                        pv = ps(s2, "m_pv", [128, 512], F32)
                        b_pv = p.buf()
                        kr = [sb(s2, "m_kr%d" % i, [128, 96], BF16) for i in range(2)]
                        b_kr = p.bufs(2)
                        for i in range(2):
                            p.op("pool", lambda e, i=i: e.memset(kr[i][:], 0.0), writes=[b_kr[i]])
                        pkr = ps(s2, "m_pkr", [128, 128], BF16)
                        b_pkr = p.buf()
                        krT = sb(s2, "m_krT", [128, 128], BF16)
                        b_krT = p.buf()
                        wukv_v = wukv[:, 0, :].rearrange("p (h c) -> p h c", c=128)
                        MSUB = int(os.environ.get('MLA_SUB', '99'))
                        for t in range(int(os.environ.get('MLA_NT', NT))):
                            i = t % 2
                            ts_ = slice(t * 128, (t + 1) * 128)
                            if MSUB < 0:
                                continue
                            p.dma("sync", lambda e, i=i, ts_=ts_: e.dma_start(out=pt_[i][:], in_=pml[ts_, :]), reads=[B_pml[t]], writes=[b_pt[i]])
                            p.op("pool", lambda e, i=i: e.memset(ss[i][:], 0.0), writes=[b_ss[i]])
                            p.op("act", lambda e, i=i: e.activation(out=junk[:, 0:256], in_=pt_[i][:, 0:256], func=AF.Square, accum_out=ss[i][:, 0:1]),
                                 reads=[b_pt[i], b_ss[i]], writes=[b_junk, b_ss[i]])
                            p.op("act", lambda e, i=i: e.activation(out=junk[:, 0:128], in_=pt_[i][:, 256:384], func=AF.Square, accum_out=ss[i][:, 1:2]),
                                 reads=[b_pt[i], b_ss[i]], writes=[b_junk, b_ss[i]])
                            p.op("act", lambda e, i=i: e.activation(out=ss[i][:, 0:1], in_=ss[i][:, 0:1], func=AF.Sqrt, scale=1.0 / 256, bias=g.eps_tiles[1e-6][:]),
                                 reads=[b_ss[i], b_eps], writes=[b_ss[i]])
                            p.op("act", lambda e, i=i: e.activation(out=ss[i][:, 1:2], in_=ss[i][:, 1:2], func=AF.Sqrt, scale=1.0 / 128, bias=g.eps_tiles[1e-6][:]),
                                 reads=[b_ss[i], b_eps], writes=[b_ss[i]])
                            p.op("dve", lambda e, i=i: e.reciprocal(out=ss[i][:], in_=ss[i][:]), reads=[b_ss[i]], writes=[b_ss[i]])
                            p.op("dve", lambda e, i=i: e.scalar_tensor_tensor(out=cb_[i][:, 0:256], in0=pt_[i][:, 0:256], scalar=ss[i][:, 0:1], in1=qn[:],
                                                                              op0=ALU.mult, op1=ALU.mult), reads=[b_pt[i], b_ss[i], b_qn], writes=[b_cb[i]])
                            p.op("dve", lambda e, i=i: e.scalar_tensor_tensor(out=cb_[i][:, 256:384], in0=pt_[i][:, 256:384], scalar=ss[i][:, 1:2], in1=kvn[:],
                                                                              op0=ALU.mult, op1=ALU.mult), reads=[b_pt[i], b_ss[i], b_kvn], writes=[b_cb[i]])
                            if MSUB < 2:
                                continue
                            for k in range(3):
                                p.op("pe", lambda e, i=i, k=k: e.transpose(out=pT[:, k, :], in_=cb_[i][:, k * 128:(k + 1) * 128], identity=ident[:]),
                                     reads=[b_cb[i], b_ident], writes=[b_pT], sig=(k == 2))
                            p.op("act", lambda e, i=i: e.copy(out=cT[i][:], in_=pT[:]), reads=[b_pT], writes=[b_cT[i]])
                            if MSUB < 3:
                                continue
                            for (c0, ncol, pi) in ((0, 512, 0), (512, 256, 1)):
                                for kk in range(2):
                                    p.op("pe", lambda e, i=i, kk=kk, c0=c0, ncol=ncol, pi=pi: e.matmul(pq[pi][:, 0:ncol], lhsT=cT[i][:, kk, :], rhs=wuq[:, kk, c0:c0 + ncol],
                                                                                                 start=(kk == 0), stop=(kk == 1)),
                                         reads=[b_cT[i], b_wuq], writes=[b_pq[pi]], sig=(kk == 1))
                            if MSUB < 4:
                                continue
                            p.op("act", lambda e: e.copy(out=qf[:, 0:512], in_=pq[0][:, 0:512]), reads=[b_pq[0]], writes=[b_qf])
                            p.op("act", lambda e: e.copy(out=qf[:, 512:768], in_=pq[1][:, 0:256]), reads=[b_pq[1]], writes=[b_qf])
                            for h in range(hg * 4, hg * 4 + 4):
                                c0 = h * 96
                                p.op("act", lambda e, i=i, h=h, c0=c0: e.copy(out=qb[i][:, h, 0:64], in_=qf[:, c0:c0 + 64]), reads=[b_qf], writes=[b_qb[i]])
                                x1 = qf[:, c0 + 64:c0 + 80]
                                x2 = qf[:, c0 + 80:c0 + 96]
                                cst = cs_t[:, t, :]
                                snt = sn_t[:, t, :]
                                p.op("dve", lambda e, h=h, x1=x1, cst=cst: e.tensor_tensor(out=tA[:, h, :], in0=x1, in1=cst, op=ALU.mult), reads=[b_qf, b_rope], writes=[b_tA])
                                p.op("dve", lambda e, h=h, x2=x2, snt=snt: e.tensor_tensor(out=tB[:, h, :], in0=x2, in1=snt, op=ALU.mult), reads=[b_qf, b_rope], writes=[b_tB])
                                p.op("dve", lambda e, i=i, h=h: e.tensor_tensor(out=qb[i][:, h, 64:80], in0=tA[:, h, :], in1=tB[:, h, :], op=ALU.subtract),
                                     reads=[b_tA, b_tB], writes=[b_qb[i]])
                                p.op("dve", lambda e, h=h, x1=x1, snt=snt: e.tensor_tensor(out=tC[:, h, :], in0=x1, in1=snt, op=ALU.mult), reads=[b_qf, b_rope], writes=[b_tC])
                                p.op("dve", lambda e, h=h, x2=x2, cst=cst: e.tensor_tensor(out=tD[:, h, :], in0=x2, in1=cst, op=ALU.mult), reads=[b_qf, b_rope], writes=[b_tD])
                                p.op("dve", lambda e, i=i, h=h: e.tensor_tensor(out=qb[i][:, h, 80:96], in0=tC[:, h, :], in1=tD[:, h, :], op=ALU.add),
                                     reads=[b_tC, b_tD], writes=[b_qb[i]])
                            if MSUB < 5:
                                continue
                            for hh in range(4):
                                h = hg * 4 + hh
                                p.op("pe", lambda e, i=i, h=h, hh=hh: e.transpose(out=pqT[0:96, hh, :], in_=qb[i][:, h, :], identity=ident[:]),
                                     reads=[b_qb[i], b_ident], writes=[b_pqT], sig=(hh == 3))
                            p.op("act", lambda e, ts_=ts_: e.copy(out=QT[0:96, :, ts_], in_=pqT[0:96, 0:4, :]), reads=[b_pqT], writes=[b_QT])
                            if MSUB < 6:
                                continue
                            for hh in range(4):
                                h = hg * 4 + hh
                                p.op("pe", lambda e, i=i, h=h, hh=hh: e.matmul(pkn[0][:, hh, :], lhsT=wukv[:, 0, h * 128:h * 128 + 64], rhs=cT[i][:, 2, :], start=True, stop=True),
                                     reads=[b_cT[i], b_wukv], writes=[b_pkn[0]], sig=(hh == 3))
                            p.op("dve", lambda e, ts_=ts_: e.tensor_copy(out=KT[0:64, :, ts_], in_=pkn[0][:]), reads=[b_pkn[0]], writes=[b_KT])
                            if MSUB < 7:
                                continue
                            for hh in range(4):
                                h = hg * 4 + hh
                                p.op("pe", lambda e, i=i, h=h, hh=hh: e.matmul(pv[:, hh * 64:(hh + 1) * 64], lhsT=cT[i][:, 2, :], rhs=wukv[:, 0, h * 128 + 64:h * 128 + 128], start=True, stop=True),
                                     reads=[b_cT[i], b_wukv], writes=[b_pv], sig=(hh == 3))
                            p.op("act", lambda e, t=t: e.copy(out=V[:, t, :, 1:65], in_=pv[:, 0:256].rearrange("p (h c) -> p h c", c=64)), reads=[b_pv], writes=[b_V])
                            if MSUB < 8:
                                continue
                            cst = cs_t[:, t, :]
                            snt = sn_t[:, t, :]
                            x1 = pt_[i][:, 384:400]
                            x2 = pt_[i][:, 400:416]
                            p.op("dve", lambda e, x1=x1, cst=cst: e.tensor_tensor(out=tA[:, 0, :], in0=x1, in1=cst, op=ALU.mult), reads=[b_pt[i], b_rope], writes=[b_tA])
                            p.op("dve", lambda e, x2=x2, snt=snt: e.tensor_tensor(out=tB[:, 0, :], in0=x2, in1=snt, op=ALU.mult), reads=[b_pt[i], b_rope], writes=[b_tB])
                            p.op("dve", lambda e, i=i: e.tensor_tensor(out=kr[i][:, 64:80], in0=tA[:, 0, :], in1=tB[:, 0, :], op=ALU.subtract), reads=[b_tA, b_tB], writes=[b_kr[i]])
                            p.op("dve", lambda e, x1=x1, snt=snt: e.tensor_tensor(out=tA[:, 0, :], in0=x1, in1=snt, op=ALU.mult), reads=[b_pt[i], b_rope], writes=[b_tA])
                            p.op("dve", lambda e, x2=x2, cst=cst: e.tensor_tensor(out=tB[:, 0, :], in0=x2, in1=cst, op=ALU.mult), reads=[b_pt[i], b_rope], writes=[b_tB])
                            p.op("dve", lambda e, i=i: e.tensor_tensor(out=kr[i][:, 80:96], in0=tA[:, 0, :], in1=tB[:, 0, :], op=ALU.add), reads=[b_tA, b_tB], writes=[b_kr[i]])
                            p.op("pe", lambda e, i=i: e.transpose(out=pkr[0:96, :], in_=kr[i][:, :], identity=ident[:]), reads=[b_kr[i], b_ident], writes=[b_pkr])
                            p.op("act", lambda e: e.copy(out=krT[64:96, :], in_=pkr[64:96, :]), reads=[b_pkr], writes=[b_krT])
                            for h in range(4):
                                p.op("pool", lambda e, h=h, ts_=ts_: e.tensor_copy(out=KT[64:96, h, ts_], in_=krT[64:96, :]), reads=[b_krT], writes=[b_KT])
                        p.barrier()
                    if os.environ.get("MLA_STAGE") == "prep":
                        continue
                    with ExitStack() as s3:
                        r = make_attn_res(s3)
                        po = [ps(s3, "m_po%d" % i, [128, 512], F32) for i in range(2)]
                        b_po = p.bufs(2)
                        pbc = ps(s3, "m_pbc", [128, 512], F32)
                        b_pbc = p.buf()
                        rc = [sb(s3, "m_rc%d" % i, [1, 512], F32) for i in range(2)]
                        b_rc = p.bufs(2)
                        bcs = [sb(s3, "m_bcs%d" % i, [65, 512], F32) for i in range(2)]
                        b_bcs = p.bufs(2)
                        ob = [sb(s3, "m_ob%d" % i, [65, 512], BF16) for i in range(2)]
                        b_ob = p.bufs(2)
                        n = 0
                        scale = 96 ** -0.5
                        for hh in range(4):
                            h = hg * 4 + hh
                            for c in range(int(os.environ.get('MLA_NC', 8))):
                                i = n % 2
                                n += 1
                                attn_chunk(r, c,
                                           lambda q0, nq, hh=hh: QT[0:96, hh, q0:q0 + nq], b_QT,
                                           lambda kt, hh=hh: KT[0:96, hh, kt * 128:(kt + 1) * 128], b_KT,
                                           lambda kt, hh=hh: V[:, kt, hh, :], b_V, 65, scale,
                                           maskD[:], b_maskD, None, None, None, po[i], b_po[i], None, None)
                                p.op("dve", lambda e, i=i: e.reciprocal(out=rc[i][:], in_=po[i][0:1, :]), reads=[b_po[i]], writes=[b_rc[i]])
                                p.op("pe", lambda e, i=i: e.matmul(pbc[0:65, :], lhsT=ones_f[0:1, 0:65], rhs=rc[i][:], start=True, stop=True),
                                     reads=[b_rc[i], b_ones], writes=[b_pbc])
                                p.op("act", lambda e, i=i: e.copy(out=bcs[i][:], in_=pbc[0:65, :]), reads=[b_pbc], writes=[b_bcs[i]])
                                p.op("dve", lambda e, i=i: e.tensor_tensor(out=ob[i][:], in0=po[i][0:65, :], in1=bcs[i][:], op=ALU.mult),
                                     reads=[b_po[i], b_bcs[i]], writes=[b_ob[i]])
                                p.dma("sync", lambda e, i=i, h=h, c=c: e.dma_start(out=oT["mla"][h * 64:(h + 1) * 64, c * 512:(c + 1) * 512], in_=ob[i][1:65, :]),
                                      reads=[b_ob[i]], writes=[B_oT["mla"][c]])
                        p.barrier()

        def phase_diff(l):
            lambda_init = 0.8 - 0.6 * math.exp(-0.3 * l)
            with ExitStack() as st:
                r = make_attn_res(st)
                lam = sb(st, "d_lam", [1, 256], F32)
                lamp = sb(st, "d_lamp", [1, 128], F32)
                lams = sb(st, "d_lams", [1, 4], F32)
                b_lam = p.buf()
                p.dma("sync", lambda e: e.dma_start(out=lam[:], in_=diff_lambda[l]), writes=[b_lam])
                p.op("pool", lambda e: e.memset(lams[:], 0.0), writes=[b_lam])
                p.op("dve", lambda e: e.tensor_tensor(out=lamp[:, 0:64], in0=lam[:, 0:64], in1=lam[:, 64:128], op=ALU.mult), reads=[b_lam], writes=[b_lam])
                p.op("dve", lambda e: e.tensor_tensor(out=lamp[:, 64:128], in0=lam[:, 128:192], in1=lam[:, 192:256], op=ALU.mult), reads=[b_lam], writes=[b_lam])
                p.op("dve", lambda e: e.tensor_reduce(out=lams[:, 0:2], in_=lamp[:].rearrange("p (a b) -> p a b", b=64), axis=AX.X, op=ALU.add), reads=[b_lam], writes=[b_lam])
                p.op("act", lambda e: e.activation(out=lams[:, 0:2], in_=lams[:, 0:2], func=AF.Exp), reads=[b_lam], writes=[b_lam])
                p.op("dve", lambda e: e.scalar_tensor_tensor(out=lams[:, 2:3], in0=lams[:, 1:2], scalar=-lambda_init, in1=lams[:, 0:1], op0=ALU.add, op1=ALU.subtract),
                     reads=[b_lam], writes=[b_lam])
                sub = sb(st, "d_sub", [128, 1], F32)
                b_sub = p.buf()
                p.dma("sync", lambda e: e.dma_start(out=sub[:], in_=subln_cm[l]), writes=[b_sub])
                p.op("dve", lambda e: e.tensor_scalar(out=sub[:], in0=sub[:], scalar1=1.0 - lambda_init, scalar2=None, op0=ALU.mult), reads=[b_sub], writes=[b_sub])
                QT = [sb(st, "d_QT%d" % i, [64, S], BF16) for i in range(2)]
                KT = [sb(st, "d_KT%d" % i, [64, S], BF16) for i in range(2)]
                b_QK = p.bufs(2)
                V = sb(st, "d_V", [128, NT, 128], BF16)
                b_V = p.buf()
                po = [ps(st, "d_po%d" % i, [128, 512], F32) for i in range(2)]
                b_po = p.bufs(2)
                psm = [ps(st, "d_psm%d" % i, [1, 512], F32) for i in range(2)]
                b_psm = p.bufs(2)
                pbc = ps(st, "d_pbc", [128, 512], F32)
                b_pbc = p.buf()
                rc = [sb(st, "d_rc%d" % i, [1, 512], F32) for i in range(2)]
                b_rc = p.bufs(2)
                bcs = [sb(st, "d_bcs%d" % i, [128, 512], F32) for i in range(2)]
                b_bcs = p.bufs(2)
                o0 = sb(st, "d_o0", [128, 512], F32)
                o1 = sb(st, "d_o1", [128, 512], F32)
                sq = sb(st, "d_sq", [128, 512], F32)
                b_o0, b_o1, b_sq = p.buf(), p.buf(), p.buf()
                ob = [sb(st, "d_ob%d" % i, [128, 512], BF16) for i in range(2)]
                b_ob = p.bufs(2)
                n = 0
                scale = 0.125
                for h in range(4):
                    for m in range(2):
                        rq = (h * 2 + m) * 64
                        p.dma("sync", lambda e, m=m, rq=rq: e.dma_start(out=QT[m][:], in_=qkT[rq:rq + 64, :]), reads=B_qkT, writes=[b_QK[m]])
                        p.dma("sync", lambda e, m=m, rq=rq: e.dma_start(out=KT[m][:], in_=qkT[512 + rq:512 + rq + 64, :]), reads=B_qkT, writes=[b_QK[m]])
                    p.dma("sync", lambda e, h=h: e.dma_start(out=V[:], in_=vdf[:, h * 128:(h + 1) * 128].rearrange("(n p) c -> p n c", p=128)),
                          reads=B_vdf, writes=[b_V])
                    farb = relb[:, 31 * 4 + h:31 * 4 + h + 1]
                    for c in range(8):
                        for m in range(2):
                            attn_chunk(r, c,
                                       lambda q0, nq, m=m: QT[m][:, q0:q0 + nq], b_QK[m],
                                       lambda kt, m=m: KT[m][:, kt * 128:(kt + 1) * 128], b_QK[m],
                                       lambda kt: V[:, kt, :], b_V, 128, scale,
                                       Dn[h][:, 0:128], b_Dn, Dn[h][:, 128:256], b_Dn, farb, po[m], b_po[m], psm[m], b_psm[m])
                        for m in range(2):
                            p.op("dve", lambda e, m=m: e.reciprocal(out=rc[m][:], in_=psm[m][:]), reads=[b_psm[m]], writes=[b_rc[m]])
                        p.op("dve", lambda e: e.tensor_scalar(out=rc[1][:], in0=rc[1][:], scalar1=lams[0:1, 2:3], scalar2=None, op0=ALU.mult),
                             reads=[b_rc[1], b_lam], writes=[b_rc[1]])
                        for m in range(2):
                            p.op("pe", lambda e, m=m: e.matmul(pbc[:], lhsT=ones_f[0:1, :], rhs=rc[m][:], start=True, stop=True),
                                 reads=[b_rc[m], b_ones], writes=[b_pbc])
                            p.op("act", lambda e, m=m: e.copy(out=bcs[m][:], in_=pbc[:]), reads=[b_pbc], writes=[b_bcs[m]])
                        p.op("dve", lambda e: e.tensor_tensor(out=o0[:], in0=po[0][:], in1=bcs[0][:], op=ALU.mult), reads=[b_po[0], b_bcs[0]], writes=[b_o0])
                        p.op("dve", lambda e: e.tensor_tensor(out=o1[:], in0=po[1][:], in1=bcs[1][:], op=ALU.mult), reads=[b_po[1], b_bcs[1]], writes=[b_o1])
                        p.op("dve", lambda e: e.tensor_tensor(out=o0[:], in0=o0[:], in1=o1[:], op=ALU.add), reads=[b_o0, b_o1], writes=[b_o0])
                        p.op("act", lambda e: e.activation(out=sq[:], in_=o0[:], func=AF.Square), reads=[b_o0], writes=[b_sq])
                        p.op("pe", lambda e: e.matmul(pbc[:], lhsT=ones_f[:, :], rhs=sq[:], start=True, stop=True), reads=[b_sq, b_ones], writes=[b_pbc])
                        p.op("act", lambda e: e.activation(out=sq[:], in_=pbc[:], func=AF.Sqrt, scale=1.0 / 128, bias=g.eps_tiles[1e-5][:]),
                             reads=[b_pbc, b_eps], writes=[b_sq])
                        p.op("dve", lambda e: e.reciprocal(out=sq[:], in_=sq[:]), reads=[b_sq], writes=[b_sq])
                        i = n % 2
                        n += 1
                        p.op("dve", lambda e, i=i: e.scalar_tensor_tensor(out=ob[i][:], in0=o0[:], scalar=sub[:, 0:1], in1=sq[:], op0=ALU.mult, op1=ALU.mult),
                             reads=[b_o0, b_sub, b_sq], writes=[b_ob[i]])
                        p.dma("sync", lambda e, i=i, h=h, c=c: e.dma_start(out=oT["diff"][h * 128:(h + 1) * 128, c * 512:(c + 1) * 512], in_=ob[i][:]),
                              reads=[b_ob[i]], writes=[B_oT["diff"][c]])
                p.barrier()

        def phase_rwkv(l):
            with ExitStack() as st:
                def T_(name, shape, dt=F32):
                    return sb(st, "r_" + name, shape, dt)
                mu, b_mu = load_bcast(st, "r_mu", vec["rwkv_mu"][l:l + 1, :], 1792)
                w0, b_w0 = load_bcast(st, "r_w0", vec["rwkv_w0"][l:l + 1, :], 512)
                a0, b_a0 = load_bcast(st, "r_a0", vec["rwkv_a0"][l:l + 1, :], 512)
                k_k, b_kk_ = load_bcast(st, "r_k_k", vec["rwkv_k_k"][l:l + 1, :], 512)
                k_a, b_ka_ = load_bcast(st, "r_k_a", vec["rwkv_k_a"][l:l + 1, :], 512)
                r_k, b_rk_ = load_bcast(st, "r_r_k", vec["rwkv_r_k"][l:l + 1, :], 512)
                ln_w, b_lnw = load_bcast(st, "r_ln_w", vec["rwkv_ln_w"][l:l + 1, :], 512)
                ln_b, b_lnb = load_bcast(st, "r_ln_b", vec["rwkv_ln_b"][l:l + 1, :], 512)
                w2, b_w2 = load_weight_bf(st, "r_w2", mats["rwkv_w2"][l], 64, 512, part0=0)
                a2, b_a2 = load_weight_bf(st, "r_a2", mats["rwkv_a2"][l], 64, 512, part0=64)
                g2, b_g2 = load_weight_bf(st, "r_g2", mats["rwkv_g2"][l], 128, 512)
                triU = T_("triU", [128, 128])
                mSI = T_("mSI", [128, 256])
                mSL = T_("mSL", [128, 128])
                b_msk = p.buf()
                p.op("pool", lambda e: e.memset(triU[:], 1.0), writes=[b_msk])
                p.op("pool", lambda e: e.affine_select(out=triU[:], in_=triU[:], pattern=[[1, 128]], compare_op=ALU.is_ge, fill=0.0, base=0, channel_multiplier=-1),
                     reads=[b_msk], writes=[b_msk])
                p.op("pool", lambda e: e.memset(mSI[:], 1.0), writes=[b_msk])
                p.op("pool", lambda e: e.affine_select(out=mSI[:, 0:128], in_=mSI[:, 0:128], pattern=[[1, 128]], compare_op=ALU.is_gt, fill=0.0, base=0, channel_multiplier=-1),
                     reads=[b_msk], writes=[b_msk])
                p.op("pool", lambda e: e.affine_select(out=mSI[:, 128:256], in_=mSI[:, 128:256], pattern=[[1, 128]], compare_op=ALU.is_ge, fill=0.0, base=0, channel_multiplier=-1),
                     reads=[b_msk], writes=[b_msk])
                p.op("pool", lambda e: e.memset(mSL[:], 1.0), writes=[b_msk])
                p.op("pool", lambda e: e.affine_select(out=mSL[:], in_=mSL[:], pattern=[[-1, 128]], compare_op=ALU.is_gt, fill=0.0, base=0, channel_multiplier=1),
                     reads=[b_msk], writes=[b_msk])
                identf = T_("identf", [128, 128], BF16)
                Sst = T_("S", [64, 8, 64])
                Sb = T_("Sb", [64, 8, 64], BF16)
                b_S, b_Sb = p.buf(), p.buf()
                p.op("pool", lambda e: e.memset(Sst[:], 0.0), writes=[b_S])
                p.op("pool", lambda e: e.memset(Sb[:], 0.0), writes=[b_Sb])
                gb = [ps(st, "r_gb%d" % i, [128, 512], F32) for i in range(6)]
                b_gb = p.bufs(6)
                gbn = [0]
                tb = [ps(st, "r_tb%d" % i, [128, 8, 128], BF16) for i in range(2)]
                b_tb = p.bufs(2)
                tbn = [0]

                def bank():
                    i = gbn[0] % 6
                    gbn[0] += 1
                    return gb[i], b_gb[i]

                def tbank():
                    i = tbn[0] % 2
                    tbn[0] += 1
                    return tb[i], b_tb[i]

                P0 = [T_("P0_0", [128, 1792])] * 2
                P1 = [T_("P1_0", [128, 1792])] * 2
                b_P0, b_P1 = [p.buf()] * 2, [p.buf()] * 2
                PM = [T_("PM%d" % i, [128, 1792]) for i in range(2)]
                b_PM = p.bufs(2)
                th = T_("th", [128, 256], BF16); b_th = p.buf()
                thT = T_("thT", [128, 2, 128], BF16); b_thT = p.buf()
                lw = T_("lw", [128, 512]); b_lw = p.buf()
                alr = T_("alr", [128, 512]); b_alr = p.buf()
                gg = [T_("gg%d" % i, [128, 512]) for i in range(2)]; b_gg = p.bufs(2)
                kk = T_("kk", [128, 512]); b_kk = p.buf()
                tmp = T_("tmp", [128, 512]); b_tmp = p.buf()
                tmp2 = T_("tmp2", [128, 512]); b_tmp2 = p.buf()
                k2 = [T_("k2_%d" % i, [128, 512]) for i in range(2)]; b_k2 = p.bufs(2)
                bt = T_("bt", [128, 512]); b_bt = p.buf()
                st8 = T_("st8", [128, 8]); b_st8 = p.buf()
                cumS = T_("cumS", [128, 512]); b_cumS = p.buf()
                eC = T_("eC", [128, 512]); b_eC = p.buf()
                eCi = T_("eCi", [128, 512]); b_eCi = p.buf()
                eCx = T_("eCx", [128, 512]); b_eCx = p.buf()
                eD = T_("eD", [128, 512]); b_eD = p.buf()
                gC = [T_("gC%d" % i, [64, 8]) for i in range(2)]; b_gC = p.bufs(2)
                X4 = T_("X4", [128, 4, 512], BF16); b_X4 = p.bufs(4)
                BH = [T_("BH%d" % i, [128, 512], BF16) for i in range(2)]; b_BH = p.bufs(2)
                KH = [T_("KH%d" % i, [128, 512], BF16) for i in range(2)]; b_KH = p.bufs(2)
                VB = [T_("VB%d" % i, [128, 512], BF16) for i in range(2)]; b_VB = p.bufs(2)
                CM = [T_("CM%d" % i, [64, 8, 4, 128], BF16) for i in range(2)]; b_CM = p.bufs(2)
                MM = [T_("MM%d" % i, [128, 8, 2, 256], BF16) for i in range(2)]; b_MM = p.bufs(2)
                Pp = [T_("Pp%d" % i, [128, 8, 128], BF16) for i in range(2)]; b_Pp = p.bufs(2)
                PTp = [T_("PTp%d" % i, [128, 8, 128], BF16) for i in range(2)]; b_PTp = p.bufs(2)
                Tp = [T_("Tp%d" % i, [128, 8, 128], BF16) for i in range(2)]; b_Tp = p.bufs(2)
                Tf = [T_("Tf%d" % i, [128, 8, 128], BF16) for i in range(2)]; b_Tf = p.bufs(2)
                Ws = T_("Ws", [128, 512], BF16); b_Ws = p.buf()
                Us = T_("Us", [128, 512], BF16); b_Us = p.buf()
                yt = T_("yt", [128, 512]); b_yt = p.buf()
                yc = T_("yc", [128, 512]); b_yc = p.buf()
                yo = T_("yo", [128, 512], BF16); b_yo = p.buf()
                oTt = [T_("oTt%d" % i, [128, 4, 128], BF16) for i in range(2)]; b_oTt = p.bufs(2)
                NEGE = -math.exp(-0.5)
                RSUB = int(os.environ.get('RW_SUB', '99'))

                def H(a, h):
                    return a[:, h * 64:(h + 1) * 64]

                def pre(t):
                    i = t % 2
                    t0 = t * 128
                    p.dma("sync", lambda e: e.dma_start(out=P0[i][:], in_=prw[t0:t0 + 128, :]), reads=[B_prw[t]], writes=[b_P0[i]])
                    if t == 0:
                        p.op("pool", lambda e: e.memset(P1[i][0:1, :], 0.0), writes=[b_P1[i]])
                        p.dma("sync", lambda e: e.dma_start(out=P1[i][1:128, :], in_=prw[0:127, :]), reads=[B_prw[0]], writes=[b_P1[i]])
                    else:
                        p.dma("sync", lambda e: e.dma_start(out=P1[i][:], in_=prw[t0 - 1:t0 + 127, :]), reads=[B_prw[t], B_prw[t - 1]], writes=[b_P1[i]])
                    pm = PM[i]
                    p.op("dve", lambda e: e.tensor_tensor(out=P1[i][:], in0=P1[i][:], in1=P0[i][:], op=ALU.subtract), reads=[b_P1[i], b_P0[i]], writes=[b_P1[i]])
                    p.op("dve", lambda e: e.tensor_tensor(out=P1[i][:], in0=P1[i][:], in1=mu[:], op=ALU.mult), reads=[b_P1[i], b_mu], writes=[b_P1[i]])
                    p.op("dve", lambda e: e.tensor_tensor(out=pm[:], in0=P1[i][:], in1=P0[i][:], op=ALU.add), reads=[b_P1[i], b_P0[i]], writes=[b_PM[i]])
                    r_ = pm[:, 0:512]
                    k_ = pm[:, 512:1024]
                    v_ = pm[:, 1024:1536]
                    if RSUB < 1:
                        return
                    p.op("act", lambda e: e.activation(out=th[:, 0:64], in_=pm[:, 1536:1600], func=AF.Tanh), reads=[b_PM[i]], writes=[b_th])
                    p.op("act", lambda e: e.copy(out=th[:, 64:128], in_=pm[:, 1600:1664]), reads=[b_PM[i]], writes=[b_th])
                    p.op("act", lambda e: e.activation(out=th[:, 128:256], in_=pm[:, 1664:1792], func=AF.Sigmoid), reads=[b_PM[i]], writes=[b_th])
                    tbk0, b_tbk0 = tbank()
                    for k in range(2):
                        p.op("pe", lambda e, k=k: e.transpose(out=tbk0[:, k, :], in_=th[:, k * 128:(k + 1) * 128], identity=ident[:]),
                             reads=[b_th, b_ident], writes=[b_tbk0], sig=(k == 1))
                    p.op("act", lambda e: e.copy(out=thT[:], in_=tbk0[:, 0:2, :]), reads=[b_tbk0], writes=[b_thT])
                    pw_, b_pw = bank()
                    p.op("pe", lambda e: e.matmul(pw_[:], lhsT=thT[0:64, 0, :], rhs=w2[0:64, 0, :], start=True, stop=True), reads=[b_thT, b_w2], writes=[b_pw])
                    pa_, b_pa = bank()
                    p.op("pe", lambda e: e.matmul(pa_[:], lhsT=thT[64:128, 0, :], rhs=a2[64:128, 0, :], start=True, stop=True), reads=[b_thT, b_a2], writes=[b_pa])
                    pg_, b_pg = bank()
                    p.op("pe", lambda e: e.matmul(pg_[:], lhsT=thT[:, 1, :], rhs=g2[:, 0, :], start=True, stop=True), reads=[b_thT, b_g2], writes=[b_pg])
                    p.op("dve", lambda e: e.tensor_tensor(out=lw[:], in0=pw_[:], in1=w0[:], op=ALU.add), reads=[b_pw, b_w0], writes=[b_lw])
                    p.op("act", lambda e: e.activation(out=lw[:], in_=lw[:], func=AF.Sigmoid), reads=[b_lw], writes=[b_lw])
                    p.op("dve", lambda e: e.tensor_scalar(out=lw[:], in0=lw[:], scalar1=NEGE, scalar2=None, op0=ALU.mult), reads=[b_lw], writes=[b_lw])
                    p.op("dve", lambda e: e.tensor_tensor(out=alr[:], in0=pa_[:], in1=a0[:], op=ALU.add), reads=[b_pa, b_a0], writes=[b_alr])
                    p.op("act", lambda e: e.activation(out=alr[:], in_=alr[:], func=AF.Sigmoid), reads=[b_alr], writes=[b_alr])
                    p.op("act", lambda e: e.copy(out=gg[i][:], in_=pg_[:]), reads=[b_pg], writes=[b_gg[i]])
                    if RSUB < 2:
                        return
                    p.op("dve", lambda e: e.tensor_tensor(out=kk[:], in0=k_, in1=k_k[:], op=ALU.mult), reads=[b_PM[i], b_kk_], writes=[b_kk])
                    p.op("dve", lambda e: e.tensor_tensor(out=tmp[:], in0=kk[:], in1=kk[:], op=ALU.mult), reads=[b_kk], writes=[b_tmp])
                    p.op("dve", lambda e: e.tensor_reduce(out=st8[:], in_=tmp[:].rearrange("p (h c) -> p h c", c=64), axis=AX.X, op=ALU.add), reads=[b_tmp], writes=[b_st8])
                    p.op("act", lambda e: e.activation(out=st8[:], in_=st8[:], func=AF.Sqrt), reads=[b_st8], writes=[b_st8])
                    p.op("dve", lambda e: e.tensor_scalar(out=st8[:], in0=st8[:], scalar1=1e-12, scalar2=None, op0=ALU.max), reads=[b_st8], writes=[b_st8])
                    p.op("dve", lambda e: e.reciprocal(out=st8[:], in_=st8[:]), reads=[b_st8], writes=[b_st8])
                    for h in range(8):
                        p.op("dve", lambda e, h=h: e.tensor_scalar(out=H(kk, h), in0=H(kk, h), scalar1=st8[:, h:h + 1], scalar2=None, op0=ALU.mult),
                             reads=[b_kk, b_st8], writes=[b_kk])
                    p.op("dve", lambda e: e.scalar_tensor_tensor(out=tmp[:], in0=alr[:], scalar=-1.0, in1=k_a[:], op0=ALU.add, op1=ALU.mult),
                         reads=[b_alr, b_ka_], writes=[b_tmp])
                    p.op("dve", lambda e: e.scalar_tensor_tensor(out=k2[i][:], in0=tmp[:], scalar=1.0, in1=k_, op0=ALU.add, op1=ALU.mult),
                         reads=[b_tmp, b_PM[i]], writes=[b_k2[i]])
                    p.op("dve", lambda e: e.tensor_tensor(out=bt[:], in0=kk[:], in1=alr[:], op=ALU.mult), reads=[b_kk, b_alr], writes=[b_bt])
                    if RSUB < 3:
                        return
                    pc_, b_pc = bank()
                    p.op("pe", lambda e: e.matmul(pc_[:], lhsT=triU[:], rhs=lw[:], start=True, stop=True), reads=[b_msk, b_lw], writes=[b_pc])
                    ptot, b_ptot = bank()
                    p.op("pe", lambda e: e.matmul(ptot[:], lhsT=ones_f[:], rhs=lw[:], start=True, stop=True), reads=[b_ones, b_lw], writes=[b_ptot])
                    pgc, b_pgc = bank()
                    for hd in range(8):
                        p.op("pe", lambda e, hd=hd: e.matmul(pgc[0:64, hd:hd + 1], lhsT=lw[:, hd * 64:(hd + 1) * 64], rhs=ones_f[:, 0:1], start=True, stop=True),
                             reads=[b_lw, b_ones], writes=[b_pgc], sig=(hd == 7))
                    p.op("act", lambda e: e.activation(out=gC[i][:], in_=pgc[0:64, 0:8], func=AF.Exp), reads=[b_pgc], writes=[b_gC[i]])
                    p.op("act", lambda e: e.copy(out=cumS[:], in_=pc_[:]), reads=[b_pc], writes=[b_cumS])
                    p.op("act", lambda e: e.activation(out=eC[:], in_=cumS[:], func=AF.Exp), reads=[b_cumS], writes=[b_eC])
                    p.op("act", lambda e: e.activation(out=eCi[:], in_=cumS[:], func=AF.Exp, scale=-1.0), reads=[b_cumS], writes=[b_eCi])
                    p.op("dve", lambda e: e.tensor_tensor(out=tmp[:], in0=cumS[:], in1=lw[:], op=ALU.subtract), reads=[b_cumS, b_lw], writes=[b_tmp])
                    p.op("act", lambda e: e.activation(out=eCx[:], in_=tmp[:], func=AF.Exp), reads=[b_tmp], writes=[b_eCx])
                    p.op("dve", lambda e: e.tensor_tensor(out=tmp2[:], in0=ptot[:], in1=cumS[:], op=ALU.subtract), reads=[b_ptot, b_cumS], writes=[b_tmp2])
                    p.op("act", lambda e: e.activation(out=eD[:], in_=tmp2[:], func=AF.Exp), reads=[b_tmp2], writes=[b_eD])
                    if RSUB < 4:
                        return
                    p.op("dve", lambda e: e.scalar_tensor_tensor(out=X4[:, 0, :], in0=kk[:], scalar=-1.0, in1=eCx[:], op0=ALU.mult, op1=ALU.mult),
                         reads=[b_kk, b_eCx], writes=[b_X4[0]])
                    p.op("dve", lambda e: e.tensor_tensor(out=X4[:, 1, :], in0=r_, in1=eC[:], op=ALU.mult), reads=[b_PM[i], b_eC], writes=[b_X4[1]])
                    p.op("dve", lambda e: e.tensor_tensor(out=X4[:, 2, :], in0=bt[:], in1=eCi[:], op=ALU.mult), reads=[b_bt, b_eCi], writes=[b_X4[2]])
                    p.op("dve", lambda e: e.tensor_tensor(out=X4[:, 3, :], in0=k2[i][:], in1=eCi[:], op=ALU.mult), reads=[b_k2[i], b_eCi], writes=[b_X4[3]])
                    p.op("dve", lambda e: e.tensor_tensor(out=BH[i][:], in0=bt[:], in1=eD[:], op=ALU.mult), reads=[b_bt, b_eD], writes=[b_BH[i]])
                    p.op("dve", lambda e: e.tensor_tensor(out=KH[i][:], in0=k2[i][:], in1=eD[:], op=ALU.mult), reads=[b_k2[i], b_eD], writes=[b_KH[i]])
                    p.op("act", lambda e: e.copy(out=VB[i][:], in_=v_), reads=[b_PM[i]], writes=[b_VB[i]])
                    if RSUB < 5:
                        return
                    for hd in range(8):
                        tbk, b_tbk = tbank()
                        for x in range(4):
                            p.op("pe", lambda e, hd=hd, x=x, tbk=tbk: e.transpose(out=tbk[0:64, x, :], in_=X4[:, x, hd * 64:(hd + 1) * 64], identity=ident[:]),
                                 reads=[b_X4[x], b_ident], writes=[b_tbk], sig=(x == 3))
                        if hd % 2 == 0:
                            p.op("act", lambda e, hd=hd, tbk=tbk: e.copy(out=CM[i][:, hd, :, :], in_=tbk[0:64, 0:4, :]), reads=[b_tbk], writes=[b_CM[i]])
                        else:
                            p.op("dve", lambda e, hd=hd, tbk=tbk: e.tensor_copy(out=CM[i][:, hd, :, :], in_=tbk[0:64, 0:4, :]), reads=[b_tbk], writes=[b_CM[i]])
                    if RSUB < 6:
                        return
                    for hd in range(8):
                        pm_, b_pm = bank()
                        ar = CM[i][:, hd, :, :].rearrange("p x t -> p (x t)")[:, 0:256]
                        p.op("pe", lambda e, pm_=pm_, hd=hd, ar=ar: e.matmul(pm_[:, 0:256], lhsT=CM[i][:, hd, 2, :], rhs=ar, start=True, stop=True),
                             reads=[b_CM[i]], writes=[b_pm], sig=False)
                        p.op("pe", lambda e, pm_=pm_, hd=hd, ar=ar: e.matmul(pm_[:, 256:512], lhsT=CM[i][:, hd, 3, :], rhs=ar, start=True, stop=True),
                             reads=[b_CM[i]], writes=[b_pm])
                        if os.environ.get('RW_M') == '0':
                            continue
                        p.op("dve", lambda e, pm_=pm_, hd=hd: e.tensor_tensor(out=MM[i][:, hd, 0, :], in0=pm_[:, 0:256], in1=mSI[:], op=ALU.mult),
                             reads=[b_pm, b_msk], writes=[b_MM[i]])
                        p.op("dve", lambda e, pm_=pm_, hd=hd: e.tensor_tensor(out=MM[i][:, hd, 1, :], in0=pm_[:, 256:512], in1=mSI[:], op=ALU.mult),
                             reads=[b_pm, b_msk], writes=[b_MM[i]])
                    if os.environ.get('RW_M') == '1':
                        return
                    for hg in range(2):
                        pp_, b_pp_ = bank()
                        for hq in range(4):
                            hd = hg * 4 + hq
                            p.op("pe", lambda e, pp_=pp_, hd=hd, hq=hq: e.matmul(pp_[:, hq * 128:(hq + 1) * 128], lhsT=CM[i][:, hd, 0, :], rhs=CM[i][:, hd, 2, :],
                                                                                   start=True, stop=True),
                                 reads=[b_CM[i]], writes=[b_pp_], sig=(hq == 3))
                        for hq in range(4):
                            hd = hg * 4 + hq
                            p.op("dve", lambda e, pp_=pp_, hq=hq, hd=hd: e.tensor_tensor(out=PTp[0][:, hd, :], in0=pp_[:, hq * 128:(hq + 1) * 128], in1=mSL[:], op=ALU.mult),
                                 reads=[b_pp_, b_msk], writes=[b_PTp[0]])
                    if RSUB < 7:
                        return
                    for hd in range(8):
                        p.op("pool", lambda e, hd=hd: e.tensor_copy(out=Pp[0][:, hd, :], in_=MM[i][:, hd, 0, 0:128]), reads=[b_MM[i]], writes=[b_Pp[0]])
                        p.op("dve", lambda e, hd=hd: e.tensor_tensor(out=Tp[0][:, hd, :], in0=MM[i][:, hd, 0, 0:128], in1=ident[:], op=ALU.add),
                             reads=[b_MM[i], b_ident], writes=[b_Tp[0]])
                    cur = 0
                    for lvl in range(6):
                        nxt = 1 - cur
                        last = (lvl == 5)
                        for hg in range(2):
                            if not last:
                                p2, b_p2 = bank()
                            p2t, b_p2t = bank()
                            for hq in range(4):
                                hd = hg * 4 + hq
                                sl = slice(hq * 128, (hq + 1) * 128)
                                if not last:
                                    p.op("pe", lambda e, p2=p2, hd=hd, sl=sl, cur=cur: e.matmul(p2[:, sl], lhsT=PTp[cur][:, hd, :], rhs=Pp[cur][:, hd, :], start=True, stop=True),
                                         reads=[b_PTp[cur], b_Pp[cur]], writes=[b_p2], sig=(hq == 3))
                                p.op("pe", lambda e, p2t=p2t, hd=hd, sl=sl, cur=cur: e.matmul(p2t[:, sl], lhsT=Pp[cur][:, hd, :], rhs=PTp[cur][:, hd, :], start=True, stop=True),
                                     reads=[b_PTp[cur], b_Pp[cur]], writes=[b_p2t], sig=(hq == 3))
                            hs = slice(hg * 4, hg * 4 + 4)
                            if not last:
                                p.op("act", lambda e, p2=p2, hs=hs, nxt=nxt: e.copy(out=Pp[nxt][:, hs, :], in_=p2[:].rearrange("p (h c) -> p h c", c=128)),
                                     reads=[b_p2], writes=[b_Pp[nxt]])
                            p.op("dve", lambda e, p2t=p2t, hs=hs, nxt=nxt: e.tensor_copy(out=PTp[nxt][:, hs, :], in_=p2t[:].rearrange("p (h c) -> p h c", c=128)),
                                 reads=[b_p2t], writes=[b_PTp[nxt]])
                            ptu, b_ptu = bank()
                            for hq in range(4):
                                hd = hg * 4 + hq
                                sl = slice(hq * 128, (hq + 1) * 128)
                                p.op("pe", lambda e, ptu=ptu, hd=hd, sl=sl, cur=cur, nxt=nxt: e.matmul(ptu[:, sl], lhsT=PTp[nxt][:, hd, :], rhs=Tp[cur][:, hd, :], start=True, stop=True),
                                     reads=[b_PTp[nxt], b_Tp[cur]], writes=[b_ptu], sig=(hq == 3))
                            dstT = Tf[i] if last else Tp[nxt]
                            b_dstT = b_Tf[i] if last else b_Tp[nxt]
                            p.op("dve", lambda e, ptu=ptu, hs=hs, cur=cur, dstT=dstT: e.tensor_tensor(out=dstT[:, hs, :], in0=ptu[:].rearrange("p (h c) -> p h c", c=128),
                                                                                                 in1=Tp[cur][:, hs, :], op=ALU.add),
                                 reads=[b_ptu, b_Tp[cur]], writes=[b_dstT])
                        cur = nxt

                def chain(t):
                    i = t % 2
                    if RSUB < 8:
                        return
                    pw_, b_pw = bank()
                    for hd in range(8):
                        hp, h2 = hd // 2, hd % 2
                        pr = slice(h2 * 64, h2 * 64 + 64)
                        cs_ = slice(hd * 64, hd * 64 + 64)
                        p.op("pe", lambda e, hd=hd, cs_=cs_: e.matmul(pw_[:, cs_], lhsT=MM[i][:, hd, 1, 0:128], rhs=VB[i][:, cs_], start=True, stop=False),
                             reads=[b_MM[i], b_VB[i]], writes=[b_pw], sig=False)
                        p.op("pe", lambda e, hd=hd, cs_=cs_: e.matmul(pw_[:, cs_], lhsT=CM[i][:, hd, 0, :], rhs=Sb[:, hd, :], start=False, stop=True),
                             reads=[b_CM[i], b_Sb], writes=[b_pw], sig=(hd == 7))
                    p.op("act", lambda e: e.copy(out=Ws[:], in_=pw_[:]), reads=[b_pw], writes=[b_Ws])
                    pu_, b_pu = bank()
                    for hd in range(8):
                        cs_ = slice(hd * 64, hd * 64 + 64)
                        p.op("pe", lambda e, hd=hd, cs_=cs_: e.matmul(pu_[:, cs_], lhsT=Tf[i][:, hd, :], rhs=Ws[:, cs_], start=True, stop=True),
                             reads=[b_Tf[i], b_Ws], writes=[b_pu], sig=(hd == 7))
                    p.op("dve", lambda e: e.tensor_copy(out=Us[:], in_=pu_[:]), reads=[b_pu], writes=[b_Us])
                    py_, b_py = bank()
                    for hd in range(8):
                        hp, h2 = hd // 2, hd % 2
                        pr = slice(h2 * 64, h2 * 64 + 64)
                        cs_ = slice(hd * 64, hd * 64 + 64)
                        p.op("pe", lambda e, hd=hd, cs_=cs_: e.matmul(py_[:, cs_], lhsT=MM[i][:, hd, 1, 128:256], rhs=VB[i][:, cs_], start=True, stop=False),
                             reads=[b_MM[i], b_VB[i]], writes=[b_py], sig=False)
                        p.op("pe", lambda e, hd=hd, cs_=cs_: e.matmul(py_[:, cs_], lhsT=CM[i][:, hd, 1, :], rhs=Sb[:, hd, :], start=False, stop=False),
                             reads=[b_CM[i], b_Sb], writes=[b_py], sig=False)
                        p.op("pe", lambda e, hd=hd, cs_=cs_: e.matmul(py_[:, cs_], lhsT=MM[i][:, hd, 0, 128:256], rhs=Us[:, cs_], start=False, stop=True),
                             reads=[b_MM[i], b_Us], writes=[b_py], sig=(hd == 7))
                    pS_, b_pS = bank()
                    for hd in range(8):
                        sl = slice(hd * 64, (hd + 1) * 64)
                        p.op("pe", lambda e, sl=sl: e.matmul(pS_[0:64, sl], lhsT=BH[i][:, sl], rhs=Us[:, sl], start=True, stop=False),
                             reads=[b_BH[i], b_Us], writes=[b_pS], sig=False)
                        p.op("pe", lambda e, sl=sl: e.matmul(pS_[0:64, sl], lhsT=KH[i][:, sl], rhs=VB[i][:, sl], start=False, stop=True),
                             reads=[b_KH[i], b_VB[i]], writes=[b_pS], sig=(hd == 7))
                    p.op("act", lambda e: e.copy(out=yt[:], in_=py_[:]), reads=[b_py], writes=[b_yt])
                    for hd in range(8):
                        sl = slice(hd * 64, (hd + 1) * 64)
                        p.op("dve", lambda e, hd=hd, sl=sl: e.scalar_tensor_tensor(out=Sst[:, hd, :], in0=Sst[:, hd, :], scalar=gC[i][:, hd:hd + 1],
                                                                             in1=pS_[0:64, sl], op0=ALU.mult, op1=ALU.add),
                             reads=[b_S, b_gC[i], b_pS], writes=[b_S])
                    p.op("act", lambda e: e.copy(out=Sb[:], in_=Sst[:]), reads=[b_S], writes=[b_Sb])

                def post(t):
                    i = t % 2
                    if RSUB < 9:
                        return
                    pm = PM[i]
                    r_ = pm[:, 0:512]
                    v_ = pm[:, 1024:1536]
                    p.op("dve", lambda e: e.tensor_reduce(out=st8[:], in_=yt[:].rearrange("p (h c) -> p h c", c=64), axis=AX.X, op=ALU.add), reads=[b_yt], writes=[b_st8])
                    p.op("dve", lambda e: e.tensor_scalar(out=st8[:], in0=st8[:], scalar1=1.0 / 64, scalar2=None, op0=ALU.mult), reads=[b_st8], writes=[b_st8])
                    for h in range(8):
                        p.op("dve", lambda e, h=h: e.tensor_scalar(out=H(yc, h), in0=H(yt, h), scalar1=st8[:, h:h + 1], scalar2=None, op0=ALU.subtract),
                             reads=[b_yt, b_st8], writes=[b_yc])
                    p.op("dve", lambda e: e.tensor_tensor(out=tmp[:], in0=yc[:], in1=yc[:], op=ALU.mult), reads=[b_yc], writes=[b_tmp])
                    p.op("dve", lambda e: e.tensor_reduce(out=st8[:], in_=tmp[:].rearrange("p (h c) -> p h c", c=64), axis=AX.X, op=ALU.add), reads=[b_tmp], writes=[b_st8])
                    p.op("act", lambda e: e.activation(out=st8[:], in_=st8[:], func=AF.Sqrt, scale=1.0 / 64, bias=g.eps_tiles[64e-5][:]), reads=[b_st8, b_eps], writes=[b_st8])
                    p.op("dve", lambda e: e.reciprocal(out=st8[:], in_=st8[:]), reads=[b_st8], writes=[b_st8])
                    for h in range(8):
                        p.op("dve", lambda e, h=h: e.tensor_scalar(out=H(yc, h), in0=H(yc, h), scalar1=st8[:, h:h + 1], scalar2=None, op0=ALU.mult),
                             reads=[b_yc, b_st8], writes=[b_yc])
                    p.op("dve", lambda e: e.tensor_tensor(out=yc[:], in0=yc[:], in1=ln_w[:], op=ALU.mult), reads=[b_yc, b_lnw], writes=[b_yc])
                    p.op("dve", lambda e: e.tensor_tensor(out=yc[:], in0=yc[:], in1=ln_b[:], op=ALU.add), reads=[b_yc, b_lnb], writes=[b_yc])
                    p.op("dve", lambda e: e.tensor_tensor(out=tmp2[:], in0=r_, in1=k2[i][:], op=ALU.mult), reads=[b_PM[i], b_k2[i]], writes=[b_tmp2])
                    p.op("dve", lambda e: e.tensor_tensor(out=tmp2[:], in0=tmp2[:], in1=r_k[:], op=ALU.mult), reads=[b_tmp2, b_rk_], writes=[b_tmp2])
                    p.op("dve", lambda e: e.tensor_reduce(out=st8[:], in_=tmp2[:].rearrange("p (h c) -> p h c", c=64), axis=AX.X, op=ALU.add), reads=[b_tmp2], writes=[b_st8])
                    for h in range(8):
                        p.op("dve", lambda e, h=h: e.scalar_tensor_tensor(out=H(yc, h), in0=v_[:, h * 64:(h + 1) * 64], scalar=st8[:, h:h + 1], in1=H(yc, h),
                                                                         op0=ALU.mult, op1=ALU.add),
                             reads=[b_PM[i], b_st8, b_yc], writes=[b_yc])
                    p.op("dve", lambda e: e.tensor_tensor(out=yo[:], in0=yc[:], in1=gg[i][:], op=ALU.mult), reads=[b_yc, b_gg[i]], writes=[b_yo])
                    tbk, b_tbk = tbank()
                    for k in range(4):
                        p.op("pe", lambda e, k=k: e.transpose(out=tbk[:, k, :], in_=yo[:, k * 128:(k + 1) * 128], identity=ident[:]),
                             reads=[b_yo, b_ident], writes=[b_tbk], sig=(k == 3))
                    p.op("act", lambda e: e.copy(out=oTt[i][:], in_=tbk[:, 0:4, :]), reads=[b_tbk], writes=[b_oTt[i]])
                    p.dma("sync", lambda e: e.dma_start(out=oT["rwkv"][:, t * 128:(t + 1) * 128].rearrange("(k p) t -> p k t", p=128), in_=oTt[i][:]),
                          reads=[b_oTt[i]], writes=[B_oT["rwkv"][t // 4]])

                NTR = int(os.environ.get('RW_NT', NT))
                pre(0)
                for t in range(NTR):
                    if t + 1 < NTR:
                        pre(t + 1)
                    chain(t)
                    post(t)
                p.barrier()

        p.barrier()
        PH = {"proj": phase_proj, "rwkv": phase_rwkv, "mla": phase_mla, "diff": phase_diff, "merge": phase_merge, "ffn": phase_ffn}
        if phases is None:
            phases = [("consts", 0)]
            for l in range(DEPTH):
                phases += [(n, l) for n in ("proj", "rwkv", "mla", "diff", "merge", "ffn")]
            phases += [("final", 0)]
        for (n, l) in phases:
            if n == "consts":
                setup_attn_consts()
            elif n == "final":
                phase_final()
            else:
                PH[n](l)
        p.barrier()
        p.emit()
    return nc


def prep_inputs(inputs, b):
    f = np.float32
    m = {}
    m["x"] = np.ascontiguousarray(inputs["x"][b])
    pos = np.asarray(inputs["positions"][b]).astype(np.int32)
    m["pos_tm"] = np.ascontiguousarray(pos.reshape(NT, 128).T)
    m["pos_row"] = np.ascontiguousarray(pos[:256].reshape(1, 256))
    m["rel_bias"] = np.ascontiguousarray(np.asarray(inputs["rel_bias"], f).reshape(1, 128))
    for n in VEC_ROWS:
        m[n] = np.ascontiguousarray(np.asarray(inputs[n], f).reshape(DEPTH, VEC_LEN[n]))
    for n in MATS:
        m[n] = np.ascontiguousarray(np.asarray(inputs[n], f))
    m["b_gate_cm"] = np.ascontiguousarray(np.asarray(inputs["b_gate"], f).reshape(DEPTH, 24, 128).transpose(0, 2, 1))
    m["conv_w_cm"] = np.ascontiguousarray(np.asarray(inputs["ffn_conv_w"], f).reshape(DEPTH, 3, 44, 128).transpose(0, 3, 1, 2))
    m["conv_b_cm"] = np.ascontiguousarray(np.asarray(inputs["ffn_conv_b"], f).reshape(DEPTH, 44, 128).transpose(0, 2, 1))
    m["subln_cm"] = np.ascontiguousarray(np.asarray(inputs["diff_subln"], f).reshape(DEPTH, 128, 1))
    m["diff_lambda"] = np.ascontiguousarray(np.asarray(inputs["diff_lambda"], f).reshape(DEPTH, 1, 256))
    m["norm_final"] = np.ascontiguousarray(np.asarray(inputs["norm_final"], f).reshape(1, D))
    return m


_NC = {}


def kernel(**inputs):
    if "nc" not in _NC:
        _NC["nc"] = build()
    nc = _NC["nc"]
    in_maps = [prep_inputs(inputs, b) for b in range(8)]
    res = run_bass_kernel_spmd(nc, in_maps, core_ids=list(range(8)))
    return np.stack([np.asarray(r["out"], np.float32) for r in res.results], axis=0)
```

```python
import math
import os
import numpy as np
from contextlib import ExitStack
import concourse.bass as bass
import concourse.mybir as mybir
from concourse.bass_utils import run_bass_kernel_spmd

F32 = mybir.dt.float32
BF16 = mybir.dt.bfloat16
I32 = mybir.dt.int32
AF = mybir.ActivationFunctionType
ALU = mybir.AluOpType
AX = mybir.AxisListType

S = 4096
NT = 32
D = 1024
DEPTH = 2
DFF = 2816
INC = 6816
C_RW, C_ML, C_DF, C_GT = 0, 1792, 2208, 3744

EPOCH = 24000
NDSEM = 48


class Buf:
    __slots__ = ("w", "r", "name")

    def __init__(self, name=""):
        self.w = {}
        self.r = {}
        self.name = name


class EngState:
    def __init__(self, name):
        self.name = name
        self.ops = []
        self.sem = None
        self.cnt = 0
        self.wm = {}
        self.pending = False


class P:
    def __init__(self, nc, stack):
        self.nc = nc
        self.stack = stack
        self.sems = []
        self.eng = {k: EngState(k) for k in ("sync", "act", "dve", "pool", "pe")}
        self.dsem = []
        self.dval = []
        self.dnext = 0
        for i in range(NDSEM):
            self.dsem.append(self._newsem("d%d" % i))
            self.dval.append(0)
        for e in self.eng.values():
            e.sem = self._newsem(e.name + "0")
        self.nops = 0

    def _newsem(self, name):
        s = self.stack.enter_context(self.nc.semaphore(name))
        self.sems.append(s)
        return len(self.sems) - 1

    def buf(self, name=""):
        return Buf(name)

    def bufs(self, n, name=""):
        return [Buf(name + str(i)) for i in range(n)]

    def _need(self, es, waits, sem, val):
        if es.wm.get(sem, 0) >= val:
            return
        es.wm[sem] = val
        for i, (s, v) in enumerate(waits):
            if s == sem:
                waits[i] = (s, max(v, val))
                return
        waits.append((sem, val))

    def _deps(self, es, mysem, waits, reads, writes):
        for b in reads:
            for s, v in b.w.items():
                self._need(es, waits, s, v)
        skip_own = (es.name == "pe")
        for b in writes:
            for s, v in b.w.items():
                if s != mysem or not skip_own:
                    self._need(es, waits, s, v)
            for s, v in b.r.items():
                if s != mysem or not skip_own:
                    self._need(es, waits, s, v)

    def _mark(self, sem, val, reads, writes):
        for b in reads:
            if b.r.get(sem, 0) < val:
                b.r[sem] = val
        for b in writes:
            b.w = {sem: val}
            b.r = {}

    def op(self, eng, fn, reads=(), writes=(), sig=True):
        es = self.eng[eng]
        if es.cnt >= EPOCH and not es.pending:
            es.sem = self._newsem(es.name + str(len(self.sems)))
            es.cnt = 0
        waits = []
        self._deps(es, es.sem, waits, reads, writes)
        val = es.cnt + 1
        if sig:
            es.cnt = val
            es.pending = False
        else:
            es.pending = True
        es.ops.append((waits, fn, es.sem if sig else None, 1))
        self._mark(es.sem, val, reads, writes)
        self.nops += 1

    def dma(self, q, fn, reads=(), writes=()):
        es = self.eng[q]
        i = self.dnext
        self.dnext = (self.dnext + 1) % NDSEM
        sem = self.dsem[i]
        waits = []
        if self.dval[i] > 0:
            self._need(es, waits, sem, self.dval[i])
        self._deps(es, sem, waits, reads, writes)
        self.dval[i] += 16
        es.ops.append((waits, fn, sem, 16))
        self._mark(sem, self.dval[i], reads, writes)
        self.nops += 1

    def barrier(self):
        targets = []
        for e in self.eng.values():
            if e.cnt > 0:
                assert not e.pending
                targets.append((e.sem, e.cnt))
        for i in range(NDSEM):
            if self.dval[i] > 0:
                targets.append((self.dsem[i], self.dval[i]))
        for es in self.eng.values():
            waits = []
            for s, v in targets:
                if s != es.sem:
                    self._need(es, waits, s, v)
            if waits:
                es.ops.append((waits, None, None, 0))
        self.emit()

    def emit(self):
        nc = self.nc
        sems = self.sems
        engs = self.eng

        def run(e, es):
            for waits, fn, sem, inc in es.ops:
                for s, v in waits:
                    e.wait_ge(sems[s], v)
                if fn is None:
                    continue
                ins = fn(e)
                if sem is not None:
                    ins.then_inc(sems[sem], inc)

        if not any(es.ops for es in engs.values()):
            return
        with nc.Block() as block:
            @block.sync
            def _(e):
                run(e, engs["sync"])

            @block.scalar
            def _(e):
                run(e, engs["act"])

            @block.vector
            def _(e):
                run(e, engs["dve"])

            @block.gpsimd
            def _(e):
                run(e, engs["pool"])

            @block.tensor
            def _(e):
                run(e, engs["pe"])
        for es in engs.values():
            es.ops = []


VEC_ROWS = ["norm_mix", "rwkv_mu", "rwkv_w0", "rwkv_a0", "rwkv_k_k", "rwkv_k_a", "rwkv_r_k",
            "rwkv_ln_w", "rwkv_ln_b", "mla_q_norm", "mla_kv_norm", "norm_ffn", "b_gate"]
VEC_LEN = {"norm_mix": 1024, "rwkv_mu": 1792, "rwkv_w0": 512, "rwkv_a0": 512, "rwkv_k_k": 512, "rwkv_k_a": 512,
           "rwkv_r_k": 512, "rwkv_ln_w": 512, "rwkv_ln_b": 512, "mla_q_norm": 256, "mla_kv_norm": 128,
           "norm_ffn": 1024, "b_gate": 3072}
MATS = {"w_in": (1024, INC), "rwkv_w2": (64, 512), "rwkv_a2": (64, 512), "rwkv_g2": (128, 512),
        "mla_w_uq": (256, 768), "mla_w_ukv": (128, 1024), "w_branch_rwkv": (512, 1024), "w_branch_mla": (512, 1024),
        "w_branch_diff": (512, 1024), "w_o": (1024, 1024), "ffn_w_up": (1024, 2 * DFF), "ffn_w_down": (DFF, 1024)}


class K:
    pass


def build(dbg=None, feed=None, phases=None):
    dbg = dbg or set()
    feed = feed or set()
    nc = bass.Bass("TRN2", target_bir_lowering=False)
    g = K()
    g.nc = nc

    def din(name, shape, dt=F32):
        return nc.dram_tensor(name, list(shape), dt, kind="ExternalInput").ap()

    def dscr(name, shape, dt=F32):
        kind = "ExternalOutput" if name in dbg else ("ExternalInput" if name in feed else "Internal")
        return nc.dram_tensor(name, list(shape), dt, kind=kind).ap()

    x_in = din("x", [S, D])
    pos_tm = din("pos_tm", [128, NT], I32)
    pos_row = din("pos_row", [1, 256], I32)
    rel_bias = din("rel_bias", [1, 128])
    vec = {n: din(n, [DEPTH, VEC_LEN[n]]) for n in VEC_ROWS}
    mats = {n: din(n, [DEPTH, MATS[n][0], MATS[n][1]]) for n in MATS}
    b_gate_cm = din("b_gate_cm", [DEPTH, 128, 24])
    conv_w_cm = din("conv_w_cm", [DEPTH, 128, 3, 44])
    conv_b_cm = din("conv_b_cm", [DEPTH, 128, 44])
    subln_cm = din("subln_cm", [DEPTH, 128, 1])
    diff_lambda = din("diff_lambda", [DEPTH, 1, 256])
    norm_final = din("norm_final", [1, D])
    out = nc.dram_tensor("out", [S, D], F32, kind="ExternalOutput").ap()

    xres = dscr("xres", [S, D])
    prw = dscr("prw", [S, 1792])
    pml = dscr("pml", [S, 416])
    qkT = dscr("qkT", [1024, S], BF16)
    vdf = dscr("vdf", [S, 512], BF16)
    gT = dscr("gT", [3072, S], BF16)
    oT = {n: dscr("oT_" + n, [512, S], BF16) for n in ("rwkv", "mla", "diff")}

    with ExitStack() as top:
        p = P(nc, top)
        g.p = p

        uid = [0]

        def sb(st, name, shape, dt):
            uid[0] += 1
            return st.enter_context(nc.sbuf_tensor("%s_u%d" % (name, uid[0]), list(shape), dt))

        def ps(st, name, shape, dt=F32):
            uid[0] += 1
            return st.enter_context(nc.psum_tensor("%s_u%d" % (name, uid[0]), list(shape), dt))

        B_x = p.bufs(NT, "x")
        B_prw = p.bufs(NT, "prw")
        B_pml = p.bufs(NT, "pml")
        B_qkT = p.bufs(8, "qkT")
        B_vdf = p.bufs(NT, "vdf")
        B_gT = p.bufs(8, "gT")
        B_oT = {n: p.bufs(8, "oT" + n) for n in oT}
        B_out = p.buf("out")

        ident = sb(top, "ident", [128, 128], BF16)
        b_ident = p.buf()
        p.op("pool", lambda e: e.memset(ident[:], 1.0), writes=[b_ident])
        p.op("pool", lambda e: e.affine_select(out=ident[:], in_=ident[:], pattern=[[-1, 128]],
                                               compare_op=ALU.is_equal, fill=0.0, base=0, channel_multiplier=1),
             reads=[b_ident], writes=[b_ident])
        ones_bf = sb(top, "ones_bf", [128, 128], BF16)
        ones_f = sb(top, "ones_f", [128, 128], F32)
        b_ones = p.buf()
        p.op("pool", lambda e: e.memset(ones_bf[:], 1.0), writes=[b_ones])
        p.op("pool", lambda e: e.memset(ones_f[:], 1.0), writes=[b_ones])
        g.ident, g.b_ident, g.ones_bf, g.ones_f, g.b_ones = ident, b_ident, ones_bf, ones_f, b_ones

        def norm_transpose_phase(st, src_ap_fn, src_bufs, gvec_ap, hT, b_hT, eps=1e-6):
            gt = sb(st, "nt_g", [128, D], F32)
            b_g = p.buf()
            p.dma("sync", lambda e: e.dma_start(out=gt[:], in_=gvec_ap.partition_broadcast(128)), writes=[b_g])
            xt = [sb(st, "nt_x%d" % i, [128, D], F32) for i in range(2)]
            b_xt = p.bufs(2)
            junk = sb(st, "nt_junk", [128, D], BF16)
            b_junk = p.buf()
            ss = [sb(st, "nt_ss%d" % i, [128, 1], F32) for i in range(2)]
            b_ss = p.bufs(2)
            hb = [sb(st, "nt_hb%d" % i, [128, D], BF16) for i in range(2)]
            b_hb = p.bufs(2)
            pt = [ps(st, "nt_pt%d" % i, [128, 8, 128], BF16) for i in range(2)]
            b_pt = p.bufs(2)
            for t in range(NT):
                i = t % 2
                p.dma("sync", lambda e, t=t, i=i: e.dma_start(out=xt[i][:], in_=src_ap_fn(t)),
                      reads=[src_bufs[t]], writes=[b_xt[i]])
                p.op("pool", lambda e, i=i: e.memset(ss[i][:], 0.0), writes=[b_ss[i]])
                p.op("act", lambda e, i=i: e.activation(out=junk[:], in_=xt[i][:], func=AF.Square, accum_out=ss[i][:]),
                     reads=[b_xt[i], b_ss[i]], writes=[b_junk, b_ss[i]])
                p.op("act", lambda e, i=i: e.activation(out=ss[i][:], in_=ss[i][:], func=AF.Sqrt, scale=1.0 / D, bias=g.eps_tiles[eps][:]),
                     reads=[b_ss[i]], writes=[b_ss[i]])
                p.op("dve", lambda e, i=i: e.reciprocal(out=ss[i][:], in_=ss[i][:]), reads=[b_ss[i]], writes=[b_ss[i]])
                p.op("dve", lambda e, i=i: e.scalar_tensor_tensor(out=hb[i][:], in0=xt[i][:], scalar=ss[i][:, 0:1], in1=gt[:],
                                                                  op0=ALU.mult, op1=ALU.mult),
                     reads=[b_xt[i], b_ss[i], b_g], writes=[b_hb[i]])
                for k in range(8):
                    p.op("pe", lambda e, i=i, k=k: e.transpose(out=pt[i][:, k, :], in_=hb[i][:, k * 128:(k + 1) * 128], identity=ident[:]),
                         reads=[b_hb[i], b_ident], writes=[b_pt[i]], sig=(k == 7))
                p.op("act", lambda e, i=i, t=t: e.copy(out=hT[:, :, t * 128:(t + 1) * 128], in_=pt[i][:]),
                     reads=[b_pt[i]], writes=[b_hT[t]])

        g.eps_tiles = {}
        b_eps = p.buf()
        for ev in (1e-6, 1e-5, 64e-5, 0.0, 1.0):
            tl = sb(top, "eps%d" % len(g.eps_tiles), [128, 1], F32)
            p.op("pool", lambda e, tl=tl, ev=ev: e.memset(tl[:], ev), writes=[b_eps])
            g.eps_tiles[ev] = tl

        g.wstg = [sb(top, "wstg%d" % i, [128, 8, 512], F32) for i in range(2)]
        g.b_wstg = p.bufs(2)
        g.nstg = [0]

        class WLoader:
            def __init__(self, st, name, kch, maxcol, nbuf=2):
                self.wb = [sb(st, name + "_b%d" % i, [128, kch, maxcol], BF16) for i in range(nbuf)]
                self.b_wb = p.bufs(nbuf)
                self.n = 0
                self.nbuf = nbuf
                self.kch = kch

            def load(self, wap, c0, ncol, krows=None):
                i = self.n % self.nbuf
                self.n += 1
                kch = self.kch
                si = g.nstg[0] % 2
                g.nstg[0] += 1
                stg, wb = g.wstg[si], self.wb[i]
                p.dma("sync", lambda e: e.dma_start(out=stg[:, 0:kch, 0:ncol],
                                                    in_=wap[:, c0:c0 + ncol].rearrange("(k p) n -> p k n", p=128)),
                      writes=[g.b_wstg[si]])
                p.op("pool", lambda e: e.tensor_copy(out=wb[:, :, 0:ncol], in_=stg[:, 0:kch, 0:ncol]),
                     reads=[g.b_wstg[si]], writes=[self.b_wb[i]])
                return wb, self.b_wb[i]

        def phase_proj(l):
            with ExitStack() as st:
                hT = sb(st, "hT", [128, 8, S], BF16)
                b_hT = p.bufs(NT)
                with ExitStack() as st2:
                    if l == 0:
                        norm_transpose_phase(st2, lambda t: x_in[t * 128:(t + 1) * 128, :], B_x, vec["norm_mix"][l:l + 1, :], hT, b_hT)
                    else:
                        norm_transpose_phase(st2, lambda t: xres[t * 128:(t + 1) * 128, :], B_x, vec["norm_mix"][l:l + 1, :], hT, b_hT)
                    p.barrier()
                wl = WLoader(st, "wl", 8, 512)
                pp = [ps(st, "pp%d" % i, [128, 512], F32) for i in range(4)]
                b_pp = p.bufs(4)
                ostf = [sb(st, "ostf%d" % i, [128, 512], F32) for i in range(4)]
                ostb = [sb(st, "ostb%d" % i, [128, 512], BF16) for i in range(4)]
                b_ost = p.bufs(4)
                bg = sb(st, "bgcm", [128, 24], F32)
                b_bg = p.buf()
                p.dma("sync", lambda e: e.dma_start(out=bg[:], in_=b_gate_cm[l]), writes=[b_bg])
                cnt = [0]
                win = mats["w_in"][l]

                def tok_major(c0, ncol, dst_fn, dst_bufs, bf):
                    wb, b_wb = wl.load(win, c0, ncol)
                    for t in range(NT):
                        i = cnt[0] % 4
                        cnt[0] += 1
                        for k in range(8):
                            p.op("pe", lambda e, i=i, k=k, t=t: e.matmul(pp[i][:, 0:ncol], lhsT=hT[:, k, t * 128:(t + 1) * 128],
                                                                         rhs=wb[:, k, 0:ncol], start=(k == 0), stop=(k == 7)),
                                 reads=[b_hT[t], b_wb], writes=[b_pp[i]], sig=(k == 7))
                        o = ostb[i] if bf else ostf[i]
                        eng = "act" if (cnt[0] % 2) else "dve"
                        if eng == "act":
                            p.op("act", lambda e, i=i, o=o: e.copy(out=o[:, 0:ncol], in_=pp[i][:, 0:ncol]), reads=[b_pp[i]], writes=[b_ost[i]])
                        else:
                            p.op("dve", lambda e, i=i, o=o: e.tensor_copy(out=o[:, 0:ncol], in_=pp[i][:, 0:ncol]), reads=[b_pp[i]], writes=[b_ost[i]])
                        p.dma("sync", lambda e, o=o, t=t: e.dma_start(out=dst_fn(t), in_=o[:, 0:ncol]), reads=[b_ost[i]], writes=[dst_bufs[t]])

                def ch_major(c0, nchunk, dst, dst_row0, dst_bufs, gate_idx0=None):
                    for cc0 in range(0, nchunk, 4):
                        ncc = min(4, nchunk - cc0)
                        wb, b_wb = wl.load(win, c0 + cc0 * 128, ncc * 128)
                        for cc in range(ncc):
                            for tq in range(8):
                                i = cnt[0] % 4
                                cnt[0] += 1
                                for k in range(8):
                                    p.op("pe", lambda e, i=i, k=k, tq=tq, cc=cc, wb=wb: e.matmul(pp[i][:, :], lhsT=wb[:, k, cc * 128:(cc + 1) * 128],
                                                                                          rhs=hT[:, k, tq * 512:(tq + 1) * 512], start=(k == 0), stop=(k == 7)),
                                         reads=[b_hT[tq * 4 + j] for j in range(4)] + [b_wb], writes=[b_pp[i]], sig=(k == 7))
                                o = ostb[i]
                                if gate_idx0 is not None:
                                    gi = gate_idx0 + cc0 + cc
                                    p.op("act", lambda e, i=i, o=o, gi=gi: e.activation(out=o[:], in_=pp[i][:], func=AF.Sigmoid, bias=bg[:, gi:gi + 1]),
                                         reads=[b_pp[i], b_bg], writes=[b_ost[i]])
                                else:
                                    p.op("dve", lambda e, i=i, o=o: e.tensor_copy(out=o[:], in_=pp[i][:]), reads=[b_pp[i]], writes=[b_ost[i]])
                                r0 = dst_row0 + (cc0 + cc) * 128
                                p.dma("sync", lambda e, o=o, r0=r0, tq=tq: e.dma_start(out=dst[r0:r0 + 128, tq * 512:(tq + 1) * 512], in_=o[:]),
                                      reads=[b_ost[i]], writes=[dst_bufs[tq]])

                for c0, ncol in ((0, 512), (512, 512), (1024, 512), (1536, 256)):
                    tok_major(C_RW + c0, ncol, lambda t, c0=c0, ncol=ncol: prw[t * 128:(t + 1) * 128, c0:c0 + ncol], B_prw, False)
                tok_major(C_ML, 416, lambda t: pml[t * 128:(t + 1) * 128, :], B_pml, False)
                ch_major(C_DF, 8, qkT, 0, B_qkT)
                tok_major(C_DF + 1024, 512, lambda t: vdf[t * 128:(t + 1) * 128, :], B_vdf, True)
                ch_major(C_GT, 24, gT, 0, B_gT, gate_idx0=0)
                p.barrier()

        def load_bcast(st, name, ap_row, n):
            t = sb(st, name, [128, n], F32)
            b = p.buf()
            p.dma("sync", lambda e: e.dma_start(out=t[:], in_=ap_row.partition_broadcast(128)), writes=[b])
            return t, b

        def load_weight_bf(st, name, wap, krows, ncols, part0=0):
            kch = max(1, krows // 128)
            pr = min(128, krows)
            wbt = sb(st, name, [128, kch, ncols], BF16)
            b_w = p.buf()
            stg = g.wstg
            b_stg = g.b_wstg
            cs = 512 if kch <= 8 else 128
            for c0 in range(0, ncols, cs):
                nc_ = min(cs, ncols - c0)
                i = g.nstg[0] % 2
                g.nstg[0] += 1
                if krows >= 128 and kch <= 8:
                    src = wap[:, c0:c0 + nc_].rearrange("(k p) n -> p k n", p=128)
                    p.dma("sync", lambda e, i=i, src=src, nc_=nc_: e.dma_start(out=stg[i][:, 0:kch, 0:nc_], in_=src), writes=[b_stg[i]])
                    p.op("pool", lambda e, i=i, c0=c0, nc_=nc_: e.tensor_copy(out=wbt[:, :, c0:c0 + nc_], in_=stg[i][:, 0:kch, 0:nc_]),
                         reads=[b_stg[i]], writes=[b_w])
                elif krows >= 128:
                    sv = stg[i][:].rearrange("p a b -> p (a b)")[:, 0:kch * 128].rearrange("p (k n) -> p k n", n=128)
                    src = wap[:, c0:c0 + nc_].rearrange("(k p) n -> p k n", p=128)
                    p.dma("sync", lambda e, sv=sv, src=src, nc_=nc_: e.dma_start(out=sv[:, :, 0:nc_], in_=src), writes=[b_stg[i]])
                    p.op("pool", lambda e, sv=sv, c0=c0, nc_=nc_: e.tensor_copy(out=wbt[:, :, c0:c0 + nc_], in_=sv[:, :, 0:nc_]),
                         reads=[b_stg[i]], writes=[b_w])
                else:
                    src = wap[:, c0:c0 + nc_]
                    p.dma("sync", lambda e, i=i, src=src, nc_=nc_: e.dma_start(out=stg[i][part0:part0 + pr, 0, 0:nc_], in_=src), writes=[b_stg[i]])
                    p.op("pool", lambda e, i=i, c0=c0, nc_=nc_: e.tensor_copy(out=wbt[part0:part0 + pr, 0, c0:c0 + nc_], in_=stg[i][part0:part0 + pr, 0, 0:nc_]),
                         reads=[b_stg[i]], writes=[b_w])
            return wbt, b_w

        def xsrc(l, t):
            return (x_in if l == 0 else xres)[t * 128:(t + 1) * 128, :]

        def phase_merge(l):
            with ExitStack() as st:
                wbr = []
                for n in ("rwkv", "mla", "diff"):
                    wbr.append(load_weight_bf(st, "wbr_" + n, mats["w_branch_" + n][l], 512, 1024))
                wo, b_wo = load_weight_bf(st, "wo", mats["w_o"][l], 1024, 1024)
                oc = [[sb(st, "oc%d_%d" % (i, j), [128, 4, 512], BF16) for j in range(3)] for i in range(2)]
                b_oc = [p.bufs(3) for i in range(2)]
                gc = [sb(st, "gc%d" % i, [128, 24, 512], BF16) for i in range(2)]
                b_gc = p.bufs(2)
                mT = [sb(st, "mT%d" % i, [128, 8, 512], BF16) for i in range(2)]
                b_mT = p.bufs(2)
                pb = [ps(st, "mpb%d" % i, [128, 512], F32) for i in range(6)]
                b_pb = p.bufs(6)
                m0 = [sb(st, "m0_%d" % i, [128, 512], F32) for i in range(2)]
                m1 = [sb(st, "m1_%d" % i, [128, 512], F32) for i in range(2)]
                b_m0 = p.bufs(2)
                b_m1 = p.bufs(2)
                xt = [sb(st, "mxt%d" % i, [128, D], F32) for i in range(2)]
                b_xt = p.bufs(2)
                po = [ps(st, "mpo%d" % i, [128, 512], F32) for i in range(2)]
                b_po = p.bufs(2)
                names = ("rwkv", "mla", "diff")
                nd = 0
                nx = 0
                npo = 0
                for c in range(8):
                    ci = c % 2
                    for j, n in enumerate(names):
                        p.dma("sync", lambda e, ci=ci, j=j, n=n, c=c: e.dma_start(
                            out=oc[ci][j][:], in_=oT[n][:, c * 512:(c + 1) * 512].rearrange("(k p) t -> p k t", p=128)),
                            reads=[B_oT[n][c]], writes=[b_oc[ci][j]])
                    p.dma("sync", lambda e, ci=ci, c=c: e.dma_start(
                        out=gc[ci][:], in_=gT[:, c * 512:(c + 1) * 512].rearrange("(k p) t -> p k t", p=128)),
                        reads=[B_gT[c]], writes=[b_gc[ci]])
                    for dc in range(8):
                        di = nd % 2
                        nd += 1
                        for j in range(3):
                            pj = di * 3 + j
                            w_j, b_wj = wbr[j]
                            for k in range(4):
                                p.op("pe", lambda e, pj=pj, k=k, dc=dc, ci=ci, j=j, w_j=w_j: e.matmul(
                                    pb[pj][:], lhsT=w_j[:, k, dc * 128:(dc + 1) * 128], rhs=oc[ci][j][:, k, :], start=(k == 0), stop=(k == 3)),
                                    reads=[b_wj, b_oc[ci][j]], writes=[b_pb[pj]], sig=(k == 3))
                        p.op("dve", lambda e, di=di, ci=ci, dc=dc: e.tensor_tensor(out=m0[di][:], in0=pb[di * 3][:], in1=gc[ci][:, dc, :], op=ALU.mult),
                             reads=[b_pb[di * 3], b_gc[ci]], writes=[b_m0[di]])
                        p.op("dve", lambda e, di=di, ci=ci, dc=dc: e.tensor_tensor(out=m1[di][:], in0=pb[di * 3 + 1][:], in1=gc[ci][:, 8 + dc, :], op=ALU.mult),
                             reads=[b_pb[di * 3 + 1], b_gc[ci]], writes=[b_m1[di]])
                        p.op("dve", lambda e, di=di: e.tensor_tensor(out=m0[di][:], in0=m0[di][:], in1=m1[di][:], op=ALU.add),
                             reads=[b_m0[di], b_m1[di]], writes=[b_m0[di]])
                        p.op("dve", lambda e, di=di, ci=ci, dc=dc: e.tensor_tensor(out=m1[di][:], in0=pb[di * 3 + 2][:], in1=gc[ci][:, 16 + dc, :], op=ALU.mult),
                             reads=[b_pb[di * 3 + 2], b_gc[ci]], writes=[b_m1[di]])
                        p.op("dve", lambda e, di=di, ci=ci, dc=dc: e.tensor_tensor(out=mT[ci][:, dc, :], in0=m0[di][:], in1=m1[di][:], op=ALU.add),
                             reads=[b_m0[di], b_m1[di]], writes=[b_mT[ci]])
                    for tt in range(4):
                        t = c * 4 + tt
                        xi = nx % 2
                        nx += 1
                        p.dma("sync", lambda e, xi=xi, t=t: e.dma_start(out=xt[xi][:], in_=xsrc(l, t)), reads=[B_x[t]], writes=[b_xt[xi]])
                        for hf in range(2):
                            pi = npo % 2
                            npo += 1
                            for k in range(8):
                                p.op("pe", lambda e, pi=pi, k=k, ci=ci, tt=tt, hf=hf: e.matmul(
                                    po[pi][:], lhsT=mT[ci][:, k, tt * 128:(tt + 1) * 128], rhs=wo[:, k, hf * 512:(hf + 1) * 512], start=(k == 0), stop=(k == 7)),
                                    reads=[b_mT[ci], b_wo], writes=[b_po[pi]], sig=(k == 7))
                            p.op("dve", lambda e, pi=pi, xi=xi, hf=hf: e.tensor_tensor(out=xt[xi][:, hf * 512:(hf + 1) * 512], in0=po[pi][:],
                                                                                 in1=xt[xi][:, hf * 512:(hf + 1) * 512], op=ALU.add),
                                 reads=[b_po[pi], b_xt[xi]], writes=[b_xt[xi]])
                        p.dma("sync", lambda e, xi=xi, t=t: e.dma_start(out=xres[t * 128:(t + 1) * 128, :], in_=xt[xi][:]),
                              reads=[b_xt[xi]], writes=[B_x[t]])
                p.barrier()

        def phase_ffn(l):
            G = 512
            NG = S // G
            TG = G // 128
            with ExitStack() as st:
                wdn, b_wdn = load_weight_bf(st, "wdn", mats["ffn_w_down"][l], DFF, 1024)
                gt, b_g = load_bcast(st, "f_g", vec["norm_ffn"][l:l + 1, :], D)
                cw = sb(st, "f_cw", [128, 3, 44], F32)
                cb = sb(st, "f_cb", [128, 44], F32)
                b_cw = p.buf()
                p.dma("sync", lambda e: e.dma_start(out=cw[:], in_=conv_w_cm[l]), writes=[b_cw])
                p.dma("sync", lambda e: e.dma_start(out=cb[:], in_=conv_b_cm[l]), writes=[b_cw])
                halo = sb(st, "f_halo", [128, 44, 2], F32)
                b_halo = p.bufs(44)
                p.op("pool", lambda e: e.memset(halo[:], 0.0), writes=b_halo)
                xg = sb(st, "f_xg", [128, TG, D], F32)
                b_xg = p.bufs(TG)
                hTg = sb(st, "f_hT", [128, 8, G], BF16)
                b_hTg = p.bufs(TG)
                junk = sb(st, "f_junk", [128, D], BF16)
                b_junk = p.buf()
                ss = [sb(st, "f_ss%d" % i, [128, 1], F32) for i in range(2)]
                b_ss = p.bufs(2)
                hb = [sb(st, "f_hb%d" % i, [128, D], BF16) for i in range(2)]
                b_hb = p.bufs(2)
                pt = [ps(st, "f_pt%d" % i, [128, 8, 128], BF16) for i in range(2)]
                b_pt = p.bufs(2)
                wubg = [sb(st, "f_wubg%d" % i, [128, 8, 512], BF16) for i in range(2)]
                wubv = [sb(st, "f_wubv%d" % i, [128, 8, 512], BF16) for i in range(2)]
                b_wubg = p.bufs(2)
                b_wubv = p.bufs(2)
                pu = [ps(st, "f_pu%d" % i, [128, 512], F32) for i in range(4)]
                b_pu = p.bufs(4)
                ug = [sb(st, "f_ug%d" % i, [128, G + 2], F32) for i in range(2)]
                uv = [sb(st, "f_uv%d" % i, [128, G + 2], F32) for i in range(2)]
                b_ug = p.bufs(2)
                b_uv = p.bufs(2)
                cg = [sb(st, "f_cg%d" % i, [128, G], F32) for i in range(2)]
                cv = [sb(st, "f_cv%d" % i, [128, G], F32) for i in range(2)]
                b_cg = p.bufs(2)
                b_cv = p.bufs(2)
                actT = sb(st, "f_actT", [128, 22, G], BF16)
                b_act = p.bufs(22)
                pd = [ps(st, "f_pd%d" % i, [128, 512], F32) for i in range(2)]
                b_pd = p.bufs(2)
                wup = mats["ffn_w_up"][l]
                nw = 0
                npu = 0
                npd = 0
                NF_ = int(os.environ.get('FFN_NF', 22))
                for gi in range(int(os.environ.get('FFN_NG', NG))):
                    for tt in range(TG):
                        t = gi * TG + tt
                        i = tt % 2
                        p.dma("sync", lambda e, tt=tt, t=t: e.dma_start(out=xg[:, tt, :], in_=xres[t * 128:(t + 1) * 128, :]),
                              reads=[B_x[t]], writes=[b_xg[tt]])
                        p.op("pool", lambda e, i=i: e.memset(ss[i][:], 0.0), writes=[b_ss[i]])
                        p.op("act", lambda e, i=i, tt=tt: e.activation(out=junk[:], in_=xg[:, tt, :], func=AF.Square, accum_out=ss[i][:]),
                             reads=[b_xg[tt], b_ss[i]], writes=[b_junk, b_ss[i]])
                        p.op("act", lambda e, i=i: e.activation(out=ss[i][:], in_=ss[i][:], func=AF.Sqrt, scale=1.0 / D, bias=g.eps_tiles[1e-6][:]),
                             reads=[b_ss[i], b_eps], writes=[b_ss[i]])
                        p.op("dve", lambda e, i=i: e.reciprocal(out=ss[i][:], in_=ss[i][:]), reads=[b_ss[i]], writes=[b_ss[i]])
                        p.op("dve", lambda e, i=i, tt=tt: e.scalar_tensor_tensor(out=hb[i][:], in0=xg[:, tt, :], scalar=ss[i][:, 0:1], in1=gt[:],
                                                                             op0=ALU.mult, op1=ALU.mult),
                             reads=[b_xg[tt], b_ss[i], b_g], writes=[b_hb[i]])
                        for k in range(8):
                            p.op("pe", lambda e, i=i, k=k: e.transpose(out=pt[i][:, k, :], in_=hb[i][:, k * 128:(k + 1) * 128], identity=ident[:]),
                                 reads=[b_hb[i], b_ident], writes=[b_pt[i]], sig=(k == 7))
                        p.op("act", lambda e, i=i, tt=tt: e.copy(out=hTg[:, :, tt * 128:(tt + 1) * 128], in_=pt[i][:]),
                             reads=[b_pt[i]], writes=[b_hTg[tt]])
                    for f in range(NF_):
                        fb = (f // 4) * 4
                        if f == fb:
                            nfb = min(4, 22 - fb)
                            wi = nw % 2
                            nw += 1
                            for (dstw, b_dstw, cbase) in ((wubg[wi], b_wubg[wi], fb * 128), (wubv[wi], b_wubv[wi], DFF + fb * 128)):
                                si = g.nstg[0] % 2
                                g.nstg[0] += 1
                                p.dma("sync", lambda e, si=si, cbase=cbase, nfb=nfb: e.dma_start(out=g.wstg[si][:, :, 0:nfb * 128],
                                                                                          in_=wup[:, cbase:cbase + nfb * 128].rearrange("(k p) n -> p k n", p=128)),
                                      writes=[g.b_wstg[si]])
                                p.op("pool", lambda e, si=si, dstw=dstw, nfb=nfb: e.tensor_copy(out=dstw[:, :, 0:nfb * 128], in_=g.wstg[si][:, :, 0:nfb * 128]),
                                     reads=[g.b_wstg[si]], writes=[b_dstw])
                        fo = (f - fb) * 128
                        ui = f % 2
                        p.op("pool", lambda e, ui=ui, f=f: e.tensor_copy(out=ug[ui][:, 0:2], in_=halo[:, f, :]), reads=[b_halo[f]], writes=[b_ug[ui]])
                        p.op("pool", lambda e, ui=ui, f=f: e.tensor_copy(out=uv[ui][:, 0:2], in_=halo[:, 22 + f, :]), reads=[b_halo[22 + f]], writes=[b_uv[ui]])
                        for gv in range(2):
                            for hf in range(G // 512):
                                pi = npu % 4
                                npu += 1
                                for k in range(8):
                                    wsrc = wubg[wi] if gv == 0 else wubv[wi]
                                    b_wsrc = b_wubg[wi] if gv == 0 else b_wubv[wi]
                                    p.op("pe", lambda e, pi=pi, k=k, wsrc=wsrc, fo=fo, hf=hf: e.matmul(
                                        pu[pi][:], lhsT=wsrc[:, k, fo:fo + 128], rhs=hTg[:, k, hf * 512:(hf + 1) * 512],
                                        start=(k == 0), stop=(k == 7)),
                                        reads=[b_wsrc] + [b_hTg[hf * 4 + j] for j in range(4)], writes=[b_pu[pi]], sig=(k == 7))
                                dst = ug[ui] if gv == 0 else uv[ui]
                                b_dst = b_ug[ui] if gv == 0 else b_uv[ui]
                                cdst0 = cg[ui] if gv == 0 else cv[ui]
                                b_c0 = b_cg[ui] if gv == 0 else b_cv[ui]
                                ch0 = f if gv == 0 else 22 + f
                                p.op("act", lambda e, pi=pi, dst=dst, hf=hf: e.copy(out=dst[:, 2 + hf * 512:2 + (hf + 1) * 512], in_=pu[pi][:]),
                                     reads=[b_pu[pi]], writes=[b_dst])
                                p.op("act", lambda e, pi=pi, cdst0=cdst0, hf=hf, ch0=ch0: e.activation(out=cdst0[:, hf * 512:(hf + 1) * 512], in_=pu[pi][:], func=AF.Identity,
                                                                                                  scale=cw[:, 2, ch0:ch0 + 1], bias=cb[:, ch0:ch0 + 1]),
                                     reads=[b_pu[pi], b_cw], writes=[b_c0])
                        p.op("pool", lambda e, ui=ui, f=f: e.tensor_copy(out=halo[:, f, :], in_=ug[ui][:, G:G + 2]), reads=[b_ug[ui]], writes=[b_halo[f]])
                        p.op("pool", lambda e, ui=ui, f=f: e.tensor_copy(out=halo[:, 22 + f, :], in_=uv[ui][:, G:G + 2]), reads=[b_uv[ui]], writes=[b_halo[22 + f]])
                        for eng, u, b_u, cdst, b_c, ch in (("dve", ug[ui], b_ug[ui], cg[ui], b_cg[ui], f), ("dve", uv[ui], b_uv[ui], cv[ui], b_cv[ui], 22 + f)):
                            p.op(eng, lambda e, u=u, cdst=cdst, ch=ch: e.scalar_tensor_tensor(out=cdst[:], in0=u[:, 1:G + 1], scalar=cw[:, 1, ch:ch + 1],
                                                                                         in1=cdst[:], op0=ALU.mult, op1=ALU.add),
                                 reads=[b_u, b_cw, b_c], writes=[b_c])
                            p.op(eng, lambda e, u=u, cdst=cdst, ch=ch: e.scalar_tensor_tensor(out=cdst[:], in0=u[:, 0:G], scalar=cw[:, 0, ch:ch + 1],
                                                                                         in1=cdst[:], op0=ALU.mult, op1=ALU.add),
                                 reads=[b_u, b_cw, b_c], writes=[b_c])
                        p.op("act", lambda e, ui=ui: e.activation(out=ug[ui][:, 2:G + 2], in_=cg[ui][:], func=AF.Silu),
                             reads=[b_cg[ui]], writes=[b_ug[ui]])
                        p.op("dve", lambda e, ui=ui, f=f: e.tensor_tensor(out=actT[:, f, :], in0=ug[ui][:, 2:G + 2], in1=cv[ui][:], op=ALU.mult),
                             reads=[b_ug[ui], b_cv[ui]], writes=[b_act[f]])
                    for tt in range(TG):
                        t = gi * TG + tt
                        for hf in range(2):
                            pi = npd % 2
                            npd += 1
                            for f in range(NF_):
                                p.op("pe", lambda e, pi=pi, f=f, tt=tt, hf=hf: e.matmul(
                                    pd[pi][:], lhsT=actT[:, f, tt * 128:(tt + 1) * 128], rhs=wdn[:, f, hf * 512:(hf + 1) * 512],
                                    start=(f == 0), stop=(f == NF_ - 1)),
                                    reads=[b_act[f], b_wdn], writes=[b_pd[pi]], sig=(f == NF_ - 1))
                            p.op("dve", lambda e, pi=pi, tt=tt, hf=hf: e.tensor_tensor(out=xg[:, tt, hf * 512:(hf + 1) * 512], in0=pd[pi][:],
                                                                                 in1=xg[:, tt, hf * 512:(hf + 1) * 512], op=ALU.add),
                                 reads=[b_pd[pi], b_xg[tt]], writes=[b_xg[tt]])
                        p.dma("sync", lambda e, tt=tt, t=t: e.dma_start(out=xres[t * 128:(t + 1) * 128, :], in_=xg[:, tt, :]),
                              reads=[b_xg[tt]], writes=[B_x[t]])
                p.barrier()

        def phase_final():
            with ExitStack() as st:
                gt, b_g = load_bcast(st, "fin_g", norm_final, D)
                xt = [sb(st, "fin_x%d" % i, [128, D], F32) for i in range(3)]
                b_xt = p.bufs(3)
                junk = sb(st, "fin_junk", [128, D], BF16)
                b_junk = p.buf()
                ss = [sb(st, "fin_ss%d" % i, [128, 1], F32) for i in range(3)]
                b_ss = p.bufs(3)
                for t in range(NT):
                    i = t % 3
                    p.dma("sync", lambda e, i=i, t=t: e.dma_start(out=xt[i][:], in_=xres[t * 128:(t + 1) * 128, :]), reads=[B_x[t]], writes=[b_xt[i]])
                    p.op("pool", lambda e, i=i: e.memset(ss[i][:], 0.0), writes=[b_ss[i]])
                    p.op("act", lambda e, i=i: e.activation(out=junk[:], in_=xt[i][:], func=AF.Square, accum_out=ss[i][:]),
                         reads=[b_xt[i], b_ss[i]], writes=[b_junk, b_ss[i]])
                    p.op("act", lambda e, i=i: e.activation(out=ss[i][:], in_=ss[i][:], func=AF.Sqrt, scale=1.0 / D, bias=g.eps_tiles[1e-6][:]),
                         reads=[b_ss[i], b_eps], writes=[b_ss[i]])
                    p.op("dve", lambda e, i=i: e.reciprocal(out=ss[i][:], in_=ss[i][:]), reads=[b_ss[i]], writes=[b_ss[i]])
                    p.op("dve", lambda e, i=i: e.scalar_tensor_tensor(out=xt[i][:], in0=xt[i][:], scalar=ss[i][:, 0:1], in1=gt[:], op0=ALU.mult, op1=ALU.mult),
                         reads=[b_xt[i], b_ss[i], b_g], writes=[b_xt[i]])
                    p.dma("sync", lambda e, i=i, t=t: e.dma_start(out=out[t * 128:(t + 1) * 128, :], in_=xt[i][:]), reads=[b_xt[i]], writes=[B_out])
                p.barrier()

        cs_t = sb(top, "cs_t", [128, NT, 16], F32)
        sn_t = sb(top, "sn_t", [128, NT, 16], F32)
        b_rope = p.buf()
        maskD = sb(top, "maskD", [128, 128], F32)
        b_maskD = p.buf()
        Dn = [sb(top, "Dn%d" % h, [128, 256], F32) for h in range(4)]
        b_Dn = p.buf()
        relb = sb(top, "relb", [128, 128], F32)
        b_relb = p.buf()
        negpi = sb(top, "negpi", [128, 1], F32)

        def setup_attn_consts():
            with ExitStack() as st:
                p.op("pool", lambda e: e.memset(negpi[:], -math.pi), writes=[b_rope])
                posi = sb(st, "posi", [128, NT], I32)
                posf = sb(st, "posf", [128, NT], F32)
                b_pos = p.buf()
                p.dma("sync", lambda e: e.dma_start(out=posi[:], in_=pos_tm), writes=[b_pos])
                p.op("dve", lambda e: e.tensor_copy(out=posf[:], in_=posi[:]), reads=[b_pos], writes=[b_pos])
                invf = sb(st, "invf", [128, 16], F32)
                b_invf = p.buf()
                for i in range(16):
                    v = float(np.float32(10000.0) ** np.float32(-(2.0 * i) / 32.0))
                    p.op("pool", lambda e, i=i, v=v: e.memset(invf[:, i:i + 1], v), writes=[b_invf])
                ang = sb(st, "ang", [128, NT, 16], F32)
                ang2 = sb(st, "ang2", [128, NT, 16], F32)
                b_ang = p.buf()
                for t in range(NT):
                    p.op("dve", lambda e, t=t: e.tensor_scalar(out=ang[:, t, :], in0=invf[:], scalar1=posf[:, t:t + 1], scalar2=None, op0=ALU.mult),
                         reads=[b_invf, b_pos], writes=[b_ang])
                ni = sb(st, "ang_ni", [128, NT, 16], I32)
                nf = sb(st, "ang_nf", [128, NT, 16], F32)
                mk = sb(st, "ang_mk", [128, NT, 16], F32)
                b_red = p.buf()

                def reduce_sin(src, add, dst):
                    p.op("dve", lambda e: e.tensor_scalar(out=ang2[:], in0=src[:], scalar1=1.0 / (2 * math.pi), scalar2=add, op0=ALU.mult, op1=ALU.add),
                         reads=[b_ang], writes=[b_red])
                    p.op("dve", lambda e: e.tensor_copy(out=ni[:], in_=ang2[:]), reads=[b_red], writes=[b_red])
                    p.op("dve", lambda e: e.tensor_copy(out=nf[:], in_=ni[:]), reads=[b_red], writes=[b_red])
                    p.op("dve", lambda e: e.tensor_tensor(out=ang2[:], in0=ang2[:], in1=nf[:], op=ALU.subtract), reads=[b_red], writes=[b_red])
                    p.op("dve", lambda e: e.tensor_single_scalar(out=mk[:], in_=ang2[:], scalar=0.5, op=ALU.is_gt), reads=[b_red], writes=[b_red])
                    p.op("dve", lambda e: e.tensor_tensor(out=ang2[:], in0=ang2[:], in1=mk[:], op=ALU.subtract), reads=[b_red], writes=[b_red])
                    p.op("dve", lambda e: e.tensor_single_scalar(out=mk[:], in_=ang2[:], scalar=-0.5, op=ALU.is_lt), reads=[b_red], writes=[b_red])
                    p.op("dve", lambda e: e.tensor_tensor(out=ang2[:], in0=ang2[:], in1=mk[:], op=ALU.add), reads=[b_red], writes=[b_red])
                    p.op("act", lambda e: e.activation(out=dst[:], in_=ang2[:], func=AF.Sin, scale=6.283184), reads=[b_red], writes=[b_rope])

                CST = int(os.environ.get("CSTAGE", "99"))
                if CST < 1:
                    p.barrier()
                    return
                reduce_sin(ang, 0.0, sn_t)
                reduce_sin(ang, 0.25, cs_t)
                if CST < 2:
                    p.barrier()
                    return
                p.op("pool", lambda e: e.memset(maskD[:], 0.0), writes=[b_maskD])
                p.op("pool", lambda e: e.affine_select(out=maskD[:], in_=maskD[:], pattern=[[1, 128]], compare_op=ALU.is_ge, fill=-30000.0,
                                                       base=0, channel_multiplier=-1), reads=[b_maskD], writes=[b_maskD])
                if CST < 3:
                    p.barrier()
                    return
                p.dma("sync", lambda e: e.dma_start(out=relb[:], in_=rel_bias.partition_broadcast(128)), writes=[b_relb])
                dl = sb(st, "dl", [128, 128], F32)
                b_dl = p.buf()
                p.op("dve", lambda e: e.tensor_tensor(out=dl[:, 4:128], in0=relb[:, 4:128], in1=relb[:, 0:124], op=ALU.subtract),
                     reads=[b_relb], writes=[b_dl])
                pri = sb(st, "pri", [128, 256], I32)
                prf = sb(st, "prf", [128, 256], F32)
                b_pr = p.buf()
                p.dma("sync", lambda e: e.dma_start(out=pri[:], in_=pos_row.partition_broadcast(128)), writes=[b_pr])
                p.op("dve", lambda e: e.tensor_copy(out=prf[:], in_=pri[:]), reads=[b_pr], writes=[b_pr])
                p.op("dve", lambda e: e.tensor_scalar(out=prf[:], in0=prf[:], scalar1=posf[:, 0:1], scalar2=0.0, op0=ALU.subtract, op1=ALU.max),
                     reads=[b_pr, b_pos], writes=[b_pr])
                if CST < 4:
                    p.barrier()
                    return
                ge = [sb(st, "ge%d" % i, [128, 256], F32) for i in range(2)]
                b_ge = p.bufs(2)
                for h in range(4):
                    p.op("dve", lambda e, h=h: e.tensor_scalar(out=Dn[h][:], in0=prf[:], scalar1=0.0, scalar2=relb[:, h:h + 1], op0=ALU.mult, op1=ALU.add),
                         reads=[b_pr, b_relb], writes=[b_Dn])
                def bucket(n):
                    if n < 16:
                        return n
                    return min(31, 16 + int(np.float32(np.log(np.float32(n) / np.float32(16))) / np.float32(math.log(128 / 16)) * np.float32(16)))
                thr = {}
                for n in range(0, 300):
                    bk = bucket(n)
                    for bb in range(1, bk + 1):
                        if bb not in thr:
                            thr[bb] = n
                for bb in range(1, 32):
                    gi = bb % 2
                    tv = float(thr[bb]) - 0.5
                    p.op("dve", lambda e, gi=gi, tv=tv: e.tensor_single_scalar(out=ge[gi][:], in_=prf[:], scalar=tv, op=ALU.is_ge),
                         reads=[b_pr], writes=[b_ge[gi]])
                    for h in range(4):
                        p.op("dve", lambda e, gi=gi, h=h, bb=bb: e.scalar_tensor_tensor(out=Dn[h][:], in0=ge[gi][:], scalar=dl[:, bb * 4 + h:bb * 4 + h + 1],
                                                                                    in1=Dn[h][:], op0=ALU.mult, op1=ALU.add),
                             reads=[b_ge[gi], b_dl, b_Dn], writes=[b_Dn])
                for h in range(4):
                    p.op("dve", lambda e, h=h: e.tensor_scalar(out=Dn[h][:], in0=Dn[h][:], scalar1=8.0, scalar2=None, op0=ALU.mult), reads=[b_Dn], writes=[b_Dn])
                    p.op("pool", lambda e, h=h: e.affine_select(out=Dn[h][:, 0:128], in_=Dn[h][:, 0:128], pattern=[[1, 128]], compare_op=ALU.is_ge,
                                                                fill=-30000.0, base=0, channel_multiplier=-1), reads=[b_Dn], writes=[b_Dn])
                p.barrier()

        class AttnRes:
            pass

        def make_attn_res(st):
            r = AttnRes()
            r.sc = [ps(st, "a_sc%d" % i, [128, 512], F32) for i in range(3)]
            r.b_sc = p.bufs(3)
            r.eT = [sb(st, "a_eT%d" % i, [128, 512], BF16) for i in range(3)]
            r.b_eT = p.bufs(3)
            r.nsc = 0
            r.neT = 0
            return r

        def attn_chunk(r, c, QTf, b_Q, KTf, b_K, Vf, b_V, dv, scale, d0, b_d0, d1, b_d1, farb, po, b_po, psm, b_psm):
            nk = 4 * c + 4

            def qk(kt):
                j0 = max(4 * c, kt)
                off = (j0 - 4 * c) * 128
                ncol = 512 - off
                si = r.nsc % 3
                r.nsc += 1
                sc, b_s = r.sc[si], r.b_sc[si]
                p.op("pe", lambda e, sc=sc, kt=kt, off=off, ncol=ncol: e.matmul(sc[:, off:512], lhsT=KTf(kt), rhs=QTf(c * 512 + off, ncol), start=True, stop=True),
                     reads=[b_K, b_Q], writes=[b_s])
                nnear = 0
                if kt >= 4 * c:
                    p.op("dve", lambda e, sc=sc, off=off: e.tensor_tensor(out=sc[:, off:off + 128], in0=sc[:, off:off + 128], in1=d0, op=ALU.add),
                         reads=[b_s, b_d0], writes=[b_s])
                    nnear = 1
                    if d1 is not None and kt + 1 <= 4 * c + 3:
                        p.op("dve", lambda e, sc=sc, off=off: e.tensor_tensor(out=sc[:, off + 128:off + 256], in0=sc[:, off + 128:off + 256], in1=d1, op=ALU.add),
                             reads=[b_s, b_d1], writes=[b_s])
                        nnear = 2
                elif d1 is not None and kt == 4 * c - 1:
                    p.op("dve", lambda e, sc=sc: e.tensor_tensor(out=sc[:, 0:128], in0=sc[:, 0:128], in1=d1, op=ALU.add),
                         reads=[b_s, b_d1], writes=[b_s])
                    nnear = 1
                return sc, b_s, off, nnear

            pend = [qk(0)]
            if nk > 1:
                pend.append(qk(1))
            for kt in range(nk):
                if kt + 2 < nk:
                    pend.append(qk(kt + 2))
                sc, b_s, off, nnear = pend.pop(0)
                ei = r.neT % 3
                r.neT += 1
                eT, b_e = r.eT[ei], r.b_eT[ei]
                if farb is None:
                    p.op("act", lambda e, sc=sc, eT=eT, off=off: e.activation(out=eT[:, off:512], in_=sc[:, off:512], func=AF.Exp, scale=scale),
                         reads=[b_s], writes=[b_e])
                else:
                    nn = nnear * 128
                    if nn > 0:
                        p.op("act", lambda e, sc=sc, eT=eT, off=off, nn=nn: e.activation(out=eT[:, off:off + nn], in_=sc[:, off:off + nn], func=AF.Exp, scale=scale),
                             reads=[b_s], writes=[b_e])
                    if off + nn < 512:
                        p.op("act", lambda e, sc=sc, eT=eT, off=off, nn=nn: e.activation(out=eT[:, off + nn:512], in_=sc[:, off + nn:512], func=AF.Exp, scale=scale, bias=farb),
                             reads=[b_s, b_relb], writes=[b_e])
                if psm is None:
                    p.op("pe", lambda e, eT=eT, kt=kt, off=off: e.matmul(po[0:dv, off:512], lhsT=Vf(kt), rhs=eT[:, off:512], start=(kt == 0), stop=(kt == nk - 1)),
                         reads=[b_V, b_e], writes=[b_po])
                else:
                    p.op("pe", lambda e, eT=eT, kt=kt, off=off: e.matmul(po[0:dv, off:512], lhsT=Vf(kt), rhs=eT[:, off:512], start=(kt == 0), stop=(kt == nk - 1)),
                         reads=[b_V, b_e], writes=[b_po], sig=False)
                    p.op("pe", lambda e, eT=eT, kt=kt, off=off: e.matmul(psm[0:1, off:512], lhsT=ones_bf[:, 0:1], rhs=eT[:, off:512], start=(kt == 0), stop=(kt == nk - 1)),
                         reads=[b_e, b_ones], writes=[b_psm])

        def phase_mla(l):
            with ExitStack() as st:
                QT = sb(st, "m_QT", [128, 4, S], BF16)
                KT = sb(st, "m_KT", [128, 4, S], BF16)
                V = sb(st, "m_V", [128, NT, 4, 65], BF16)
                b_QT, b_KT, b_V = p.buf(), p.buf(), p.buf()
                for hg in range(2):
                    with ExitStack() as s2:
                        p.op("pool", lambda e: e.memset(V[:], 1.0), writes=[b_V])
                        wuq, b_wuq = load_weight_bf(s2, "m_wuq", mats["mla_w_uq"][l], 256, 768)
                        wukv, b_wukv = load_weight_bf(s2, "m_wukv", mats["mla_w_ukv"][l], 128, 1024)
                        qn, b_qn = load_bcast(s2, "m_qn", vec["mla_q_norm"][l:l + 1, :], 256)
                        kvn, b_kvn = load_bcast(s2, "m_kvn", vec["mla_kv_norm"][l:l + 1, :], 128)
                        pt_ = [sb(s2, "m_p%d" % i, [128, 416], F32) for i in range(2)]
                        b_pt = p.bufs(2)
                        junk = sb(s2, "m_junk", [128, 256], BF16)
                        b_junk = p.buf()
                        ss = [sb(s2, "m_ss%d" % i, [128, 2], F32) for i in range(2)]
                        b_ss = p.bufs(2)
                        cb_ = [sb(s2, "m_cb%d" % i, [128, 384], BF16) for i in range(2)]
                        b_cb = p.bufs(2)
                        pT = ps(s2, "m_pT", [128, 3, 128], BF16)
                        b_pT = p.buf()
                        cT = [sb(s2, "m_cT%d" % i, [128, 3, 128], BF16) for i in range(2)]
                        b_cT = p.bufs(2)
                        pq = [ps(s2, "m_pq%d" % i, [128, 512], F32) for i in range(2)]
                        b_pq = p.bufs(2)
                        qb = [sb(s2, "m_qb%d" % i, [128, 8, 96], BF16) for i in range(2)]
                        b_qb = p.bufs(2)
                        tA = sb(s2, "m_tA", [128, 8, 16], F32)
                        tB = sb(s2, "m_tB", [128, 8, 16], F32)
                        tC = sb(s2, "m_tC", [128, 8, 16], F32)
                        tD = sb(s2, "m_tD", [128, 8, 16], F32)
                        qf = sb(s2, "m_qf", [128, 768], F32)
                        b_tA, b_tB, b_tC, b_tD, b_qf = p.buf(), p.buf(), p.buf(), p.buf(), p.buf()
                        pqT = ps(s2, "m_pqT", [128, 8, 128], BF16)
                        b_pqT = p.buf()
                        pkn = [ps(s2, "m_pkn%d" % i, [64, 4, 128], F32) for i in range(2)]
                        b_pkn = p.bufs(2)
                        pv = ps(s2, "m_pv", [128, 512], F32)
                        b_pv = p.buf()
                        kr = [sb(s2, "m_kr%d" % i, [128, 96], BF16) for i in range(2)]
                        b_kr = p.bufs(2)
                        for i in range(2):
                            p.op("pool", lambda e, i=i: e.memset(kr[i][:], 0.0), writes=[b_kr[i]])
                        pkr = ps(s2, "m_pkr", [128, 128], BF16)
                        b_pkr = p.buf()
                        krT = sb(s2, "m_krT", [128, 128], BF16)
                        b_krT = p.buf()
                        wukv_v = wukv[:, 0, :].rearrange("p (h c) -> p h c", c=128)
                        MSUB = int(os.environ.get('MLA_SUB', '99'))
                        for t in range(int(os.environ.get('MLA_NT', NT))):
                            i = t % 2
                            ts_ = slice(t * 128, (t + 1) * 128)
                            if MSUB < 0:
                                continue
                            p.dma("sync", lambda e, i=i, ts_=ts_: e.dma_start(out=pt_[i][:], in_=pml[ts_, :]), reads=[B_pml[t]], writes=[b_pt[i]])
                            p.op("pool", lambda e, i=i: e.memset(ss[i][:], 0.0), writes=[b_ss[i]])
                            p.op("act", lambda e, i=i: e.activation(out=junk[:, 0:256], in_=pt_[i][:, 0:256], func=AF.Square, accum_out=ss[i][:, 0:1]),
                                 reads=[b_pt[i], b_ss[i]], writes=[b_junk, b_ss[i]])
                            p.op("act", lambda e, i=i: e.activation(out=junk[:, 0:128], in_=pt_[i][:, 256:384], func=AF.Square, accum_out=ss[i][:, 1:2]),
                                 reads=[b_pt[i], b_ss[i]], writes=[b_junk, b_ss[i]])
                            p.op("act", lambda e, i=i: e.activation(out=ss[i][:, 0:1], in_=ss[i][:, 0:1], func=AF.Sqrt, scale=1.0 / 256, bias=g.eps_tiles[1e-6][:]),
                                 reads=[b_ss[i], b_eps], writes=[b_ss[i]])
                            p.op("act", lambda e, i=i: e.activation(out=ss[i][:, 1:2], in_=ss[i][:, 1:2], func=AF.Sqrt, scale=1.0 / 128, bias=g.eps_tiles[1e-6][:]),
                                 reads=[b_ss[i], b_eps], writes=[b_ss[i]])
                            p.op("dve", lambda e, i=i: e.reciprocal(out=ss[i][:], in_=ss[i][:]), reads=[b_ss[i]], writes=[b_ss[i]])
                            p.op("dve", lambda e, i=i: e.scalar_tensor_tensor(out=cb_[i][:, 0:256], in0=pt_[i][:, 0:256], scalar=ss[i][:, 0:1], in1=qn[:],
                                                                              op0=ALU.mult, op1=ALU.mult), reads=[b_pt[i], b_ss[i], b_qn], writes=[b_cb[i]])
                            p.op("dve", lambda e, i=i: e.scalar_tensor_tensor(out=cb_[i][:, 256:384], in0=pt_[i][:, 256:384], scalar=ss[i][:, 1:2], in1=kvn[:],
                                                                              op0=ALU.mult, op1=ALU.mult), reads=[b_pt[i], b_ss[i], b_kvn], writes=[b_cb[i]])
                            if MSUB < 2:
                                continue
                            for k in range(3):
                                p.op("pe", lambda e, i=i, k=k: e.transpose(out=pT[:, k, :], in_=cb_[i][:, k * 128:(k + 1) * 128], identity=ident[:]),
                                     reads=[b_cb[i], b_ident], writes=[b_pT], sig=(k == 2))
                            p.op("act", lambda e, i=i: e.copy(out=cT[i][:], in_=pT[:]), reads=[b_pT], writes=[b_cT[i]])
                            if MSUB < 3:
                                continue
                            for (c0, ncol, pi) in ((0, 512, 0), (512, 256, 1)):
                                for kk in range(2):
                                    p.op("pe", lambda e, i=i, kk=kk, c0=c0, ncol=ncol, pi=pi: e.matmul(pq[pi][:, 0:ncol], lhsT=cT[i][:, kk, :], rhs=wuq[:, kk, c0:c0 + ncol],
                                                                                                 start=(kk == 0), stop=(kk == 1)),
                                         reads=[b_cT[i], b_wuq], writes=[b_pq[pi]], sig=(kk == 1))
                            if MSUB < 4:
                                continue
                            p.op("act", lambda e: e.copy(out=qf[:, 0:512], in_=pq[0][:, 0:512]), reads=[b_pq[0]], writes=[b_qf])
                            p.op("act", lambda e: e.copy(out=qf[:, 512:768], in_=pq[1][:, 0:256]), reads=[b_pq[1]], writes=[b_qf])
                            for h in range(hg * 4, hg * 4 + 4):
                                c0 = h * 96
                                p.op("act", lambda e, i=i, h=h, c0=c0: e.copy(out=qb[i][:, h, 0:64], in_=qf[:, c0:c0 + 64]), reads=[b_qf], writes=[b_qb[i]])
                                x1 = qf[:, c0 + 64:c0 + 80]
                                x2 = qf[:, c0 + 80:c0 + 96]
                                cst = cs_t[:, t, :]
                                snt = sn_t[:, t, :]
                                p.op("dve", lambda e, h=h, x1=x1, cst=cst: e.tensor_tensor(out=tA[:, h, :], in0=x1, in1=cst, op=ALU.mult), reads=[b_qf, b_rope], writes=[b_tA])
                                p.op("dve", lambda e, h=h, x2=x2, snt=snt: e.tensor_tensor(out=tB[:, h, :], in0=x2, in1=snt, op=ALU.mult), reads=[b_qf, b_rope], writes=[b_tB])
                                p.op("dve", lambda e, i=i, h=h: e.tensor_tensor(out=qb[i][:, h, 64:80], in0=tA[:, h, :], in1=tB[:, h, :], op=ALU.subtract),
                                     reads=[b_tA, b_tB], writes=[b_qb[i]])
                                p.op("dve", lambda e, h=h, x1=x1, snt=snt: e.tensor_tensor(out=tC[:, h, :], in0=x1, in1=snt, op=ALU.mult), reads=[b_qf, b_rope], writes=[b_tC])
                                p.op("dve", lambda e, h=h, x2=x2, cst=cst: e.tensor_tensor(out=tD[:, h, :], in0=x2, in1=cst, op=ALU.mult), reads=[b_qf, b_rope], writes=[b_tD])
                                p.op("dve", lambda e, i=i, h=h: e.tensor_tensor(out=qb[i][:, h, 80:96], in0=tC[:, h, :], in1=tD[:, h, :], op=ALU.add),
                                     reads=[b_tC, b_tD], writes=[b_qb[i]])
                            if MSUB < 5:
                                continue
                            for hh in range(4):
                                h = hg * 4 + hh
                                p.op("pe", lambda e, i=i, h=h, hh=hh: e.transpose(out=pqT[0:96, hh, :], in_=qb[i][:, h, :], identity=ident[:]),
                                     reads=[b_qb[i], b_ident], writes=[b_pqT], sig=(hh == 3))
                            p.op("act", lambda e, ts_=ts_: e.copy(out=QT[0:96, :, ts_], in_=pqT[0:96, 0:4, :]), reads=[b_pqT], writes=[b_QT])
                            if MSUB < 6:
                                continue
                            for hh in range(4):
                                h = hg * 4 + hh
                                p.op("pe", lambda e, i=i, h=h, hh=hh: e.matmul(pkn[0][:, hh, :], lhsT=wukv[:, 0, h * 128:h * 128 + 64], rhs=cT[i][:, 2, :], start=True, stop=True),
                                     reads=[b_cT[i], b_wukv], writes=[b_pkn[0]], sig=(hh == 3))
                            p.op("dve", lambda e, ts_=ts_: e.tensor_copy(out=KT[0:64, :, ts_], in_=pkn[0][:]), reads=[b_pkn[0]], writes=[b_KT])
                            if MSUB < 7:
                                continue
                            for hh in range(4):
                                h = hg * 4 + hh
                                p.op("pe", lambda e, i=i, h=h, hh=hh: e.matmul(pv[:, hh * 64:(hh + 1) * 64], lhsT=cT[i][:, 2, :], rhs=wukv[:, 0, h * 128 + 64:h * 128 + 128], start=True, stop=True),
                                     reads=[b_cT[i], b_wukv], writes=[b_pv], sig=(hh == 3))
                            p.op("act", lambda e, t=t: e.copy(out=V[:, t, :, 1:65], in_=pv[:, 0:256].rearrange("p (h c) -> p h c", c=64)), reads=[b_pv], writes=[b_V])
                            if MSUB < 8:
                                continue
                            cst = cs_t[:, t, :]
                            snt = sn_t[:, t, :]
                            x1 = pt_[i][:, 384:400]
                            x2 = pt_[i][:, 400:416]
                            p.op("dve", lambda e, x1=x1, cst=cst: e.tensor_tensor(out=tA[:, 0, :], in0=x1, in1=cst, op=ALU.mult), reads=[b_pt[i], b_rope], writes=[b_tA])
                            p.op("dve", lambda e, x2=x2, snt=snt: e.tensor_tensor(out=tB[:, 0, :], in0=x2, in1=snt, op=ALU.mult), reads=[b_pt[i], b_rope], writes=[b_tB])
                            p.op("dve", lambda e, i=i: e.tensor_tensor(out=kr[i][:, 64:80], in0=tA[:, 0, :], in1=tB[:, 0, :], op=ALU.subtract), reads=[b_tA, b_tB], writes=[b_kr[i]])
                            p.op("dve", lambda e, x1=x1, snt=snt: e.tensor_tensor(out=tA[:, 0, :], in0=x1, in1=snt, op=ALU.mult), reads=[b_pt[i], b_rope], writes=[b_tA])
                            p.op("dve", lambda e, x2=x2, cst=cst: e.tensor_tensor(out=tB[:, 0, :], in0=x2, in1=cst, op=ALU.mult), reads=[b_pt[i], b_rope], writes=[b_tB])
                            p.op("dve", lambda e, i=i: e.tensor_tensor(out=kr[i][:, 80:96], in0=tA[:, 0, :], in1=tB[:, 0, :], op=ALU.add), reads=[b_tA, b_tB], writes=[b_kr[i]])
                            p.op("pe", lambda e, i=i: e.transpose(out=pkr[0:96, :], in_=kr[i][:, :], identity=ident[:]), reads=[b_kr[i], b_ident], writes=[b_pkr])
                            p.op("act", lambda e: e.copy(out=krT[64:96, :], in_=pkr[64:96, :]), reads=[b_pkr], writes=[b_krT])
                            for h in range(4):
                                p.op("pool", lambda e, h=h, ts_=ts_: e.tensor_copy(out=KT[64:96, h, ts_], in_=krT[64:96, :]), reads=[b_krT], writes=[b_KT])
                        p.barrier()
                    if os.environ.get("MLA_STAGE") == "prep":
                        continue
                    with ExitStack() as s3:
                        r = make_attn_res(s3)
                        po = [ps(s3, "m_po%d" % i, [128, 512], F32) for i in range(2)]
                        b_po = p.bufs(2)
                        pbc = ps(s3, "m_pbc", [128, 512], F32)
                        b_pbc = p.buf()
                        rc = [sb(s3, "m_rc%d" % i, [1, 512], F32) for i in range(2)]
                        b_rc = p.bufs(2)
                        bcs = [sb(s3, "m_bcs%d" % i, [65, 512], F32) for i in range(2)]
                        b_bcs = p.bufs(2)
                        ob = [sb(s3, "m_ob%d" % i, [65, 512], BF16) for i in range(2)]
                        b_ob = p.bufs(2)
                        n = 0
                        scale = 96 ** -0.5
                        for hh in range(4):
                            h = hg * 4 + hh
                            for c in range(int(os.environ.get('MLA_NC', 8))):
                                i = n % 2
                                n += 1
                                attn_chunk(r, c,
                                           lambda q0, nq, hh=hh: QT[0:96, hh, q0:q0 + nq], b_QT,
                                           lambda kt, hh=hh: KT[0:96, hh, kt * 128:(kt + 1) * 128], b_KT,
                                           lambda kt, hh=hh: V[:, kt, hh, :], b_V, 65, scale,
                                           maskD[:], b_maskD, None, None, None, po[i], b_po[i], None, None)
                                p.op("dve", lambda e, i=i: e.reciprocal(out=rc[i][:], in_=po[i][0:1, :]), reads=[b_po[i]], writes=[b_rc[i]])
                                p.op("pe", lambda e, i=i: e.matmul(pbc[0:65, :], lhsT=ones_f[0:1, 0:65], rhs=rc[i][:], start=True, stop=True),
                                     reads=[b_rc[i], b_ones], writes=[b_pbc])
                                p.op("act", lambda e, i=i: e.copy(out=bcs[i][:], in_=pbc[0:65, :]), reads=[b_pbc], writes=[b_bcs[i]])
                                p.op("dve", lambda e, i=i: e.tensor_tensor(out=ob[i][:], in0=po[i][0:65, :], in1=bcs[i][:], op=ALU.mult),
                                     reads=[b_po[i], b_bcs[i]], writes=[b_ob[i]])
                                p.dma("sync", lambda e, i=i, h=h, c=c: e.dma_start(out=oT["mla"][h * 64:(h + 1) * 64, c * 512:(c + 1) * 512], in_=ob[i][1:65, :]),
                                      reads=[b_ob[i]], writes=[B_oT["mla"][c]])
                        p.barrier()

        def phase_diff(l):
            lambda_init = 0.8 - 0.6 * math.exp(-0.3 * l)
            with ExitStack() as st:
                r = make_attn_res(st)
                lam = sb(st, "d_lam", [1, 256], F32)
                lamp = sb(st, "d_lamp", [1, 128], F32)
                lams = sb(st, "d_lams", [1, 4], F32)
                b_lam = p.buf()
                p.dma("sync", lambda e: e.dma_start(out=lam[:], in_=diff_lambda[l]), writes=[b_lam])
                p.op("pool", lambda e: e.memset(lams[:], 0.0), writes=[b_lam])
                p.op("dve", lambda e: e.tensor_tensor(out=lamp[:, 0:64], in0=lam[:, 0:64], in1=lam[:, 64:128], op=ALU.mult), reads=[b_lam], writes=[b_lam])
                p.op("dve", lambda e: e.tensor_tensor(out=lamp[:, 64:128], in0=lam[:, 128:192], in1=lam[:, 192:256], op=ALU.mult), reads=[b_lam], writes=[b_lam])
                p.op("dve", lambda e: e.tensor_reduce(out=lams[:, 0:2], in_=lamp[:].rearrange("p (a b) -> p a b", b=64), axis=AX.X, op=ALU.add), reads=[b_lam], writes=[b_lam])
                p.op("act", lambda e: e.activation(out=lams[:, 0:2], in_=lams[:, 0:2], func=AF.Exp), reads=[b_lam], writes=[b_lam])
                p.op("dve", lambda e: e.scalar_tensor_tensor(out=lams[:, 2:3], in0=lams[:, 1:2], scalar=-lambda_init, in1=lams[:, 0:1], op0=ALU.add, op1=ALU.subtract),
                     reads=[b_lam], writes=[b_lam])
                sub = sb(st, "d_sub", [128, 1], F32)
                b_sub = p.buf()
                p.dma("sync", lambda e: e.dma_start(out=sub[:], in_=subln_cm[l]), writes=[b_sub])
                p.op("dve", lambda e: e.tensor_scalar(out=sub[:], in0=sub[:], scalar1=1.0 - lambda_init, scalar2=None, op0=ALU.mult), reads=[b_sub], writes=[b_sub])
                QT = [sb(st, "d_QT%d" % i, [64, S], BF16) for i in range(2)]
                KT = [sb(st, "d_KT%d" % i, [64, S], BF16) for i in range(2)]
                b_QK = p.bufs(2)
                V = sb(st, "d_V", [128, NT, 128], BF16)
                b_V = p.buf()
                po = [ps(st, "d_po%d" % i, [128, 512], F32) for i in range(2)]
                b_po = p.bufs(2)
                psm = [ps(st, "d_psm%d" % i, [1, 512], F32) for i in range(2)]
                b_psm = p.bufs(2)
                pbc = ps(st, "d_pbc", [128, 512], F32)
                b_pbc = p.buf()
                rc = [sb(st, "d_rc%d" % i, [1, 512], F32) for i in range(2)]
                b_rc = p.bufs(2)
                bcs = [sb(st, "d_bcs%d" % i, [128, 512], F32) for i in range(2)]
                b_bcs = p.bufs(2)
                o0 = sb(st, "d_o0", [128, 512], F32)
                o1 = sb(st, "d_o1", [128, 512], F32)
                sq = sb(st, "d_sq", [128, 512], F32)
                b_o0, b_o1, b_sq = p.buf(), p.buf(), p.buf()
                ob = [sb(st, "d_ob%d" % i, [128, 512], BF16) for i in range(2)]
                b_ob = p.bufs(2)
                n = 0
                scale = 0.125
                for h in range(4):
                    for m in range(2):
                        rq = (h * 2 + m) * 64
                        p.dma("sync", lambda e, m=m, rq=rq: e.dma_start(out=QT[m][:], in_=qkT[rq:rq + 64, :]), reads=B_qkT, writes=[b_QK[m]])
                        p.dma("sync", lambda e, m=m, rq=rq: e.dma_start(out=KT[m][:], in_=qkT[512 + rq:512 + rq + 64, :]), reads=B_qkT, writes=[b_QK[m]])
                    p.dma("sync", lambda e, h=h: e.dma_start(out=V[:], in_=vdf[:, h * 128:(h + 1) * 128].rearrange("(n p) c -> p n c", p=128)),
                          reads=B_vdf, writes=[b_V])
                    farb = relb[:, 31 * 4 + h:31 * 4 + h + 1]
                    for c in range(8):
                        for m in range(2):
                            attn_chunk(r, c,
                                       lambda q0, nq, m=m: QT[m][:, q0:q0 + nq], b_QK[m],
                                       lambda kt, m=m: KT[m][:, kt * 128:(kt + 1) * 128], b_QK[m],
                                       lambda kt: V[:, kt, :], b_V, 128, scale,
                                       Dn[h][:, 0:128], b_Dn, Dn[h][:, 128:256], b_Dn, farb, po[m], b_po[m], psm[m], b_psm[m])
                        for m in range(2):
                            p.op("dve", lambda e, m=m: e.reciprocal(out=rc[m][:], in_=psm[m][:]), reads=[b_psm[m]], writes=[b_rc[m]])
                        p.op("dve", lambda e: e.tensor_scalar(out=rc[1][:], in0=rc[1][:], scalar1=lams[0:1, 2:3], scalar2=None, op0=ALU.mult),
                             reads=[b_rc[1], b_lam], writes=[b_rc[1]])
                        for m in range(2):
                            p.op("pe", lambda e, m=m: e.matmul(pbc[:], lhsT=ones_f[0:1, :], rhs=rc[m][:], start=True, stop=True),
                                 reads=[b_rc[m], b_ones], writes=[b_pbc])
                            p.op("act", lambda e, m=m: e.copy(out=bcs[m][:], in_=pbc[:]), reads=[b_pbc], writes=[b_bcs[m]])
                        p.op("dve", lambda e: e.tensor_tensor(out=o0[:], in0=po[0][:], in1=bcs[0][:], op=ALU.mult), reads=[b_po[0], b_bcs[0]], writes=[b_o0])
                        p.op("dve", lambda e: e.tensor_tensor(out=o1[:], in0=po[1][:], in1=bcs[1][:], op=ALU.mult), reads=[b_po[1], b_bcs[1]], writes=[b_o1])
                        p.op("dve", lambda e: e.tensor_tensor(out=o0[:], in0=o0[:], in1=o1[:], op=ALU.add), reads=[b_o0, b_o1], writes=[b_o0])
                        p.op("act", lambda e: e.activation(out=sq[:], in_=o0[:], func=AF.Square), reads=[b_o0], writes=[b_sq])
                        p.op("pe", lambda e: e.matmul(pbc[:], lhsT=ones_f[:, :], rhs=sq[:], start=True, stop=True), reads=[b_sq, b_ones], writes=[b_pbc])
                        p.op("act", lambda e: e.activation(out=sq[:], in_=pbc[:], func=AF.Sqrt, scale=1.0 / 128, bias=g.eps_tiles[1e-5][:]),
                             reads=[b_pbc, b_eps], writes=[b_sq])
                        p.op("dve", lambda e: e.reciprocal(out=sq[:], in_=sq[:]), reads=[b_sq], writes=[b_sq])
                        i = n % 2
                        n += 1
                        p.op("dve", lambda e, i=i: e.scalar_tensor_tensor(out=ob[i][:], in0=o0[:], scalar=sub[:, 0:1], in1=sq[:], op0=ALU.mult, op1=ALU.mult),
                             reads=[b_o0, b_sub, b_sq], writes=[b_ob[i]])
                        p.dma("sync", lambda e, i=i, h=h, c=c: e.dma_start(out=oT["diff"][h * 128:(h + 1) * 128, c * 512:(c + 1) * 512], in_=ob[i][:]),
                              reads=[b_ob[i]], writes=[B_oT["diff"][c]])
                p.barrier()

        def phase_rwkv(l):
            with ExitStack() as st:
                def T_(name, shape, dt=F32):
                    return sb(st, "r_" + name, shape, dt)
                mu, b_mu = load_bcast(st, "r_mu", vec["rwkv_mu"][l:l + 1, :], 1792)
                w0, b_w0 = load_bcast(st, "r_w0", vec["rwkv_w0"][l:l + 1, :], 512)
                a0, b_a0 = load_bcast(st, "r_a0", vec["rwkv_a0"][l:l + 1, :], 512)
                k_k, b_kk_ = load_bcast(st, "r_k_k", vec["rwkv_k_k"][l:l + 1, :], 512)
                k_a, b_ka_ = load_bcast(st, "r_k_a", vec["rwkv_k_a"][l:l + 1, :], 512)
                r_k, b_rk_ = load_bcast(st, "r_r_k", vec["rwkv_r_k"][l:l + 1, :], 512)
                ln_w, b_lnw = load_bcast(st, "r_ln_w", vec["rwkv_ln_w"][l:l + 1, :], 512)
                ln_b, b_lnb = load_bcast(st, "r_ln_b", vec["rwkv_ln_b"][l:l + 1, :], 512)
                w2, b_w2 = load_weight_bf(st, "r_w2", mats["rwkv_w2"][l], 64, 512, part0=0)
                a2, b_a2 = load_weight_bf(st, "r_a2", mats["rwkv_a2"][l], 64, 512, part0=64)
                g2, b_g2 = load_weight_bf(st, "r_g2", mats["rwkv_g2"][l], 128, 512)
                triU = T_("triU", [128, 128])
                mSI = T_("mSI", [128, 256])
                mSL = T_("mSL", [128, 128])
                b_msk = p.buf()
                p.op("pool", lambda e: e.memset(triU[:], 1.0), writes=[b_msk])
                p.op("pool", lambda e: e.affine_select(out=triU[:], in_=triU[:], pattern=[[1, 128]], compare_op=ALU.is_ge, fill=0.0, base=0, channel_multiplier=-1),
                     reads=[b_msk], writes=[b_msk])
                p.op("pool", lambda e: e.memset(mSI[:], 1.0), writes=[b_msk])
                p.op("pool", lambda e: e.affine_select(out=mSI[:, 0:128], in_=mSI[:, 0:128], pattern=[[1, 128]], compare_op=ALU.is_gt, fill=0.0, base=0, channel_multiplier=-1),
                     reads=[b_msk], writes=[b_msk])
                p.op("pool", lambda e: e.affine_select(out=mSI[:, 128:256], in_=mSI[:, 128:256], pattern=[[1, 128]], compare_op=ALU.is_ge, fill=0.0, base=0, channel_multiplier=-1),
                     reads=[b_msk], writes=[b_msk])
                p.op("pool", lambda e: e.memset(mSL[:], 1.0), writes=[b_msk])
                p.op("pool", lambda e: e.affine_select(out=mSL[:], in_=mSL[:], pattern=[[-1, 128]], compare_op=ALU.is_gt, fill=0.0, base=0, channel_multiplier=1),
                     reads=[b_msk], writes=[b_msk])
                identf = T_("identf", [128, 128], BF16)
                Sst = T_("S", [64, 8, 64])
                Sb = T_("Sb", [64, 8, 64], BF16)
                b_S, b_Sb = p.buf(), p.buf()
                p.op("pool", lambda e: e.memset(Sst[:], 0.0), writes=[b_S])
                p.op("pool", lambda e: e.memset(Sb[:], 0.0), writes=[b_Sb])
                gb = [ps(st, "r_gb%d" % i, [128, 512], F32) for i in range(6)]
                b_gb = p.bufs(6)
                gbn = [0]
                tb = [ps(st, "r_tb%d" % i, [128, 8, 128], BF16) for i in range(2)]
                b_tb = p.bufs(2)
                tbn = [0]

                def bank():
                    i = gbn[0] % 6
                    gbn[0] += 1
                    return gb[i], b_gb[i]

                def tbank():
                    i = tbn[0] % 2
                    tbn[0] += 1
                    return tb[i], b_tb[i]

                P0 = [T_("P0_0", [128, 1792])] * 2
                P1 = [T_("P1_0", [128, 1792])] * 2
                b_P0, b_P1 = [p.buf()] * 2, [p.buf()] * 2
                PM = [T_("PM%d" % i, [128, 1792]) for i in range(2)]
                b_PM = p.bufs(2)
                th = T_("th", [128, 256], BF16); b_th = p.buf()
                thT = T_("thT", [128, 2, 128], BF16); b_thT = p.buf()
                lw = T_("lw", [128, 512]); b_lw = p.buf()
                alr = T_("alr", [128, 512]); b_alr = p.buf()
                gg = [T_("gg%d" % i, [128, 512]) for i in range(2)]; b_gg = p.bufs(2)
                kk = T_("kk", [128, 512]); b_kk = p.buf()
                tmp = T_("tmp", [128, 512]); b_tmp = p.buf()
                tmp2 = T_("tmp2", [128, 512]); b_tmp2 = p.buf()
                k2 = [T_("k2_%d" % i, [128, 512]) for i in range(2)]; b_k2 = p.bufs(2)
                bt = T_("bt", [128, 512]); b_bt = p.buf()
                st8 = T_("st8", [128, 8]); b_st8 = p.buf()
                cumS = T_("cumS", [128, 512]); b_cumS = p.buf()
                eC = T_("eC", [128, 512]); b_eC = p.buf()
                eCi = T_("eCi", [128, 512]); b_eCi = p.buf()
                eCx = T_("eCx", [128, 512]); b_eCx = p.buf()
                eD = T_("eD", [128, 512]); b_eD = p.buf()
                gC = [T_("gC%d" % i, [64, 8]) for i in range(2)]; b_gC = p.bufs(2)
                X4 = T_("X4", [128, 4, 512], BF16); b_X4 = p.bufs(4)
                BH = [T_("BH%d" % i, [128, 512], BF16) for i in range(2)]; b_BH = p.bufs(2)
                KH = [T_("KH%d" % i, [128, 512], BF16) for i in range(2)]; b_KH = p.bufs(2)
                VB = [T_("VB%d" % i, [128, 512], BF16) for i in range(2)]; b_VB = p.bufs(2)
                CM = [T_("CM%d" % i, [64, 8, 4, 128], BF16) for i in range(2)]; b_CM = p.bufs(2)
                MM = [T_("MM%d" % i, [128, 8, 2, 256], BF16) for i in range(2)]; b_MM = p.bufs(2)
                Pp = [T_("Pp%d" % i, [128, 8, 128], BF16) for i in range(2)]; b_Pp = p.bufs(2)
                PTp = [T_("PTp%d" % i, [128, 8, 128], BF16) for i in range(2)]; b_PTp = p.bufs(2)
                Tp = [T_("Tp%d" % i, [128, 8, 128], BF16) for i in range(2)]; b_Tp = p.bufs(2)
                Tf = [T_("Tf%d" % i, [128, 8, 128], BF16) for i in range(2)]; b_Tf = p.bufs(2)
                Ws = T_("Ws", [128, 512], BF16); b_Ws = p.buf()
                Us = T_("Us", [128, 512], BF16); b_Us = p.buf()
                yt = T_("yt", [128, 512]); b_yt = p.buf()
                yc = T_("yc", [128, 512]); b_yc = p.buf()
                yo = T_("yo", [128, 512], BF16); b_yo = p.buf()
                oTt = [T_("oTt%d" % i, [128, 4, 128], BF16) for i in range(2)]; b_oTt = p.bufs(2)
                NEGE = -math.exp(-0.5)
                RSUB = int(os.environ.get('RW_SUB', '99'))

                def H(a, h):
                    return a[:, h * 64:(h + 1) * 64]

                def V3(a):
                    return a[:].rearrange("p (h c) -> p h c", c=64)

                def B8(a):
                    return a[:].unsqueeze(2).to_broadcast([128, 8, 64])

                def pre(t):
                    i = t % 2
                    t0 = t * 128
                    p.dma("sync", lambda e: e.dma_start(out=P0[i][:], in_=prw[t0:t0 + 128, :]), reads=[B_prw[t]], writes=[b_P0[i]])
                    if t == 0:
                        p.op("pool", lambda e: e.memset(P1[i][0:1, :], 0.0), writes=[b_P1[i]])
                        p.dma("sync", lambda e: e.dma_start(out=P1[i][1:128, :], in_=prw[0:127, :]), reads=[B_prw[0]], writes=[b_P1[i]])
                    else:
                        p.dma("sync", lambda e: e.dma_start(out=P1[i][:], in_=prw[t0 - 1:t0 + 127, :]), reads=[B_prw[t], B_prw[t - 1]], writes=[b_P1[i]])
                    pm = PM[i]
                    p.op("dve", lambda e: e.tensor_tensor(out=P1[i][:], in0=P1[i][:], in1=P0[i][:], op=ALU.subtract), reads=[b_P1[i], b_P0[i]], writes=[b_P1[i]])
                    p.op("dve", lambda e: e.tensor_tensor(out=P1[i][:], in0=P1[i][:], in1=mu[:], op=ALU.mult), reads=[b_P1[i], b_mu], writes=[b_P1[i]])
                    p.op("dve", lambda e: e.tensor_tensor(out=pm[:], in0=P1[i][:], in1=P0[i][:], op=ALU.add), reads=[b_P1[i], b_P0[i]], writes=[b_PM[i]])
                    r_ = pm[:, 0:512]
                    k_ = pm[:, 512:1024]
                    v_ = pm[:, 1024:1536]
                    if RSUB < 1:
                        return
                    p.op("act", lambda e: e.activation(out=th[:, 0:64], in_=pm[:, 1536:1600], func=AF.Tanh), reads=[b_PM[i]], writes=[b_th])
                    p.op("act", lambda e: e.copy(out=th[:, 64:128], in_=pm[:, 1600:1664]), reads=[b_PM[i]], writes=[b_th])
                    p.op("act", lambda e: e.activation(out=th[:, 128:256], in_=pm[:, 1664:1792], func=AF.Sigmoid), reads=[b_PM[i]], writes=[b_th])
                    tbk0, b_tbk0 = tbank()
                    for k in range(2):
                        p.op("pe", lambda e, k=k: e.transpose(out=tbk0[:, k, :], in_=th[:, k * 128:(k + 1) * 128], identity=ident[:]),
                             reads=[b_th, b_ident], writes=[b_tbk0], sig=(k == 1))
                    p.op("act", lambda e: e.copy(out=thT[:], in_=tbk0[:, 0:2, :]), reads=[b_tbk0], writes=[b_thT])
                    pw_, b_pw = bank()
                    p.op("pe", lambda e: e.matmul(pw_[:], lhsT=thT[0:64, 0, :], rhs=w2[0:64, 0, :], start=True, stop=True), reads=[b_thT, b_w2], writes=[b_pw])
                    pa_, b_pa = bank()
                    p.op("pe", lambda e: e.matmul(pa_[:], lhsT=thT[64:128, 0, :], rhs=a2[64:128, 0, :], start=True, stop=True), reads=[b_thT, b_a2], writes=[b_pa])
                    pg_, b_pg = bank()
                    p.op("pe", lambda e: e.matmul(pg_[:], lhsT=thT[:, 1, :], rhs=g2[:, 0, :], start=True, stop=True), reads=[b_thT, b_g2], writes=[b_pg])
                    p.op("dve", lambda e: e.tensor_tensor(out=lw[:], in0=pw_[:], in1=w0[:], op=ALU.add), reads=[b_pw, b_w0], writes=[b_lw])
                    p.op("act", lambda e: e.activation(out=lw[:], in_=lw[:], func=AF.Sigmoid), reads=[b_lw], writes=[b_lw])
                    p.op("dve", lambda e: e.tensor_scalar(out=lw[:], in0=lw[:], scalar1=NEGE, scalar2=None, op0=ALU.mult), reads=[b_lw], writes=[b_lw])
                    p.op("dve", lambda e: e.tensor_tensor(out=alr[:], in0=pa_[:], in1=a0[:], op=ALU.add), reads=[b_pa, b_a0], writes=[b_alr])
                    p.op("act", lambda e: e.activation(out=alr[:], in_=alr[:], func=AF.Sigmoid), reads=[b_alr], writes=[b_alr])
                    p.op("act", lambda e: e.copy(out=gg[i][:], in_=pg_[:]), reads=[b_pg], writes=[b_gg[i]])
                    if RSUB < 2:
                        return
                    p.op("dve", lambda e: e.tensor_tensor(out=kk[:], in0=k_, in1=k_k[:], op=ALU.mult), reads=[b_PM[i], b_kk_], writes=[b_kk])
                    p.op("dve", lambda e: e.tensor_tensor(out=tmp[:], in0=kk[:], in1=kk[:], op=ALU.mult), reads=[b_kk], writes=[b_tmp])
                    p.op("dve", lambda e: e.tensor_reduce(out=st8[:], in_=tmp[:].rearrange("p (h c) -> p h c", c=64), axis=AX.X, op=ALU.add), reads=[b_tmp], writes=[b_st8])
                    p.op("act", lambda e: e.activation(out=st8[:], in_=st8[:], func=AF.Sqrt), reads=[b_st8], writes=[b_st8])
                    p.op("dve", lambda e: e.tensor_scalar(out=st8[:], in0=st8[:], scalar1=1e-12, scalar2=None, op0=ALU.max), reads=[b_st8], writes=[b_st8])
                    p.op("dve", lambda e: e.reciprocal(out=st8[:], in_=st8[:]), reads=[b_st8], writes=[b_st8])
                    p.op("dve", lambda e: e.tensor_tensor(out=V3(kk), in0=V3(kk), in1=B8(st8), op=ALU.mult), reads=[b_kk, b_st8], writes=[b_kk])
                    p.op("dve", lambda e: e.scalar_tensor_tensor(out=tmp[:], in0=alr[:], scalar=-1.0, in1=k_a[:], op0=ALU.add, op1=ALU.mult),
                         reads=[b_alr, b_ka_], writes=[b_tmp])
                    p.op("dve", lambda e: e.scalar_tensor_tensor(out=k2[i][:], in0=tmp[:], scalar=1.0, in1=k_, op0=ALU.add, op1=ALU.mult),
                         reads=[b_tmp, b_PM[i]], writes=[b_k2[i]])
                    p.op("dve", lambda e: e.tensor_tensor(out=bt[:], in0=kk[:], in1=alr[:], op=ALU.mult), reads=[b_kk, b_alr], writes=[b_bt])
                    if RSUB < 3:
                        return
                    pc_, b_pc = bank()
                    p.op("pe", lambda e: e.matmul(pc_[:], lhsT=triU[:], rhs=lw[:], start=True, stop=True), reads=[b_msk, b_lw], writes=[b_pc])
                    ptot, b_ptot = bank()
                    p.op("pe", lambda e: e.matmul(ptot[:], lhsT=ones_f[:], rhs=lw[:], start=True, stop=True), reads=[b_ones, b_lw], writes=[b_ptot])
                    pgc, b_pgc = bank()
                    for hd in range(8):
                        p.op("pe", lambda e, hd=hd: e.matmul(pgc[0:64, hd:hd + 1], lhsT=lw[:, hd * 64:(hd + 1) * 64], rhs=ones_f[:, 0:1], start=True, stop=True),
                             reads=[b_lw, b_ones], writes=[b_pgc], sig=(hd == 7))
                    p.op("act", lambda e: e.activation(out=gC[i][:], in_=pgc[0:64, 0:8], func=AF.Exp), reads=[b_pgc], writes=[b_gC[i]])
                    p.op("act", lambda e: e.copy(out=cumS[:], in_=pc_[:]), reads=[b_pc], writes=[b_cumS])
                    p.op("act", lambda e: e.activation(out=eC[:], in_=cumS[:], func=AF.Exp), reads=[b_cumS], writes=[b_eC])
                    p.op("act", lambda e: e.activation(out=eCi[:], in_=cumS[:], func=AF.Exp, scale=-1.0), reads=[b_cumS], writes=[b_eCi])
                    p.op("dve", lambda e: e.tensor_tensor(out=tmp[:], in0=cumS[:], in1=lw[:], op=ALU.subtract), reads=[b_cumS, b_lw], writes=[b_tmp])
                    p.op("act", lambda e: e.activation(out=eCx[:], in_=tmp[:], func=AF.Exp), reads=[b_tmp], writes=[b_eCx])
                    p.op("dve", lambda e: e.tensor_tensor(out=tmp2[:], in0=ptot[:], in1=cumS[:], op=ALU.subtract), reads=[b_ptot, b_cumS], writes=[b_tmp2])
                    p.op("act", lambda e: e.activation(out=eD[:], in_=tmp2[:], func=AF.Exp), reads=[b_tmp2], writes=[b_eD])
                    if RSUB < 4:
                        return
                    p.op("dve", lambda e: e.scalar_tensor_tensor(out=X4[:, 0, :], in0=kk[:], scalar=-1.0, in1=eCx[:], op0=ALU.mult, op1=ALU.mult),
                         reads=[b_kk, b_eCx], writes=[b_X4[0]])
                    p.op("dve", lambda e: e.tensor_tensor(out=X4[:, 1, :], in0=r_, in1=eC[:], op=ALU.mult), reads=[b_PM[i], b_eC], writes=[b_X4[1]])
                    p.op("dve", lambda e: e.tensor_tensor(out=X4[:, 2, :], in0=bt[:], in1=eCi[:], op=ALU.mult), reads=[b_bt, b_eCi], writes=[b_X4[2]])
                    p.op("dve", lambda e: e.tensor_tensor(out=X4[:, 3, :], in0=k2[i][:], in1=eCi[:], op=ALU.mult), reads=[b_k2[i], b_eCi], writes=[b_X4[3]])
                    p.op("dve", lambda e: e.tensor_tensor(out=BH[i][:], in0=bt[:], in1=eD[:], op=ALU.mult), reads=[b_bt, b_eD], writes=[b_BH[i]])
                    p.op("dve", lambda e: e.tensor_tensor(out=KH[i][:], in0=k2[i][:], in1=eD[:], op=ALU.mult), reads=[b_k2[i], b_eD], writes=[b_KH[i]])
                    p.op("act", lambda e: e.copy(out=VB[i][:], in_=v_), reads=[b_PM[i]], writes=[b_VB[i]])
                    if RSUB < 5:
                        return
                    for hd in range(8):
                        tbk, b_tbk = tbank()
                        for x in range(4):
                            p.op("pe", lambda e, hd=hd, x=x, tbk=tbk: e.transpose(out=tbk[0:64, x, :], in_=X4[:, x, hd * 64:(hd + 1) * 64], identity=ident[:]),
                                 reads=[b_X4[x], b_ident], writes=[b_tbk], sig=(x == 3))
                        if hd % 2 == 0:
                            p.op("act", lambda e, hd=hd, tbk=tbk: e.copy(out=CM[i][:, hd, :, :], in_=tbk[0:64, 0:4, :]), reads=[b_tbk], writes=[b_CM[i]])
                        else:
                            p.op("dve", lambda e, hd=hd, tbk=tbk: e.tensor_copy(out=CM[i][:, hd, :, :], in_=tbk[0:64, 0:4, :]), reads=[b_tbk], writes=[b_CM[i]])
                    if RSUB < 6:
                        return
                    for hd in range(8):
                        pm_, b_pm = bank()
                        ar = CM[i][:, hd, :, :].rearrange("p x t -> p (x t)")[:, 0:256]
                        p.op("pe", lambda e, pm_=pm_, hd=hd, ar=ar: e.matmul(pm_[:, 0:256], lhsT=CM[i][:, hd, 2, :], rhs=ar, start=True, stop=True),
                             reads=[b_CM[i]], writes=[b_pm], sig=False)
                        p.op("pe", lambda e, pm_=pm_, hd=hd, ar=ar: e.matmul(pm_[:, 256:512], lhsT=CM[i][:, hd, 3, :], rhs=ar, start=True, stop=True),
                             reads=[b_CM[i]], writes=[b_pm])
                        if os.environ.get('RW_M') == '0':
                            continue
                        p.op("dve", lambda e, pm_=pm_, hd=hd: e.tensor_tensor(out=MM[i][:, hd, 0, :], in0=pm_[:, 0:256], in1=mSI[:], op=ALU.mult),
                             reads=[b_pm, b_msk], writes=[b_MM[i]])
                        p.op("dve", lambda e, pm_=pm_, hd=hd: e.tensor_tensor(out=MM[i][:, hd, 1, :], in0=pm_[:, 256:512], in1=mSI[:], op=ALU.mult),
                             reads=[b_pm, b_msk], writes=[b_MM[i]])
                    if os.environ.get('RW_M') == '1':
                        return
                    for hg in range(2):
                        pp_, b_pp_ = bank()
                        for hq in range(4):
                            hd = hg * 4 + hq
                            p.op("pe", lambda e, pp_=pp_, hd=hd, hq=hq: e.matmul(pp_[:, hq * 128:(hq + 1) * 128], lhsT=CM[i][:, hd, 0, :], rhs=CM[i][:, hd, 2, :],
                                                                                   start=True, stop=True),
                                 reads=[b_CM[i]], writes=[b_pp_], sig=(hq == 3))
                        for hq in range(4):
                            hd = hg * 4 + hq
                            p.op("dve", lambda e, pp_=pp_, hq=hq, hd=hd: e.tensor_tensor(out=PTp[0][:, hd, :], in0=pp_[:, hq * 128:(hq + 1) * 128], in1=mSL[:], op=ALU.mult),
                                 reads=[b_pp_, b_msk], writes=[b_PTp[0]])
                    if RSUB < 7:
                        return
                    for hd in range(8):
                        p.op("pool", lambda e, hd=hd: e.tensor_copy(out=Pp[0][:, hd, :], in_=MM[i][:, hd, 0, 0:128]), reads=[b_MM[i]], writes=[b_Pp[0]])
                        p.op("dve", lambda e, hd=hd: e.tensor_tensor(out=Tp[0][:, hd, :], in0=MM[i][:, hd, 0, 0:128], in1=ident[:], op=ALU.add),
                             reads=[b_MM[i], b_ident], writes=[b_Tp[0]])
                    cur = 0
                    for lvl in range(6):
                        nxt = 1 - cur
                        last = (lvl == 5)
                        for hg in range(2):
                            if not last:
                                p2, b_p2 = bank()
                            p2t, b_p2t = bank()
                            for hq in range(4):
                                hd = hg * 4 + hq
                                sl = slice(hq * 128, (hq + 1) * 128)
                                if not last:
                                    p.op("pe", lambda e, p2=p2, hd=hd, sl=sl, cur=cur: e.matmul(p2[:, sl], lhsT=PTp[cur][:, hd, :], rhs=Pp[cur][:, hd, :], start=True, stop=True),
                                         reads=[b_PTp[cur], b_Pp[cur]], writes=[b_p2], sig=(hq == 3))
                                p.op("pe", lambda e, p2t=p2t, hd=hd, sl=sl, cur=cur: e.matmul(p2t[:, sl], lhsT=Pp[cur][:, hd, :], rhs=PTp[cur][:, hd, :], start=True, stop=True),
                                     reads=[b_PTp[cur], b_Pp[cur]], writes=[b_p2t], sig=(hq == 3))
                            hs = slice(hg * 4, hg * 4 + 4)
                            if not last:
                                p.op("act", lambda e, p2=p2, hs=hs, nxt=nxt: e.copy(out=Pp[nxt][:, hs, :], in_=p2[:].rearrange("p (h c) -> p h c", c=128)),
                                     reads=[b_p2], writes=[b_Pp[nxt]])
                            p.op("dve", lambda e, p2t=p2t, hs=hs, nxt=nxt: e.tensor_copy(out=PTp[nxt][:, hs, :], in_=p2t[:].rearrange("p (h c) -> p h c", c=128)),
                                 reads=[b_p2t], writes=[b_PTp[nxt]])
                            ptu, b_ptu = bank()
                            for hq in range(4):
                                hd = hg * 4 + hq
                                sl = slice(hq * 128, (hq + 1) * 128)
                                p.op("pe", lambda e, ptu=ptu, hd=hd, sl=sl, cur=cur, nxt=nxt: e.matmul(ptu[:, sl], lhsT=PTp[nxt][:, hd, :], rhs=Tp[cur][:, hd, :], start=True, stop=True),
                                     reads=[b_PTp[nxt], b_Tp[cur]], writes=[b_ptu], sig=(hq == 3))
                            dstT = Tf[i] if last else Tp[nxt]
                            b_dstT = b_Tf[i] if last else b_Tp[nxt]
                            p.op("dve", lambda e, ptu=ptu, hs=hs, cur=cur, dstT=dstT: e.tensor_tensor(out=dstT[:, hs, :], in0=ptu[:].rearrange("p (h c) -> p h c", c=128),
                                                                                                 in1=Tp[cur][:, hs, :], op=ALU.add),
                                 reads=[b_ptu, b_Tp[cur]], writes=[b_dstT])
                        cur = nxt

                def chain(t):
                    i = t % 2
                    if RSUB < 8:
                        return
                    pw_, b_pw = bank()
                    for hd in range(8):
                        hp, h2 = hd // 2, hd % 2
                        pr = slice(h2 * 64, h2 * 64 + 64)
                        cs_ = slice(hd * 64, hd * 64 + 64)
                        p.op("pe", lambda e, hd=hd, cs_=cs_: e.matmul(pw_[:, cs_], lhsT=MM[i][:, hd, 1, 0:128], rhs=VB[i][:, cs_], start=True, stop=False),
                             reads=[b_MM[i], b_VB[i]], writes=[b_pw], sig=False)
                        p.op("pe", lambda e, hd=hd, cs_=cs_: e.matmul(pw_[:, cs_], lhsT=CM[i][:, hd, 0, :], rhs=Sb[:, hd, :], start=False, stop=True),
                             reads=[b_CM[i], b_Sb], writes=[b_pw], sig=(hd == 7))
                    p.op("act", lambda e: e.copy(out=Ws[:], in_=pw_[:]), reads=[b_pw], writes=[b_Ws])
                    pu_, b_pu = bank()
                    for hd in range(8):
                        cs_ = slice(hd * 64, hd * 64 + 64)
                        p.op("pe", lambda e, hd=hd, cs_=cs_: e.matmul(pu_[:, cs_], lhsT=Tf[i][:, hd, :], rhs=Ws[:, cs_], start=True, stop=True),
                             reads=[b_Tf[i], b_Ws], writes=[b_pu], sig=(hd == 7))
                    p.op("dve", lambda e: e.tensor_copy(out=Us[:], in_=pu_[:]), reads=[b_pu], writes=[b_Us])
                    py_, b_py = bank()
                    for hd in range(8):
                        hp, h2 = hd // 2, hd % 2
                        pr = slice(h2 * 64, h2 * 64 + 64)
                        cs_ = slice(hd * 64, hd * 64 + 64)
                        p.op("pe", lambda e, hd=hd, cs_=cs_: e.matmul(py_[:, cs_], lhsT=MM[i][:, hd, 1, 128:256], rhs=VB[i][:, cs_], start=True, stop=False),
                             reads=[b_MM[i], b_VB[i]], writes=[b_py], sig=False)
                        p.op("pe", lambda e, hd=hd, cs_=cs_: e.matmul(py_[:, cs_], lhsT=CM[i][:, hd, 1, :], rhs=Sb[:, hd, :], start=False, stop=False),
                             reads=[b_CM[i], b_Sb], writes=[b_py], sig=False)
                        p.op("pe", lambda e, hd=hd, cs_=cs_: e.matmul(py_[:, cs_], lhsT=MM[i][:, hd, 0, 128:256], rhs=Us[:, cs_], start=False, stop=True),
                             reads=[b_MM[i], b_Us], writes=[b_py], sig=(hd == 7))
                    pS_, b_pS = bank()
                    for hd in range(8):
                        sl = slice(hd * 64, (hd + 1) * 64)
                        p.op("pe", lambda e, sl=sl: e.matmul(pS_[0:64, sl], lhsT=BH[i][:, sl], rhs=Us[:, sl], start=True, stop=False),
                             reads=[b_BH[i], b_Us], writes=[b_pS], sig=False)
                        p.op("pe", lambda e, sl=sl: e.matmul(pS_[0:64, sl], lhsT=KH[i][:, sl], rhs=VB[i][:, sl], start=False, stop=True),
                             reads=[b_KH[i], b_VB[i]], writes=[b_pS], sig=(hd == 7))
                    p.op("act", lambda e: e.copy(out=yt[:], in_=py_[:]), reads=[b_py], writes=[b_yt])
                    for hd in range(8):
                        sl = slice(hd * 64, (hd + 1) * 64)
                        p.op("dve", lambda e, hd=hd, sl=sl: e.scalar_tensor_tensor(out=Sst[:, hd, :], in0=Sst[:, hd, :], scalar=gC[i][:, hd:hd + 1],
                                                                             in1=pS_[0:64, sl], op0=ALU.mult, op1=ALU.add),
                             reads=[b_S, b_gC[i], b_pS], writes=[b_S])
                    p.op("act", lambda e: e.copy(out=Sb[:], in_=Sst[:]), reads=[b_S], writes=[b_Sb])

                def post(t):
                    i = t % 2
                    if RSUB < 9:
                        return
                    pm = PM[i]
                    r_ = pm[:, 0:512]
                    v_ = pm[:, 1024:1536]
                    p.op("dve", lambda e: e.tensor_reduce(out=st8[:], in_=yt[:].rearrange("p (h c) -> p h c", c=64), axis=AX.X, op=ALU.add), reads=[b_yt], writes=[b_st8])
                    p.op("dve", lambda e: e.tensor_scalar(out=st8[:], in0=st8[:], scalar1=1.0 / 64, scalar2=None, op0=ALU.mult), reads=[b_st8], writes=[b_st8])
                    p.op("dve", lambda e: e.tensor_tensor(out=V3(yc), in0=V3(yt), in1=B8(st8), op=ALU.subtract), reads=[b_yt, b_st8], writes=[b_yc])
                    p.op("dve", lambda e: e.tensor_tensor(out=tmp[:], in0=yc[:], in1=yc[:], op=ALU.mult), reads=[b_yc], writes=[b_tmp])
                    p.op("dve", lambda e: e.tensor_reduce(out=st8[:], in_=tmp[:].rearrange("p (h c) -> p h c", c=64), axis=AX.X, op=ALU.add), reads=[b_tmp], writes=[b_st8])
                    p.op("act", lambda e: e.activation(out=st8[:], in_=st8[:], func=AF.Sqrt, scale=1.0 / 64, bias=g.eps_tiles[64e-5][:]), reads=[b_st8, b_eps], writes=[b_st8])
                    p.op("dve", lambda e: e.reciprocal(out=st8[:], in_=st8[:]), reads=[b_st8], writes=[b_st8])
                    p.op("dve", lambda e: e.tensor_tensor(out=V3(yc), in0=V3(yc), in1=B8(st8), op=ALU.mult), reads=[b_yc, b_st8], writes=[b_yc])
                    p.op("dve", lambda e: e.tensor_tensor(out=yc[:], in0=yc[:], in1=ln_w[:], op=ALU.mult), reads=[b_yc, b_lnw], writes=[b_yc])
                    p.op("dve", lambda e: e.tensor_tensor(out=yc[:], in0=yc[:], in1=ln_b[:], op=ALU.add), reads=[b_yc, b_lnb], writes=[b_yc])
                    p.op("dve", lambda e: e.tensor_tensor(out=tmp2[:], in0=r_, in1=k2[i][:], op=ALU.mult), reads=[b_PM[i], b_k2[i]], writes=[b_tmp2])
                    p.op("dve", lambda e: e.tensor_tensor(out=tmp2[:], in0=tmp2[:], in1=r_k[:], op=ALU.mult), reads=[b_tmp2, b_rk_], writes=[b_tmp2])
                    p.op("dve", lambda e: e.tensor_reduce(out=st8[:], in_=tmp2[:].rearrange("p (h c) -> p h c", c=64), axis=AX.X, op=ALU.add), reads=[b_tmp2], writes=[b_st8])
                    for h in range(8):
                        p.op("dve", lambda e, h=h: e.scalar_tensor_tensor(out=H(yc, h), in0=v_[:, h * 64:(h + 1) * 64], scalar=st8[:, h:h + 1], in1=H(yc, h),
                                                                         op0=ALU.mult, op1=ALU.add),
                             reads=[b_PM[i], b_st8, b_yc], writes=[b_yc])
                    p.op("dve", lambda e: e.tensor_tensor(out=yo[:], in0=yc[:], in1=gg[i][:], op=ALU.mult), reads=[b_yc, b_gg[i]], writes=[b_yo])
                    tbk, b_tbk = tbank()
                    for k in range(4):
                        p.op("pe", lambda e, k=k: e.transpose(out=tbk[:, k, :], in_=yo[:, k * 128:(k + 1) * 128], identity=ident[:]),
                             reads=[b_yo, b_ident], writes=[b_tbk], sig=(k == 3))
                    p.op("act", lambda e: e.copy(out=oTt[i][:], in_=tbk[:, 0:4, :]), reads=[b_tbk], writes=[b_oTt[i]])
                    p.dma("sync", lambda e: e.dma_start(out=oT["rwkv"][:, t * 128:(t + 1) * 128].rearrange("(k p) t -> p k t", p=128), in_=oTt[i][:]),
                          reads=[b_oTt[i]], writes=[B_oT["rwkv"][t // 4]])

                NTR = int(os.environ.get('RW_NT', NT))
                pre(0)
                for t in range(NTR):
                    if t + 1 < NTR:
                        pre(t + 1)
                    chain(t)
                    post(t)
                p.barrier()

        p.barrier()
        PH = {"proj": phase_proj, "rwkv": phase_rwkv, "mla": phase_mla, "diff": phase_diff, "merge": phase_merge, "ffn": phase_ffn}
        if phases is None:
            phases = [("consts", 0)]
            for l in range(DEPTH):
                phases += [(n, l) for n in ("proj", "rwkv", "mla", "diff", "merge", "ffn")]
            phases += [("final", 0)]
        for (n, l) in phases:
            if n == "consts":
                setup_attn_consts()
            elif n == "final":
                phase_final()
            else:
                PH[n](l)
        p.barrier()
        p.emit()
    return nc


def prep_inputs(inputs, b):
    f = np.float32
    m = {}
    m["x"] = np.ascontiguousarray(inputs["x"][b])
    pos = np.asarray(inputs["positions"][b]).astype(np.int32)
    m["pos_tm"] = np.ascontiguousarray(pos.reshape(NT, 128).T)
    m["pos_row"] = np.ascontiguousarray(pos[:256].reshape(1, 256))
    m["rel_bias"] = np.ascontiguousarray(np.asarray(inputs["rel_bias"], f).reshape(1, 128))
    for n in VEC_ROWS:
        m[n] = np.ascontiguousarray(np.asarray(inputs[n], f).reshape(DEPTH, VEC_LEN[n]))
    for n in MATS:
        m[n] = np.ascontiguousarray(np.asarray(inputs[n], f))
    m["b_gate_cm"] = np.ascontiguousarray(np.asarray(inputs["b_gate"], f).reshape(DEPTH, 24, 128).transpose(0, 2, 1))
    m["conv_w_cm"] = np.ascontiguousarray(np.asarray(inputs["ffn_conv_w"], f).reshape(DEPTH, 3, 44, 128).transpose(0, 3, 1, 2))
    m["conv_b_cm"] = np.ascontiguousarray(np.asarray(inputs["ffn_conv_b"], f).reshape(DEPTH, 44, 128).transpose(0, 2, 1))
    m["subln_cm"] = np.ascontiguousarray(np.asarray(inputs["diff_subln"], f).reshape(DEPTH, 128, 1))
    m["diff_lambda"] = np.ascontiguousarray(np.asarray(inputs["diff_lambda"], f).reshape(DEPTH, 1, 256))
    m["norm_final"] = np.ascontiguousarray(np.asarray(inputs["norm_final"], f).reshape(1, D))
    return m


_NC = {}


def kernel(**inputs):
    if "nc" not in _NC:
        _NC["nc"] = build()
    nc = _NC["nc"]
    in_maps = [prep_inputs(inputs, b) for b in range(8)]
    res = run_bass_kernel_spmd(nc, in_maps, core_ids=list(range(8)))
    return np.stack([np.asarray(r["out"], np.float32) for r in res.results], axis=0)
```

```python
import math
import os
import numpy as np
from contextlib import ExitStack
import concourse.bass as bass
import concourse.mybir as mybir
from concourse.bass_utils import run_bass_kernel_spmd

F32 = mybir.dt.float32
BF16 = mybir.dt.bfloat16
I32 = mybir.dt.int32
AF = mybir.ActivationFunctionType
ALU = mybir.AluOpType
AX = mybir.AxisListType

S = 4096
NT = 32
D = 1024
DEPTH = 2
DFF = 2816
INC = 6816
C_RW, C_ML, C_DF, C_GT = 0, 1792, 2208, 3744

EPOCH = 24000
NDSEM = 48


class Buf:
    __slots__ = ("w", "r", "name")

    def __init__(self, name=""):
        self.w = {}
        self.r = {}
        self.name = name


class EngState:
    def __init__(self, name):
        self.name = name
        self.ops = []
        self.sem = None
        self.cnt = 0
        self.wm = {}
        self.pending = False


class P:
    def __init__(self, nc, stack):
        self.nc = nc
        self.stack = stack
        self.sems = []
        self.eng = {k: EngState(k) for k in ("sync", "act", "dve", "pool", "pe")}
        self.dsem = []
        self.dval = []
        self.dnext = 0
        for i in range(NDSEM):
            self.dsem.append(self._newsem("d%d" % i))
            self.dval.append(0)
        for e in self.eng.values():
            e.sem = self._newsem(e.name + "0")
        self.nops = 0

    def _newsem(self, name):
        s = self.stack.enter_context(self.nc.semaphore(name))
        self.sems.append(s)
        return len(self.sems) - 1

    def buf(self, name=""):
        return Buf(name)

    def bufs(self, n, name=""):
        return [Buf(name + str(i)) for i in range(n)]

    def _need(self, es, waits, sem, val):
        if es.wm.get(sem, 0) >= val:
            return
        es.wm[sem] = val
        for i, (s, v) in enumerate(waits):
            if s == sem:
                waits[i] = (s, max(v, val))
                return
        waits.append((sem, val))

    def _deps(self, es, mysem, waits, reads, writes):
        for b in reads:
            for s, v in b.w.items():
                self._need(es, waits, s, v)
        skip_own = (es.name == "pe")
        for b in writes:
            for s, v in b.w.items():
                if s != mysem or not skip_own:
                    self._need(es, waits, s, v)
            for s, v in b.r.items():
                if s != mysem or not skip_own:
                    self._need(es, waits, s, v)

    def _mark(self, sem, val, reads, writes):
        for b in reads:
            if b.r.get(sem, 0) < val:
                b.r[sem] = val
        for b in writes:
            b.w = {sem: val}
            b.r = {}

    def op(self, eng, fn, reads=(), writes=(), sig=True):
        es = self.eng[eng]
        if es.cnt >= EPOCH and not es.pending:
            es.sem = self._newsem(es.name + str(len(self.sems)))
            es.cnt = 0
        waits = []
        self._deps(es, es.sem, waits, reads, writes)
        val = es.cnt + 1
        if sig:
            es.cnt = val
            es.pending = False
        else:
            es.pending = True
        es.ops.append((waits, fn, es.sem if sig else None, 1))
        self._mark(es.sem, val, reads, writes)
        self.nops += 1

    def dma(self, q, fn, reads=(), writes=()):
        es = self.eng[q]
        i = self.dnext
        self.dnext = (self.dnext + 1) % NDSEM
        sem = self.dsem[i]
        waits = []
        if self.dval[i] > 0:
            self._need(es, waits, sem, self.dval[i])
        self._deps(es, sem, waits, reads, writes)
        self.dval[i] += 16
        es.ops.append((waits, fn, sem, 16))
        self._mark(sem, self.dval[i], reads, writes)
        self.nops += 1

    def barrier(self):
        targets = []
        for e in self.eng.values():
            if e.cnt > 0:
                assert not e.pending
                targets.append((e.sem, e.cnt))
        for i in range(NDSEM):
            if self.dval[i] > 0:
                targets.append((self.dsem[i], self.dval[i]))
        for es in self.eng.values():
            waits = []
            for s, v in targets:
                if s != es.sem:
                    self._need(es, waits, s, v)
            if waits:
                es.ops.append((waits, None, None, 0))
        self.emit()

    def emit(self):
        nc = self.nc
        sems = self.sems
        engs = self.eng

        def run(e, es):
            for waits, fn, sem, inc in es.ops:
                for s, v in waits:
                    e.wait_ge(sems[s], v)
                if fn is None:
                    continue
                ins = fn(e)
                if sem is not None:
                    ins.then_inc(sems[sem], inc)

        if not any(es.ops for es in engs.values()):
            return
        with nc.Block() as block:
            @block.sync
            def _(e):
                run(e, engs["sync"])

            @block.scalar
            def _(e):
                run(e, engs["act"])

            @block.vector
            def _(e):
                run(e, engs["dve"])

            @block.gpsimd
            def _(e):
                run(e, engs["pool"])

            @block.tensor
            def _(e):
                run(e, engs["pe"])
        for es in engs.values():
            es.ops = []


VEC_ROWS = ["norm_mix", "rwkv_mu", "rwkv_w0", "rwkv_a0", "rwkv_k_k", "rwkv_k_a", "rwkv_r_k",
            "rwkv_ln_w", "rwkv_ln_b", "mla_q_norm", "mla_kv_norm", "norm_ffn", "b_gate"]
VEC_LEN = {"norm_mix": 1024, "rwkv_mu": 1792, "rwkv_w0": 512, "rwkv_a0": 512, "rwkv_k_k": 512, "rwkv_k_a": 512,
           "rwkv_r_k": 512, "rwkv_ln_w": 512, "rwkv_ln_b": 512, "mla_q_norm": 256, "mla_kv_norm": 128,
           "norm_ffn": 1024, "b_gate": 3072}
MATS = {"w_in": (1024, INC), "rwkv_w2": (64, 512), "rwkv_a2": (64, 512), "rwkv_g2": (128, 512),
        "mla_w_uq": (256, 768), "mla_w_ukv": (128, 1024), "w_branch_rwkv": (512, 1024), "w_branch_mla": (512, 1024),
        "w_branch_diff": (512, 1024), "w_o": (1024, 1024), "ffn_w_up": (1024, 2 * DFF), "ffn_w_down": (DFF, 1024)}


class K:
    pass


def build(dbg=None, feed=None, phases=None):
    dbg = dbg or set()
    feed = feed or set()
    nc = bass.Bass("TRN2", target_bir_lowering=False)
    g = K()
    g.nc = nc

    def din(name, shape, dt=F32):
        return nc.dram_tensor(name, list(shape), dt, kind="ExternalInput").ap()

    def dscr(name, shape, dt=F32):
        kind = "ExternalOutput" if name in dbg else ("ExternalInput" if name in feed else "Internal")
        return nc.dram_tensor(name, list(shape), dt, kind=kind).ap()

    x_in = din("x", [S, D])
    pos_tm = din("pos_tm", [128, NT], I32)
    pos_row = din("pos_row", [1, 256], I32)
    rel_bias = din("rel_bias", [1, 128])
    vec = {n: din(n, [DEPTH, VEC_LEN[n]]) for n in VEC_ROWS}
    mats = {n: din(n, [DEPTH, MATS[n][0], MATS[n][1]]) for n in MATS}
    b_gate_cm = din("b_gate_cm", [DEPTH, 128, 24])
    conv_w_cm = din("conv_w_cm", [DEPTH, 128, 3, 44])
    conv_b_cm = din("conv_b_cm", [DEPTH, 128, 44])
    subln_cm = din("subln_cm", [DEPTH, 128, 1])
    diff_lambda = din("diff_lambda", [DEPTH, 1, 256])
    norm_final = din("norm_final", [1, D])
    out = nc.dram_tensor("out", [S, D], F32, kind="ExternalOutput").ap()

    xres = dscr("xres", [S, D])
    prw = dscr("prw", [S, 1792])
    pml = dscr("pml", [S, 416])
    qkT = dscr("qkT", [1024, S], BF16)
    vdf = dscr("vdf", [S, 512], BF16)
    gT = dscr("gT", [3072, S], BF16)
    oT = {n: dscr("oT_" + n, [512, S], BF16) for n in ("rwkv", "mla", "diff")}

    with ExitStack() as top:
        p = P(nc, top)
        g.p = p

        uid = [0]

        def sb(st, name, shape, dt):
            uid[0] += 1
            return st.enter_context(nc.sbuf_tensor("%s_u%d" % (name, uid[0]), list(shape), dt))

        def ps(st, name, shape, dt=F32):
            uid[0] += 1
            return st.enter_context(nc.psum_tensor("%s_u%d" % (name, uid[0]), list(shape), dt))

        B_x = p.bufs(NT, "x")
        B_prw = p.bufs(NT, "prw")
        B_pml = p.bufs(NT, "pml")
        B_qkT = p.bufs(8, "qkT")
        B_vdf = p.bufs(NT, "vdf")
        B_gT = p.bufs(8, "gT")
        B_oT = {n: p.bufs(8, "oT" + n) for n in oT}
        B_out = p.buf("out")

        ident = sb(top, "ident", [128, 128], BF16)
        b_ident = p.buf()
        p.op("pool", lambda e: e.memset(ident[:], 1.0), writes=[b_ident])
        p.op("pool", lambda e: e.affine_select(out=ident[:], in_=ident[:], pattern=[[-1, 128]],
                                               compare_op=ALU.is_equal, fill=0.0, base=0, channel_multiplier=1),
             reads=[b_ident], writes=[b_ident])
        ones_bf = sb(top, "ones_bf", [128, 128], BF16)
        ones_f = sb(top, "ones_f", [128, 128], F32)
        b_ones = p.buf()
        p.op("pool", lambda e: e.memset(ones_bf[:], 1.0), writes=[b_ones])
        p.op("pool", lambda e: e.memset(ones_f[:], 1.0), writes=[b_ones])
        g.ident, g.b_ident, g.ones_bf, g.ones_f, g.b_ones = ident, b_ident, ones_bf, ones_f, b_ones

        def norm_transpose_phase(st, src_ap_fn, src_bufs, gvec_ap, hT, b_hT, eps=1e-6):
            gt = sb(st, "nt_g", [128, D], F32)
            b_g = p.buf()
            p.dma("sync", lambda e: e.dma_start(out=gt[:], in_=gvec_ap.partition_broadcast(128)), writes=[b_g])
            xt = [sb(st, "nt_x%d" % i, [128, D], F32) for i in range(2)]
            b_xt = p.bufs(2)
            junk = sb(st, "nt_junk", [128, D], BF16)
            b_junk = p.buf()
            ss = [sb(st, "nt_ss%d" % i, [128, 1], F32) for i in range(2)]
            b_ss = p.bufs(2)
            hb = [sb(st, "nt_hb%d" % i, [128, D], BF16) for i in range(2)]
            b_hb = p.bufs(2)
            pt = [ps(st, "nt_pt%d" % i, [128, 8, 128], BF16) for i in range(2)]
            b_pt = p.bufs(2)
            for t in range(NT):
                i = t % 2
                p.dma("sync", lambda e, t=t, i=i: e.dma_start(out=xt[i][:], in_=src_ap_fn(t)),
                      reads=[src_bufs[t]], writes=[b_xt[i]])
                p.op("pool", lambda e, i=i: e.memset(ss[i][:], 0.0), writes=[b_ss[i]])
                p.op("act", lambda e, i=i: e.activation(out=junk[:], in_=xt[i][:], func=AF.Square, accum_out=ss[i][:]),
                     reads=[b_xt[i], b_ss[i]], writes=[b_junk, b_ss[i]])
                p.op("act", lambda e, i=i: e.activation(out=ss[i][:], in_=ss[i][:], func=AF.Sqrt, scale=1.0 / D, bias=g.eps_tiles[eps][:]),
                     reads=[b_ss[i]], writes=[b_ss[i]])
                p.op("dve", lambda e, i=i: e.reciprocal(out=ss[i][:], in_=ss[i][:]), reads=[b_ss[i]], writes=[b_ss[i]])
                p.op("dve", lambda e, i=i: e.scalar_tensor_tensor(out=hb[i][:], in0=xt[i][:], scalar=ss[i][:, 0:1], in1=gt[:],
                                                                  op0=ALU.mult, op1=ALU.mult),
                     reads=[b_xt[i], b_ss[i], b_g], writes=[b_hb[i]])
                for k in range(8):
                    p.op("pe", lambda e, i=i, k=k: e.transpose(out=pt[i][:, k, :], in_=hb[i][:, k * 128:(k + 1) * 128], identity=ident[:]),
                         reads=[b_hb[i], b_ident], writes=[b_pt[i]], sig=(k == 7))
                p.op("act", lambda e, i=i, t=t: e.copy(out=hT[:, :, t * 128:(t + 1) * 128], in_=pt[i][:]),
                     reads=[b_pt[i]], writes=[b_hT[t]])

        g.eps_tiles = {}
        b_eps = p.buf()
        for ev in (1e-6, 1e-5, 64e-5, 0.0, 1.0):
            tl = sb(top, "eps%d" % len(g.eps_tiles), [128, 1], F32)
            p.op("pool", lambda e, tl=tl, ev=ev: e.memset(tl[:], ev), writes=[b_eps])
            g.eps_tiles[ev] = tl

        g.wstg = [sb(top, "wstg%d" % i, [128, 8, 512], F32) for i in range(2)]
        g.b_wstg = p.bufs(2)
        g.nstg = [0]

        class WLoader:
            def __init__(self, st, name, kch, maxcol, nbuf=2):
                self.wb = [sb(st, name + "_b%d" % i, [128, kch, maxcol], BF16) for i in range(nbuf)]
                self.b_wb = p.bufs(nbuf)
                self.n = 0
                self.nbuf = nbuf
                self.kch = kch

            def load(self, wap, c0, ncol, krows=None):
                i = self.n % self.nbuf
                self.n += 1
                kch = self.kch
                si = g.nstg[0] % 2
                g.nstg[0] += 1
                stg, wb = g.wstg[si], self.wb[i]
                p.dma("sync", lambda e: e.dma_start(out=stg[:, 0:kch, 0:ncol],
                                                    in_=wap[:, c0:c0 + ncol].rearrange("(k p) n -> p k n", p=128)),
                      writes=[g.b_wstg[si]])
                p.op("pool", lambda e: e.tensor_copy(out=wb[:, :, 0:ncol], in_=stg[:, 0:kch, 0:ncol]),
                     reads=[g.b_wstg[si]], writes=[self.b_wb[i]])
                return wb, self.b_wb[i]

        def phase_proj(l):
            with ExitStack() as st:
                hT = sb(st, "hT", [128, 8, S], BF16)
                b_hT = p.bufs(NT)
                with ExitStack() as st2:
                    if l == 0:
                        norm_transpose_phase(st2, lambda t: x_in[t * 128:(t + 1) * 128, :], B_x, vec["norm_mix"][l:l + 1, :], hT, b_hT)
                    else:
                        norm_transpose_phase(st2, lambda t: xres[t * 128:(t + 1) * 128, :], B_x, vec["norm_mix"][l:l + 1, :], hT, b_hT)
                    p.barrier()
                wl = WLoader(st, "wl", 8, 512)
                pp = [ps(st, "pp%d" % i, [128, 512], F32) for i in range(4)]
                b_pp = p.bufs(4)
                ostf = [sb(st, "ostf%d" % i, [128, 512], F32) for i in range(4)]
                ostb = [sb(st, "ostb%d" % i, [128, 512], BF16) for i in range(4)]
                b_ost = p.bufs(4)
                bg = sb(st, "bgcm", [128, 24], F32)
                b_bg = p.buf()
                p.dma("sync", lambda e: e.dma_start(out=bg[:], in_=b_gate_cm[l]), writes=[b_bg])
                cnt = [0]
                win = mats["w_in"][l]

                def tok_major(c0, ncol, dst_fn, dst_bufs, bf):
                    wb, b_wb = wl.load(win, c0, ncol)
                    for t in range(NT):
                        i = cnt[0] % 4
                        cnt[0] += 1
                        for k in range(8):
                            p.op("pe", lambda e, i=i, k=k, t=t: e.matmul(pp[i][:, 0:ncol], lhsT=hT[:, k, t * 128:(t + 1) * 128],
                                                                         rhs=wb[:, k, 0:ncol], start=(k == 0), stop=(k == 7)),
                                 reads=[b_hT[t], b_wb], writes=[b_pp[i]], sig=(k == 7))
                        o = ostb[i] if bf else ostf[i]
                        eng = "act" if (cnt[0] % 2) else "dve"
                        if eng == "act":
                            p.op("act", lambda e, i=i, o=o: e.copy(out=o[:, 0:ncol], in_=pp[i][:, 0:ncol]), reads=[b_pp[i]], writes=[b_ost[i]])
                        else:
                            p.op("dve", lambda e, i=i, o=o: e.tensor_copy(out=o[:, 0:ncol], in_=pp[i][:, 0:ncol]), reads=[b_pp[i]], writes=[b_ost[i]])
                        p.dma("sync", lambda e, o=o, t=t: e.dma_start(out=dst_fn(t), in_=o[:, 0:ncol]), reads=[b_ost[i]], writes=[dst_bufs[t]])

                def ch_major(c0, nchunk, dst, dst_row0, dst_bufs, gate_idx0=None):
                    for cc0 in range(0, nchunk, 4):
                        ncc = min(4, nchunk - cc0)
                        wb, b_wb = wl.load(win, c0 + cc0 * 128, ncc * 128)
                        for cc in range(ncc):
                            for tq in range(8):
                                i = cnt[0] % 4
                                cnt[0] += 1
                                for k in range(8):
                                    p.op("pe", lambda e, i=i, k=k, tq=tq, cc=cc, wb=wb: e.matmul(pp[i][:, :], lhsT=wb[:, k, cc * 128:(cc + 1) * 128],
                                                                                          rhs=hT[:, k, tq * 512:(tq + 1) * 512], start=(k == 0), stop=(k == 7)),
                                         reads=[b_hT[tq * 4 + j] for j in range(4)] + [b_wb], writes=[b_pp[i]], sig=(k == 7))
                                o = ostb[i]
                                if gate_idx0 is not None:
                                    gi = gate_idx0 + cc0 + cc
                                    p.op("act", lambda e, i=i, o=o, gi=gi: e.activation(out=o[:], in_=pp[i][:], func=AF.Sigmoid, bias=bg[:, gi:gi + 1]),
                                         reads=[b_pp[i], b_bg], writes=[b_ost[i]])
                                else:
                                    p.op("dve", lambda e, i=i, o=o: e.tensor_copy(out=o[:], in_=pp[i][:]), reads=[b_pp[i]], writes=[b_ost[i]])
                                r0 = dst_row0 + (cc0 + cc) * 128
                                p.dma("sync", lambda e, o=o, r0=r0, tq=tq: e.dma_start(out=dst[r0:r0 + 128, tq * 512:(tq + 1) * 512], in_=o[:]),
                                      reads=[b_ost[i]], writes=[dst_bufs[tq]])

                for c0, ncol in ((0, 512), (512, 512), (1024, 512), (1536, 256)):
                    tok_major(C_RW + c0, ncol, lambda t, c0=c0, ncol=ncol: prw[t * 128:(t + 1) * 128, c0:c0 + ncol], B_prw, False)
                tok_major(C_ML, 416, lambda t: pml[t * 128:(t + 1) * 128, :], B_pml, False)
                ch_major(C_DF, 8, qkT, 0, B_qkT)
                tok_major(C_DF + 1024, 512, lambda t: vdf[t * 128:(t + 1) * 128, :], B_vdf, True)
                ch_major(C_GT, 24, gT, 0, B_gT, gate_idx0=0)
                p.barrier()

        def load_bcast(st, name, ap_row, n):
            t = sb(st, name, [128, n], F32)
            b = p.buf()
            p.dma("sync", lambda e: e.dma_start(out=t[:], in_=ap_row.partition_broadcast(128)), writes=[b])
            return t, b

        def load_weight_bf(st, name, wap, krows, ncols, part0=0):
            kch = max(1, krows // 128)
            pr = min(128, krows)
            wbt = sb(st, name, [128, kch, ncols], BF16)
            b_w = p.buf()
            stg = g.wstg
            b_stg = g.b_wstg
            cs = 512 if kch <= 8 else 128
            for c0 in range(0, ncols, cs):
                nc_ = min(cs, ncols - c0)
                i = g.nstg[0] % 2
                g.nstg[0] += 1
                if krows >= 128 and kch <= 8:
                    src = wap[:, c0:c0 + nc_].rearrange("(k p) n -> p k n", p=128)
                    p.dma("sync", lambda e, i=i, src=src, nc_=nc_: e.dma_start(out=stg[i][:, 0:kch, 0:nc_], in_=src), writes=[b_stg[i]])
                    p.op("pool", lambda e, i=i, c0=c0, nc_=nc_: e.tensor_copy(out=wbt[:, :, c0:c0 + nc_], in_=stg[i][:, 0:kch, 0:nc_]),
                         reads=[b_stg[i]], writes=[b_w])
                elif krows >= 128:
                    sv = stg[i][:].rearrange("p a b -> p (a b)")[:, 0:kch * 128].rearrange("p (k n) -> p k n", n=128)
                    src = wap[:, c0:c0 + nc_].rearrange("(k p) n -> p k n", p=128)
                    p.dma("sync", lambda e, sv=sv, src=src, nc_=nc_: e.dma_start(out=sv[:, :, 0:nc_], in_=src), writes=[b_stg[i]])
                    p.op("pool", lambda e, sv=sv, c0=c0, nc_=nc_: e.tensor_copy(out=wbt[:, :, c0:c0 + nc_], in_=sv[:, :, 0:nc_]),
                         reads=[b_stg[i]], writes=[b_w])
                else:
                    src = wap[:, c0:c0 + nc_]
                    p.dma("sync", lambda e, i=i, src=src, nc_=nc_: e.dma_start(out=stg[i][part0:part0 + pr, 0, 0:nc_], in_=src), writes=[b_stg[i]])
                    p.op("pool", lambda e, i=i, c0=c0, nc_=nc_: e.tensor_copy(out=wbt[part0:part0 + pr, 0, c0:c0 + nc_], in_=stg[i][part0:part0 + pr, 0, 0:nc_]),
                         reads=[b_stg[i]], writes=[b_w])
            return wbt, b_w

        def xsrc(l, t):
            return (x_in if l == 0 else xres)[t * 128:(t + 1) * 128, :]

        def phase_merge(l):
            with ExitStack() as st:
                wbr = []
                for n in ("rwkv", "mla", "diff"):
                    wbr.append(load_weight_bf(st, "wbr_" + n, mats["w_branch_" + n][l], 512, 1024))
                wo, b_wo = load_weight_bf(st, "wo", mats["w_o"][l], 1024, 1024)
                oc = [[sb(st, "oc%d_%d" % (i, j), [128, 4, 512], BF16) for j in range(3)] for i in range(2)]
                b_oc = [p.bufs(3) for i in range(2)]
                gc = [sb(st, "gc%d" % i, [128, 24, 512], BF16) for i in range(2)]
                b_gc = p.bufs(2)
                mT = [sb(st, "mT%d" % i, [128, 8, 512], BF16) for i in range(2)]
                b_mT = p.bufs(2)
                pb = [ps(st, "mpb%d" % i, [128, 512], F32) for i in range(6)]
                b_pb = p.bufs(6)
                m0 = [sb(st, "m0_%d" % i, [128, 512], F32) for i in range(2)]
                m1 = [sb(st, "m1_%d" % i, [128, 512], F32) for i in range(2)]
                b_m0 = p.bufs(2)
                b_m1 = p.bufs(2)
                xt = [sb(st, "mxt%d" % i, [128, D], F32) for i in range(2)]
                b_xt = p.bufs(2)
                po = [ps(st, "mpo%d" % i, [128, 512], F32) for i in range(2)]
                b_po = p.bufs(2)
                names = ("rwkv", "mla", "diff")
                nd = 0
                nx = 0
                npo = 0
                for c in range(8):
                    ci = c % 2
                    for j, n in enumerate(names):
                        p.dma("sync", lambda e, ci=ci, j=j, n=n, c=c: e.dma_start(
                            out=oc[ci][j][:], in_=oT[n][:, c * 512:(c + 1) * 512].rearrange("(k p) t -> p k t", p=128)),
                            reads=[B_oT[n][c]], writes=[b_oc[ci][j]])
                    p.dma("sync", lambda e, ci=ci, c=c: e.dma_start(
                        out=gc[ci][:], in_=gT[:, c * 512:(c + 1) * 512].rearrange("(k p) t -> p k t", p=128)),
                        reads=[B_gT[c]], writes=[b_gc[ci]])
                    for dc in range(8):
                        di = nd % 2
                        nd += 1
                        for j in range(3):
                            pj = di * 3 + j
                            w_j, b_wj = wbr[j]
                            for k in range(4):
                                p.op("pe", lambda e, pj=pj, k=k, dc=dc, ci=ci, j=j, w_j=w_j: e.matmul(
                                    pb[pj][:], lhsT=w_j[:, k, dc * 128:(dc + 1) * 128], rhs=oc[ci][j][:, k, :], start=(k == 0), stop=(k == 3)),
                                    reads=[b_wj, b_oc[ci][j]], writes=[b_pb[pj]], sig=(k == 3))
                        p.op("dve", lambda e, di=di, ci=ci, dc=dc: e.tensor_tensor(out=m0[di][:], in0=pb[di * 3][:], in1=gc[ci][:, dc, :], op=ALU.mult),
                             reads=[b_pb[di * 3], b_gc[ci]], writes=[b_m0[di]])
                        p.op("dve", lambda e, di=di, ci=ci, dc=dc: e.tensor_tensor(out=m1[di][:], in0=pb[di * 3 + 1][:], in1=gc[ci][:, 8 + dc, :], op=ALU.mult),
                             reads=[b_pb[di * 3 + 1], b_gc[ci]], writes=[b_m1[di]])
                        p.op("dve", lambda e, di=di: e.tensor_tensor(out=m0[di][:], in0=m0[di][:], in1=m1[di][:], op=ALU.add),
                             reads=[b_m0[di], b_m1[di]], writes=[b_m0[di]])
                        p.op("dve", lambda e, di=di, ci=ci, dc=dc: e.tensor_tensor(out=m1[di][:], in0=pb[di * 3 + 2][:], in1=gc[ci][:, 16 + dc, :], op=ALU.mult),
                             reads=[b_pb[di * 3 + 2], b_gc[ci]], writes=[b_m1[di]])
                        p.op("dve", lambda e, di=di, ci=ci, dc=dc: e.tensor_tensor(out=mT[ci][:, dc, :], in0=m0[di][:], in1=m1[di][:], op=ALU.add),
                             reads=[b_m0[di], b_m1[di]], writes=[b_mT[ci]])
                    for tt in range(4):
                        t = c * 4 + tt
                        xi = nx % 2
                        nx += 1
                        p.dma("sync", lambda e, xi=xi, t=t: e.dma_start(out=xt[xi][:], in_=xsrc(l, t)), reads=[B_x[t]], writes=[b_xt[xi]])
                        for hf in range(2):
                            pi = npo % 2
                            npo += 1
                            for k in range(8):
                                p.op("pe", lambda e, pi=pi, k=k, ci=ci, tt=tt, hf=hf: e.matmul(
                                    po[pi][:], lhsT=mT[ci][:, k, tt * 128:(tt + 1) * 128], rhs=wo[:, k, hf * 512:(hf + 1) * 512], start=(k == 0), stop=(k == 7)),
                                    reads=[b_mT[ci], b_wo], writes=[b_po[pi]], sig=(k == 7))
                            p.op("dve", lambda e, pi=pi, xi=xi, hf=hf: e.tensor_tensor(out=xt[xi][:, hf * 512:(hf + 1) * 512], in0=po[pi][:],
                                                                                 in1=xt[xi][:, hf * 512:(hf + 1) * 512], op=ALU.add),
                                 reads=[b_po[pi], b_xt[xi]], writes=[b_xt[xi]])
                        p.dma("sync", lambda e, xi=xi, t=t: e.dma_start(out=xres[t * 128:(t + 1) * 128, :], in_=xt[xi][:]),
                              reads=[b_xt[xi]], writes=[B_x[t]])
                p.barrier()

        def phase_ffn(l):
            G = 512
            NG = S // G
            TG = G // 128
            with ExitStack() as st:
                wdn, b_wdn = load_weight_bf(st, "wdn", mats["ffn_w_down"][l], DFF, 1024)
                gt, b_g = load_bcast(st, "f_g", vec["norm_ffn"][l:l + 1, :], D)
                cw = sb(st, "f_cw", [128, 3, 44], F32)
                cb = sb(st, "f_cb", [128, 44], F32)
                b_cw = p.buf()
                p.dma("sync", lambda e: e.dma_start(out=cw[:], in_=conv_w_cm[l]), writes=[b_cw])
                p.dma("sync", lambda e: e.dma_start(out=cb[:], in_=conv_b_cm[l]), writes=[b_cw])
                halo = sb(st, "f_halo", [128, 44, 2], F32)
                b_halo = p.bufs(44)
                p.op("pool", lambda e: e.memset(halo[:], 0.0), writes=b_halo)
                xg = sb(st, "f_xg", [128, TG, D], F32)
                b_xg = p.bufs(TG)
                hTg = sb(st, "f_hT", [128, 8, G], BF16)
                b_hTg = p.bufs(TG)
                junk = sb(st, "f_junk", [128, D], BF16)
                b_junk = p.buf()
                ss = [sb(st, "f_ss%d" % i, [128, 1], F32) for i in range(2)]
                b_ss = p.bufs(2)
                hb = [sb(st, "f_hb%d" % i, [128, D], BF16) for i in range(2)]
                b_hb = p.bufs(2)
                pt = [ps(st, "f_pt%d" % i, [128, 8, 128], BF16) for i in range(2)]
                b_pt = p.bufs(2)
                wubg = [sb(st, "f_wubg%d" % i, [128, 8, 512], BF16) for i in range(2)]
                wubv = [sb(st, "f_wubv%d" % i, [128, 8, 512], BF16) for i in range(2)]
                b_wubg = p.bufs(2)
                b_wubv = p.bufs(2)
                pu = [ps(st, "f_pu%d" % i, [128, 512], F32) for i in range(4)]
                b_pu = p.bufs(4)
                ug = [sb(st, "f_ug%d" % i, [128, G + 2], F32) for i in range(2)]
                uv = [sb(st, "f_uv%d" % i, [128, G + 2], F32) for i in range(2)]
                b_ug = p.bufs(2)
                b_uv = p.bufs(2)
                cg = [sb(st, "f_cg%d" % i, [128, G], F32) for i in range(2)]
                cv = [sb(st, "f_cv%d" % i, [128, G], F32) for i in range(2)]
                b_cg = p.bufs(2)
                b_cv = p.bufs(2)
                actT = sb(st, "f_actT", [128, 22, G], BF16)
                b_act = p.bufs(22)
                pd = [ps(st, "f_pd%d" % i, [128, 512], F32) for i in range(2)]
                b_pd = p.bufs(2)
                wup = mats["ffn_w_up"][l]
                nw = 0
                npu = 0
                npd = 0
                NF_ = int(os.environ.get('FFN_NF', 22))
                for gi in range(int(os.environ.get('FFN_NG', NG))):
                    for tt in range(TG):
                        t = gi * TG + tt
                        i = tt % 2
                        p.dma("sync", lambda e, tt=tt, t=t: e.dma_start(out=xg[:, tt, :], in_=xres[t * 128:(t + 1) * 128, :]),
                              reads=[B_x[t]], writes=[b_xg[tt]])
                        p.op("pool", lambda e, i=i: e.memset(ss[i][:], 0.0), writes=[b_ss[i]])
                        p.op("act", lambda e, i=i, tt=tt: e.activation(out=junk[:], in_=xg[:, tt, :], func=AF.Square, accum_out=ss[i][:]),
                             reads=[b_xg[tt], b_ss[i]], writes=[b_junk, b_ss[i]])
                        p.op("act", lambda e, i=i: e.activation(out=ss[i][:], in_=ss[i][:], func=AF.Sqrt, scale=1.0 / D, bias=g.eps_tiles[1e-6][:]),
                             reads=[b_ss[i], b_eps], writes=[b_ss[i]])
                        p.op("dve", lambda e, i=i: e.reciprocal(out=ss[i][:], in_=ss[i][:]), reads=[b_ss[i]], writes=[b_ss[i]])
                        p.op("dve", lambda e, i=i, tt=tt: e.scalar_tensor_tensor(out=hb[i][:], in0=xg[:, tt, :], scalar=ss[i][:, 0:1], in1=gt[:],
                                                                             op0=ALU.mult, op1=ALU.mult),
                             reads=[b_xg[tt], b_ss[i], b_g], writes=[b_hb[i]])
                        for k in range(8):
                            p.op("pe", lambda e, i=i, k=k: e.transpose(out=pt[i][:, k, :], in_=hb[i][:, k * 128:(k + 1) * 128], identity=ident[:]),
                                 reads=[b_hb[i], b_ident], writes=[b_pt[i]], sig=(k == 7))
                        p.op("act", lambda e, i=i, tt=tt: e.copy(out=hTg[:, :, tt * 128:(tt + 1) * 128], in_=pt[i][:]),
                             reads=[b_pt[i]], writes=[b_hTg[tt]])
                    for f in range(NF_):
                        fb = (f // 4) * 4
                        if f == fb:
                            nfb = min(4, 22 - fb)
                            wi = nw % 2
                            nw += 1
                            for (dstw, b_dstw, cbase) in ((wubg[wi], b_wubg[wi], fb * 128), (wubv[wi], b_wubv[wi], DFF + fb * 128)):
                                si = g.nstg[0] % 2
                                g.nstg[0] += 1
                                p.dma("sync", lambda e, si=si, cbase=cbase, nfb=nfb: e.dma_start(out=g.wstg[si][:, :, 0:nfb * 128],
                                                                                          in_=wup[:, cbase:cbase + nfb * 128].rearrange("(k p) n -> p k n", p=128)),
                                      writes=[g.b_wstg[si]])
                                p.op("pool", lambda e, si=si, dstw=dstw, nfb=nfb: e.tensor_copy(out=dstw[:, :, 0:nfb * 128], in_=g.wstg[si][:, :, 0:nfb * 128]),
                                     reads=[g.b_wstg[si]], writes=[b_dstw])
                        fo = (f - fb) * 128
                        ui = f % 2
                        p.op("pool", lambda e, ui=ui, f=f: e.tensor_copy(out=ug[ui][:, 0:2], in_=halo[:, f, :]), reads=[b_halo[f]], writes=[b_ug[ui]])
                        p.op("pool", lambda e, ui=ui, f=f: e.tensor_copy(out=uv[ui][:, 0:2], in_=halo[:, 22 + f, :]), reads=[b_halo[22 + f]], writes=[b_uv[ui]])
                        for gv in range(2):
                            for hf in range(G // 512):
                                pi = npu % 4
                                npu += 1
                                for k in range(8):
                                    wsrc = wubg[wi] if gv == 0 else wubv[wi]
                                    b_wsrc = b_wubg[wi] if gv == 0 else b_wubv[wi]
                                    p.op("pe", lambda e, pi=pi, k=k, wsrc=wsrc, fo=fo, hf=hf: e.matmul(
                                        pu[pi][:], lhsT=wsrc[:, k, fo:fo + 128], rhs=hTg[:, k, hf * 512:(hf + 1) * 512],
                                        start=(k == 0), stop=(k == 7)),
                                        reads=[b_wsrc] + [b_hTg[hf * 4 + j] for j in range(4)], writes=[b_pu[pi]], sig=(k == 7))
                                dst = ug[ui] if gv == 0 else uv[ui]
                                b_dst = b_ug[ui] if gv == 0 else b_uv[ui]
                                cdst0 = cg[ui] if gv == 0 else cv[ui]
                                b_c0 = b_cg[ui] if gv == 0 else b_cv[ui]
                                ch0 = f if gv == 0 else 22 + f
                                p.op("act", lambda e, pi=pi, dst=dst, hf=hf: e.copy(out=dst[:, 2 + hf * 512:2 + (hf + 1) * 512], in_=pu[pi][:]),
                                     reads=[b_pu[pi]], writes=[b_dst])
                                p.op("act", lambda e, pi=pi, cdst0=cdst0, hf=hf, ch0=ch0: e.activation(out=cdst0[:, hf * 512:(hf + 1) * 512], in_=pu[pi][:], func=AF.Identity,
                                                                                                  scale=cw[:, 2, ch0:ch0 + 1], bias=cb[:, ch0:ch0 + 1]),
                                     reads=[b_pu[pi], b_cw], writes=[b_c0])
                        p.op("pool", lambda e, ui=ui, f=f: e.tensor_copy(out=halo[:, f, :], in_=ug[ui][:, G:G + 2]), reads=[b_ug[ui]], writes=[b_halo[f]])
                        p.op("pool", lambda e, ui=ui, f=f: e.tensor_copy(out=halo[:, 22 + f, :], in_=uv[ui][:, G:G + 2]), reads=[b_uv[ui]], writes=[b_halo[22 + f]])
                        for eng, u, b_u, cdst, b_c, ch in (("dve", ug[ui], b_ug[ui], cg[ui], b_cg[ui], f), ("dve", uv[ui], b_uv[ui], cv[ui], b_cv[ui], 22 + f)):
                            p.op(eng, lambda e, u=u, cdst=cdst, ch=ch: e.scalar_tensor_tensor(out=cdst[:], in0=u[:, 1:G + 1], scalar=cw[:, 1, ch:ch + 1],
                                                                                         in1=cdst[:], op0=ALU.mult, op1=ALU.add),
                                 reads=[b_u, b_cw, b_c], writes=[b_c])
                            p.op(eng, lambda e, u=u, cdst=cdst, ch=ch: e.scalar_tensor_tensor(out=cdst[:], in0=u[:, 0:G], scalar=cw[:, 0, ch:ch + 1],
                                                                                         in1=cdst[:], op0=ALU.mult, op1=ALU.add),
                                 reads=[b_u, b_cw, b_c], writes=[b_c])
                        p.op("act", lambda e, ui=ui: e.activation(out=ug[ui][:, 2:G + 2], in_=cg[ui][:], func=AF.Silu),
                             reads=[b_cg[ui]], writes=[b_ug[ui]])
                        p.op("dve", lambda e, ui=ui, f=f: e.tensor_tensor(out=actT[:, f, :], in0=ug[ui][:, 2:G + 2], in1=cv[ui][:], op=ALU.mult),
                             reads=[b_ug[ui], b_cv[ui]], writes=[b_act[f]])
                    for tt in range(TG):
                        t = gi * TG + tt
                        for hf in range(2):
                            pi = npd % 2
                            npd += 1
                            for f in range(NF_):
                                p.op("pe", lambda e, pi=pi, f=f, tt=tt, hf=hf: e.matmul(
                                    pd[pi][:], lhsT=actT[:, f, tt * 128:(tt + 1) * 128], rhs=wdn[:, f, hf * 512:(hf + 1) * 512],
                                    start=(f == 0), stop=(f == NF_ - 1)),
                                    reads=[b_act[f], b_wdn], writes=[b_pd[pi]], sig=(f == NF_ - 1))
                            p.op("dve", lambda e, pi=pi, tt=tt, hf=hf: e.tensor_tensor(out=xg[:, tt, hf * 512:(hf + 1) * 512], in0=pd[pi][:],
                                                                                 in1=xg[:, tt, hf * 512:(hf + 1) * 512], op=ALU.add),
                                 reads=[b_pd[pi], b_xg[tt]], writes=[b_xg[tt]])
                        p.dma("sync", lambda e, tt=tt, t=t: e.dma_start(out=xres[t * 128:(t + 1) * 128, :], in_=xg[:, tt, :]),
                              reads=[b_xg[tt]], writes=[B_x[t]])
                p.barrier()

        def phase_final():
            with ExitStack() as st:
                gt, b_g = load_bcast(st, "fin_g", norm_final, D)
                xt = [sb(st, "fin_x%d" % i, [128, D], F32) for i in range(3)]
                b_xt = p.bufs(3)
                junk = sb(st, "fin_junk", [128, D], BF16)
                b_junk = p.buf()
                ss = [sb(st, "fin_ss%d" % i, [128, 1], F32) for i in range(3)]
                b_ss = p.bufs(3)
                for t in range(NT):
                    i = t % 3
                    p.dma("sync", lambda e, i=i, t=t: e.dma_start(out=xt[i][:], in_=xres[t * 128:(t + 1) * 128, :]), reads=[B_x[t]], writes=[b_xt[i]])
                    p.op("pool", lambda e, i=i: e.memset(ss[i][:], 0.0), writes=[b_ss[i]])
                    p.op("act", lambda e, i=i: e.activation(out=junk[:], in_=xt[i][:], func=AF.Square, accum_out=ss[i][:]),
                         reads=[b_xt[i], b_ss[i]], writes=[b_junk, b_ss[i]])
                    p.op("act", lambda e, i=i: e.activation(out=ss[i][:], in_=ss[i][:], func=AF.Sqrt, scale=1.0 / D, bias=g.eps_tiles[1e-6][:]),
                         reads=[b_ss[i], b_eps], writes=[b_ss[i]])
                    p.op("dve", lambda e, i=i: e.reciprocal(out=ss[i][:], in_=ss[i][:]), reads=[b_ss[i]], writes=[b_ss[i]])
                    p.op("dve", lambda e, i=i: e.scalar_tensor_tensor(out=xt[i][:], in0=xt[i][:], scalar=ss[i][:, 0:1], in1=gt[:], op0=ALU.mult, op1=ALU.mult),
                         reads=[b_xt[i], b_ss[i], b_g], writes=[b_xt[i]])
                    p.dma("sync", lambda e, i=i, t=t: e.dma_start(out=out[t * 128:(t + 1) * 128, :], in_=xt[i][:]), reads=[b_xt[i]], writes=[B_out])
                p.barrier()

        cs_t = sb(top, "cs_t", [128, NT, 16], F32)
        sn_t = sb(top, "sn_t", [128, NT, 16], F32)
        b_rope = p.buf()
        maskD = sb(top, "maskD", [128, 128], F32)
        b_maskD = p.buf()
        Dn = [sb(top, "Dn%d" % h, [128, 256], F32) for h in range(4)]
        b_Dn = p.buf()
        relb = sb(top, "relb", [128, 128], F32)
        b_relb = p.buf()
        negpi = sb(top, "negpi", [128, 1], F32)

        def setup_attn_consts():
            with ExitStack() as st:
                p.op("pool", lambda e: e.memset(negpi[:], -math.pi), writes=[b_rope])
                posi = sb(st, "posi", [128, NT], I32)
                posf = sb(st, "posf", [128, NT], F32)
                b_pos = p.buf()
                p.dma("sync", lambda e: e.dma_start(out=posi[:], in_=pos_tm), writes=[b_pos])
                p.op("dve", lambda e: e.tensor_copy(out=posf[:], in_=posi[:]), reads=[b_pos], writes=[b_pos])
                invf = sb(st, "invf", [128, 16], F32)
                b_invf = p.buf()
                for i in range(16):
                    v = float(np.float32(10000.0) ** np.float32(-(2.0 * i) / 32.0))
                    p.op("pool", lambda e, i=i, v=v: e.memset(invf[:, i:i + 1], v), writes=[b_invf])
                ang = sb(st, "ang", [128, NT, 16], F32)
                ang2 = sb(st, "ang2", [128, NT, 16], F32)
                b_ang = p.buf()
                for t in range(NT):
                    p.op("dve", lambda e, t=t: e.tensor_scalar(out=ang[:, t, :], in0=invf[:], scalar1=posf[:, t:t + 1], scalar2=None, op0=ALU.mult),
                         reads=[b_invf, b_pos], writes=[b_ang])
                ni = sb(st, "ang_ni", [128, NT, 16], I32)
                nf = sb(st, "ang_nf", [128, NT, 16], F32)
                mk = sb(st, "ang_mk", [128, NT, 16], F32)
                b_red = p.buf()

                def reduce_sin(src, add, dst):
                    p.op("dve", lambda e: e.tensor_scalar(out=ang2[:], in0=src[:], scalar1=1.0 / (2 * math.pi), scalar2=add, op0=ALU.mult, op1=ALU.add),
                         reads=[b_ang], writes=[b_red])
                    p.op("dve", lambda e: e.tensor_copy(out=ni[:], in_=ang2[:]), reads=[b_red], writes=[b_red])
                    p.op("dve", lambda e: e.tensor_copy(out=nf[:], in_=ni[:]), reads=[b_red], writes=[b_red])
                    p.op("dve", lambda e: e.tensor_tensor(out=ang2[:], in0=ang2[:], in1=nf[:], op=ALU.subtract), reads=[b_red], writes=[b_red])
                    p.op("dve", lambda e: e.tensor_single_scalar(out=mk[:], in_=ang2[:], scalar=0.5, op=ALU.is_gt), reads=[b_red], writes=[b_red])
                    p.op("dve", lambda e: e.tensor_tensor(out=ang2[:], in0=ang2[:], in1=mk[:], op=ALU.subtract), reads=[b_red], writes=[b_red])
                    p.op("dve", lambda e: e.tensor_single_scalar(out=mk[:], in_=ang2[:], scalar=-0.5, op=ALU.is_lt), reads=[b_red], writes=[b_red])
                    p.op("dve", lambda e: e.tensor_tensor(out=ang2[:], in0=ang2[:], in1=mk[:], op=ALU.add), reads=[b_red], writes=[b_red])
                    p.op("act", lambda e: e.activation(out=dst[:], in_=ang2[:], func=AF.Sin, scale=6.283184), reads=[b_red], writes=[b_rope])

                CST = int(os.environ.get("CSTAGE", "99"))
                if CST < 1:
                    p.barrier()
                    return
                reduce_sin(ang, 0.0, sn_t)
                reduce_sin(ang, 0.25, cs_t)
                if CST < 2:
                    p.barrier()
                    return
                p.op("pool", lambda e: e.memset(maskD[:], 0.0), writes=[b_maskD])
                p.op("pool", lambda e: e.affine_select(out=maskD[:], in_=maskD[:], pattern=[[1, 128]], compare_op=ALU.is_ge, fill=-30000.0,
                                                       base=0, channel_multiplier=-1), reads=[b_maskD], writes=[b_maskD])
                if CST < 3:
                    p.barrier()
                    return
                p.dma("sync", lambda e: e.dma_start(out=relb[:], in_=rel_bias.partition_broadcast(128)), writes=[b_relb])
                dl = sb(st, "dl", [128, 128], F32)
                b_dl = p.buf()
                p.op("dve", lambda e: e.tensor_tensor(out=dl[:, 4:128], in0=relb[:, 4:128], in1=relb[:, 0:124], op=ALU.subtract),
                     reads=[b_relb], writes=[b_dl])
                pri = sb(st, "pri", [128, 256], I32)
                prf = sb(st, "prf", [128, 256], F32)
                b_pr = p.buf()
                p.dma("sync", lambda e: e.dma_start(out=pri[:], in_=pos_row.partition_broadcast(128)), writes=[b_pr])
                p.op("dve", lambda e: e.tensor_copy(out=prf[:], in_=pri[:]), reads=[b_pr], writes=[b_pr])
                p.op("dve", lambda e: e.tensor_scalar(out=prf[:], in0=prf[:], scalar1=posf[:, 0:1], scalar2=0.0, op0=ALU.subtract, op1=ALU.max),
                     reads=[b_pr, b_pos], writes=[b_pr])
                if CST < 4:
                    p.barrier()
                    return
                ge = [sb(st, "ge%d" % i, [128, 256], F32) for i in range(2)]
                b_ge = p.bufs(2)
                for h in range(4):
                    p.op("dve", lambda e, h=h: e.tensor_scalar(out=Dn[h][:], in0=prf[:], scalar1=0.0, scalar2=relb[:, h:h + 1], op0=ALU.mult, op1=ALU.add),
                         reads=[b_pr, b_relb], writes=[b_Dn])
                def bucket(n):
                    if n < 16:
                        return n
                    return min(31, 16 + int(np.float32(np.log(np.float32(n) / np.float32(16))) / np.float32(math.log(128 / 16)) * np.float32(16)))
                thr = {}
                for n in range(0, 300):
                    bk = bucket(n)
                    for bb in range(1, bk + 1):
                        if bb not in thr:
                            thr[bb] = n
                for bb in range(1, 32):
                    gi = bb % 2
                    tv = float(thr[bb]) - 0.5
                    p.op("dve", lambda e, gi=gi, tv=tv: e.tensor_single_scalar(out=ge[gi][:], in_=prf[:], scalar=tv, op=ALU.is_ge),
                         reads=[b_pr], writes=[b_ge[gi]])
                    for h in range(4):
                        p.op("dve", lambda e, gi=gi, h=h, bb=bb: e.scalar_tensor_tensor(out=Dn[h][:], in0=ge[gi][:], scalar=dl[:, bb * 4 + h:bb * 4 + h + 1],
                                                                                    in1=Dn[h][:], op0=ALU.mult, op1=ALU.add),
                             reads=[b_ge[gi], b_dl, b_Dn], writes=[b_Dn])
                for h in range(4):
                    p.op("dve", lambda e, h=h: e.tensor_scalar(out=Dn[h][:], in0=Dn[h][:], scalar1=8.0, scalar2=None, op0=ALU.mult), reads=[b_Dn], writes=[b_Dn])
                    p.op("pool", lambda e, h=h: e.affine_select(out=Dn[h][:, 0:128], in_=Dn[h][:, 0:128], pattern=[[1, 128]], compare_op=ALU.is_ge,
                                                                fill=-30000.0, base=0, channel_multiplier=-1), reads=[b_Dn], writes=[b_Dn])
                p.barrier()

        class AttnRes:
            pass

        def make_attn_res(st):
            r = AttnRes()
            r.sc = [ps(st, "a_sc%d" % i, [128, 512], F32) for i in range(3)]
            r.b_sc = p.bufs(3)
            r.eT = [sb(st, "a_eT%d" % i, [128, 512], BF16) for i in range(3)]
            r.b_eT = p.bufs(3)
            r.nsc = 0
            r.neT = 0
            return r

        def attn_chunk(r, c, QTf, b_Q, KTf, b_K, Vf, b_V, dv, scale, d0, b_d0, d1, b_d1, farb, po, b_po, psm, b_psm):
            nk = 4 * c + 4

            def qk(kt):
                j0 = max(4 * c, kt)
                off = (j0 - 4 * c) * 128
                ncol = 512 - off
                si = r.nsc % 3
                r.nsc += 1
                sc, b_s = r.sc[si], r.b_sc[si]
                p.op("pe", lambda e, sc=sc, kt=kt, off=off, ncol=ncol: e.matmul(sc[:, off:512], lhsT=KTf(kt), rhs=QTf(c * 512 + off, ncol), start=True, stop=True),
                     reads=[b_K, b_Q], writes=[b_s])
                nnear = 0
                if kt >= 4 * c:
                    p.op("dve", lambda e, sc=sc, off=off: e.tensor_tensor(out=sc[:, off:off + 128], in0=sc[:, off:off + 128], in1=d0, op=ALU.add),
                         reads=[b_s, b_d0], writes=[b_s])
                    nnear = 1
                    if d1 is not None and kt + 1 <= 4 * c + 3:
                        p.op("dve", lambda e, sc=sc, off=off: e.tensor_tensor(out=sc[:, off + 128:off + 256], in0=sc[:, off + 128:off + 256], in1=d1, op=ALU.add),
                             reads=[b_s, b_d1], writes=[b_s])
                        nnear = 2
                elif d1 is not None and kt == 4 * c - 1:
                    p.op("dve", lambda e, sc=sc: e.tensor_tensor(out=sc[:, 0:128], in0=sc[:, 0:128], in1=d1, op=ALU.add),
                         reads=[b_s, b_d1], writes=[b_s])
                    nnear = 1
                return sc, b_s, off, nnear

            pend = [qk(0)]
            if nk > 1:
                pend.append(qk(1))
            for kt in range(nk):
                if kt + 2 < nk:
                    pend.append(qk(kt + 2))
                sc, b_s, off, nnear = pend.pop(0)
                ei = r.neT % 3
                r.neT += 1
                eT, b_e = r.eT[ei], r.b_eT[ei]
                if farb is None:
                    p.op("act", lambda e, sc=sc, eT=eT, off=off: e.activation(out=eT[:, off:512], in_=sc[:, off:512], func=AF.Exp, scale=scale),
                         reads=[b_s], writes=[b_e])
                else:
                    nn = nnear * 128
                    if nn > 0:
                        p.op("act", lambda e, sc=sc, eT=eT, off=off, nn=nn: e.activation(out=eT[:, off:off + nn], in_=sc[:, off:off + nn], func=AF.Exp, scale=scale),
                             reads=[b_s], writes=[b_e])
                    if off + nn < 512:
                        p.op("act", lambda e, sc=sc, eT=eT, off=off, nn=nn: e.activation(out=eT[:, off + nn:512], in_=sc[:, off + nn:512], func=AF.Exp, scale=scale, bias=farb),
                             reads=[b_s, b_relb], writes=[b_e])
                if psm is None:
                    p.op("pe", lambda e, eT=eT, kt=kt, off=off: e.matmul(po[0:dv, off:512], lhsT=Vf(kt), rhs=eT[:, off:512], start=(kt == 0), stop=(kt == nk - 1)),
                         reads=[b_V, b_e], writes=[b_po])
                else:
                    p.op("pe", lambda e, eT=eT, kt=kt, off=off: e.matmul(po[0:dv, off:512], lhsT=Vf(kt), rhs=eT[:, off:512], start=(kt == 0), stop=(kt == nk - 1)),
                         reads=[b_V, b_e], writes=[b_po], sig=False)
                    p.op("pe", lambda e, eT=eT, kt=kt, off=off: e.matmul(psm[0:1, off:512], lhsT=ones_bf[:, 0:1], rhs=eT[:, off:512], start=(kt == 0), stop=(kt == nk - 1)),
                         reads=[b_e, b_ones], writes=[b_psm])

        def phase_mla(l):
            with ExitStack() as st:
                QT = sb(st, "m_QT", [128, 4, S], BF16)
                KT = sb(st, "m_KT", [128, 4, S], BF16)
                V = sb(st, "m_V", [128, NT, 4, 65], BF16)
                b_QT, b_KT, b_V = p.buf(), p.buf(), p.buf()
                for hg in range(2):
                    with ExitStack() as s2:
                        p.op("pool", lambda e: e.memset(V[:], 1.0), writes=[b_V])
                        wuq, b_wuq = load_weight_bf(s2, "m_wuq", mats["mla_w_uq"][l], 256, 768)
                        wukv, b_wukv = load_weight_bf(s2, "m_wukv", mats["mla_w_ukv"][l], 128, 1024)
                        qn, b_qn = load_bcast(s2, "m_qn", vec["mla_q_norm"][l:l + 1, :], 256)
                        kvn, b_kvn = load_bcast(s2, "m_kvn", vec["mla_kv_norm"][l:l + 1, :], 128)
                        pt_ = [sb(s2, "m_p%d" % i, [128, 416], F32) for i in range(2)]
                        b_pt = p.bufs(2)
                        junk = sb(s2, "m_junk", [128, 256], BF16)
                        b_junk = p.buf()
                        ss = [sb(s2, "m_ss%d" % i, [128, 2], F32) for i in range(2)]
                        b_ss = p.bufs(2)
                        cb_ = [sb(s2, "m_cb%d" % i, [128, 384], BF16) for i in range(2)]
                        b_cb = p.bufs(2)
                        pT = ps(s2, "m_pT", [128, 3, 128], BF16)
                        b_pT = p.buf()
                        cT = [sb(s2, "m_cT%d" % i, [128, 3, 128], BF16) for i in range(2)]
                        b_cT = p.bufs(2)
                        pq = [ps(s2, "m_pq%d" % i, [128, 512], F32) for i in range(2)]
                        b_pq = p.bufs(2)
                        qb = [sb(s2, "m_qb%d" % i, [128, 8, 96], BF16) for i in range(2)]
                        b_qb = p.bufs(2)
                        tA = sb(s2, "m_tA", [128, 8, 16], F32)
                        tB = sb(s2, "m_tB", [128, 8, 16], F32)
                        tC = sb(s2, "m_tC", [128, 8, 16], F32)
                        tD = sb(s2, "m_tD", [128, 8, 16], F32)
                        qf = sb(s2, "m_qf", [128, 768], F32)
                        b_tA, b_tB, b_tC, b_tD, b_qf = p.buf(), p.buf(), p.buf(), p.buf(), p.buf()
                        pqT = ps(s2, "m_pqT", [128, 8, 128], BF16)
                        b_pqT = p.buf()
                        pkn = [ps(s2, "m_pkn%d" % i, [64, 4, 128], F32) for i in range(2)]
                        b_pkn = p.bufs(2)
                        pv = ps(s2, "m_pv", [128, 512], F32)
                        b_pv = p.buf()
                        kr = [sb(s2, "m_kr%d" % i, [128, 96], BF16) for i in range(2)]
                        b_kr = p.bufs(2)
                        for i in range(2):
                            p.op("pool", lambda e, i=i: e.memset(kr[i][:], 0.0), writes=[b_kr[i]])
                        pkr = ps(s2, "m_pkr", [128, 128], BF16)
                        b_pkr = p.buf()
                        krT = sb(s2, "m_krT", [128, 128], BF16)
                        b_krT = p.buf()
                        wukv_v = wukv[:, 0, :].rearrange("p (h c) -> p h c", c=128)
                        MSUB = int(os.environ.get('MLA_SUB', '99'))
                        for t in range(int(os.environ.get('MLA_NT', NT))):
                            i = t % 2
                            ts_ = slice(t * 128, (t + 1) * 128)
                            if MSUB < 0:
                                continue
                            p.dma("sync", lambda e, i=i, ts_=ts_: e.dma_start(out=pt_[i][:], in_=pml[ts_, :]), reads=[B_pml[t]], writes=[b_pt[i]])
                            p.op("pool", lambda e, i=i: e.memset(ss[i][:], 0.0), writes=[b_ss[i]])
                            p.op("act", lambda e, i=i: e.activation(out=junk[:, 0:256], in_=pt_[i][:, 0:256], func=AF.Square, accum_out=ss[i][:, 0:1]),
                                 reads=[b_pt[i], b_ss[i]], writes=[b_junk, b_ss[i]])
                            p.op("act", lambda e, i=i: e.activation(out=junk[:, 0:128], in_=pt_[i][:, 256:384], func=AF.Square, accum_out=ss[i][:, 1:2]),
                                 reads=[b_pt[i], b_ss[i]], writes=[b_junk, b_ss[i]])
                            p.op("act", lambda e, i=i: e.activation(out=ss[i][:, 0:1], in_=ss[i][:, 0:1], func=AF.Sqrt, scale=1.0 / 256, bias=g.eps_tiles[1e-6][:]),
                                 reads=[b_ss[i], b_eps], writes=[b_ss[i]])
                            p.op("act", lambda e, i=i: e.activation(out=ss[i][:, 1:2], in_=ss[i][:, 1:2], func=AF.Sqrt, scale=1.0 / 128, bias=g.eps_tiles[1e-6][:]),
                                 reads=[b_ss[i], b_eps], writes=[b_ss[i]])
                            p.op("dve", lambda e, i=i: e.reciprocal(out=ss[i][:], in_=ss[i][:]), reads=[b_ss[i]], writes=[b_ss[i]])
                            p.op("dve", lambda e, i=i: e.scalar_tensor_tensor(out=cb_[i][:, 0:256], in0=pt_[i][:, 0:256], scalar=ss[i][:, 0:1], in1=qn[:],
                                                                              op0=ALU.mult, op1=ALU.mult), reads=[b_pt[i], b_ss[i], b_qn], writes=[b_cb[i]])
                            p.op("dve", lambda e, i=i: e.scalar_tensor_tensor(out=cb_[i][:, 256:384], in0=pt_[i][:, 256:384], scalar=ss[i][:, 1:2], in1=kvn[:],
                                                                              op0=ALU.mult, op1=ALU.mult), reads=[b_pt[i], b_ss[i], b_kvn], writes=[b_cb[i]])
                            if MSUB < 2:
                                continue
                            for k in range(3):
                                p.op("pe", lambda e, i=i, k=k: e.transpose(out=pT[:, k, :], in_=cb_[i][:, k * 128:(k + 1) * 128], identity=ident[:]),
                                     reads=[b_cb[i], b_ident], writes=[b_pT], sig=(k == 2))
                            p.op("act", lambda e, i=i: e.copy(out=cT[i][:], in_=pT[:]), reads=[b_pT], writes=[b_cT[i]])
                            if MSUB < 3:
                                continue
                            for (c0, ncol, pi) in ((0, 512, 0), (512, 256, 1)):
                                for kk in range(2):
                                    p.op("pe", lambda e, i=i, kk=kk, c0=c0, ncol=ncol, pi=pi: e.matmul(pq[pi][:, 0:ncol], lhsT=cT[i][:, kk, :], rhs=wuq[:, kk, c0:c0 + ncol],
                                                                                                 start=(kk == 0), stop=(kk == 1)),
                                         reads=[b_cT[i], b_wuq], writes=[b_pq[pi]], sig=(kk == 1))
                            if MSUB < 4:
                                continue
                            p.op("act", lambda e: e.copy(out=qf[:, 0:512], in_=pq[0][:, 0:512]), reads=[b_pq[0]], writes=[b_qf])
                            p.op("act", lambda e: e.copy(out=qf[:, 512:768], in_=pq[1][:, 0:256]), reads=[b_pq[1]], writes=[b_qf])
                            for h in range(hg * 4, hg * 4 + 4):
                                c0 = h * 96
                                p.op("act", lambda e, i=i, h=h, c0=c0: e.copy(out=qb[i][:, h, 0:64], in_=qf[:, c0:c0 + 64]), reads=[b_qf], writes=[b_qb[i]])
                                x1 = qf[:, c0 + 64:c0 + 80]
                                x2 = qf[:, c0 + 80:c0 + 96]
                                cst = cs_t[:, t, :]
                                snt = sn_t[:, t, :]
                                p.op("dve", lambda e, h=h, x1=x1, cst=cst: e.tensor_tensor(out=tA[:, h, :], in0=x1, in1=cst, op=ALU.mult), reads=[b_qf, b_rope], writes=[b_tA])
                                p.op("dve", lambda e, h=h, x2=x2, snt=snt: e.tensor_tensor(out=tB[:, h, :], in0=x2, in1=snt, op=ALU.mult), reads=[b_qf, b_rope], writes=[b_tB])
                                p.op("dve", lambda e, i=i, h=h: e.tensor_tensor(out=qb[i][:, h, 64:80], in0=tA[:, h, :], in1=tB[:, h, :], op=ALU.subtract),
                                     reads=[b_tA, b_tB], writes=[b_qb[i]])
                                p.op("dve", lambda e, h=h, x1=x1, snt=snt: e.tensor_tensor(out=tC[:, h, :], in0=x1, in1=snt, op=ALU.mult), reads=[b_qf, b_rope], writes=[b_tC])
                                p.op("dve", lambda e, h=h, x2=x2, cst=cst: e.tensor_tensor(out=tD[:, h, :], in0=x2, in1=cst, op=ALU.mult), reads=[b_qf, b_rope], writes=[b_tD])
                                p.op("dve", lambda e, i=i, h=h: e.tensor_tensor(out=qb[i][:, h, 80:96], in0=tC[:, h, :], in1=tD[:, h, :], op=ALU.add),
                                     reads=[b_tC, b_tD], writes=[b_qb[i]])
                            if MSUB < 5:
                                continue
                            for hh in range(4):
                                h = hg * 4 + hh
                                p.op("pe", lambda e, i=i, h=h, hh=hh: e.transpose(out=pqT[0:96, hh, :], in_=qb[i][:, h, :], identity=ident[:]),
                                     reads=[b_qb[i], b_ident], writes=[b_pqT], sig=(hh == 3))
                            p.op("act", lambda e, ts_=ts_: e.copy(out=QT[0:96, :, ts_], in_=pqT[0:96, 0:4, :]), reads=[b_pqT], writes=[b_QT])
                            if MSUB < 6:
                                continue
                            for hh in range(4):
                                h = hg * 4 + hh
                                p.op("pe", lambda e, i=i, h=h, hh=hh: e.matmul(pkn[0][:, hh, :], lhsT=wukv[:, 0, h * 128:h * 128 + 64], rhs=cT[i][:, 2, :], start=True, stop=True),
                                     reads=[b_cT[i], b_wukv], writes=[b_pkn[0]], sig=(hh == 3))
                            p.op("dve", lambda e, ts_=ts_: e.tensor_copy(out=KT[0:64, :, ts_], in_=pkn[0][:]), reads=[b_pkn[0]], writes=[b_KT])
                            if MSUB < 7:
                                continue
                            for hh in range(4):
                                h = hg * 4 + hh
                                p.op("pe", lambda e, i=i, h=h, hh=hh: e.matmul(pv[:, hh * 64:(hh + 1) * 64], lhsT=cT[i][:, 2, :], rhs=wukv[:, 0, h * 128 + 64:h * 128 + 128], start=True, stop=True),
                                     reads=[b_cT[i], b_wukv], writes=[b_pv], sig=(hh == 3))
                            p.op("act", lambda e, t=t: e.copy(out=V[:, t, :, 1:65], in_=pv[:, 0:256].rearrange("p (h c) -> p h c", c=64)), reads=[b_pv], writes=[b_V])
                            if MSUB < 8:
                                continue
                            cst = cs_t[:, t, :]
                            snt = sn_t[:, t, :]
                            x1 = pt_[i][:, 384:400]
                            x2 = pt_[i][:, 400:416]
                            p.op("dve", lambda e, x1=x1, cst=cst: e.tensor_tensor(out=tA[:, 0, :], in0=x1, in1=cst, op=ALU.mult), reads=[b_pt[i], b_rope], writes=[b_tA])
                            p.op("dve", lambda e, x2=x2, snt=snt: e.tensor_tensor(out=tB[:, 0, :], in0=x2, in1=snt, op=ALU.mult), reads=[b_pt[i], b_rope], writes=[b_tB])
                            p.op("dve", lambda e, i=i: e.tensor_tensor(out=kr[i][:, 64:80], in0=tA[:, 0, :], in1=tB[:, 0, :], op=ALU.subtract), reads=[b_tA, b_tB], writes=[b_kr[i]])
                            p.op("dve", lambda e, x1=x1, snt=snt: e.tensor_tensor(out=tA[:, 0, :], in0=x1, in1=snt, op=ALU.mult), reads=[b_pt[i], b_rope], writes=[b_tA])
                            p.op("dve", lambda e, x2=x2, cst=cst: e.tensor_tensor(out=tB[:, 0, :], in0=x2, in1=cst, op=ALU.mult), reads=[b_pt[i], b_rope], writes=[b_tB])
                            p.op("dve", lambda e, i=i: e.tensor_tensor(out=kr[i][:, 80:96], in0=tA[:, 0, :], in1=tB[:, 0, :], op=ALU.add), reads=[b_tA, b_tB], writes=[b_kr[i]])
                            p.op("pe", lambda e, i=i: e.transpose(out=pkr[0:96, :], in_=kr[i][:, :], identity=ident[:]), reads=[b_kr[i], b_ident], writes=[b_pkr])
                            p.op("act", lambda e: e.copy(out=krT[64:96, :], in_=pkr[64:96, :]), reads=[b_pkr], writes=[b_krT])
                            for h in range(4):
                                p.op("pool", lambda e, h=h, ts_=ts_: e.tensor_copy(out=KT[64:96, h, ts_], in_=krT[64:96, :]), reads=[b_krT], writes=[b_KT])
                        p.barrier()
                    if os.environ.get("MLA_STAGE") == "prep":
                        continue
                    with ExitStack() as s3:
                        r = make_attn_res(s3)
                        po = [ps(s3, "m_po%d" % i, [128, 512], F32) for i in range(2)]
                        b_po = p.bufs(2)
                        pbc = ps(s3, "m_pbc", [128, 512], F32)
                        b_pbc = p.buf()
                        rc = [sb(s3, "m_rc%d" % i, [1, 512], F32) for i in range(2)]
                        b_rc = p.bufs(2)
                        bcs = [sb(s3, "m_bcs%d" % i, [65, 512], F32) for i in range(2)]
                        b_bcs = p.bufs(2)
                        ob = [sb(s3, "m_ob%d" % i, [65, 512], BF16) for i in range(2)]
                        b_ob = p.bufs(2)
                        n = 0
                        scale = 96 ** -0.5
                        for hh in range(4):
                            h = hg * 4 + hh
                            for c in range(int(os.environ.get('MLA_NC', 8))):
                                i = n % 2
                                n += 1
                                attn_chunk(r, c,
                                           lambda q0, nq, hh=hh: QT[0:96, hh, q0:q0 + nq], b_QT,
                                           lambda kt, hh=hh: KT[0:96, hh, kt * 128:(kt + 1) * 128], b_KT,
                                           lambda kt, hh=hh: V[:, kt, hh, :], b_V, 65, scale,
                                           maskD[:], b_maskD, None, None, None, po[i], b_po[i], None, None)
                                p.op("dve", lambda e, i=i: e.reciprocal(out=rc[i][:], in_=po[i][0:1, :]), reads=[b_po[i]], writes=[b_rc[i]])
                                p.op("pe", lambda e, i=i: e.matmul(pbc[0:65, :], lhsT=ones_f[0:1, 0:65], rhs=rc[i][:], start=True, stop=True),
                                     reads=[b_rc[i], b_ones], writes=[b_pbc])
                                p.op("act", lambda e, i=i: e.copy(out=bcs[i][:], in_=pbc[0:65, :]), reads=[b_pbc], writes=[b_bcs[i]])
                                p.op("dve", lambda e, i=i: e.tensor_tensor(out=ob[i][:], in0=po[i][0:65, :], in1=bcs[i][:], op=ALU.mult),
                                     reads=[b_po[i], b_bcs[i]], writes=[b_ob[i]])
                                p.dma("sync", lambda e, i=i, h=h, c=c: e.dma_start(out=oT["mla"][h * 64:(h + 1) * 64, c * 512:(c + 1) * 512], in_=ob[i][1:65, :]),
                                      reads=[b_ob[i]], writes=[B_oT["mla"][c]])
                        p.barrier()

        def phase_diff(l):
            lambda_init = 0.8 - 0.6 * math.exp(-0.3 * l)
            with ExitStack() as st:
                r = make_attn_res(st)
                lam = sb(st, "d_lam", [1, 256], F32)
                lamp = sb(st, "d_lamp", [1, 128], F32)
                lams = sb(st, "d_lams", [1, 4], F32)
                b_lam = p.buf()
                p.dma("sync", lambda e: e.dma_start(out=lam[:], in_=diff_lambda[l]), writes=[b_lam])
                p.op("pool", lambda e: e.memset(lams[:], 0.0), writes=[b_lam])
                p.op("dve", lambda e: e.tensor_tensor(out=lamp[:, 0:64], in0=lam[:, 0:64], in1=lam[:, 64:128], op=ALU.mult), reads=[b_lam], writes=[b_lam])
                p.op("dve", lambda e: e.tensor_tensor(out=lamp[:, 64:128], in0=lam[:, 128:192], in1=lam[:, 192:256], op=ALU.mult), reads=[b_lam], writes=[b_lam])
                p.op("dve", lambda e: e.tensor_reduce(out=lams[:, 0:2], in_=lamp[:].rearrange("p (a b) -> p a b", b=64), axis=AX.X, op=ALU.add), reads=[b_lam], writes=[b_lam])
                p.op("act", lambda e: e.activation(out=lams[:, 0:2], in_=lams[:, 0:2], func=AF.Exp), reads=[b_lam], writes=[b_lam])
                p.op("dve", lambda e: e.scalar_tensor_tensor(out=lams[:, 2:3], in0=lams[:, 1:2], scalar=-lambda_init, in1=lams[:, 0:1], op0=ALU.add, op1=ALU.subtract),
                     reads=[b_lam], writes=[b_lam])
                sub = sb(st, "d_sub", [128, 1], F32)
                b_sub = p.buf()
                p.dma("sync", lambda e: e.dma_start(out=sub[:], in_=subln_cm[l]), writes=[b_sub])
                p.op("dve", lambda e: e.tensor_scalar(out=sub[:], in0=sub[:], scalar1=1.0 - lambda_init, scalar2=None, op0=ALU.mult), reads=[b_sub], writes=[b_sub])
                QT = [sb(st, "d_QT%d" % i, [64, S], BF16) for i in range(2)]
                KT = [sb(st, "d_KT%d" % i, [64, S], BF16) for i in range(2)]
                b_QK = p.bufs(2)
                V = sb(st, "d_V", [128, NT, 128], BF16)
                b_V = p.buf()
                po = [ps(st, "d_po%d" % i, [128, 512], F32) for i in range(2)]
                b_po = p.bufs(2)
                psm = [ps(st, "d_psm%d" % i, [1, 512], F32) for i in range(2)]
                b_psm = p.bufs(2)
                pbc = ps(st, "d_pbc", [128, 512], F32)
                b_pbc = p.buf()
                rc = [sb(st, "d_rc%d" % i, [1, 512], F32) for i in range(2)]
                b_rc = p.bufs(2)
                bcs = [sb(st, "d_bcs%d" % i, [128, 512], F32) for i in range(2)]
                b_bcs = p.bufs(2)
                o0 = sb(st, "d_o0", [128, 512], F32)
                o1 = sb(st, "d_o1", [128, 512], F32)
                sq = sb(st, "d_sq", [128, 512], F32)
                b_o0, b_o1, b_sq = p.buf(), p.buf(), p.buf()
                ob = [sb(st, "d_ob%d" % i, [128, 512], BF16) for i in range(2)]
                b_ob = p.bufs(2)
                n = 0
                scale = 0.125
                for h in range(4):
                    for m in range(2):
                        rq = (h * 2 + m) * 64
                        p.dma("sync", lambda e, m=m, rq=rq: e.dma_start(out=QT[m][:], in_=qkT[rq:rq + 64, :]), reads=B_qkT, writes=[b_QK[m]])
                        p.dma("sync", lambda e, m=m, rq=rq: e.dma_start(out=KT[m][:], in_=qkT[512 + rq:512 + rq + 64, :]), reads=B_qkT, writes=[b_QK[m]])
                    p.dma("sync", lambda e, h=h: e.dma_start(out=V[:], in_=vdf[:, h * 128:(h + 1) * 128].rearrange("(n p) c -> p n c", p=128)),
                          reads=B_vdf, writes=[b_V])
                    farb = relb[:, 31 * 4 + h:31 * 4 + h + 1]
                    for c in range(8):
                        for m in range(2):
                            attn_chunk(r, c,
                                       lambda q0, nq, m=m: QT[m][:, q0:q0 + nq], b_QK[m],
                                       lambda kt, m=m: KT[m][:, kt * 128:(kt + 1) * 128], b_QK[m],
                                       lambda kt: V[:, kt, :], b_V, 128, scale,
                                       Dn[h][:, 0:128], b_Dn, Dn[h][:, 128:256], b_Dn, farb, po[m], b_po[m], psm[m], b_psm[m])
                        for m in range(2):
                            p.op("dve", lambda e, m=m: e.reciprocal(out=rc[m][:], in_=psm[m][:]), reads=[b_psm[m]], writes=[b_rc[m]])
                        p.op("dve", lambda e: e.tensor_scalar(out=rc[1][:], in0=rc[1][:], scalar1=lams[0:1, 2:3], scalar2=None, op0=ALU.mult),
                             reads=[b_rc[1], b_lam], writes=[b_rc[1]])
                        for m in range(2):
                            p.op("pe", lambda e, m=m: e.matmul(pbc[:], lhsT=ones_f[0:1, :], rhs=rc[m][:], start=True, stop=True),
                                 reads=[b_rc[m], b_ones], writes=[b_pbc])
                            p.op("act", lambda e, m=m: e.copy(out=bcs[m][:], in_=pbc[:]), reads=[b_pbc], writes=[b_bcs[m]])
                        p.op("dve", lambda e: e.tensor_tensor(out=o0[:], in0=po[0][:], in1=bcs[0][:], op=ALU.mult), reads=[b_po[0], b_bcs[0]], writes=[b_o0])
                        p.op("dve", lambda e: e.tensor_tensor(out=o1[:], in0=po[1][:], in1=bcs[1][:], op=ALU.mult), reads=[b_po[1], b_bcs[1]], writes=[b_o1])
                        p.op("dve", lambda e: e.tensor_tensor(out=o0[:], in0=o0[:], in1=o1[:], op=ALU.add), reads=[b_o0, b_o1], writes=[b_o0])
                        p.op("act", lambda e: e.activation(out=sq[:], in_=o0[:], func=AF.Square), reads=[b_o0], writes=[b_sq])
                        p.op("pe", lambda e: e.matmul(pbc[:], lhsT=ones_f[:, :], rhs=sq[:], start=True, stop=True), reads=[b_sq, b_ones], writes=[b_pbc])
                        p.op("act", lambda e: e.activation(out=sq[:], in_=pbc[:], func=AF.Sqrt, scale=1.0 / 128, bias=g.eps_tiles[1e-5][:]),
                             reads=[b_pbc, b_eps], writes=[b_sq])
                        p.op("dve", lambda e: e.reciprocal(out=sq[:], in_=sq[:]), reads=[b_sq], writes=[b_sq])
                        i = n % 2
                        n += 1
                        p.op("dve", lambda e, i=i: e.scalar_tensor_tensor(out=ob[i][:], in0=o0[:], scalar=sub[:, 0:1], in1=sq[:], op0=ALU.mult, op1=ALU.mult),
                             reads=[b_o0, b_sub, b_sq], writes=[b_ob[i]])
                        p.dma("sync", lambda e, i=i, h=h, c=c: e.dma_start(out=oT["diff"][h * 128:(h + 1) * 128, c * 512:(c + 1) * 512], in_=ob[i][:]),
                              reads=[b_ob[i]], writes=[B_oT["diff"][c]])
                p.barrier()

        def phase_rwkv(l):
            with ExitStack() as st:
                def T_(name, shape, dt=F32):
                    return sb(st, "r_" + name, shape, dt)
                mu, b_mu = load_bcast(st, "r_mu", vec["rwkv_mu"][l:l + 1, :], 1792)
                w0, b_w0 = load_bcast(st, "r_w0", vec["rwkv_w0"][l:l + 1, :], 512)
                a0, b_a0 = load_bcast(st, "r_a0", vec["rwkv_a0"][l:l + 1, :], 512)
                k_k, b_kk_ = load_bcast(st, "r_k_k", vec["rwkv_k_k"][l:l + 1, :], 512)
                k_a, b_ka_ = load_bcast(st, "r_k_a", vec["rwkv_k_a"][l:l + 1, :], 512)
                r_k, b_rk_ = load_bcast(st, "r_r_k", vec["rwkv_r_k"][l:l + 1, :], 512)
                ln_w, b_lnw = load_bcast(st, "r_ln_w", vec["rwkv_ln_w"][l:l + 1, :], 512)
                ln_b, b_lnb = load_bcast(st, "r_ln_b", vec["rwkv_ln_b"][l:l + 1, :], 512)
                w2, b_w2 = load_weight_bf(st, "r_w2", mats["rwkv_w2"][l], 64, 512, part0=0)
                a2, b_a2 = load_weight_bf(st, "r_a2", mats["rwkv_a2"][l], 64, 512, part0=64)
                g2, b_g2 = load_weight_bf(st, "r_g2", mats["rwkv_g2"][l], 128, 512)
                triU = T_("triU", [128, 128])
                mSI = T_("mSI", [128, 256])
                mSL = T_("mSL", [128, 128])
                b_msk = p.buf()
                p.op("pool", lambda e: e.memset(triU[:], 1.0), writes=[b_msk])
                p.op("pool", lambda e: e.affine_select(out=triU[:], in_=triU[:], pattern=[[1, 128]], compare_op=ALU.is_ge, fill=0.0, base=0, channel_multiplier=-1),
                     reads=[b_msk], writes=[b_msk])
                p.op("pool", lambda e: e.memset(mSI[:], 1.0), writes=[b_msk])
                p.op("pool", lambda e: e.affine_select(out=mSI[:, 0:128], in_=mSI[:, 0:128], pattern=[[1, 128]], compare_op=ALU.is_gt, fill=0.0, base=0, channel_multiplier=-1),
                     reads=[b_msk], writes=[b_msk])
                p.op("pool", lambda e: e.affine_select(out=mSI[:, 128:256], in_=mSI[:, 128:256], pattern=[[1, 128]], compare_op=ALU.is_ge, fill=0.0, base=0, channel_multiplier=-1),
                     reads=[b_msk], writes=[b_msk])
                p.op("pool", lambda e: e.memset(mSL[:], 1.0), writes=[b_msk])
                p.op("pool", lambda e: e.affine_select(out=mSL[:], in_=mSL[:], pattern=[[-1, 128]], compare_op=ALU.is_gt, fill=0.0, base=0, channel_multiplier=1),
                     reads=[b_msk], writes=[b_msk])
                identf = T_("identf", [128, 128], BF16)
                Sst = T_("S", [64, 8, 64])
                Sb = T_("Sb", [64, 8, 64], BF16)
                b_S, b_Sb = p.buf(), p.buf()
                p.op("pool", lambda e: e.memset(Sst[:], 0.0), writes=[b_S])
                p.op("pool", lambda e: e.memset(Sb[:], 0.0), writes=[b_Sb])
                gb = [ps(st, "r_gb%d" % i, [128, 512], F32) for i in range(6)]
                b_gb = p.bufs(6)
                gbn = [0]
                tb = [ps(st, "r_tb%d" % i, [128, 8, 128], BF16) for i in range(2)]
                b_tb = p.bufs(2)
                tbn = [0]

                def bank():
                    i = gbn[0] % 6
                    gbn[0] += 1
                    return gb[i], b_gb[i]

                def tbank():
                    i = tbn[0] % 2
                    tbn[0] += 1
                    return tb[i], b_tb[i]

                P0 = [T_("P0_0", [128, 1792])] * 2
                P1 = [T_("P1_0", [128, 1792])] * 2
                b_P0, b_P1 = [p.buf()] * 2, [p.buf()] * 2
                PM = [T_("PM%d" % i, [128, 1792]) for i in range(2)]
                b_PM = p.bufs(2)
                th = T_("th", [128, 256], BF16); b_th = p.buf()
                thT = T_("thT", [128, 2, 128], BF16); b_thT = p.buf()
                lw = T_("lw", [128, 512]); b_lw = p.buf()
                alr = T_("alr", [128, 512]); b_alr = p.buf()
                gg = [T_("gg%d" % i, [128, 512]) for i in range(2)]; b_gg = p.bufs(2)
                kk = T_("kk", [128, 512]); b_kk = p.buf()
                tmp = T_("tmp", [128, 512]); b_tmp = p.buf()
                tmp2 = T_("tmp2", [128, 512]); b_tmp2 = p.buf()
                k2 = [T_("k2_%d" % i, [128, 512]) for i in range(2)]; b_k2 = p.bufs(2)
                bt = T_("bt", [128, 512]); b_bt = p.buf()
                st8 = T_("st8", [128, 8]); b_st8 = p.buf()
                cumS = T_("cumS", [128, 512]); b_cumS = p.buf()
                eC = T_("eC", [128, 512]); b_eC = p.buf()
                eCi = T_("eCi", [128, 512]); b_eCi = p.buf()
                eCx = T_("eCx", [128, 512]); b_eCx = p.buf()
                eD = T_("eD", [128, 512]); b_eD = p.buf()
                gC = [T_("gC%d" % i, [64, 8]) for i in range(2)]; b_gC = p.bufs(2)
                X4 = T_("X4", [128, 4, 512], BF16); b_X4 = p.bufs(4)
                BH = [T_("BH%d" % i, [128, 512], BF16) for i in range(2)]; b_BH = p.bufs(2)
                KH = [T_("KH%d" % i, [128, 512], BF16) for i in range(2)]; b_KH = p.bufs(2)
                VB = [T_("VB%d" % i, [128, 512], BF16) for i in range(2)]; b_VB = p.bufs(2)
                CM = [T_("CM%d" % i, [64, 8, 4, 128], BF16) for i in range(2)]; b_CM = p.bufs(2)
                MM = [T_("MM%d" % i, [128, 8, 2, 256], BF16) for i in range(2)]; b_MM = p.bufs(2)
                Pp = [T_("Pp%d" % i, [128, 8, 128], BF16) for i in range(2)]; b_Pp = p.bufs(2)
                PTp = [T_("PTp%d" % i, [128, 8, 128], BF16) for i in range(2)]; b_PTp = p.bufs(2)
                Tp = [T_("Tp%d" % i, [128, 8, 128], BF16) for i in range(2)]; b_Tp = p.bufs(2)
                Tf = [T_("Tf%d" % i, [128, 8, 128], BF16) for i in range(2)]; b_Tf = p.bufs(2)
                Ws = T_("Ws", [128, 512], BF16); b_Ws = p.buf()
                Us = T_("Us", [128, 512], BF16); b_Us = p.buf()
                yt = T_("yt", [128, 512]); b_yt = p.buf()
                yc = T_("yc", [128, 512]); b_yc = p.buf()
                yo = T_("yo", [128, 512], BF16); b_yo = p.buf()
                oTt = [T_("oTt%d" % i, [128, 4, 128], BF16) for i in range(2)]; b_oTt = p.bufs(2)
                NEGE = -math.exp(-0.5)
                RSUB = int(os.environ.get('RW_SUB', '99'))

                def H(a, h):
                    return a[:, h * 64:(h + 1) * 64]

                def V3(a):
                    return a[:].rearrange("p (h c) -> p h c", c=64)

                def B8(a):
                    return a[:].unsqueeze(2).to_broadcast([128, 8, 64])

                def pre(t):
                    i = t % 2
                    t0 = t * 128
                    p.dma("sync", lambda e: e.dma_start(out=P0[i][:], in_=prw[t0:t0 + 128, :]), reads=[B_prw[t]], writes=[b_P0[i]])
                    if t == 0:
                        p.op("pool", lambda e: e.memset(P1[i][0:1, :], 0.0), writes=[b_P1[i]])
                        p.dma("sync", lambda e: e.dma_start(out=P1[i][1:128, :], in_=prw[0:127, :]), reads=[B_prw[0]], writes=[b_P1[i]])
                    else:
                        p.dma("sync", lambda e: e.dma_start(out=P1[i][:], in_=prw[t0 - 1:t0 + 127, :]), reads=[B_prw[t], B_prw[t - 1]], writes=[b_P1[i]])
                    pm = PM[i]
                    p.op("dve", lambda e: e.tensor_tensor(out=P1[i][:], in0=P1[i][:], in1=P0[i][:], op=ALU.subtract), reads=[b_P1[i], b_P0[i]], writes=[b_P1[i]])
                    p.op("dve", lambda e: e.tensor_tensor(out=P1[i][:], in0=P1[i][:], in1=mu[:], op=ALU.mult), reads=[b_P1[i], b_mu], writes=[b_P1[i]])
                    p.op("dve", lambda e: e.tensor_tensor(out=pm[:], in0=P1[i][:], in1=P0[i][:], op=ALU.add), reads=[b_P1[i], b_P0[i]], writes=[b_PM[i]])
                    r_ = pm[:, 0:512]
                    k_ = pm[:, 512:1024]
                    v_ = pm[:, 1024:1536]
                    if RSUB < 1:
                        return
                    p.op("act", lambda e: e.activation(out=th[:, 0:64], in_=pm[:, 1536:1600], func=AF.Tanh), reads=[b_PM[i]], writes=[b_th])
                    p.op("act", lambda e: e.copy(out=th[:, 64:128], in_=pm[:, 1600:1664]), reads=[b_PM[i]], writes=[b_th])
                    p.op("act", lambda e: e.activation(out=th[:, 128:256], in_=pm[:, 1664:1792], func=AF.Sigmoid), reads=[b_PM[i]], writes=[b_th])
                    tbk0, b_tbk0 = tbank()
                    for k in range(2):
                        p.op("pe", lambda e, k=k: e.transpose(out=tbk0[:, k, :], in_=th[:, k * 128:(k + 1) * 128], identity=ident[:]),
                             reads=[b_th, b_ident], writes=[b_tbk0], sig=(k == 1))
                    p.op("act", lambda e: e.copy(out=thT[:], in_=tbk0[:, 0:2, :]), reads=[b_tbk0], writes=[b_thT])
                    pw_, b_pw = bank()
                    p.op("pe", lambda e: e.matmul(pw_[:], lhsT=thT[0:64, 0, :], rhs=w2[0:64, 0, :], start=True, stop=True), reads=[b_thT, b_w2], writes=[b_pw])
                    pa_, b_pa = bank()
                    p.op("pe", lambda e: e.matmul(pa_[:], lhsT=thT[64:128, 0, :], rhs=a2[64:128, 0, :], start=True, stop=True), reads=[b_thT, b_a2], writes=[b_pa])
                    pg_, b_pg = bank()
                    p.op("pe", lambda e: e.matmul(pg_[:], lhsT=thT[:, 1, :], rhs=g2[:, 0, :], start=True, stop=True), reads=[b_thT, b_g2], writes=[b_pg])
                    p.op("dve", lambda e: e.tensor_tensor(out=lw[:], in0=pw_[:], in1=w0[:], op=ALU.add), reads=[b_pw, b_w0], writes=[b_lw])
                    p.op("act", lambda e: e.activation(out=lw[:], in_=lw[:], func=AF.Sigmoid), reads=[b_lw], writes=[b_lw])
                    p.op("dve", lambda e: e.tensor_scalar(out=lw[:], in0=lw[:], scalar1=NEGE, scalar2=None, op0=ALU.mult), reads=[b_lw], writes=[b_lw])
                    p.op("dve", lambda e: e.tensor_tensor(out=alr[:], in0=pa_[:], in1=a0[:], op=ALU.add), reads=[b_pa, b_a0], writes=[b_alr])
                    p.op("act", lambda e: e.activation(out=alr[:], in_=alr[:], func=AF.Sigmoid), reads=[b_alr], writes=[b_alr])
                    p.op("act", lambda e: e.copy(out=gg[i][:], in_=pg_[:]), reads=[b_pg], writes=[b_gg[i]])
                    if RSUB < 2:
                        return
                    p.op("dve", lambda e: e.tensor_tensor(out=kk[:], in0=k_, in1=k_k[:], op=ALU.mult), reads=[b_PM[i], b_kk_], writes=[b_kk])
                    p.op("dve", lambda e: e.tensor_tensor(out=tmp[:], in0=kk[:], in1=kk[:], op=ALU.mult), reads=[b_kk], writes=[b_tmp])
                    p.op("dve", lambda e: e.tensor_reduce(out=st8[:], in_=tmp[:].rearrange("p (h c) -> p h c", c=64), axis=AX.X, op=ALU.add), reads=[b_tmp], writes=[b_st8])
                    p.op("act", lambda e: e.activation(out=st8[:], in_=st8[:], func=AF.Sqrt), reads=[b_st8], writes=[b_st8])
                    p.op("dve", lambda e: e.tensor_scalar(out=st8[:], in0=st8[:], scalar1=1e-12, scalar2=None, op0=ALU.max), reads=[b_st8], writes=[b_st8])
                    p.op("dve", lambda e: e.reciprocal(out=st8[:], in_=st8[:]), reads=[b_st8], writes=[b_st8])
                    p.op("dve", lambda e: e.tensor_tensor(out=V3(kk), in0=V3(kk), in1=B8(st8), op=ALU.mult), reads=[b_kk, b_st8], writes=[b_kk])
                    p.op("dve", lambda e: e.scalar_tensor_tensor(out=tmp[:], in0=alr[:], scalar=-1.0, in1=k_a[:], op0=ALU.add, op1=ALU.mult),
                         reads=[b_alr, b_ka_], writes=[b_tmp])
                    p.op("dve", lambda e: e.scalar_tensor_tensor(out=k2[i][:], in0=tmp[:], scalar=1.0, in1=k_, op0=ALU.add, op1=ALU.mult),
                         reads=[b_tmp, b_PM[i]], writes=[b_k2[i]])
                    p.op("dve", lambda e: e.tensor_tensor(out=bt[:], in0=kk[:], in1=alr[:], op=ALU.mult), reads=[b_kk, b_alr], writes=[b_bt])
                    if RSUB < 3:
                        return
                    pc_, b_pc = bank()
                    p.op("pe", lambda e: e.matmul(pc_[:], lhsT=triU[:], rhs=lw[:], start=True, stop=True), reads=[b_msk, b_lw], writes=[b_pc])
                    ptot, b_ptot = bank()
                    p.op("pe", lambda e: e.matmul(ptot[:], lhsT=ones_f[:], rhs=lw[:], start=True, stop=True), reads=[b_ones, b_lw], writes=[b_ptot])
                    pgc, b_pgc = bank()
                    for hd in range(8):
                        p.op("pe", lambda e, hd=hd: e.matmul(pgc[0:64, hd:hd + 1], lhsT=lw[:, hd * 64:(hd + 1) * 64], rhs=ones_f[:, 0:1], start=True, stop=True),
                             reads=[b_lw, b_ones], writes=[b_pgc], sig=(hd == 7))
                    p.op("act", lambda e: e.activation(out=gC[i][:], in_=pgc[0:64, 0:8], func=AF.Exp), reads=[b_pgc], writes=[b_gC[i]])
                    p.op("act", lambda e: e.copy(out=cumS[:], in_=pc_[:]), reads=[b_pc], writes=[b_cumS])
                    p.op("act", lambda e: e.activation(out=eC[:], in_=cumS[:], func=AF.Exp), reads=[b_cumS], writes=[b_eC])
                    p.op("act", lambda e: e.activation(out=eCi[:], in_=cumS[:], func=AF.Exp, scale=-1.0), reads=[b_cumS], writes=[b_eCi])
                    p.op("dve", lambda e: e.tensor_tensor(out=tmp[:], in0=cumS[:], in1=lw[:], op=ALU.subtract), reads=[b_cumS, b_lw], writes=[b_tmp])
                    p.op("act", lambda e: e.activation(out=eCx[:], in_=tmp[:], func=AF.Exp), reads=[b_tmp], writes=[b_eCx])
                    p.op("dve", lambda e: e.tensor_tensor(out=tmp2[:], in0=ptot[:], in1=cumS[:], op=ALU.subtract), reads=[b_ptot, b_cumS], writes=[b_tmp2])
                    p.op("act", lambda e: e.activation(out=eD[:], in_=tmp2[:], func=AF.Exp), reads=[b_tmp2], writes=[b_eD])
                    if RSUB < 4:
                        return
                    p.op("dve", lambda e: e.scalar_tensor_tensor(out=X4[:, 0, :], in0=kk[:], scalar=-1.0, in1=eCx[:], op0=ALU.mult, op1=ALU.mult),
                         reads=[b_kk, b_eCx], writes=[b_X4[0]])
                    p.op("dve", lambda e: e.tensor_tensor(out=X4[:, 1, :], in0=r_, in1=eC[:], op=ALU.mult), reads=[b_PM[i], b_eC], writes=[b_X4[1]])
                    p.op("dve", lambda e: e.tensor_tensor(out=X4[:, 2, :], in0=bt[:], in1=eCi[:], op=ALU.mult), reads=[b_bt, b_eCi], writes=[b_X4[2]])
                    p.op("dve", lambda e: e.tensor_tensor(out=X4[:, 3, :], in0=k2[i][:], in1=eCi[:], op=ALU.mult), reads=[b_k2[i], b_eCi], writes=[b_X4[3]])
                    p.op("dve", lambda e: e.tensor_tensor(out=BH[i][:], in0=bt[:], in1=eD[:], op=ALU.mult), reads=[b_bt, b_eD], writes=[b_BH[i]])
                    p.op("dve", lambda e: e.tensor_tensor(out=KH[i][:], in0=k2[i][:], in1=eD[:], op=ALU.mult), reads=[b_k2[i], b_eD], writes=[b_KH[i]])
                    p.op("act", lambda e: e.copy(out=VB[i][:], in_=v_), reads=[b_PM[i]], writes=[b_VB[i]])
                    if RSUB < 5:
                        return
                    for hd in range(8):
                        tbk, b_tbk = tbank()
                        for x in range(4):
                            p.op("pe", lambda e, hd=hd, x=x, tbk=tbk: e.transpose(out=tbk[0:64, x, :], in_=X4[:, x, hd * 64:(hd + 1) * 64], identity=ident[:]),
                                 reads=[b_X4[x], b_ident], writes=[b_tbk], sig=(x == 3))
                        if hd % 2 == 0:
                            p.op("act", lambda e, hd=hd, tbk=tbk: e.copy(out=CM[i][:, hd, :, :], in_=tbk[0:64, 0:4, :]), reads=[b_tbk], writes=[b_CM[i]])
                        else:
                            p.op("dve", lambda e, hd=hd, tbk=tbk: e.tensor_copy(out=CM[i][:, hd, :, :], in_=tbk[0:64, 0:4, :]), reads=[b_tbk], writes=[b_CM[i]])
                    if RSUB < 6:
                        return
                    for hd in range(8):
                        pm_, b_pm = bank()
                        ar = CM[i][:, hd, :, :].rearrange("p x t -> p (x t)")[:, 0:256]
                        p.op("pe", lambda e, pm_=pm_, hd=hd, ar=ar: e.matmul(pm_[:, 0:256], lhsT=CM[i][:, hd, 2, :], rhs=ar, start=True, stop=True),
                             reads=[b_CM[i]], writes=[b_pm], sig=False)
                        p.op("pe", lambda e, pm_=pm_, hd=hd, ar=ar: e.matmul(pm_[:, 256:512], lhsT=CM[i][:, hd, 3, :], rhs=ar, start=True, stop=True),
                             reads=[b_CM[i]], writes=[b_pm])
                        if os.environ.get('RW_M') == '0':
                            continue
                        p.op("dve", lambda e, pm_=pm_, hd=hd: e.tensor_tensor(out=MM[i][:, hd, :, :], in0=pm_[:].rearrange("p (a b) -> p a b", b=256),
                                                                          in1=mSI[:].unsqueeze(1).to_broadcast([128, 2, 256]), op=ALU.mult),
                             reads=[b_pm, b_msk], writes=[b_MM[i]])
                    if os.environ.get('RW_M') == '1':
                        return
                    for hg in range(2):
                        pp_, b_pp_ = bank()
                        for hq in range(4):
                            hd = hg * 4 + hq
                            p.op("pe", lambda e, pp_=pp_, hd=hd, hq=hq: e.matmul(pp_[:, hq * 128:(hq + 1) * 128], lhsT=CM[i][:, hd, 0, :], rhs=CM[i][:, hd, 2, :],
                                                                                   start=True, stop=True),
                                 reads=[b_CM[i]], writes=[b_pp_], sig=(hq == 3))
                        p.op("dve", lambda e, pp_=pp_, hg=hg: e.tensor_tensor(out=PTp[0][:, hg * 4:hg * 4 + 4, :], in0=pp_[:].rearrange("p (h c) -> p h c", c=128),
                                                                          in1=mSL[:].unsqueeze(1).to_broadcast([128, 4, 128]), op=ALU.mult),
                             reads=[b_pp_, b_msk], writes=[b_PTp[0]])
                    if RSUB < 7:
                        return
                    p.op("act", lambda e: e.copy(out=Pp[0][:], in_=MM[i][:, :, 0, 0:128]), reads=[b_MM[i]], writes=[b_Pp[0]])
                    p.op("dve", lambda e: e.tensor_tensor(out=Tp[0][:], in0=MM[i][:, :, 0, 0:128], in1=ident[:].unsqueeze(1).to_broadcast([128, 8, 128]), op=ALU.add),
                         reads=[b_MM[i], b_ident], writes=[b_Tp[0]])
                    cur = 0
                    for lvl in range(6):
                        nxt = 1 - cur
                        last = (lvl == 5)
                        for hg in range(2):
                            if not last:
                                p2, b_p2 = bank()
                            p2t, b_p2t = bank()
                            for hq in range(4):
                                hd = hg * 4 + hq
                                sl = slice(hq * 128, (hq + 1) * 128)
                                if not last:
                                    p.op("pe", lambda e, p2=p2, hd=hd, sl=sl, cur=cur: e.matmul(p2[:, sl], lhsT=PTp[cur][:, hd, :], rhs=Pp[cur][:, hd, :], start=True, stop=True),
                                         reads=[b_PTp[cur], b_Pp[cur]], writes=[b_p2], sig=(hq == 3))
                                p.op("pe", lambda e, p2t=p2t, hd=hd, sl=sl, cur=cur: e.matmul(p2t[:, sl], lhsT=Pp[cur][:, hd, :], rhs=PTp[cur][:, hd, :], start=True, stop=True),
                                     reads=[b_PTp[cur], b_Pp[cur]], writes=[b_p2t], sig=(hq == 3))
                            hs = slice(hg * 4, hg * 4 + 4)
                            if not last:
                                p.op("act", lambda e, p2=p2, hs=hs, nxt=nxt: e.copy(out=Pp[nxt][:, hs, :], in_=p2[:].rearrange("p (h c) -> p h c", c=128)),
                                     reads=[b_p2], writes=[b_Pp[nxt]])
                            p.op("dve", lambda e, p2t=p2t, hs=hs, nxt=nxt: e.tensor_copy(out=PTp[nxt][:, hs, :], in_=p2t[:].rearrange("p (h c) -> p h c", c=128)),
                                 reads=[b_p2t], writes=[b_PTp[nxt]])
                            ptu, b_ptu = bank()
                            for hq in range(4):
                                hd = hg * 4 + hq
                                sl = slice(hq * 128, (hq + 1) * 128)
                                p.op("pe", lambda e, ptu=ptu, hd=hd, sl=sl, cur=cur, nxt=nxt: e.matmul(ptu[:, sl], lhsT=PTp[nxt][:, hd, :], rhs=Tp[cur][:, hd, :], start=True, stop=True),
                                     reads=[b_PTp[nxt], b_Tp[cur]], writes=[b_ptu], sig=(hq == 3))
                            dstT = Tf[i] if last else Tp[nxt]
                            b_dstT = b_Tf[i] if last else b_Tp[nxt]
                            p.op("dve", lambda e, ptu=ptu, hs=hs, cur=cur, dstT=dstT: e.tensor_tensor(out=dstT[:, hs, :], in0=ptu[:].rearrange("p (h c) -> p h c", c=128),
                                                                                                 in1=Tp[cur][:, hs, :], op=ALU.add),
                                 reads=[b_ptu, b_Tp[cur]], writes=[b_dstT])
                        cur = nxt

                def chain(t):
                    i = t % 2
                    if RSUB < 8:
                        return
                    pw_, b_pw = bank()
                    for hd in range(8):
                        hp, h2 = hd // 2, hd % 2
                        pr = slice(h2 * 64, h2 * 64 + 64)
                        cs_ = slice(hd * 64, hd * 64 + 64)
                        p.op("pe", lambda e, hd=hd, cs_=cs_: e.matmul(pw_[:, cs_], lhsT=MM[i][:, hd, 1, 0:128], rhs=VB[i][:, cs_], start=True, stop=False),
                             reads=[b_MM[i], b_VB[i]], writes=[b_pw], sig=False)
                        p.op("pe", lambda e, hd=hd, cs_=cs_: e.matmul(pw_[:, cs_], lhsT=CM[i][:, hd, 0, :], rhs=Sb[:, hd, :], start=False, stop=True),
                             reads=[b_CM[i], b_Sb], writes=[b_pw], sig=(hd == 7))
                    p.op("act", lambda e: e.copy(out=Ws[:], in_=pw_[:]), reads=[b_pw], writes=[b_Ws])
                    pu_, b_pu = bank()
                    for hd in range(8):
                        cs_ = slice(hd * 64, hd * 64 + 64)
                        p.op("pe", lambda e, hd=hd, cs_=cs_: e.matmul(pu_[:, cs_], lhsT=Tf[i][:, hd, :], rhs=Ws[:, cs_], start=True, stop=True),
                             reads=[b_Tf[i], b_Ws], writes=[b_pu], sig=(hd == 7))
                    p.op("dve", lambda e: e.tensor_copy(out=Us[:], in_=pu_[:]), reads=[b_pu], writes=[b_Us])
                    py_, b_py = bank()
                    for hd in range(8):
                        hp, h2 = hd // 2, hd % 2
                        pr = slice(h2 * 64, h2 * 64 + 64)
                        cs_ = slice(hd * 64, hd * 64 + 64)
                        p.op("pe", lambda e, hd=hd, cs_=cs_: e.matmul(py_[:, cs_], lhsT=MM[i][:, hd, 1, 128:256], rhs=VB[i][:, cs_], start=True, stop=False),
                             reads=[b_MM[i], b_VB[i]], writes=[b_py], sig=False)
                        p.op("pe", lambda e, hd=hd, cs_=cs_: e.matmul(py_[:, cs_], lhsT=CM[i][:, hd, 1, :], rhs=Sb[:, hd, :], start=False, stop=False),
                             reads=[b_CM[i], b_Sb], writes=[b_py], sig=False)
                        p.op("pe", lambda e, hd=hd, cs_=cs_: e.matmul(py_[:, cs_], lhsT=MM[i][:, hd, 0, 128:256], rhs=Us[:, cs_], start=False, stop=True),
                             reads=[b_MM[i], b_Us], writes=[b_py], sig=(hd == 7))
                    pS_, b_pS = bank()
                    for hd in range(8):
                        sl = slice(hd * 64, (hd + 1) * 64)
                        p.op("pe", lambda e, sl=sl: e.matmul(pS_[0:64, sl], lhsT=BH[i][:, sl], rhs=Us[:, sl], start=True, stop=False),
                             reads=[b_BH[i], b_Us], writes=[b_pS], sig=False)
                        p.op("pe", lambda e, sl=sl: e.matmul(pS_[0:64, sl], lhsT=KH[i][:, sl], rhs=VB[i][:, sl], start=False, stop=True),
                             reads=[b_KH[i], b_VB[i]], writes=[b_pS], sig=(hd == 7))
                    p.op("act", lambda e: e.copy(out=yt[:], in_=py_[:]), reads=[b_py], writes=[b_yt])
                    p.op("dve", lambda e: e.tensor_tensor(out=Sst[:], in0=Sst[:], in1=gC[i][:].unsqueeze(2).to_broadcast([64, 8, 64]), op=ALU.mult),
                         reads=[b_S, b_gC[i]], writes=[b_S])
                    p.op("dve", lambda e: e.tensor_tensor(out=Sst[:], in0=Sst[:], in1=pS_[0:64, :].rearrange("p (h c) -> p h c", c=64), op=ALU.add),
                         reads=[b_S, b_pS], writes=[b_S])
                    p.op("act", lambda e: e.copy(out=Sb[:], in_=Sst[:]), reads=[b_S], writes=[b_Sb])

                def post(t):
                    i = t % 2
                    if RSUB < 9:
                        return
                    pm = PM[i]
                    r_ = pm[:, 0:512]
                    v_ = pm[:, 1024:1536]
                    p.op("dve", lambda e: e.tensor_reduce(out=st8[:], in_=yt[:].rearrange("p (h c) -> p h c", c=64), axis=AX.X, op=ALU.add), reads=[b_yt], writes=[b_st8])
                    p.op("dve", lambda e: e.tensor_scalar(out=st8[:], in0=st8[:], scalar1=1.0 / 64, scalar2=None, op0=ALU.mult), reads=[b_st8], writes=[b_st8])
                    p.op("dve", lambda e: e.tensor_tensor(out=V3(yc), in0=V3(yt), in1=B8(st8), op=ALU.subtract), reads=[b_yt, b_st8], writes=[b_yc])
                    p.op("dve", lambda e: e.tensor_tensor(out=tmp[:], in0=yc[:], in1=yc[:], op=ALU.mult), reads=[b_yc], writes=[b_tmp])
                    p.op("dve", lambda e: e.tensor_reduce(out=st8[:], in_=tmp[:].rearrange("p (h c) -> p h c", c=64), axis=AX.X, op=ALU.add), reads=[b_tmp], writes=[b_st8])
                    p.op("act", lambda e: e.activation(out=st8[:], in_=st8[:], func=AF.Sqrt, scale=1.0 / 64, bias=g.eps_tiles[64e-5][:]), reads=[b_st8, b_eps], writes=[b_st8])
                    p.op("dve", lambda e: e.reciprocal(out=st8[:], in_=st8[:]), reads=[b_st8], writes=[b_st8])
                    p.op("dve", lambda e: e.tensor_tensor(out=V3(yc), in0=V3(yc), in1=B8(st8), op=ALU.mult), reads=[b_yc, b_st8], writes=[b_yc])
                    p.op("dve", lambda e: e.tensor_tensor(out=yc[:], in0=yc[:], in1=ln_w[:], op=ALU.mult), reads=[b_yc, b_lnw], writes=[b_yc])
                    p.op("dve", lambda e: e.tensor_tensor(out=yc[:], in0=yc[:], in1=ln_b[:], op=ALU.add), reads=[b_yc, b_lnb], writes=[b_yc])
                    p.op("dve", lambda e: e.tensor_tensor(out=tmp2[:], in0=r_, in1=k2[i][:], op=ALU.mult), reads=[b_PM[i], b_k2[i]], writes=[b_tmp2])
                    p.op("dve", lambda e: e.tensor_tensor(out=tmp2[:], in0=tmp2[:], in1=r_k[:], op=ALU.mult), reads=[b_tmp2, b_rk_], writes=[b_tmp2])
                    p.op("dve", lambda e: e.tensor_reduce(out=st8[:], in_=tmp2[:].rearrange("p (h c) -> p h c", c=64), axis=AX.X, op=ALU.add), reads=[b_tmp2], writes=[b_st8])
                    p.op("dve", lambda e: e.tensor_tensor(out=V3(tmp2), in0=v_.rearrange("p (h c) -> p h c", c=64), in1=B8(st8), op=ALU.mult),
                         reads=[b_PM[i], b_st8], writes=[b_tmp2])
                    p.op("dve", lambda e: e.tensor_tensor(out=yc[:], in0=yc[:], in1=tmp2[:], op=ALU.add), reads=[b_yc, b_tmp2], writes=[b_yc])
                    p.op("dve", lambda e: e.tensor_tensor(out=yo[:], in0=yc[:], in1=gg[i][:], op=ALU.mult), reads=[b_yc, b_gg[i]], writes=[b_yo])
                    tbk, b_tbk = tbank()
                    for k in range(4):
                        p.op("pe", lambda e, k=k: e.transpose(out=tbk[:, k, :], in_=yo[:, k * 128:(k + 1) * 128], identity=ident[:]),
                             reads=[b_yo, b_ident], writes=[b_tbk], sig=(k == 3))
                    p.op("act", lambda e: e.copy(out=oTt[i][:], in_=tbk[:, 0:4, :]), reads=[b_tbk], writes=[b_oTt[i]])
                    p.dma("sync", lambda e: e.dma_start(out=oT["rwkv"][:, t * 128:(t + 1) * 128].rearrange("(k p) t -> p k t", p=128), in_=oTt[i][:]),
                          reads=[b_oTt[i]], writes=[B_oT["rwkv"][t // 4]])

                NTR = int(os.environ.get('RW_NT', NT))
                pre(0)
                for t in range(NTR):
                    if t + 1 < NTR:
                        pre(t + 1)
                    chain(t)
                    post(t)
                p.barrier()

        p.barrier()
        PH = {"proj": phase_proj, "rwkv": phase_rwkv, "mla": phase_mla, "diff": phase_diff, "merge": phase_merge, "ffn": phase_ffn}
        if phases is None:
            phases = [("consts", 0)]
            for l in range(DEPTH):
                phases += [(n, l) for n in ("proj", "rwkv", "mla", "diff", "merge", "ffn")]
            phases += [("final", 0)]
        for (n, l) in phases:
            if n == "consts":
                setup_attn_consts()
            elif n == "final":
                phase_final()
            else:
                PH[n](l)
        p.barrier()
        p.emit()
    return nc


def prep_inputs(inputs, b):
    f = np.float32
    m = {}
    m["x"] = np.ascontiguousarray(inputs["x"][b])
    pos = np.asarray(inputs["positions"][b]).astype(np.int32)
    m["pos_tm"] = np.ascontiguousarray(pos.reshape(NT, 128).T)
    m["pos_row"] = np.ascontiguousarray(pos[:256].reshape(1, 256))
    m["rel_bias"] = np.ascontiguousarray(np.asarray(inputs["rel_bias"], f).reshape(1, 128))
    for n in VEC_ROWS:
        m[n] = np.ascontiguousarray(np.asarray(inputs[n], f).reshape(DEPTH, VEC_LEN[n]))
    for n in MATS:
        m[n] = np.ascontiguousarray(np.asarray(inputs[n], f))
    m["b_gate_cm"] = np.ascontiguousarray(np.asarray(inputs["b_gate"], f).reshape(DEPTH, 24, 128).transpose(0, 2, 1))
    m["conv_w_cm"] = np.ascontiguousarray(np.asarray(inputs["ffn_conv_w"], f).reshape(DEPTH, 3, 44, 128).transpose(0, 3, 1, 2))
    m["conv_b_cm"] = np.ascontiguousarray(np.asarray(inputs["ffn_conv_b"], f).reshape(DEPTH, 44, 128).transpose(0, 2, 1))
    m["subln_cm"] = np.ascontiguousarray(np.asarray(inputs["diff_subln"], f).reshape(DEPTH, 128, 1))
    m["diff_lambda"] = np.ascontiguousarray(np.asarray(inputs["diff_lambda"], f).reshape(DEPTH, 1, 256))
    m["norm_final"] = np.ascontiguousarray(np.asarray(inputs["norm_final"], f).reshape(1, D))
    return m


_NC = {}


def kernel(**inputs):
    if "nc" not in _NC:
        _NC["nc"] = build()
    nc = _NC["nc"]
    in_maps = [prep_inputs(inputs, b) for b in range(8)]
    res = run_bass_kernel_spmd(nc, in_maps, core_ids=list(range(8)))
    return np.stack([np.asarray(r["out"], np.float32) for r in res.results], axis=0)
```
